# Optimizing a Trainium2 kernel written in Bass

```python
import math, functools
import jax, jax.numpy as jnp
from jax import lax
import numpy as np

D_MODEL = 1024
BATCH = 4
SEQ = 4096
DEPTH = 1
DEC_BATCH = 128
DEC_SEQ = 1
PAST_LEN = 2048
PAGE_SIZE = 128

HEAD_DIM = 128
N_HEADS = D_MODEL // HEAD_DIM
GDN_HEADS = N_HEADS // 2
ATT_HEADS = N_HEADS - GDN_HEADS
N_KV = 2
GDN_W = GDN_HEADS * HEAD_DIM
ATT_W = ATT_HEADS * HEAD_DIM
KV_W = N_KV * HEAD_DIM
CONV_W = 4
CHUNK = 64
IDX_HEADS = 8
IDX_DIM = 64
TOPK_MAX = 256
QBLOCK = 128
ROPE_THETA = 500000.0
D_FF = ((8 * D_MODEL + 2) // 3 + 255) // 256 * 256
EPS = 1e-6
IN_SIZES = (3 * GDN_W, GDN_W, GDN_HEADS, GDN_HEADS, ATT_W, KV_W, KV_W, IDX_HEADS * IDX_DIM, IDX_DIM, IDX_HEADS)
IN_COLS = 3 * GDN_W + GDN_W + 2 * GDN_HEADS + ATT_W + 2 * KV_W + IDX_HEADS * IDX_DIM + IDX_DIM + IDX_HEADS

kernel_name = 'hybrid_gdn_dsa_adaln_step'


def _rmsnorm(x, g):
    xf = x.astype(jnp.float32)
    y = xf * lax.rsqrt(jnp.mean(xf * xf, axis=-1, keepdims=True) + EPS)
    return (y * g.astype(jnp.float32)).astype(x.dtype)


def _l2norm(x):
    return x * lax.rsqrt(jnp.sum(x * x, axis=-1, keepdims=True) + EPS)


def _partial_rope(x, pos):
    d = x.shape[-1]
    half = d // 8
    rot = 2 * half
    inv = ROPE_THETA ** (-jnp.arange(half, dtype=jnp.float32) / half)
    ang = pos.astype(jnp.float32)[:, None] * inv[None, :]
    cos = jnp.cos(ang)[:, None, :]
    sin = jnp.sin(ang)[:, None, :]
    xf = x.astype(jnp.float32)
    x1, x2 = xf[..., :half], xf[..., half:rot]
    out = jnp.concatenate([x1 * cos - x2 * sin, x2 * cos + x1 * sin, xf[..., rot:]], axis=-1)
    return out.astype(x.dtype)


def _modulation(c, w_ada, b_ada):
    mod = jax.nn.silu(c) @ w_ada + b_ada
    return [m[:, None, :] for m in jnp.split(mod, 6, axis=-1)]


def _delta_rule_chunked(q, k, v, g, beta, s0):
    b, t, h, _ = q.shape
    dv = v.shape[-1]
    n = t // CHUNK

    def blocks(a):
        return jnp.swapaxes(a, 1, 2).reshape((b, h, n, CHUNK) + a.shape[3:])

    qc, kc, vc, gc, bc = blocks(q), blocks(k), blocks(v), blocks(g), blocks(beta)
    gc = jnp.cumsum(gc, axis=-1)
    causal = jnp.tril(jnp.ones((CHUNK, CHUNK), bool))
    strict = jnp.tril(jnp.ones((CHUNK, CHUNK), bool), -1)
    decay = jnp.exp(jnp.where(causal, gc[..., :, None] - gc[..., None, :], -jnp.inf))
    kb = kc * bc[..., None]
    lower = jnp.where(strict, jnp.einsum('bhncd,bhnsd->bhncs', kb, kc) * decay, 0.0)
    m = lower + jnp.eye(CHUNK, dtype=q.dtype)
    u = lax.linalg.triangular_solve(m, vc * bc[..., None], left_side=True, lower=True, unit_diagonal=True)
    w = lax.linalg.triangular_solve(m, kb * jnp.exp(gc)[..., None], left_side=True, lower=True, unit_diagonal=True)
    intra = jnp.einsum('bhncd,bhnsd->bhncs', qc, kc) * decay

    def step(s, xs):
        q_i, k_i, u_i, w_i, a_i, g_i = xs
        v_new = u_i - jnp.einsum('bhcd,bhde->bhce', w_i, s)
        o = (jnp.einsum('bhcd,bhde->bhce', q_i * jnp.exp(g_i)[..., None], s)
             + jnp.einsum('bhcs,bhse->bhce', a_i, v_new))
        g_last = g_i[..., -1]
        s = (s * jnp.exp(g_last)[..., None, None]
             + jnp.einsum('bhcd,bhce->bhde', k_i * jnp.exp(g_last[..., None] - g_i)[..., None], v_new))
        return s, o

    xs = tuple(jnp.moveaxis(a, 2, 0) for a in (qc, kc, u, w, intra, gc))
    s, o = lax.scan(step, s0, xs)
    o = jnp.transpose(o, (1, 0, 3, 2, 4)).reshape(b, t, h, dv)
    return o, s


def _delta_rule_recurrent(q, k, v, g, beta, s0):
    def step(s, xs):
        q_t, k_t, v_t, g_t, b_t = xs
        s = s * jnp.exp(g_t)[..., None, None]
        delta = (v_t - jnp.einsum('bhd,bhde->bhe', k_t, s)) * b_t[..., None]
        s = s + jnp.einsum('bhd,bhe->bhde', k_t, delta)
        return s, jnp.einsum('bhd,bhde->bhe', q_t, s)

    xs = tuple(jnp.moveaxis(a, 1, 0) for a in (q, k, v, g, beta))
    s, o = lax.scan(step, s0, xs)
    return jnp.moveaxis(o, 0, 1), s


def _index_scores(iq, ik, iw):
    dots = jnp.einsum('bqhd,bsd->bqhs', iq.astype(jnp.float32), ik.astype(jnp.float32))
    return jnp.einsum('bqh,bqhs->bqs', iw.astype(jnp.float32), jax.nn.relu(dots))


def _attend_selected(q, ks, vs, valid):
    b, nq, h, d = q.shape
    qg = q.reshape(b, nq, N_KV, h // N_KV, d)
    s = jnp.einsum('bqgrd,bqkgd->bqgrk', qg, ks).astype(jnp.float32) * d ** -0.5
    s = jnp.where(valid[:, :, None, None, :], s, -jnp.inf)
    p = jax.nn.softmax(s, axis=-1).astype(vs.dtype)
    return jnp.einsum('bqgrk,bqkgd->bqgrd', p, vs).reshape(b, nq, h * d)


def _sparse_attn_prompt(q, k, v, iq, ik, iw):
    b, t = q.shape[:2]
    topk = min(TOPK_MAX, t // 4)
    nblk = t // QBLOCK
    key_pos = jnp.arange(t)

    def to_blocks(a):
        return jnp.moveaxis(a.reshape((b, nblk, QBLOCK) + a.shape[2:]), 1, 0)

    def block(xs):
        q_b, iq_b, iw_b, t0 = xs
        qpos = t0 + jnp.arange(QBLOCK)
        score = _index_scores(iq_b, ik, iw_b)
        score = jnp.where(key_pos[None, None, :] <= qpos[None, :, None], score, -jnp.inf)
        _, idx = lax.top_k(score, topk)
        valid = idx <= qpos[None, :, None]
        ks = jax.vmap(lambda kk, ii: kk[ii])(k, idx)
        vs = jax.vmap(lambda vv, ii: vv[ii])(v, idx)
        return _attend_selected(q_b, ks, vs, valid)

    starts = jnp.arange(nblk, dtype=jnp.int32) * QBLOCK
    o = lax.map(block, (to_blocks(q), to_blocks(iq), to_blocks(iw), starts))
    return jnp.moveaxis(o, 0, 1).reshape(b, t, -1)


def _sparse_attn_sample(q, k, v, iq, ik, iw, cache_k, cache_v, cache_ik, page_table):
    b, tn = q.shape[:2]
    page = cache_k.shape[1]
    past = page_table.shape[1] * page
    n_keys = past + tn
    topk = min(TOPK_MAX, n_keys // 4)
    ik_past = cache_ik[page_table].reshape(b, past, IDX_DIM)
    ik_all = jnp.concatenate([ik_past.astype(ik.dtype), ik], axis=1)
    qpos = past + jnp.arange(tn)
    score = _index_scores(iq, ik_all, iw)
    score = jnp.where(jnp.arange(n_keys)[None, None, :] <= qpos[None, :, None], score, -jnp.inf)
    _, idx = lax.top_k(score, topk)
    valid = idx <= qpos[None, :, None]
    pidx = jnp.minimum(idx, past - 1)
    phys = jnp.take_along_axis(page_table, (pidx // page).reshape(b, -1), axis=1).reshape(pidx.shape)
    off = pidx % page
    nidx = jnp.clip(idx - past, 0, tn - 1)
    in_past = (idx < past)[..., None, None]
    ks = jnp.where(in_past, cache_k[phys, off].astype(k.dtype), jax.vmap(lambda kk, ii: kk[ii])(k, nidx))
    vs = jnp.where(in_past, cache_v[phys, off].astype(v.dtype), jax.vmap(lambda vv, ii: vv[ii])(v, nidx))
    return _attend_selected(q, ks, vs, valid)


def _layer(x, c, pos, conv_prefix, s0, sparse_attn, delta_rule,
           w_ada, b_ada, g_norm1, w_in, w_conv, a_log, dt_bias, g_gdn_norm, w_out, g_norm2, w_ffn_in, w_ffn_out):
    b, t, _ = x.shape
    sh1, sc1, ga1, sh2, sc2, ga2 = _modulation(c, w_ada, b_ada)
    h = _rmsnorm(x, g_norm1) * (1.0 + sc1) + sh1
    cuts = [int(n) for n in np.cumsum(IN_SIZES)[:-1]]
    qkv, z, beta_in, a_in, aq, ak, av, iq, ik, iw = jnp.split(h @ w_in, cuts, axis=-1)
    xcat = jnp.concatenate([conv_prefix.astype(qkv.dtype), qkv], axis=1)
    conv = sum(w_conv[i] * xcat[:, i:i + t] for i in range(CONV_W))
    conv = jax.nn.silu(conv).astype(jnp.float32).reshape(b, t, 3 * GDN_HEADS, HEAD_DIM)
    gq, gk, gv = jnp.split(conv, 3, axis=2)
    gq = _l2norm(gq) * HEAD_DIM ** -0.5
    gk = _l2norm(gk)
    beta = jax.nn.sigmoid(beta_in.astype(jnp.float32))
    g = -jnp.exp(a_log.astype(jnp.float32)) * jax.nn.softplus(a_in.astype(jnp.float32) + dt_bias.astype(jnp.float32))
    o_gdn, s_new = delta_rule(gq, gk, gv, g, beta, s0.astype(jnp.float32))
    o_gdn = _rmsnorm(o_gdn, g_gdn_norm) * jax.nn.silu(z.astype(jnp.float32).reshape(b, t, GDN_HEADS, HEAD_DIM))
    aq = _partial_rope(aq.reshape(b, t, ATT_HEADS, HEAD_DIM), pos)
    ak = _partial_rope(ak.reshape(b, t, N_KV, HEAD_DIM), pos)
    av = av.reshape(b, t, N_KV, HEAD_DIM)
    iq = _partial_rope(iq.reshape(b, t, IDX_HEADS, IDX_DIM), pos)
    ik = _partial_rope(ik.reshape(b, t, 1, IDX_DIM), pos)[:, :, 0]
    iw = iw * IDX_HEADS ** -0.5
    o_att = sparse_attn(aq, ak, av, iq, ik, iw)
    mixed = jnp.concatenate([o_gdn.reshape(b, t, GDN_W).astype(x.dtype), o_att.astype(x.dtype)], axis=-1) @ w_out
    x = x + ga1 * mixed
    h2 = _rmsnorm(x, g_norm2) * (1.0 + sc2) + sh2
    gate, up = jnp.split(h2 @ w_ffn_in, 2, axis=-1)
    x = x + ga2 * ((jax.nn.silu(gate) * up) @ w_ffn_out)
    return x, (ak, av, ik, xcat[:, t:], s_new)


def _stack(states, i):
    return jnp.stack([s[i] for s in states], axis=0)


def setup_inputs(seed: int = 0) -> dict:
    key = jax.random.key(seed)
    ks = jax.random.split(key, 24)
    n_pages = PAST_LEN // PAGE_SIZE
    n_used = DEC_BATCH * n_pages
    n_phys = n_used + max(1, n_used // 4)
    f32 = jnp.float32

    def nrm(k, shape, scale):
        return jax.random.normal(k, shape, f32) * scale

    dt = jnp.exp(jax.random.uniform(ks[14], (DEPTH, GDN_HEADS), f32, math.log(1e-3), math.log(1e-1)))
    return {
        'x_prompt': nrm(ks[0], (BATCH, SEQ, D_MODEL), 1.0),
        'x_sample': nrm(ks[1], (DEC_BATCH, DEC_SEQ, D_MODEL), 1.0),
        'c_prompt': nrm(ks[2], (BATCH, D_MODEL), 1.0),
        'c_sample': nrm(ks[3], (DEC_BATCH, D_MODEL), 1.0),
        'cache_k': nrm(ks[4], (DEPTH, n_phys, PAGE_SIZE, N_KV, HEAD_DIM), 1.0),
        'cache_v': nrm(ks[5], (DEPTH, n_phys, PAGE_SIZE, N_KV, HEAD_DIM), 1.0),
        'cache_idx_k': nrm(ks[6], (DEPTH, n_phys, PAGE_SIZE, IDX_DIM), 1.0),
        'page_table': jax.random.permutation(ks[7], n_phys)[:n_used].reshape(DEC_BATCH, n_pages).astype(jnp.int32),
        'state_conv': nrm(ks[8], (DEPTH, DEC_BATCH, CONV_W - 1, 3 * GDN_W), 1.0),
        'state_ssm': nrm(ks[9], (DEPTH, DEC_BATCH, GDN_HEADS, HEAD_DIM, HEAD_DIM), 0.1),
        'w_ada': nrm(ks[10], (DEPTH, D_MODEL, 6 * D_MODEL), 0.2 * D_MODEL ** -0.5),
        'b_ada': nrm(ks[11], (DEPTH, 6 * D_MODEL), 0.01),
        'g_norm1': 1.0 + nrm(ks[12], (DEPTH, D_MODEL), 0.01),
        'w_in': nrm(ks[15], (DEPTH, D_MODEL, IN_COLS), D_MODEL ** -0.5),
        'w_conv': nrm(ks[16], (DEPTH, CONV_W, 3 * GDN_W), CONV_W ** -0.5),
        'a_log': jnp.log(jax.random.uniform(ks[13], (DEPTH, GDN_HEADS), f32, 1.0, 16.0)),
        'dt_bias': dt + jnp.log(-jnp.expm1(-dt)),
        'g_gdn_norm': 1.0 + nrm(ks[17], (DEPTH, HEAD_DIM), 0.01),
        'w_out': nrm(ks[18], (DEPTH, D_MODEL, D_MODEL), D_MODEL ** -0.5),
        'g_norm2': 1.0 + nrm(ks[19], (DEPTH, D_MODEL), 0.01),
        'w_ffn_in': nrm(ks[20], (DEPTH, D_MODEL, 2 * D_FF), D_MODEL ** -0.5),
        'w_ffn_out': nrm(ks[21], (DEPTH, D_FF, D_MODEL), D_FF ** -0.5),
        'g_final': 1.0 + nrm(ks[22], (D_MODEL,), 0.01),
    }


def reference(x_prompt, x_sample, c_prompt, c_sample, cache_k, cache_v, cache_idx_k, page_table, state_conv, state_ssm,
              w_ada, b_ada, g_norm1, w_in, w_conv, a_log, dt_bias, g_gdn_norm, w_out, g_norm2, w_ffn_in, w_ffn_out, g_final):
    bp, tp, _ = x_prompt.shape
    ts = x_sample.shape[1]
    past = page_table.shape[1] * cache_k.shape[2]
    pos_p = jnp.arange(tp, dtype=jnp.int32)
    pos_s = past + jnp.arange(ts, dtype=jnp.int32)
    xp, xs = x_prompt, x_sample
    new_p, new_s = [], []
    for l in range(DEPTH):
        lw = (w_ada[l], b_ada[l], g_norm1[l], w_in[l], w_conv[l], a_log[l], dt_bias[l], g_gdn_norm[l],
              w_out[l], g_norm2[l], w_ffn_in[l], w_ffn_out[l])
        conv0 = jnp.zeros((bp, CONV_W - 1, 3 * GDN_W), xp.dtype)
        ssm0 = jnp.zeros((bp, GDN_HEADS, HEAD_DIM, HEAD_DIM), jnp.float32)
        xp, st_p = _layer(xp, c_prompt, pos_p, conv0, ssm0, _sparse_attn_prompt, _delta_rule_chunked, *lw)
        attn_s = functools.partial(_sparse_attn_sample, cache_k=cache_k[l], cache_v=cache_v[l],
                                   cache_ik=cache_idx_k[l], page_table=page_table)
        xs, st_s = _layer(xs, c_sample, pos_s, state_conv[l], state_ssm[l], attn_s, _delta_rule_recurrent, *lw)
        new_p.append(st_p)
        new_s.append(st_s)
    y_prompt = _rmsnorm(xp, g_final)
    y_sample = _rmsnorm(xs, g_final)
    return (y_prompt, y_sample,
            _stack(new_p, 0), _stack(new_p, 1), _stack(new_p, 2), _stack(new_p, 3), _stack(new_p, 4),
            _stack(new_s, 0), _stack(new_s, 1), _stack(new_s, 2), _stack(new_s, 3), _stack(new_s, 4))
```

```python
import numpy as np
from contextlib import ExitStack
import concourse.bass as bass
import concourse.mybir as mybir
from concourse.bass_utils import run_bass_kernel_spmd

F32 = mybir.dt.float32
BF16 = mybir.dt.bfloat16
I32 = mybir.dt.int32
U32 = mybir.dt.uint32
AF = mybir.ActivationFunctionType
ALU = mybir.AluOpType
AX = mybir.AxisListType

import os
G_NT = int(os.environ.get("MK_GNT", "16"))
GSTOP = int(os.environ.get("MK_GSTOP", "99"))
NOS = int(os.environ.get("MK_NOS", "0"))
NOG = int(os.environ.get("MK_NOG", "0"))
ENGS = ["pe", "act", "dve", "pool", "sp"]
EPOCH = 12000
N_DMA_SEMS = 40
N_SW_SEMS = 12

D = 1024
T = 4096
HALF = 2048
NT = 16
NS = 16
INC = 3664
DFF = 2816
C_K, C_V, C_B, C_A, C_AK, C_AV, C_IK = 0, 512, 1024, 1028, 1032, 1288, 1544
N_OTH = 1608
C_Q, C_Z, C_AQ, C_IQ, C_IW = 1608, 2120, 2632, 3144, 3656
PERM = np.concatenate([np.arange(512, 1024), np.arange(1024, 1536), np.arange(2048, 2056),
                       np.arange(2568, 2824), np.arange(2824, 3080), np.arange(3592, 3656),
                       np.arange(0, 512), np.arange(1536, 2048), np.arange(2056, 2568),
                       np.arange(3080, 3592), np.arange(3656, 3664)])
GS_W = 2056
TABW = 256


class Prog:
    def __init__(self, nc):
        self.nc = nc
        self.ops = {e: [] for e in ENGS}
        self.res = {}
        self.seed = []
        self.dma_cum = [0] * (N_DMA_SEMS + N_SW_SEMS)
        self.dma_rr = 0
        self.sw_rr = 0

    def _deps(self, r, w):
        deps = []
        for name in r:
            st = self.res.get(name)
            if st and st[0] is not None:
                deps.append(st[0])
        for name in w:
            st = self.res.get(name)
            if st:
                if st[0] is not None:
                    deps.append(st[0])
                deps.extend(st[1])
            else:
                deps.extend(self.seed)
        return deps

    @staticmethod
    def _tkey(t):
        return (t[0], t[1])

    def _commit(self, tok, r, w):
        k = self._tkey(tok)
        for name in r:
            st = self.res.setdefault(name, [None, []])
            if not os.environ.get("MK_NOPRUNE"):
                st[1] = [t for t in st[1] if self._tkey(t) != k]
            st[1].append(tok)
        for name in w:
            self.res[name] = [tok, []]

    def op(self, eng, fn, r=(), w=()):
        deps = self._deps(r, w)
        tok = ("e", eng, len(self.ops[eng]))
        self.ops[eng].append({"deps": deps, "fn": fn, "dma": None, "sig": False})
        self._commit(tok, r, w)
        return tok

    def dma_raw(self, q, fn, r=(), w=(), sw=False):
        deps = self._deps(r, w)
        if sw:
            s = N_DMA_SEMS + self.sw_rr
            self.sw_rr = (self.sw_rr + 1) % N_SW_SEMS
        else:
            s = self.dma_rr
            self.dma_rr = (self.dma_rr + 1) % N_DMA_SEMS
        if self.dma_cum[s] > 0:
            deps.append(("d", s, self.dma_cum[s]))
        self.dma_cum[s] += 16
        tok = ("d", s, self.dma_cum[s])
        self.ops[q].append({"deps": deps, "fn": fn, "dma": (s, self.dma_cum[s]), "sig": False})
        self._commit(tok, r, w)
        return tok

    def dma(self, q, out, in_, r=(), w=(), **kw):
        return self.dma_raw(q, lambda e, out=out, in_=in_, kw=kw: e.dma_start(out=out, in_=in_, **kw), r, w)

    def seed_after_staging(self, names=("wst0", "wst1")):
        toks = list(self.seed)
        for n in names:
            st = self.res.get(n)
            if st:
                if st[0] is not None:
                    toks.append(st[0])
                toks.extend(st[1])
        self.seed = toks

    def barrier(self):
        deps_all = []
        for st in self.res.values():
            if st[0] is not None:
                deps_all.append(st[0])
            deps_all.extend(st[1])
        best = {}
        for t in ([] if os.environ.get("MK_NOPRUNE") else deps_all):
            k = self._tkey(t)
            if k not in best or best[k][2] < t[2]:
                best[k] = t
        for e in ENGS:
            for i in range(len(self.ops[e]) - 1, -1, -1):
                if self.ops[e][i]["dma"] is None:
                    best[("e", e)] = ("e", e, i)
                    break
        if not os.environ.get("MK_NOPRUNE"):
            deps_all = list(best.values())
        toks = []
        for e in ENGS:
            toks.append(("e", e, len(self.ops[e])))
            self.ops[e].append({"deps": list(deps_all), "fn": None, "dma": None, "sig": False})
        for e in ENGS:
            self.ops[e].append({"deps": list(toks), "fn": None, "dma": None, "sig": False})
        self.res = {}
        self.seed = []

    def build(self, ctx):
        nc = self.nc
        fin = [("d", s, c) for s, c in enumerate(self.dma_cum) if c > 0]
        self.ops["sp"].append({"deps": fin, "fn": None, "dma": None, "sig": False})
        for e in ENGS:
            for o in self.ops[e]:
                for d in o["deps"]:
                    if d[0] == "e":
                        if d[1] == "pe" and e == "pe":
                            continue
                        self.ops[d[1]][d[2]]["sig"] = True
        signo = {}
        nsig = {}
        for e in ENGS:
            c = 0
            for i, o in enumerate(self.ops[e]):
                if o["sig"]:
                    c += 1
                    signo[(e, i)] = c
            nsig[e] = c
        esem = {e: [ctx.enter_context(nc.semaphore(f"s_{e}_{k}")) for k in range(max(1, (nsig[e] + EPOCH - 1) // EPOCH))]
                for e in ENGS}
        dsem = [ctx.enter_context(nc.semaphore(f"s_dma_{k}")) for k in range(N_DMA_SEMS + N_SW_SEMS)]
        block = ctx.enter_context(nc.Block())
        eobj = {"pe": nc.tensor, "act": nc.scalar, "dve": nc.vector, "pool": nc.gpsimd, "sp": nc.sync}

        self.trace = {e: [] for e in ENGS}

        def body_for(e):
            def body(eng):
                known = {}
                tr = self.trace[e]
                for i, o in enumerate(self.ops[e]):
                    need = {}
                    for d in o["deps"]:
                        if d[0] == "e":
                            if d[1] == "pe" and e == "pe":
                                continue
                            key, val = ("e", d[1]), signo[(d[1], d[2])]
                        else:
                            key, val = ("d", d[1]), d[2]
                        if known.get(key, 0) >= val:
                            continue
                        if need.get(key, 0) < val:
                            need[key] = val
                    for key, val in need.items():
                        if key[0] == "e":
                            ep = (val - 1) // EPOCH
                            eng.wait_ge(esem[key[1]][ep], val - ep * EPOCH)
                            tr.append(("w", (key[1], ep), val - ep * EPOCH))
                        else:
                            eng.wait_ge(dsem[key[1]], val)
                            tr.append(("w", ("d", key[1]), val))
                        known[key] = val
                    if o["fn"] is None:
                        if o["sig"]:
                            sn = signo[(e, i)]
                            ep = (sn - 1) // EPOCH
                            eng.nop().then_inc(esem[e][ep], 1)
                            tr.append(("i", (e, ep), 1))
                        continue
                    ins = o["fn"](eng)
                    if o["dma"] is not None:
                        ins.then_inc(dsem[o["dma"][0]], 16)
                        tr.append(("i", ("d", o["dma"][0]), 16))
                    elif o["sig"]:
                        sn = signo[(e, i)]
                        ep = (sn - 1) // EPOCH
                        ins.then_inc(esem[e][ep], 1)
                        tr.append(("i", (e, ep), 1))
            return body

        block.tensor(body_for("pe"))
        block.scalar(body_for("act"))
        block.vector(body_for("dve"))
        block.gpsimd(body_for("pool"))
        block.sync(body_for("sp"))


class Arena:
    def __init__(self, t, n):
        self.t, self.n, self.off, self.uid = t, n, 0, 0

    def reset(self, to=0):
        self.off = to

    def f32(self, cols):
        a = self.t[:, self.off:self.off + cols]
        self.off += cols
        assert self.off <= self.n, ("arena overflow", self.off, self.n)
        return a

    def bf16(self, cols):
        c32 = (cols + 1) // 2
        a = self.t[:, self.off:self.off + c32].bitcast(BF16)
        self.off += c32
        assert self.off <= self.n, ("arena overflow", self.off, self.n)
        return a


def build_program(stage=99):
    nc = bass.Bass("TRN2", target_bir_lowering=False)
    dt_in = lambda name, shape, dt=F32: nc.dram_tensor(name, list(shape), dt, kind="ExternalInput").ap()
    dt_out = lambda name, shape, dt=F32: nc.dram_tensor(name, list(shape), dt, kind="ExternalOutput").ap()
    dt_scr = lambda name, shape, dt=F32: nc.dram_tensor(name, list(shape), dt, kind="Internal").ap()

    I = {}
    I["x_own"] = dt_in("x_own", [HALF, D]); I["x_oth"] = dt_in("x_oth", [HALF, D])
    I["cin"] = dt_in("cin", [17, D]); I["xs"] = dt_in("xs", [NS, D])
    I["w_ada"] = dt_in("w_ada", [D, 6 * D]); I["b_ada"] = dt_in("b_ada", [1, 6 * D])
    I["g1"] = dt_in("g1", [1, D]); I["w_in"] = dt_in("w_in", [D, INC])
    I["w_conv"] = dt_in("w_conv", [4, 1536]); I["a_log"] = dt_in("a_log", [1, 4]); I["dt_bias"] = dt_in("dt_bias", [1, 4])
    I["g_gdn"] = dt_in("g_gdn", [1, 128]); I["w_out"] = dt_in("w_out", [D, D]); I["g2"] = dt_in("g2", [1, D])
    I["w_ffn_in"] = dt_in("w_ffn_in", [D, 2 * DFF]); I["w_ffn_out"] = dt_in("w_ffn_out", [DFF, D]); I["g_final"] = dt_in("g_final", [1, D])
    I["tab_own"] = dt_in("tab_own", [HALF, TABW]); I["tab_oth"] = dt_in("tab_oth", [HALF, TABW]); I["tab_s"] = dt_in("tab_s", [NS, TABW])
    I["flags"] = dt_in("flags", [128, 4]); I["ident"] = dt_in("ident", [128, 128])
    I["state_conv"] = dt_in("state_conv", [NS, 3, 1536])
    I["gconst"] = dt_in("gconst", [128, 1024]); I["wc_p"] = dt_in("wc_p", [4, 1536])
    I["state_ssm"] = dt_in("state_ssm", [NS, 4, 128, 128]); I["eye16"] = dt_in("eye16", [128, 256])
    I["caus"] = dt_in("caus", [128, 128])
    I["pt"] = dt_in("pt", [1, NS * 16], I32); I["tsel"] = dt_in("tsel", [128, 257])
    NPHYS = 2560
    if not os.environ.get('MK_NOTS'):
        I["cache_ik"] = dt_in("cache_ik", [NPHYS * 128, 64]); I["cache_k"] = dt_in("cache_k", [NPHYS * 128, 256]); I["cache_v"] = dt_in("cache_v", [NPHYS * 128, 256])

    O = {}
    O["y_own"] = dt_out("y_own", [HALF, D]); O["k_own"] = dt_out("k_own", [HALF, 256]); O["v_own"] = dt_out("v_own", [HALF, 256])
    O["ik_own"] = dt_out("ik_own", [HALF, 64]); O["conv_tail"] = dt_out("conv_tail", [3, 1536]); O["ssm_fin"] = dt_out("ssm_fin", [4, 128, 128])
    O["y_s"] = dt_out("y_s", [NS, D]); O["k_s"] = dt_out("k_s", [NS, 256]); O["v_s"] = dt_out("v_s", [NS, 256]); O["ik_s"] = dt_out("ik_s", [NS, 64])
    O["conv_s"] = dt_out("conv_s", [NS, 3, 1536]); O["ssm_s"] = dt_out("ssm_s", [NS, 4, 128, 128])

    S = {}
    S["mod"] = dt_scr("mod_scr", [17, 6 * D])
    S["gs"] = dt_scr("gs_scr", [2, 3 + HALF, GS_W])
    S["qT"] = dt_scr("qT_scr", [NT, 128, 4, 128], BF16)
    S["iqT"] = dt_scr("iqT_scr", [NT, 128, 4, 128], BF16)
    S["cT"] = dt_scr("cT_scr", [NT + 1, 128, 8, 128], BF16)
    S["x1"] = dt_scr("x1_scr", [NT + 1, 128, D])
    S["uT"] = dt_scr("uT_scr", [5, 128, 22, 512], BF16)
    S["ps"] = dt_scr("ps_scr", [NS, INC])

    ctx = ExitStack()
    with ctx:
        ctx.enter_context(nc.allow_low_precision(reason="bf16 matmul operands, fp32 accumulate"))
        P = Prog(nc)
        ARN = 52800
        arena_t = ctx.enter_context(nc.sbuf_tensor("arena", [128, ARN], F32))
        A = Arena(arena_t, ARN)
        pb = [ctx.enter_context(nc.psum_tensor(f"pb{i}", [128, 512], F32))[:, :] for i in range(8)]
        PB = [f"pb{i}" for i in range(8)]
        cnt = [0]

        def alt():
            cnt[0] += 1
            return "act" if cnt[0] % 2 else "dve"

        def evac(eng, out, in_, r, w):
            if eng == "act":
                P.op("act", lambda e: e.activation(out=out, in_=in_, func=AF.Copy), r=r, w=w)
            else:
                P.op(eng, lambda e: e.tensor_copy(out=out, in_=in_), r=r, w=w)

        ident = A.f32(128)
        flags = A.f32(4)
        P.dma("sp", ident, I["ident"], w=["ident"])
        P.dma("sp", flags, I["flags"], w=["flags"])
        mod_sb = A.f32(6 * D)
        persist0 = A.off

        cs = A.f32(D)
        csT = A.f32(8 * 17)
        csT3 = csT.rearrange("p (k m) -> p k m", k=8)
        ones = A.f32(128)
        bstage = A.f32(512)
        P.op("dve", lambda e: e.memset(ones, 1.0), w=["ones"])
        P.dma("sp", cs[0:17, :], I["cin"], w=["cs"])
        P.op("act", lambda e: e.activation(out=cs[0:17, :], in_=cs[0:17, :], func=AF.Silu), r=["cs"], w=["cs"])
        for k in range(8):
            P.op("pe", lambda e, k=k: e.transpose(out=pb[0][:, k * 17:(k + 1) * 17], in_=cs[0:17, k * 128:(k + 1) * 128], identity=ident[0:17, 0:17]),
                 r=["cs", "ident"], w=[PB[0]])
        evac("dve", csT, pb[0][:, 0:8 * 17], [PB[0]], ["csT"])
        wst = [A.f32(8 * 512), A.f32(8 * 512)]
        for cb in range(12):
            ws = wst[cb % 2]
            ws3 = ws.rearrange("p (k n) -> p k n", k=8)
            P.dma("sp", ws3, I["w_ada"][:, cb * 512:(cb + 1) * 512].rearrange("(k p) n -> p k n", p=128), w=[f"wst{cb % 2}"])
            P.dma("sp", bstage[0:1, :], I["b_ada"][:, cb * 512:(cb + 1) * 512], w=["bstage"])
            bank = 1 + cb % 2
            for k in range(8):
                P.op("pe", lambda e, k=k, ws3=ws3, bank=bank: e.matmul(out=pb[bank][0:17, :], lhsT=csT3[:, k, :], rhs=ws3[:, k, :], start=(k == 0), stop=False),
                     r=["csT", f"wst{cb % 2}"], w=[PB[bank]])
            P.op("pe", lambda e, bank=bank: e.matmul(out=pb[bank][0:17, :], lhsT=ones[0:1, 0:17], rhs=bstage[0:1, :], start=False, stop=True),
                 r=["ones", "bstage"], w=[PB[bank]])
            evac("act", mod_sb[0:17, cb * 512:(cb + 1) * 512], pb[bank][0:17, :], [PB[bank]], ["mod_sb"])
        P.dma("sp", S["mod"], mod_sb[0:17, :], r=["mod_sb"], w=["mod_scr"])
        A.reset(persist0)
        def alloc_att():
            KT = A.bf16(2 * T); ikT = A.bf16(T); VA = A.bf16(32 * 2 * 130); iwabs = A.f32(NT * 8); iwsgn = A.f32(NT * 8)
            ksq_ = A.f32(64); qsq_ = A.f32(64)
            return KT, ikT, VA, iwabs, iwsgn, ksq_, qsq_
        OLDALLOC = bool(os.environ.get("MK_OLDALLOC"))
        if not OLDALLOC:
            KT, ikT, VA, iwabs, iwsgn, ksq, qsq = alloc_att()
        persist1 = A.off
        a1_bc = A.f32(D); sh1_bc = A.f32(D)
        g1_bc = A.f32(D); a1_s = A.f32(D)
        P.dma("sp", g1_bc, I["g1"].to_broadcast([128, D]), w=["g1_bc"])
        P.dma("sp", a1_bc, S["mod"][16:17, D:2 * D].to_broadcast([128, D]), r=["mod_scr"], w=["a1_bc"])
        P.dma("sp", sh1_bc, S["mod"][16:17, 0:D].to_broadcast([128, D]), r=["mod_scr"], w=["sh1_bc"])
        P.op("dve", lambda e: e.scalar_tensor_tensor(out=a1_bc, in0=a1_bc, scalar=1.0, in1=g1_bc, op0=ALU.add, op1=ALU.mult),
             r=["a1_bc", "g1_bc"], w=["a1_bc"])
        P.op("dve", lambda e: e.scalar_tensor_tensor(out=a1_s[0:NS, :], in0=mod_sb[0:NS, D:2 * D], scalar=1.0, in1=g1_bc[0:NS, :], op0=ALU.add, op1=ALU.mult),
             r=["mod_sb", "g1_bc"], w=["a1_s"])
        persistA = A.off

        w_in_b = A.bf16(8 * INC)
        w_in3 = w_in_b.rearrange("p (k n) -> p k n", k=8)
        if OLDALLOC:
            KT, ikT, VA, iwabs, iwsgn, ksq, qsq = alloc_att()
        KT3 = KT.rearrange("p (g s) -> p g s", g=2)
        VA4 = VA.rearrange("p (t g d) -> p t g d", t=32, g=2)
        persistB = A.off
        wst = [A.f32(8 * 512), A.f32(8 * 512)]
        nblk = (INC + 511) // 512
        for cb in range(nblk):
            c0, c1 = cb * 512, min(INC, cb * 512 + 512)
            ws3 = wst[cb % 2].rearrange("p (k n) -> p k n", k=8)
            P.dma("sp", ws3[:, :, 0:c1 - c0], I["w_in"][:, c0:c1].rearrange("(k p) n -> p k n", p=128), w=[f"wst{cb % 2}"])
            eng = "pool" if cb % 2 else "dve"
            P.op(eng, lambda e, ws3=ws3, c0=c0, c1=c1: e.tensor_copy(out=w_in3[:, :, c0:c1], in_=ws3[:, :, 0:c1 - c0]),
                 r=[f"wst{cb % 2}"], w=["w_in_b"])
        P.seed_after_staging()
        A.reset(persistB)
        xt = [A.f32(D), A.f32(D)]
        hh = A.f32(D)
        sq = A.f32(D)
        hT = [A.bf16(8 * 128), A.bf16(8 * 128)]
        Psb_l = [A.f32(INC), A.f32(INC)]
        tab = [A.f32(TABW), A.f32(TABW)]
        small = A.f32(16)
        rt = [A.f32(128) for _ in range(4)]
        ikd = A.f32(128)
        tstage = [A.bf16(4 * 128), A.bf16(4 * 128)]
        zrow = A.f32(GS_W)
        P.op("pool", lambda e: e.memset(zrow[0:3, :], 0.0), w=["zrow"])
        P.dma("sp", S["gs"][0, 0:3, :], zrow[0:3, :], r=["zrow"], w=["gs_pre0"])

        def rope(Psb, PN, rows, base, H, Dh, half, tb, coff, soff, tname):
            xv = Psb[0:rows, base:base + H * Dh].rearrange("p (h d) -> p h d", d=Dh)
            x1, x2 = xv[:, :, 0:half], xv[:, :, half:2 * half]
            cosv = tb[0:rows, coff:coff + H * half].rearrange("p (h i) -> p h i", i=half)
            sinv = tb[0:rows, soff:soff + H * half].rearrange("p (h i) -> p h i", i=half)
            t = [r_[0:rows, 0:H * half].rearrange("p (h i) -> p h i", i=half) for r_ in rt]
            P.op("dve", lambda e: e.tensor_tensor(out=t[0], in0=x1, in1=cosv, op=ALU.mult), r=[PN, tname], w=["rt0"])
            P.op("pool", lambda e: e.tensor_tensor(out=t[1], in0=x2, in1=sinv, op=ALU.mult), r=[PN, tname], w=["rt1"])
            P.op("dve", lambda e: e.tensor_tensor(out=t[2], in0=x2, in1=cosv, op=ALU.mult), r=[PN, tname], w=["rt2"])
            P.op("pool", lambda e: e.tensor_tensor(out=t[3], in0=x1, in1=sinv, op=ALU.mult), r=[PN, tname], w=["rt3"])
            P.op("dve", lambda e: e.tensor_tensor(out=x1, in0=t[0], in1=t[1], op=ALU.subtract), r=["rt0", "rt1", PN], w=[PN])
            P.op("dve", lambda e: e.tensor_tensor(out=x2, in0=t[2], in1=t[3], op=ALU.add), r=["rt2", "rt3", PN], w=[PN])

        itc = [0]

        def proj_tile(mode, ti):
            it = itc[0]; itc[0] += 1
            rows = NS if mode == 2 else 128
            Psb = Psb_l[it % 2]; PN = f"Psb{it % 2}"
            xb = xt[it % 2]; xn = f"xt{it % 2}"
            tb = tab[it % 2]; tn = f"tab{it % 2}"
            hTb = hT[it % 2]; hTn = f"hT{it % 2}"
            hT3 = hTb.rearrange("p (k t) -> p k t", k=8)
            if mode == 2:
                xsrc, tsrc = I["xs"], I["tab_s"]
                a_t, a_n, sh_t, sh_n = a1_s, "a1_s", mod_sb[:, 0:D], "mod_sb"
                ncols = INC
            else:
                xsrc = (I["x_oth"] if mode == 0 else I["x_own"])[ti * 128:(ti + 1) * 128, :]
                tsrc = (I["tab_oth"] if mode == 0 else I["tab_own"])[ti * 128:(ti + 1) * 128, :]
                a_t, a_n, sh_t, sh_n = a1_bc, "a1_bc", sh1_bc, "sh1_bc"
                ncols = INC if mode == 1 else (C_Q + 512 if ti == NT - 1 else N_OTH)
            P.dma("sp", xb[0:rows, :], xsrc, w=[xn])
            P.dma("sp", tb[0:rows, :], tsrc, w=[tn])
            ss, rs = small[0:rows, 0:1], small[0:rows, 1:2]
            P.op("act", lambda e: e.activation(out=sq[0:rows, :], in_=xb[0:rows, :], func=AF.Square, accum_out=ss), r=[xn], w=["sq", "ss"])
            P.op("dve", lambda e: e.tensor_scalar(out=rs, in0=ss, scalar1=1.0 / D, scalar2=1e-6, op0=ALU.mult, op1=ALU.add), r=["ss"], w=["rs"])
            P.op("act", lambda e: e.activation(out=rs, in_=rs, func=AF.Sqrt), r=["rs"], w=["rs"])
            P.op("dve", lambda e: e.reciprocal(out=rs, in_=rs), r=["rs"], w=["rs"])
            P.op("dve", lambda e: e.scalar_tensor_tensor(out=hh[0:rows, :], in0=xb[0:rows, :], scalar=rs, in1=a_t[0:rows, :], op0=ALU.mult, op1=ALU.mult),
                 r=[xn, "rs", a_n], w=["hh"])
            P.op("pool", lambda e: e.tensor_tensor(out=hh[0:rows, :], in0=hh[0:rows, :], in1=sh_t[0:rows, :], op=ALU.add), r=["hh", sh_n], w=["hh"])
            for k in range(8):
                bank = k // 4
                P.op("pe", lambda e, k=k, bank=bank: e.transpose(out=pb[bank][:, (k % 4) * 128:(k % 4) * 128 + rows], in_=hh[0:rows, k * 128:(k + 1) * 128], identity=ident[0:rows, 0:rows]),
                     r=["hh", "ident"], w=[PB[bank]])
            if rows == 128:
                evac("act", hTb[:, 0:512], pb[0], [PB[0]], [hTn])
                evac("dve", hTb[:, 512:1024], pb[1], [PB[1]], [hTn])
            else:
                for half_ in range(2):
                    evac("act" if half_ == 0 else "dve", hT3[:, half_ * 4:(half_ + 1) * 4, 0:rows], pb[half_].rearrange("p (k t) -> p k t", k=4)[:, :, 0:rows], [PB[half_]], [hTn])
            nb = (ncols + 511) // 512
            for cb in range(nb):
                c0, c1 = cb * 512, min(ncols, cb * 512 + 512)
                bank = 2 + cb % 4
                for k in range(8):
                    P.op("pe", lambda e, k=k, bank=bank, c0=c0, c1=c1: e.matmul(out=pb[bank][0:rows, 0:c1 - c0], lhsT=hT3[:, k, 0:rows], rhs=w_in3[:, k, c0:c1], start=(k == 0), stop=(k == 7)),
                         r=[hTn, "w_in_b"], w=[PB[bank]])
                evac(alt(), Psb[0:rows, c0:c1], pb[bank][0:rows, 0:c1 - c0], [PB[bank]], [PN])
            if mode == 2:
                cvs = sq[0:rows, :]
                for j in range(2):
                    for hh_ in range(2):
                        P.dma("sp", sq[0:rows, 0:768], I["state_conv"][:, 1 + j, hh_ * 768:(hh_ + 1) * 768], w=["sq"])
                        P.dma("sp", O["conv_s"][:, j, hh_ * 768:(hh_ + 1) * 768], sq[0:rows, 0:768], r=["sq"])
                P.dma("sp", O["conv_s"][:, 2, 0:512], Psb[0:rows, C_Q:C_Q + 512], r=[PN])
                P.dma("sp", O["conv_s"][:, 2, 512:1536], Psb[0:rows, 0:1024], r=[PN])
            else:
                row0 = 3 + ti * 128
                P.dma("sp", S["gs"][mode, row0:row0 + 128, 0:1032], Psb[:, 0:1032], r=[PN], w=[f"gs{mode}_{ti}"])
                if mode == 1:
                    P.dma("sp", S["gs"][mode, row0:row0 + 128, 1032:2056], Psb[:, C_Q:C_Q + 1024], r=[PN], w=[f"gs{mode}_{ti}"])
                elif ti == NT - 1:
                    P.dma("sp", S["gs"][mode, row0:row0 + 128, 1032:1544], Psb[:, C_Q:C_Q + 512], r=[PN], w=[f"gs{mode}_{ti}"])
            rope(Psb, PN, rows, C_AK, 2, 128, 16, tb, 0, 64, tn)
            rope(Psb, PN, rows, C_IK, 1, 64, 8, tb, 128, 192, tn)
            if mode >= 1:
                rope(Psb, PN, rows, C_AQ, 4, 128, 16, tb, 0, 64, tn)
                rope(Psb, PN, rows, C_IQ, 8, 64, 8, tb, 128, 192, tn)
            if mode == 2:
                P.dma("sp", O["k_s"], Psb[0:rows, C_AK:C_AK + 256], r=[PN])
                P.dma("sp", O["v_s"], Psb[0:rows, C_AV:C_AV + 256], r=[PN])
                P.dma("sp", O["ik_s"], Psb[0:rows, C_IK:C_IK + 64], r=[PN])
                P.dma("sp", S["ps"], Psb[0:rows, :], r=[PN], w=["ps_scr"])
                return
            if mode == 1:
                P.dma("sp", O["k_own"][ti * 128:(ti + 1) * 128, :], Psb[:, C_AK:C_AK + 256], r=[PN])
                P.dma("sp", O["v_own"][ti * 128:(ti + 1) * 128, :], Psb[:, C_AV:C_AV + 256], r=[PN])
                P.dma("sp", O["ik_own"][ti * 128:(ti + 1) * 128, :], Psb[:, C_IK:C_IK + 64], r=[PN])
            slot = (0 if mode == 0 else 16) + ti
            for g in range(2):
                P.op("act", lambda e, g=g: e.activation(out=sq[:, 0:128], in_=Psb[:, C_AK + g * 128:C_AK + (g + 1) * 128], func=AF.Square, accum_out=ksq[:, slot * 2 + g:slot * 2 + g + 1]),
                     r=[PN, "a1_bc"], w=["sq", "ksq"])
            if mode == 1:
                for h in range(4):
                    P.op("act", lambda e, h=h: e.activation(out=sq[:, 0:128], in_=Psb[:, C_AQ + h * 128:C_AQ + (h + 1) * 128], func=AF.Square, accum_out=qsq[:, ti * 4 + h:ti * 4 + h + 1]),
                         r=[PN, "a1_bc"], w=["sq", "qsq"])
            P.op("dve", lambda e: e.tensor_copy(out=ikd[:, 0:64], in_=Psb[:, C_IK:C_IK + 64]), r=[PN], w=["ikd"])
            P.op("pool", lambda e: e.tensor_copy(out=ikd[:, 64:128], in_=Psb[:, C_IK:C_IK + 64]), r=[PN], w=["ikd"])
            for g in range(2):
                P.op("pe", lambda e, g=g: e.transpose(out=pb[6][:, g * 128:(g + 1) * 128], in_=Psb[:, C_AK + g * 128:C_AK + (g + 1) * 128], identity=ident),
                     r=[PN, "ident"], w=[PB[6]])
            P.op("pe", lambda e: e.transpose(out=pb[6][:, 256:384], in_=ikd, identity=ident), r=["ikd", "ident"], w=[PB[6]])
            for g in range(2):
                evac(alt(), KT3[:, g, slot * 128:(slot + 1) * 128], pb[6][:, g * 128:(g + 1) * 128], [PB[6]], ["KT"])
            evac(alt(), ikT[:, slot * 128:(slot + 1) * 128], pb[6][:, 256:384], [PB[6]], ["ikT"])
            P.op("pool", lambda e: e.memset(VA4[:, slot, :, 128:130], 1.0), r=["a1_bc"], w=["VA"])
            P.op("act", lambda e: e.activation(out=VA4[:, slot, :, 0:128], in_=Psb[:, C_AV:C_AV + 256].rearrange("p (g d) -> p g d", g=2), func=AF.Copy),
                 r=[PN], w=["VA"])
            if mode == 1:
                ts_ = tstage[0]; tsn = "tstage0"
                for h in range(4):
                    P.op("pe", lambda e, h=h: e.transpose(out=pb[7][:, h * 128:(h + 1) * 128], in_=Psb[:, C_AQ + h * 128:C_AQ + (h + 1) * 128], identity=ident),
                         r=[PN, "ident"], w=[PB[7]])
                evac(alt(), ts_, pb[7], [PB[7]], [tsn])
                P.dma("sp", S["qT"][ti], ts_.rearrange("p (h t) -> p h t", h=4), r=[tsn], w=[f"qT{ti}"])
                ts2 = tstage[1]; tsn2 = "tstage1"
                for h in range(4):
                    P.op("pe", lambda e, h=h: e.transpose(out=pb[7][:, h * 128:(h + 1) * 128], in_=Psb[:, C_IQ + h * 128:C_IQ + (h + 1) * 128], identity=ident),
                         r=[PN, "ident"], w=[PB[7]])
                evac(alt(), ts2, pb[7], [PB[7]], [tsn2])
                P.dma("sp", S["iqT"][ti], ts2.rearrange("p (h t) -> p h t", h=4), r=[tsn2], w=[f"iqT{ti}"])
                P.op("act", lambda e: e.activation(out=iwabs[:, ti * 8:(ti + 1) * 8], in_=Psb[:, C_IW:C_IW + 8], func=AF.Abs, scale=8 ** -0.5),
                     r=[PN], w=["iwabs"])
                P.op("act", lambda e: e.activation(out=iwsgn[:, ti * 8:(ti + 1) * 8], in_=Psb[:, C_IW:C_IW + 8], func=AF.Sign), r=[PN], w=["iwsgn"])
                if ti == NT - 1:
                    P.dma("sp", O["conv_tail"][:, 0:512], S["gs"][1, 3 + HALF - 3:3 + HALF, 1032:1544], r=[f"gs1_{ti}"])
                    P.dma("sp", O["conv_tail"][:, 512:1536], S["gs"][1, 3 + HALF - 3:3 + HALF, 0:1024], r=[f"gs1_{ti}"])

        NA0 = int(os.environ.get("MK_NA0", NT)); NA1 = int(os.environ.get("MK_NA1", NT))
        for ti in range(NT - NA0, NT):
            proj_tile(0, ti)
        for ti in range(NA1):
            proj_tile(1, ti)
        if not NOS:
            proj_tile(2, 0)

        if not NOG:
            P.barrier()
            A.reset(persist1)
            cst = A.f32(1024)
            TRIU, ONESM, MASKL, MASKU = [cst[:, i * 128:(i + 1) * 128] for i in range(4)]
            ident4 = cst[:, 512:1024]
            P.dma("sp", cst, I["gconst"], w=["cst"])
            wc = [A.f32(1536) for _ in range(4)]
            for i in range(4):
                P.dma("sp", wc[i], I["wc_p"][i:i + 1, :].to_broadcast([128, 1536]), w=[f"wc{i}"])
            dtb = A.f32(4); negA = A.f32(4); ggd = A.f32(512)
            P.dma("sp", dtb, I["dt_bias"].to_broadcast([128, 4]), w=["dtb"])
            P.dma("sp", negA, I["a_log"].to_broadcast([128, 4]), w=["negA"])
            for h in range(4):
                P.dma("sp", ggd[:, h * 128:(h + 1) * 128], I["g_gdn"].to_broadcast([128, 128]), w=["ggd"])
            P.op("act", lambda e: e.activation(out=negA, in_=negA, func=AF.Exp), r=["negA"], w=["negA"])
            P.op("dve", lambda e: e.tensor_scalar(out=negA, in0=negA, scalar1=-1.0, scalar2=None, op0=ALU.mult), r=["negA"], w=["negA"])
            gs_base = A.off
            Sst = A.f32(512)
            P.op("dve", lambda e: e.memset(Sst, 0.0), w=["S"])
            X = [A.f32(GS_W) for _ in range(4)]
            cv = A.f32(1536); tmpa = A.f32(1536); tmpb = A.f32(1536)
            sm = A.f32(64)
            kn = A.f32(512); kt = A.f32(512); vb = A.f32(512); qn = A.f32(512); qt = A.f32(512)
            knT = A.f32(512); qnT = A.f32(512); qtT = A.f32(512)
            Dg = A.f32(512); dec = A.f32(512); decT = A.f32(512)
            Pbuf = [A.f32(512), A.f32(512)]; PTbuf = [A.f32(512), A.f32(512)]
            Wm = A.f32(512); ATm = A.f32(512); Rm = A.f32(512); vnew = A.f32(512); o_sb = A.f32(512); og = A.f32(512); szb = A.f32(512)
            cTst = A.bf16(512)
            pre = A.f32(GS_W)
            H4 = lambda ap, h: ap[:, h * 128:(h + 1) * 128]
            sc = lambda lo, h: sm[:, lo + h:lo + h + 1]

            def ts_mul(eng, out, in0, scal, r, w):
                if eng == "pool":
                    P.op("pool", lambda e: e.tensor_scalar(out=out, in0=in0, scalar1=scal, scalar2=1.0, op0=ALU.mult, op1=ALU.mult), r=r, w=w)
                else:
                    P.op("dve", lambda e: e.tensor_scalar(out=out, in0=in0, scalar1=scal, scalar2=None, op0=ALU.mult), r=r, w=w)

            for hf in range(2):
                own = (hf == 1)
                if own:
                    P.op("dve", lambda e: e.tensor_scalar(out=Sst, in0=Sst, scalar1=flags[:, 0:1], scalar2=None, op0=ALU.mult), r=["S", "flags"], w=["S"])
                    P.op("dve", lambda e: e.memset(pre[0:3, :], 0.0), w=["pre"])
                    P.dma("sp", pre[0:3, 0:1544], S["gs"][0, HALF:HALF + 3, 0:1544], w=["pre"])
                    P.op("dve", lambda e: e.tensor_scalar(out=pre[0:3, :], in0=pre[0:3, :], scalar1=flags[0:3, 0:1], scalar2=None, op0=ALU.mult), r=["pre", "flags"], w=["pre"])
                    P.dma("sp", S["gs"][1, 0:3, :], pre[0:3, :], r=["pre"], w=["gs_pre1"])
                segs = [(0, 1024, 0), (1024, 1536, 1032)] if own else [(0, 1024, 0)]
                nsc = 8 if own else 4
                for ti in range(G_NT):
                    W_ = GS_W if own else 1032
                    for i in range(4):
                        P.dma("sp", X[i][:, 0:W_], S["gs"][hf, ti * 128 + i:ti * 128 + i + 128, 0:W_], r=(["gs_pre1"] if (own and ti == 0) else []), w=[f"X{i}"])
                    for (d0, d1, s0) in segs:
                        n = d1 - d0
                        P.op("dve", lambda e, d0=d0, d1=d1, s0=s0, n=n: e.tensor_tensor(out=cv[:, d0:d1], in0=X[0][:, s0:s0 + n], in1=wc[0][:, d0:d1], op=ALU.mult), r=["X0", "wc0"], w=["cv"])
                        for i in range(1, 4):
                            tb_, tbn = (tmpa, "tmpa") if i % 2 else (tmpb, "tmpb")
                            P.op("pool", lambda e, i=i, d0=d0, d1=d1, s0=s0, n=n, tb_=tb_: e.tensor_tensor(out=tb_[:, d0:d1], in0=X[i][:, s0:s0 + n], in1=wc[i][:, d0:d1], op=ALU.mult), r=[f"X{i}", f"wc{i}"], w=[tbn])
                            P.op("dve", lambda e, d0=d0, d1=d1, tb_=tb_: e.tensor_tensor(out=cv[:, d0:d1], in0=cv[:, d0:d1], in1=tb_[:, d0:d1], op=ALU.add), r=["cv", tbn], w=["cv"])
                        P.op("act", lambda e, d0=d0, d1=d1: e.activation(out=cv[:, d0:d1], in_=cv[:, d0:d1], func=AF.Silu), r=["cv"], w=["cv"])
                    if GSTOP <= 1:
                        continue
                    P.op("pool", lambda e: e.tensor_tensor(out=tmpa[:, 0:512], in0=cv[:, 0:512], in1=cv[:, 0:512], op=ALU.mult), r=["cv"], w=["tmpa"])
                    P.op("dve", lambda e: e.tensor_reduce(out=sm[:, 0:4], in_=tmpa[:, 0:512].rearrange("p (h d) -> p h d", h=4), axis=AX.X, op=ALU.add), r=["tmpa"], w=["sm_ss"])
                    if own:
                        P.op("pool", lambda e: e.tensor_tensor(out=tmpb[:, 0:512], in0=cv[:, 1024:1536], in1=cv[:, 1024:1536], op=ALU.mult), r=["cv"], w=["tmpb"])
                        P.op("dve", lambda e: e.tensor_reduce(out=sm[:, 4:8], in_=tmpb[:, 0:512].rearrange("p (h d) -> p h d", h=4), axis=AX.X, op=ALU.add), r=["tmpb"], w=["sm_ss"])
                    P.op("dve", lambda e, nsc=nsc: e.tensor_scalar(out=sm[:, 0:nsc], in0=sm[:, 0:nsc], scalar1=1e-6, scalar2=None, op0=ALU.add), r=["sm_ss"], w=["sm_ss"])
                    P.op("act", lambda e, nsc=nsc: e.activation(out=sm[:, 0:nsc], in_=sm[:, 0:nsc], func=AF.Sqrt), r=["sm_ss"], w=["sm_ss"])
                    P.op("dve", lambda e, nsc=nsc: e.reciprocal(out=sm[:, 0:nsc], in_=sm[:, 0:nsc]), r=["sm_ss"], w=["sm_ss"])
                    if own:
                        P.op("dve", lambda e: e.tensor_scalar(out=sm[:, 4:8], in0=sm[:, 4:8], scalar1=128 ** -0.5, scalar2=None, op0=ALU.mult), r=["sm_ss"], w=["sm_ss"])
                    P.op("act", lambda e: e.activation(out=sm[:, 8:12], in_=X[3][:, 1024:1028], func=AF.Sigmoid), r=["X3"], w=["sm_b"])
                    P.op("dve", lambda e: e.tensor_tensor(out=sm[:, 12:16], in0=X[3][:, 1028:1032], in1=dtb, op=ALU.add), r=["X3", "dtb"], w=["sm_g"])
                    P.op("act", lambda e: e.activation(out=sm[:, 12:16], in_=sm[:, 12:16], func=AF.Exp), r=["sm_g"], w=["sm_g"])
                    P.op("act", lambda e: e.activation(out=sm[:, 12:16], in_=sm[:, 12:16], func=AF.Ln, bias=1.0), r=["sm_g"], w=["sm_g"])
                    P.op("dve", lambda e: e.tensor_tensor(out=sm[:, 12:16], in0=sm[:, 12:16], in1=negA, op=ALU.mult), r=["sm_g", "negA"], w=["sm_g"])
                    P.op("pe", lambda e: e.matmul(out=pb[3][:, 0:4], lhsT=TRIU, rhs=sm[:, 12:16], start=True, stop=True), r=["cst", "sm_g"], w=[PB[3]])
                    P.op("pe", lambda e: e.matmul(out=pb[3][:, 4:8], lhsT=ONESM, rhs=sm[:, 12:16], start=True, stop=True), r=["cst", "sm_g"], w=[PB[3]])
                    P.op("dve", lambda e: e.tensor_copy(out=sm[:, 16:24], in_=pb[3][:, 0:8]), r=[PB[3]], w=["sm_gc"])
                    P.op("dve", lambda e: e.tensor_copy(out=sm[:, 24:28], in_=sm[:, 16:20]), r=["sm_gc"], w=["sm_e"])
                    P.op("dve", lambda e: e.tensor_tensor(out=sm[:, 28:32], in0=sm[:, 20:24], in1=sm[:, 16:20], op=ALU.subtract), r=["sm_gc"], w=["sm_e"])
                    P.op("dve", lambda e: e.tensor_copy(out=sm[:, 32:36], in_=sm[:, 20:24]), r=["sm_gc"], w=["sm_e"])
                    P.op("act", lambda e: e.activation(out=sm[:, 24:36], in_=sm[:, 24:36], func=AF.Exp), r=["sm_e"], w=["sm_e"])
                    P.op("dve", lambda e: e.scalar_tensor_tensor(out=sm[:, 36:40], in0=sm[:, 8:12], scalar=-1.0, in1=sm[:, 24:28], op0=ALU.mult, op1=ALU.mult), r=["sm_b", "sm_e"], w=["sm_x"])
                    P.op("dve", lambda e: e.tensor_scalar(out=sm[:, 40:44], in0=sm[:, 16:20], scalar1=-1.0, scalar2=None, op0=ALU.mult), r=["sm_gc"], w=["sm_x"])
                    P.op("dve", lambda e: e.tensor_scalar(out=sm[:, 44:48], in0=sm[:, 8:12], scalar1=-1.0, scalar2=None, op0=ALU.mult), r=["sm_b"], w=["sm_x"])
                    if own:
                        P.op("dve", lambda e: e.tensor_tensor(out=sm[:, 48:52], in0=sm[:, 4:8], in1=sm[:, 24:28], op=ALU.mult), r=["sm_ss", "sm_e"], w=["sm_x"])
                    if GSTOP <= 2:
                        continue
                    for h in range(4):
                        ts_mul("dve", H4(kn, h), cv[:, h * 128:(h + 1) * 128], sc(0, h), ["cv", "sm_ss"], ["kn"])
                        ts_mul("pool", H4(kt, h), H4(kn, h), sc(28, h), ["kn", "sm_e"], ["kt"])
                        ts_mul("pool", H4(vb, h), cv[:, 512 + h * 128:512 + (h + 1) * 128], sc(8, h), ["cv", "sm_b"], ["vb"])
                        if own:
                            ts_mul("dve", H4(qn, h), cv[:, 1024 + h * 128:1024 + (h + 1) * 128], sc(4, h), ["cv", "sm_ss"], ["qn"])
                            ts_mul("pool", H4(qt, h), cv[:, 1024 + h * 128:1024 + (h + 1) * 128], sc(48, h), ["cv", "sm_x"], ["qt"])
                    for (src, sn, bank, dst, dn, eng) in ([(kn, "kn", 0, knT, "knT", "act")] + ([(qn, "qn", 1, qnT, "qnT", "dve"), (qt, "qt", 2, qtT, "qtT", "act")] if own else [])):
                        for h in range(4):
                            P.op("pe", lambda e, h=h, src=src, bank=bank: e.transpose(out=H4(pb[bank], h), in_=H4(src, h), identity=ident), r=[sn, "ident"], w=[PB[bank]])
                        evac(eng, dst, pb[bank], [PB[bank]], [dn])
                    if GSTOP <= 3:
                        continue
                    for h in range(4):
                        P.op("pe", lambda e, h=h: e.matmul(out=H4(pb[0], h), lhsT=H4(knT, h), rhs=H4(knT, h), start=True, stop=True), r=["knT"], w=[PB[0]])
                        P.op("pool", lambda e, h=h: e.tensor_scalar(out=H4(Dg, h), in0=ident, scalar1=sc(16, h), scalar2=1.0, op0=ALU.mult, op1=ALU.mult), r=["ident", "sm_gc"], w=["Dg"])
                    for h in range(4):
                        P.op("pe", lambda e, h=h: e.matmul(out=H4(pb[1], h), lhsT=ONESM, rhs=H4(Dg, h), start=True, stop=False), r=["cst", "Dg"], w=[PB[1]])
                        P.op("pe", lambda e, h=h: e.matmul(out=H4(pb[1], h), lhsT=ident, rhs=MASKL, start=False, stop=True), r=["cst", "ident"], w=[PB[1]])
                        P.op("act", lambda e, h=h: e.activation(out=H4(dec, h), in_=H4(pb[1], h), func=AF.Exp, scale=-1.0, bias=sc(16, h)), r=[PB[1], "sm_gc"], w=["dec"])
                        P.op("dve", lambda e, h=h: e.scalar_tensor_tensor(out=H4(Pbuf[0], h), in0=H4(pb[0], h), scalar=sc(44, h), in1=H4(dec, h), op0=ALU.mult, op1=ALU.mult),
                             r=[PB[0], "sm_x", "dec"], w=["P0"])
                    if own:
                        for h in range(4):
                            P.op("pe", lambda e, h=h: e.matmul(out=H4(pb[2], h), lhsT=ONESM, rhs=H4(Dg, h), start=True, stop=False), r=["cst", "Dg"], w=[PB[2]])
                            P.op("pe", lambda e, h=h: e.matmul(out=H4(pb[2], h), lhsT=ident, rhs=MASKU, start=False, stop=True), r=["cst", "ident"], w=[PB[2]])
                            P.op("act", lambda e, h=h: e.activation(out=H4(decT, h), in_=H4(pb[2], h), func=AF.Exp, scale=1.0, bias=sc(40, h)), r=[PB[2], "sm_x"], w=["decT"])
                            P.op("pe", lambda e, h=h: e.matmul(out=H4(pb[3], h), lhsT=H4(knT, h), rhs=H4(qnT, h), start=True, stop=True), r=["knT", "qnT"], w=[PB[3]])
                            P.op("dve", lambda e, h=h: e.tensor_tensor(out=H4(ATm, h), in0=H4(pb[3], h), in1=H4(decT, h), op=ALU.mult), r=[PB[3], "decT"], w=["ATm"])
                    if GSTOP <= 4:
                        continue
                    for h in range(4):
                        P.op("pe", lambda e, h=h: e.transpose(out=H4(pb[4], h), in_=H4(Pbuf[0], h), identity=ident), r=["P0", "ident"], w=[PB[4]])
                    evac("act", PTbuf[0], pb[4], [PB[4]], ["PT0"])
                    P.op("dve", lambda e: e.tensor_tensor(out=Wm, in0=PTbuf[0], in1=ident4, op=ALU.add), r=["PT0", "cst"], w=["Wm"])
                    for l in range(1, 7):
                        pc, ptc, pn_, ptn = Pbuf[(l - 1) % 2], PTbuf[(l - 1) % 2], Pbuf[l % 2], PTbuf[l % 2]
                        pcn, ptcn, pnn, ptnn = f"P{(l - 1) % 2}", f"PT{(l - 1) % 2}", f"P{l % 2}", f"PT{l % 2}"
                        for h in range(4):
                            P.op("pe", lambda e, h=h, pc=pc, ptc=ptc: e.matmul(out=H4(pb[4], h), lhsT=H4(ptc, h), rhs=H4(pc, h), start=True, stop=True), r=[pcn, ptcn], w=[PB[4]])
                        if l < 6:
                            for h in range(4):
                                P.op("pe", lambda e, h=h, pc=pc, ptc=ptc: e.matmul(out=H4(pb[5], h), lhsT=H4(pc, h), rhs=H4(ptc, h), start=True, stop=True), r=[pcn, ptcn], w=[PB[5]])
                        evac("act", pn_, pb[4], [PB[4]], [pnn])
                        if l < 6:
                            evac("dve", ptn, pb[5], [PB[5]], [ptnn])
                        for h in range(4):
                            P.op("pe", lambda e, h=h, pn_=pn_: e.matmul(out=H4(pb[6], h), lhsT=H4(pn_, h), rhs=H4(Wm, h), start=True, stop=True), r=[pnn, "Wm"], w=[PB[6]])
                        P.op("dve", lambda e: e.tensor_tensor(out=Wm, in0=Wm, in1=pb[6], op=ALU.add), r=["Wm", PB[6]], w=["Wm"])
                    if GSTOP <= 5:
                        continue
                    for h in range(4):
                        P.op("pe", lambda e, h=h: e.matmul(out=H4(pb[7], h), lhsT=H4(knT, h), rhs=H4(Sst, h), start=True, stop=True), r=["knT", "S"], w=[PB[7]])
                    for h in range(4):
                        P.op("dve", lambda e, h=h: e.scalar_tensor_tensor(out=H4(Rm, h), in0=H4(pb[7], h), scalar=sc(36, h), in1=H4(vb, h), op0=ALU.mult, op1=ALU.add),
                             r=[PB[7], "sm_x", "vb"], w=["Rm"])
                    for h in range(4):
                        P.op("pe", lambda e, h=h: e.matmul(out=H4(pb[0], h), lhsT=H4(Wm, h), rhs=H4(Rm, h), start=True, stop=True), r=["Wm", "Rm"], w=[PB[0]])
                    evac("act", vnew, pb[0], [PB[0]], ["vnew"])
                    if own:
                        for h in range(4):
                            P.op("pe", lambda e, h=h: e.matmul(out=H4(pb[1], h), lhsT=H4(qtT, h), rhs=H4(Sst, h), start=True, stop=False), r=["qtT", "S"], w=[PB[1]])
                            P.op("pe", lambda e, h=h: e.matmul(out=H4(pb[1], h), lhsT=H4(ATm, h), rhs=H4(vnew, h), start=False, stop=True), r=["ATm", "vnew"], w=[PB[1]])
                        evac("act", o_sb, pb[1], [PB[1]], ["o_sb"])
                    for h in range(4):
                        P.op("pe", lambda e, h=h: e.matmul(out=H4(pb[2], h), lhsT=H4(kt, h), rhs=H4(vnew, h), start=True, stop=True), r=["kt", "vnew"], w=[PB[2]])
                    for h in range(4):
                        P.op("dve", lambda e, h=h: e.scalar_tensor_tensor(out=H4(Sst, h), in0=H4(Sst, h), scalar=sc(32, h), in1=H4(pb[2], h), op0=ALU.mult, op1=ALU.add),
                             r=["S", "sm_e", PB[2]], w=["S"])
                    if own:
                        P.op("pool", lambda e: e.tensor_tensor(out=tmpa[:, 0:512], in0=o_sb, in1=o_sb, op=ALU.mult), r=["o_sb"], w=["tmpa"])
                        P.op("dve", lambda e: e.tensor_reduce(out=sm[:, 52:56], in_=tmpa[:, 0:512].rearrange("p (h d) -> p h d", h=4), axis=AX.X, op=ALU.add), r=["tmpa"], w=["sm_o"])
                        P.op("dve", lambda e: e.tensor_scalar(out=sm[:, 52:56], in0=sm[:, 52:56], scalar1=1.0 / 128, scalar2=1e-6, op0=ALU.mult, op1=ALU.add), r=["sm_o"], w=["sm_o"])
                        P.op("act", lambda e: e.activation(out=sm[:, 52:56], in_=sm[:, 52:56], func=AF.Sqrt), r=["sm_o"], w=["sm_o"])
                        P.op("dve", lambda e: e.reciprocal(out=sm[:, 52:56], in_=sm[:, 52:56]), r=["sm_o"], w=["sm_o"])
                        P.op("act", lambda e: e.activation(out=szb, in_=X[3][:, 1544:2056], func=AF.Silu), r=["X3"], w=["szb"])
                        for h in range(4):
                            P.op("dve", lambda e, h=h: e.scalar_tensor_tensor(out=H4(og, h), in0=H4(o_sb, h), scalar=sc(52, h), in1=H4(ggd, h), op0=ALU.mult, op1=ALU.mult),
                                 r=["o_sb", "sm_o", "ggd"], w=["og"])
                        P.op("pool", lambda e: e.tensor_tensor(out=og, in0=og, in1=szb, op=ALU.mult), r=["og", "szb"], w=["og"])
                        for h in range(4):
                            P.op("pe", lambda e, h=h: e.transpose(out=H4(pb[3], h), in_=H4(og, h), identity=ident), r=["og", "ident"], w=[PB[3]])
                        evac("act", cTst, pb[3], [PB[3]], ["cTst"])
                        P.dma("sp", S["cT"][ti][:, 0:4, :], cTst.rearrange("p (h t) -> p h t", h=4), r=["cTst"], w=[f"cTg{ti}"])
            P.dma("sp", O["ssm_fin"].rearrange("h d e -> d h e"), Sst.rearrange("p (h e) -> p h e", h=4), r=["S"])

            P.barrier()
            A.reset(gs_base)
            eye16 = A.f32(256)
            P.dma("sp", eye16, I["eye16"], w=["eye16"])
            Sall = A.f32(NS * 512); Sall4 = Sall.rearrange("p (s h e) -> p s h e", s=NS, h=4)
            for s_ in range(NS):
                P.dma("sp", Sall4[:, s_], I["state_ssm"][s_].rearrange("h d e -> d h e"), w=[f"S{s_}"])
            Ps2 = A.f32(INC); scv = A.f32(3 * 1536); scv3 = scv.rearrange("p (i c) -> p i c", i=3)
            P.dma("sp", Ps2[0:NS, :], S["ps"], w=["Ps2"])
            P.dma("sp", scv3[0:NS], I["state_conv"], w=["scv"])
            cvs = A.f32(1536); tms = A.f32(1536); sm2 = A.f32(64)
            kn2 = A.f32(512); qn2 = A.f32(512)
            knT2 = A.f32(64); qnT2 = A.f32(64)
            KTm = A.f32(1024); QTm = A.f32(1024)
            Km = [A.f32(512), A.f32(512)]
            EgD = A.f32(64); EGB = A.f32(64); Dl = A.f32(512); o_s = A.f32(512); og_s = A.f32(512); sz_s = A.f32(512)
            cTs = A.bf16(64)
            R = slice(0, NS)
            for (d0, d1, sc0, p0) in [(0, 1024, 512, 0), (1024, 1536, 0, C_Q)]:
                n = d1 - d0
                P.op("dve", lambda e, d0=d0, d1=d1, sc0=sc0, n=n: e.tensor_tensor(out=cvs[R, d0:d1], in0=scv3[R, 0, sc0:sc0 + n], in1=wc[0][R, d0:d1], op=ALU.mult), r=["scv", "wc0"], w=["cvs"])
                for i in range(1, 4):
                    src = (lambda i=i, sc0=sc0, n=n, p0=p0: scv3[R, i, sc0:sc0 + n] if i < 3 else Ps2[R, p0:p0 + n])()
                    P.op("pool", lambda e, i=i, d0=d0, d1=d1, src=src: e.tensor_tensor(out=tms[R, d0:d1], in0=src, in1=wc[i][R, d0:d1], op=ALU.mult), r=["scv", "Ps2", f"wc{i}"], w=["tms"])
                    P.op("dve", lambda e, d0=d0, d1=d1: e.tensor_tensor(out=cvs[R, d0:d1], in0=cvs[R, d0:d1], in1=tms[R, d0:d1], op=ALU.add), r=["cvs", "tms"], w=["cvs"])
                P.op("act", lambda e, d0=d0, d1=d1: e.activation(out=cvs[R, d0:d1], in_=cvs[R, d0:d1], func=AF.Silu), r=["cvs"], w=["cvs"])
            for (c0, o0) in [(0, 0), (1024, 4)]:
                P.op("pool", lambda e, c0=c0: e.tensor_tensor(out=tms[R, 0:512], in0=cvs[R, c0:c0 + 512], in1=cvs[R, c0:c0 + 512], op=ALU.mult), r=["cvs"], w=["tms"])
                P.op("dve", lambda e, o0=o0: e.tensor_reduce(out=sm2[R, o0:o0 + 4], in_=tms[R, 0:512].rearrange("p (h d) -> p h d", h=4), axis=AX.X, op=ALU.add), r=["tms"], w=["sm2"])
            P.op("dve", lambda e: e.tensor_scalar(out=sm2[R, 0:8], in0=sm2[R, 0:8], scalar1=1e-6, scalar2=None, op0=ALU.add), r=["sm2"], w=["sm2"])
            P.op("act", lambda e: e.activation(out=sm2[R, 0:8], in_=sm2[R, 0:8], func=AF.Sqrt), r=["sm2"], w=["sm2"])
            P.op("dve", lambda e: e.reciprocal(out=sm2[R, 0:8], in_=sm2[R, 0:8]), r=["sm2"], w=["sm2"])
            P.op("dve", lambda e: e.tensor_scalar(out=sm2[R, 4:8], in0=sm2[R, 4:8], scalar1=128 ** -0.5, scalar2=None, op0=ALU.mult), r=["sm2"], w=["sm2"])
            P.op("act", lambda e: e.activation(out=sm2[R, 8:12], in_=Ps2[R, C_B:C_B + 4], func=AF.Sigmoid), r=["Ps2"], w=["sm2b"])
            P.op("dve", lambda e: e.tensor_tensor(out=sm2[R, 12:16], in0=Ps2[R, C_A:C_A + 4], in1=dtb[R, :], op=ALU.add), r=["Ps2", "dtb"], w=["sm2g"])
            P.op("act", lambda e: e.activation(out=sm2[R, 12:16], in_=sm2[R, 12:16], func=AF.Exp), r=["sm2g"], w=["sm2g"])
            P.op("act", lambda e: e.activation(out=sm2[R, 12:16], in_=sm2[R, 12:16], func=AF.Ln, bias=1.0), r=["sm2g"], w=["sm2g"])
            P.op("dve", lambda e: e.tensor_tensor(out=sm2[R, 12:16], in0=sm2[R, 12:16], in1=negA[R, :], op=ALU.mult), r=["sm2g", "negA"], w=["sm2g"])
            P.op("act", lambda e: e.activation(out=sm2[R, 12:16], in_=sm2[R, 12:16], func=AF.Exp), r=["sm2g"], w=["sm2g"])
            P.op("dve", lambda e: e.tensor_scalar(out=sm2[R, 16:20], in0=sm2[R, 12:16], scalar1=-1.0, scalar2=None, op0=ALU.mult), r=["sm2g"], w=["sm2n"])
            for h in range(4):
                P.op("dve", lambda e, h=h: e.tensor_scalar(out=kn2[R, h * 128:(h + 1) * 128], in0=cvs[R, h * 128:(h + 1) * 128], scalar1=sm2[R, h:h + 1], scalar2=None, op0=ALU.mult), r=["cvs", "sm2"], w=["kn2"])
                P.op("dve", lambda e, h=h: e.tensor_scalar(out=qn2[R, h * 128:(h + 1) * 128], in0=cvs[R, 1024 + h * 128:1024 + (h + 1) * 128], scalar1=sm2[R, 4 + h:5 + h], scalar2=None, op0=ALU.mult), r=["cvs", "sm2"], w=["qn2"])
            for h in range(4):
                P.op("pe", lambda e, h=h: e.transpose(out=pb[6][:, h * 16:(h + 1) * 16], in_=kn2[R, h * 128:(h + 1) * 128], identity=ident[R, R]), r=["kn2", "ident"], w=[PB[6]])
                P.op("pe", lambda e, h=h: e.transpose(out=pb[6][:, 64 + h * 16:64 + (h + 1) * 16], in_=qn2[R, h * 128:(h + 1) * 128], identity=ident[R, R]), r=["qn2", "ident"], w=[PB[6]])
            evac("act", knT2, pb[6][:, 0:64], [PB[6]], ["knT2"])
            evac("dve", qnT2, pb[6][:, 64:128], [PB[6]], ["qnT2"])
            eye3 = eye16.rearrange("p (s m) -> p s m", s=NS)
            for h in range(4):
                for s_ in range(NS):
                    j = h * NS + s_
                    P.op("dve", lambda e, j=j, s_=s_: e.tensor_scalar(out=KTm[:, j * 16:(j + 1) * 16], in0=eye3[:, s_, :], scalar1=knT2[:, j:j + 1], scalar2=None, op0=ALU.mult), r=["eye16", "knT2"], w=["KTm"])
                    P.op("pool", lambda e, j=j, s_=s_: e.tensor_scalar(out=QTm[:, j * 16:(j + 1) * 16], in0=eye3[:, s_, :], scalar1=qnT2[:, j:j + 1], scalar2=1.0, op0=ALU.mult, op1=ALU.mult), r=["eye16", "qnT2"], w=["QTm"])
            for h in range(4):
                for s_ in range(NS):
                    j = h * NS + s_
                    P.op("pe", lambda e, h=h, s_=s_, j=j: e.matmul(out=pb[h][R, 0:128], lhsT=KTm[:, j * 16:(j + 1) * 16], rhs=Sall4[:, s_, h, :], start=(s_ == 0), stop=(s_ == NS - 1)),
                         r=["KTm", f"S{s_}"], w=[PB[h]])
            for h in range(4):
                P.op("dve", lambda e, h=h: e.scalar_tensor_tensor(out=Dl[R, h * 128:(h + 1) * 128], in0=pb[h][R, 0:128], scalar=sm2[R, 16 + h:17 + h], in1=cvs[R, 512 + h * 128:512 + (h + 1) * 128], op0=ALU.mult, op1=ALU.add),
                     r=[PB[h], "sm2n", "cvs"], w=["Dl"])
                P.op("dve", lambda e, h=h: e.tensor_scalar(out=Dl[R, h * 128:(h + 1) * 128], in0=Dl[R, h * 128:(h + 1) * 128], scalar1=sm2[R, 8 + h:9 + h], scalar2=None, op0=ALU.mult), r=["Dl", "sm2b"], w=["Dl"])
            for s_ in range(NS):
                P.op("dve", lambda e, s_=s_: e.tensor_scalar(out=EgD[R, s_ * 4:(s_ + 1) * 4], in0=sm2[R, 12:16], scalar1=ident[R, s_:s_ + 1], scalar2=None, op0=ALU.mult), r=["sm2g", "ident"], w=["EgD"])
            P.op("pe", lambda e: e.matmul(out=pb[6][:, 0:64], lhsT=ONESM[R, :], rhs=EgD[R, :], start=True, stop=True), r=["cst", "EgD"], w=[PB[6]])
            evac("act", EGB, pb[6][:, 0:64], [PB[6]], ["EGB"])
            for s_ in range(NS):
                km = Km[s_ % 2]; kmn = f"Km{s_ % 2}"
                bank = 4 + s_ % 2
                P.op("pool", lambda e, s_=s_, km=km: e.tensor_scalar(out=km[R, :], in0=kn2[R, :], scalar1=ident[R, s_:s_ + 1], scalar2=1.0, op0=ALU.mult, op1=ALU.mult), r=["kn2", "ident"], w=[kmn])
                for h in range(4):
                    P.op("pe", lambda e, h=h, km=km, bank=bank: e.matmul(out=pb[bank][:, h * 128:(h + 1) * 128], lhsT=km[R, h * 128:(h + 1) * 128], rhs=Dl[R, h * 128:(h + 1) * 128], start=True, stop=True),
                         r=[kmn, "Dl"], w=[PB[bank]])
                for h in range(4):
                    P.op("dve", lambda e, h=h, s_=s_, bank=bank: e.scalar_tensor_tensor(out=Sall4[:, s_, h, :], in0=Sall4[:, s_, h, :], scalar=EGB[:, s_ * 4 + h:s_ * 4 + h + 1], in1=pb[bank][:, h * 128:(h + 1) * 128], op0=ALU.mult, op1=ALU.add),
                         r=[f"S{s_}", "EGB", PB[bank]], w=[f"S{s_}"])
                P.dma("sp", O["ssm_s"][s_].rearrange("h d e -> d h e"), Sall4[:, s_], r=[f"S{s_}"])
            for h in range(4):
                for s_ in range(NS):
                    j = h * NS + s_
                    P.op("pe", lambda e, h=h, s_=s_, j=j: e.matmul(out=pb[h][R, 0:128], lhsT=QTm[:, j * 16:(j + 1) * 16], rhs=Sall4[:, s_, h, :], start=(s_ == 0), stop=(s_ == NS - 1)),
                         r=["QTm", f"S{s_}"], w=[PB[h]])
            for h in range(4):
                evac("act", o_s[R, h * 128:(h + 1) * 128], pb[h][R, 0:128], [PB[h]], ["o_s"])
            P.op("pool", lambda e: e.tensor_tensor(out=tms[R, 0:512], in0=o_s[R, :], in1=o_s[R, :], op=ALU.mult), r=["o_s"], w=["tms"])
            P.op("dve", lambda e: e.tensor_reduce(out=sm2[R, 20:24], in_=tms[R, 0:512].rearrange("p (h d) -> p h d", h=4), axis=AX.X, op=ALU.add), r=["tms"], w=["sm2o"])
            P.op("dve", lambda e: e.tensor_scalar(out=sm2[R, 20:24], in0=sm2[R, 20:24], scalar1=1.0 / 128, scalar2=1e-6, op0=ALU.mult, op1=ALU.add), r=["sm2o"], w=["sm2o"])
            P.op("act", lambda e: e.activation(out=sm2[R, 20:24], in_=sm2[R, 20:24], func=AF.Sqrt), r=["sm2o"], w=["sm2o"])
            P.op("dve", lambda e: e.reciprocal(out=sm2[R, 20:24], in_=sm2[R, 20:24]), r=["sm2o"], w=["sm2o"])
            P.op("act", lambda e: e.activation(out=sz_s[R, :], in_=Ps2[R, C_Z:C_Z + 512], func=AF.Silu), r=["Ps2"], w=["sz_s"])
            for h in range(4):
                P.op("dve", lambda e, h=h: e.scalar_tensor_tensor(out=og_s[R, h * 128:(h + 1) * 128], in0=o_s[R, h * 128:(h + 1) * 128], scalar=sm2[R, 20 + h:21 + h], in1=ggd[R, h * 128:(h + 1) * 128], op0=ALU.mult, op1=ALU.mult),
                     r=["o_s", "sm2o", "ggd"], w=["og_s"])
            P.op("pool", lambda e: e.tensor_tensor(out=og_s[R, :], in0=og_s[R, :], in1=sz_s[R, :], op=ALU.mult), r=["og_s", "sz_s"], w=["og_s"])
            for h in range(4):
                P.op("pe", lambda e, h=h: e.transpose(out=pb[7][:, h * 16:(h + 1) * 16], in_=og_s[R, h * 128:(h + 1) * 128], identity=ident[R, R]), r=["og_s", "ident"], w=[PB[7]])
            evac("act", cTs, pb[7][:, 0:64], [PB[7]], ["cTs"])
            P.dma("sp", S["cT"][NT][:, 0:4, 0:NS], cTs.rearrange("p (h t) -> p h t", h=4), r=["cTs"], w=["cTgs"])

        if not os.environ.get('MK_NOT'):
            P.barrier()
            A.reset(persist1)
            SCALE = 128 ** -0.5
            NEG = -30000.0
            NBIS = 14
            caus = A.f32(128); ii2 = A.bf16(256); onesr = A.f32(128)
            P.dma("sp", caus, I["caus"], w=["caus"])
            P.op("dve", lambda e: e.tensor_copy(out=ii2[:, 0:128], in_=ident), r=["ident"], w=["ii2"])
            P.op("dve", lambda e: e.tensor_copy(out=ii2[:, 128:256], in_=ident), r=["ident"], w=["ii2"])
            P.op("dve", lambda e: e.memset(onesr, 1.0), w=["onesr"])
            tsm = A.f32(256)
            krow = A.f32(128)
            P.op("dve", lambda e: e.tensor_reduce(out=tsm[:, 0:1], in_=ksq, axis=AX.X, op=ALU.max), r=["ksq"], w=["tsm0"])
            P.op("pe", lambda e: e.transpose(out=pb[0][0:1, 0:128], in_=tsm[:, 0:1], identity=ident), r=["tsm0", "ident"], w=[PB[0]])
            evac("dve", krow[0:1, :], pb[0][0:1, 0:128], [PB[0]], ["krow"])
            P.op("dve", lambda e: e.tensor_reduce(out=krow[0:1, 0:1], in_=krow[0:1, :], axis=AX.X, op=ALU.max), r=["krow"], w=["krow"])
            P.op("pe", lambda e: e.matmul(out=pb[0][:, 0:1], lhsT=onesr[0:1, :], rhs=krow[0:1, 0:1], start=True, stop=True), r=["onesr", "krow"], w=[PB[0]])
            evac("dve", tsm[:, 1:2], pb[0][:, 0:1], [PB[0]], ["tsm1"])
            P.op("dve", lambda e: e.tensor_reduce(out=tsm[:, 16:32], in_=qsq.rearrange("p (t h) -> p t h", h=4), axis=AX.X, op=ALU.max), r=["qsq"], w=["tsmq"])
            P.op("dve", lambda e: e.tensor_scalar(out=tsm[:, 32:48], in0=tsm[:, 16:32], scalar1=tsm[:, 1:2], scalar2=None, op0=ALU.mult), r=["tsmq", "tsm1"], w=["negm"])
            P.op("act", lambda e: e.activation(out=tsm[:, 32:48], in_=tsm[:, 32:48], func=AF.Sqrt), r=["negm"], w=["negm"])
            P.op("dve", lambda e: e.tensor_scalar(out=tsm[:, 32:48], in0=tsm[:, 32:48], scalar1=-1.0, scalar2=None, op0=ALU.mult), r=["negm"], w=["negm"])
            Iscs = [A.f32(T), A.f32(T)]
            junk = A.bf16(T); MBs_ = [A.bf16(T), A.bf16(T)]
            qTt = [A.bf16(512), A.bf16(512), A.bf16(512)]; iqTt = [A.bf16(512), A.bf16(512)]
            oaccs = [A.f32(4 * 130), A.f32(4 * 130)]
            rh = [A.bf16(512) for _ in range(4)]
            Dhs = [A.bf16(8 * 128), A.bf16(8 * 128)]
            PT = [A.bf16(256), A.bf16(256)]
            oatt = A.f32(512); cTa = A.bf16(512)
            bss = [A.f32(16), A.f32(16)]
            T_NT = int(os.environ.get("MK_TNT", NT))

            def t_index(j):
                p = j % 2
                Isc = Iscs[p]; In = f"Isc{p}"; Dh = Dhs[p]; Dn = f"Dh{p}"
                qb = qTt[j % 3]; qn_ = f"qTt{j % 3}"; iqb = iqTt[p]; iqn = f"iqTt{p}"
                qb3 = qb.rearrange("p (h t) -> p h t", h=4); iqb3 = iqb.rearrange("p (h t) -> p h t", h=4)
                P.dma("sp", iqb3, S["iqT"][j], w=[iqn])
                P.dma("sp", qb3, S["qT"][j], w=[qn_])
                ncol = HALF + 128 * (j + 1)
                for h in range(8):
                    P.op("dve", lambda e, h=h: e.tensor_scalar(out=Dh[:, h * 128:(h + 1) * 128], in0=ident, scalar1=iwsgn[:, j * 8 + h:j * 8 + h + 1], scalar2=None, op0=ALU.mult),
                         r=["ident", "iwsgn"], w=[Dn])
                nblk = (ncol + 511) // 512
                for kb in range(nblk):
                    c0, c1 = kb * 512, min(ncol, kb * 512 + 512)
                    w_ = c1 - c0
                    for h in range(8):
                        p_, hf_ = h // 2, h % 2
                        ba = h % 2
                        rb = rh[h % 4]; rbn = f"rh{h % 4}"
                        P.op("pe", lambda e, p_=p_, hf_=hf_, ba=ba, c0=c0, c1=c1, w_=w_: e.matmul(out=pb[ba][:, 0:w_], lhsT=iqb3[hf_ * 64:(hf_ + 1) * 64, p_, :], rhs=ikT[hf_ * 64:(hf_ + 1) * 64, c0:c1], start=True, stop=True),
                             r=[iqn, "ikT"], w=[PB[ba]])
                        P.op("act", lambda e, h=h, ba=ba, w_=w_, rb=rb: e.activation(out=rb[:, 0:w_], in_=pb[ba][:, 0:w_], func=AF.Relu, scale=iwabs[:, j * 8 + h:j * 8 + h + 1]),
                             r=[PB[ba], "iwabs"], w=[rbn])
                        P.op("pe", lambda e, h=h, w_=w_, rb=rb: e.matmul(out=pb[2][:, 0:w_], lhsT=Dh[:, h * 128:(h + 1) * 128], rhs=rb[:, 0:w_], start=(h == 0), stop=(h == 7)),
                             r=[Dn, rbn], w=[PB[2]])
                    if c0 < HALF:
                        P.op("act", lambda e, c0=c0, c1=c1, w_=w_: e.activation(out=Isc[:, c0:c1], in_=pb[2][:, 0:w_], func=AF.Identity, bias=flags[:, 1:2]), r=[PB[2], "flags"], w=[In])
                    else:
                        P.op("act", lambda e, c0=c0, c1=c1, w_=w_: e.activation(out=Isc[:, c0:c1], in_=pb[2][:, 0:w_], func=AF.Copy), r=[PB[2]], w=[In])
                P.op("pool", lambda e: e.tensor_tensor(out=Isc[:, ncol - 128:ncol], in0=Isc[:, ncol - 128:ncol], in1=caus, op=ALU.add), r=[In, "caus"], w=[In])

            def t_bisect(j):
                p = j % 2
                Isc = Iscs[p]; In = f"Isc{p}"; MB = MBs_[p]; Mn = f"MB{p}"; bs = bss[p]
                B = lambda n: f"bs{p}_{n}"
                ncol = HALF + 128 * (j + 1)
                lo, rng, thr, cntc, mm = bs[:, 0:1], bs[:, 1:2], bs[:, 2:3], bs[:, 3:4], bs[:, 4:5]
                P.op("dve", lambda e: e.tensor_reduce(out=lo, in_=Isc[:, 0:ncol], axis=AX.X, op=ALU.min), r=[In], w=[B("lo")])
                P.op("dve", lambda e: e.tensor_reduce(out=rng, in_=Isc[:, 0:ncol], axis=AX.X, op=ALU.max), r=[In], w=[B("rng")])
                P.op("dve", lambda e: e.tensor_scalar(out=thr, in0=rng, scalar1=-128.0, scalar2=None, op0=ALU.add), r=[B("rng")], w=[B("thr")])
                P.op("dve", lambda e: e.tensor_tensor(out=lo, in0=lo, in1=thr, op=ALU.max), r=[B("lo"), B("thr")], w=[B("lo")])
                P.op("dve", lambda e: e.tensor_tensor(out=rng, in0=rng, in1=lo, op=ALU.subtract), r=[B("rng"), B("lo")], w=[B("rng")])
                base = bs[:, 5:6]
                P.op("dve", lambda e: e.scalar_tensor_tensor(out=thr, in0=rng, scalar=0.5, in1=lo, op0=ALU.mult, op1=ALU.add), r=[B("rng"), B("lo")], w=[B("thr")])
                for it in range(NBIS):
                    st = 2.0 ** -(it + 1)
                    P.op("dve", lambda e: e.tensor_scalar(out=junk[:, 0:ncol], in0=Isc[:, 0:ncol], scalar1=thr, scalar2=None, op0=ALU.is_ge, op1=ALU.add, accum_out=cntc),
                         r=[In, B("thr")], w=["junk", B("cnt")])
                    P.op("dve", lambda e, st=st: e.scalar_tensor_tensor(out=base, in0=rng, scalar=-0.5 * st, in1=thr, op0=ALU.mult, op1=ALU.add), r=[B("rng"), B("thr")], w=[B("base")])
                    P.op("dve", lambda e: e.tensor_scalar(out=mm, in0=cntc, scalar1=255.5, scalar2=rng, op0=ALU.is_ge, op1=ALU.mult), r=[B("cnt"), B("rng")], w=[B("m")])
                    P.op("dve", lambda e, st=st: e.scalar_tensor_tensor(out=thr, in0=mm, scalar=st, in1=base, op0=ALU.mult, op1=ALU.add), r=[B("m"), B("base")], w=[B("thr")])
                P.op("dve", lambda e: e.scalar_tensor_tensor(out=lo, in0=rng, scalar=-(2.0 ** -(NBIS + 1)), in1=thr, op0=ALU.mult, op1=ALU.add), r=[B("rng"), B("thr")], w=[B("lo")])
                P.op("dve", lambda e: e.tensor_scalar(out=MB[:, 0:ncol], in0=Isc[:, 0:ncol], scalar1=lo, scalar2=NEG, op0=ALU.is_lt, op1=ALU.mult), r=[In, B("lo")], w=[Mn])
                P.op("dve", lambda e: e.tensor_scalar(out=MB[:, 0:ncol], in0=MB[:, 0:ncol], scalar1=tsm[:, 32 + j:33 + j], scalar2=None, op0=ALU.add), r=[Mn, "negm"], w=[Mn])

            def t_attend(j):
                p = j % 2
                MB = MBs_[p]; Mn = f"MB{p}"; bs = bss[p]
                qb = qTt[j % 3]; qn_ = f"qTt{j % 3}"
                qb3 = qb.rearrange("p (h t) -> p h t", h=4)
                oacc = oaccs[p]; oan = f"oacc{p}"
                ntile = 16 + j + 1
                for g in range(2):
                    for t in range(ntile):
                        sb_ = t % 2
                        ptb = PT[t % 2]; ptn = f"PT{t % 2}"
                        P.op("pe", lambda e, g=g, t=t, sb_=sb_: e.matmul(out=pb[sb_][:, 0:256], lhsT=KT3[:, g, t * 128:(t + 1) * 128], rhs=qb3[:, g * 2:(g + 1) * 2, :], start=True, stop=False),
                             r=["KT", qn_], w=[PB[sb_]])
                        P.op("pe", lambda e, t=t, sb_=sb_: e.matmul(out=pb[sb_][:, 0:256], lhsT=MB[:, t * 128:(t + 1) * 128], rhs=ii2, start=False, stop=True),
                             r=[Mn, "ii2"], w=[PB[sb_]])
                        P.op("act", lambda e, sb_=sb_, ptb=ptb: e.activation(out=ptb, in_=pb[sb_][:, 0:256], func=AF.Exp, scale=SCALE), r=[PB[sb_]], w=[ptn])
                        for h2i in range(2):
                            P.op("pe", lambda e, g=g, t=t, h2i=h2i, ptb=ptb: e.matmul(out=pb[4 + g * 2 + h2i][:, 0:130], lhsT=ptb[:, h2i * 128:(h2i + 1) * 128], rhs=VA4[:, t, g, :], start=(t == 0), stop=(t == ntile - 1)),
                                 r=[ptn, "VA"], w=[PB[4 + g * 2 + h2i]])
                for h in range(4):
                    P.op("act", lambda e, h=h: e.activation(out=oacc[:, h * 130:(h + 1) * 130], in_=pb[4 + h][:, 0:130], func=AF.Copy), r=[PB[4 + h]], w=[oan])

            def t_final(j):
                p = j % 2
                bs = bss[p]; oacc = oaccs[p]; oan = f"oacc{p}"
                for h in range(4):
                    P.op("dve", lambda e, h=h: e.reciprocal(out=bs[:, 8 + h:9 + h], in_=oacc[:, h * 130 + 128:h * 130 + 129]), r=[oan], w=[f"bs{p}_r"])
                    P.op("dve", lambda e, h=h: e.tensor_scalar(out=oatt[:, h * 128:(h + 1) * 128], in0=oacc[:, h * 130:h * 130 + 128], scalar1=bs[:, 8 + h:9 + h], scalar2=None, op0=ALU.mult),
                         r=[oan, f"bs{p}_r"], w=["oatt"])
                for h in range(4):
                    P.op("pe", lambda e, h=h: e.transpose(out=pb[3][:, h * 128:(h + 1) * 128], in_=oatt[:, h * 128:(h + 1) * 128], identity=ident), r=["oatt", "ident"], w=[PB[3]])
                evac("act", cTa, pb[3], [PB[3]], ["cTa"])
                P.dma("sp", S["cT"][j][:, 4:8, :], cTa.rearrange("p (h t) -> p h t", h=4), r=["cTa"], w=[f"cTa{j}"])

            if T_NT > 0:
                t_index(0)
            if T_NT > 1:
                t_index(1)
            if T_NT > 0:
                t_bisect(0)
            for j in range(T_NT):
                t_attend(j)
                if j + 2 < T_NT:
                    t_index(j + 2)
                if j + 1 < T_NT:
                    t_bisect(j + 1)
                t_final(j)

        if not os.environ.get('MK_NOTS'):
            P.barrier()
            A.reset(persist0)
            zSCALE = 128 ** -0.5
            NPG = 16
            zidx_i = A.f32(32).bitcast(I32)
            zpt_i = A.f32(256).bitcast(I32)
            zidx_f = A.f32(256); zsel = A.f32(257); zix = A.f32(32)
            P.dma("sp", zpt_i[:, :], I["pt"].to_broadcast([128, 256]), w=["zpt_i"])
            P.dma("sp", zsel, I["tsel"], w=["zsel"])
            P.op("dve", lambda e: e.tensor_copy(out=zidx_f, in_=zpt_i[:, :]), r=["zpt_i"], w=["zidx_f"])
            P.op("dve", lambda e: e.tensor_tensor(out=zidx_f, in0=zidx_f, in1=zsel[:, 0:256], op=ALU.mult), r=["zidx_f", "zsel"], w=["zidx_f"])
            P.op("dve", lambda e: e.tensor_reduce(out=zix, in_=zidx_f.rearrange("p (q j) -> p q j", j=8), axis=AX.X, op=ALU.add), r=["zidx_f"], w=["zix"])
            P.op("dve", lambda e: e.tensor_scalar(out=zix, in0=zix, scalar1=16.0, scalar2=zsel[:, 256:257], op0=ALU.mult, op1=ALU.add), r=["zix", "zsel"], w=["zix"])
            P.op("dve", lambda e: e.tensor_copy(out=zidx_i[:, :], in_=zix), r=["zix"], w=["zidx_i"])
            zPs = A.f32(INC)
            P.dma("sp", zPs[0:NS, :], S["ps"], w=["zPs"])
            zones = A.f32(128)
            P.op("dve", lambda e: e.memset(zones, 1.0), w=["zones"])
            ziqT = A.f32(NS * 8)
            ziwT = A.f32(NS)
            zikn = A.f32(NS)
            zqT = A.bf16(4 * NS)
            zkTn = A.bf16(2 * NS)
            ziqT3 = ziqT.rearrange("p (s h) -> p s h", h=8)
            R = slice(0, NS)
            for h in range(8):
                P.op("pe", lambda e, h=h: e.transpose(out=pb[0][0:64, h * NS:(h + 1) * NS], in_=zPs[R, C_IQ + h * 64:C_IQ + (h + 1) * 64], identity=ident[R, R]), r=["zPs", "ident"], w=[PB[0]])
            P.op("dve", lambda e: e.tensor_copy(out=ziqT3[0:64], in_=pb[0][0:64, 0:8 * NS].rearrange("p (h s) -> p s h", h=8)), r=[PB[0]], w=["ziqT"])
            P.op("pe", lambda e: e.transpose(out=pb[1][0:8, 0:NS], in_=zPs[R, C_IW:C_IW + 8], identity=ident[R, R]), r=["zPs", "ident"], w=[PB[1]])
            P.op("dve", lambda e: e.tensor_scalar(out=ziwT[0:8, :], in0=pb[1][0:8, 0:NS], scalar1=8 ** -0.5, scalar2=None, op0=ALU.mult), r=[PB[1]], w=["ziwT"])
            P.op("pe", lambda e: e.transpose(out=pb[1][0:64, 64:64 + NS], in_=zPs[R, C_IK:C_IK + 64], identity=ident[R, R]), r=["zPs", "ident"], w=[PB[1]])
            P.op("dve", lambda e: e.tensor_copy(out=zikn[0:64, :], in_=pb[1][0:64, 64:64 + NS]), r=[PB[1]], w=["zikn"])
            for h in range(4):
                P.op("pe", lambda e, h=h: e.transpose(out=pb[2][:, h * NS:(h + 1) * NS], in_=zPs[R, C_AQ + h * 128:C_AQ + (h + 1) * 128], identity=ident[R, R]), r=["zPs", "ident"], w=[PB[2]])
            for g in range(2):
                P.op("pe", lambda e, g=g: e.transpose(out=pb[2][:, 64 + g * NS:64 + (g + 1) * NS], in_=zPs[R, C_AK + g * 128:C_AK + (g + 1) * 128], identity=ident[R, R]), r=["zPs", "ident"], w=[PB[2]])
            P.op("dve", lambda e: e.tensor_copy(out=zqT, in_=pb[2][:, 0:4 * NS]), r=[PB[2]], w=["zqT"])
            P.op("dve", lambda e: e.tensor_copy(out=zkTn, in_=pb[2][:, 64:64 + 2 * NS]), r=[PB[2]], w=["zkTn"])
            zvn_f = A.f32(NS * 256); zvn = A.bf16(NS * 2 * 130)
            zvn4 = zvn.rearrange("p (s g d) -> p s g d", s=NS, g=2)
            P.dma("sp", zvn_f[0:1, :].rearrange("p (s c) -> p s c", s=NS), S["ps"][:, C_AV:C_AV + 256].rearrange("(o s) c -> o s c", o=1), w=["zvn_f"])
            P.op("dve", lambda e: e.memset(zvn[0:1, :], 1.0), w=["zvn"])
            P.op("dve", lambda e: e.tensor_copy(out=zvn4[0:1, :, :, 0:128], in_=zvn_f[0:1, :].rearrange("p (s g d) -> p s g d", s=NS, g=2)), r=["zvn_f", "zvn"], w=["zvn"])
            NKS = 2049
            zIall = A.f32(NKS + 3); zikg = A.f32(NPG * 64); zikT = A.f32(NKS + 3)
            zik_rows = I["cache_ik"].rearrange("(r t) d -> r (t d)", t=8)
            zk_rows = I["cache_k"].rearrange("(r t) d -> r (t d)", t=8)
            zv_rows = I["cache_v"].rearrange("(r t) d -> r (t d)", t=8)
            zikg4 = zikg.rearrange("p (a t d) -> p a t d", a=2, t=8)
            zr8 = [A.f32(512), A.f32(512)]; zrw = A.f32(NKS + 3)
            zikg3 = zikg.rearrange("p (g d) -> p g d", g=NPG)
            for s_ in range(NS):
                for a_ in range(2):
                    col = s_ * 2 + a_
                    P.dma_raw("pool", lambda e, a_=a_, col=col: e.indirect_dma_start(out=zikg4[:, a_].rearrange("p t d -> p (t d)"), out_offset=None, in_=zik_rows, in_offset=bass.IndirectOffsetOnAxis(ap=zidx_i[:, col:col + 1], axis=0)),
                              r=["zidx_i"], w=["zikg"], sw=True)
                for q4 in range(4):
                    for i4 in range(4):
                        bk = q4 * 4 + i4
                        t8, a_ = bk // 2, bk % 2
                        P.op("pe", lambda e, t8=t8, a_=a_, i4=i4: e.transpose(out=pb[3][0:64, i4 * 128:(i4 + 1) * 128], in_=zikg4[:, a_, t8, :], identity=ident), r=["zikg", "ident"], w=[PB[3]])
                    evac("act" if q4 % 2 else "dve", zikT[0:64, q4 * 512:(q4 + 1) * 512], pb[3][0:64, :], [PB[3]], ["zikT"])
                P.op("dve", lambda e, s_=s_: e.tensor_copy(out=zikT[0:64, 2048:2049], in_=zikn[0:64, s_:s_ + 1]), r=["zikn"], w=["zikT"])
                for kb in range(5):
                    c0, c1 = kb * 512, min(NKS, kb * 512 + 512)
                    w_ = c1 - c0
                    rb = zr8[kb % 2]; rbn = f"zr8{kb % 2}"
                    P.op("pe", lambda e, s_=s_, c0=c0, c1=c1, w_=w_, kb=kb: e.matmul(out=pb[4 + kb % 2][0:8, 0:w_], lhsT=ziqT3[0:64, s_, :], rhs=zikT[0:64, c0:c1], start=True, stop=True), r=["ziqT", "zikT"], w=[PB[4 + kb % 2]])
                    P.op("dve", lambda e, s_=s_, w_=w_, kb=kb, rb=rb: e.tensor_scalar(out=rb[0:8, 0:w_], in0=pb[4 + kb % 2][0:8, 0:w_], scalar1=0.0, scalar2=ziwT[0:8, s_:s_ + 1], op0=ALU.max, op1=ALU.mult),
                         r=[PB[4 + kb % 2], "ziwT"], w=[rbn])
                    P.op("pe", lambda e, w_=w_, kb=kb, rb=rb: e.matmul(out=pb[6 + kb % 2][0:1, 0:w_], lhsT=zones[0:8, 0:1], rhs=rb[0:8, 0:w_], start=True, stop=True), r=["zones", rbn], w=[PB[6 + kb % 2]])
                    evac("act", zrw[0:1, c0:c1], pb[6 + kb % 2][0:1, 0:w_], [PB[6 + kb % 2]], ["zrw"])
                P.dma("sp", zIall[s_:s_ + 1, 0:NKS], zrw[0:1, 0:NKS], r=["zrw"], w=["zIall"])
            zbs = A.f32(16); zjunk = A.bf16(NKS + 3); zMB = A.f32(NKS + 3)
            zlo, zrng, zthr, zcnt, zmm = zbs[R, 0:1], zbs[R, 1:2], zbs[R, 2:3], zbs[R, 3:4], zbs[R, 4:5]
            P.op("dve", lambda e: e.tensor_reduce(out=zlo, in_=zIall[R, 0:NKS], axis=AX.X, op=ALU.min), r=["zIall"], w=["zlo"])
            P.op("dve", lambda e: e.tensor_reduce(out=zrng, in_=zIall[R, 0:NKS], axis=AX.X, op=ALU.max), r=["zIall"], w=["zrng"])
            P.op("dve", lambda e: e.tensor_tensor(out=zrng, in0=zrng, in1=zlo, op=ALU.subtract), r=["zrng", "zlo"], w=["zrng"])
            for it in range(24):
                st = 2.0 ** -(it + 1)
                P.op("dve", lambda e, st=st: e.scalar_tensor_tensor(out=zthr, in0=zrng, scalar=st, in1=zlo, op0=ALU.mult, op1=ALU.add), r=["zrng", "zlo"], w=["zthr"])
                P.op("dve", lambda e: e.tensor_scalar(out=zjunk[R, 0:NKS], in0=zIall[R, 0:NKS], scalar1=zthr, scalar2=None, op0=ALU.is_ge, op1=ALU.add, accum_out=zcnt), r=["zIall", "zthr"], w=["zjunk", "zcnt"])
                P.op("dve", lambda e: e.tensor_scalar(out=zmm, in0=zcnt, scalar1=255.5, scalar2=zrng, op0=ALU.is_ge, op1=ALU.mult), r=["zcnt", "zrng"], w=["zmm"])
                P.op("dve", lambda e, st=st: e.scalar_tensor_tensor(out=zlo, in0=zmm, scalar=st, in1=zlo, op0=ALU.mult, op1=ALU.add), r=["zmm", "zlo"], w=["zlo"])
            P.op("dve", lambda e: e.tensor_scalar(out=zMB[R, 0:NKS], in0=zIall[R, 0:NKS], scalar1=zlo, scalar2=None, op0=ALU.is_ge), r=["zIall", "zlo"], w=["zMB"])
            zMT = A.f32(NPG * NS); zMT3 = zMT.rearrange("p (g s) -> p g s", g=NPG); zMn = A.f32(NS)
            for pg in range(NPG):
                P.op("pe", lambda e, pg=pg: e.transpose(out=pb[0][:, pg * NS:(pg + 1) * NS], in_=zMB[R, pg * 128:(pg + 1) * 128], identity=ident[R, R]), r=["zMB", "ident"], w=[PB[0]])
            evac("dve", zMT, pb[0][:, 0:NPG * NS], [PB[0]], ["zMT"])
            P.op("pe", lambda e: e.transpose(out=pb[1][0:1, 0:NS], in_=zMB[R, 2048:2049], identity=ident[R, R]), r=["zMB", "ident"], w=[PB[1]])
            evac("dve", zMn[0:1, :], pb[1][0:1, 0:NS], [PB[1]], ["zMn"])
            zkg = A.f32(NPG * 256); zvg = A.f32(NPG * 256)
            zkg4 = zkg.rearrange("p (a t c) -> p a t c", a=2, t=8); zvg4 = zvg.rearrange("p (a t c) -> p a t c", a=2, t=8)
            zKT = A.bf16(NPG * 256); zKT4 = zKT.rearrange("p (g k c) -> p g k c", g=NPG, k=2)
            zVb = A.bf16(NPG * 2 * 130); zVb4 = zVb.rearrange("p (g k d) -> p g k d", g=NPG, k=2)
            zP = A.bf16(NPG * 4); zP3 = zP.rearrange("p (g h) -> p g h", g=NPG); zPn = A.bf16(4)
            zsm = A.f32(16); zcr = A.f32(128)
            zo = A.f32(256); zoT = A.bf16(4 * NS); zoT3 = zoT.rearrange("p (h s) -> p h s", h=4)
            P.op("dve", lambda e: e.memset(zVb, 1.0), w=["zVb"])
            for s_ in range(NS):
                for a_ in range(2):
                    col = s_ * 2 + a_
                    P.dma_raw("pool", lambda e, a_=a_, col=col: e.indirect_dma_start(out=zkg4[:, a_].rearrange("p t c -> p (t c)"), out_offset=None, in_=zk_rows, in_offset=bass.IndirectOffsetOnAxis(ap=zidx_i[:, col:col + 1], axis=0)),
                              r=["zidx_i"], w=["zkg"], sw=True)
                    P.dma_raw("pool", lambda e, a_=a_, col=col: e.indirect_dma_start(out=zvg4[:, a_].rearrange("p t c -> p (t c)"), out_offset=None, in_=zv_rows, in_offset=bass.IndirectOffsetOnAxis(ap=zidx_i[:, col:col + 1], axis=0)),
                              r=["zidx_i"], w=["zvg"], sw=True)
                for a_ in range(2):
                    P.op("act", lambda e, a_=a_: e.activation(out=zVb.rearrange("p (t a k d) -> p a t k d", t=8, a=2, k=2)[:, a_, :, :, 0:128], in_=zvg4[:, a_].rearrange("p t (k d) -> p t k d", k=2), func=AF.Copy),
                         r=["zvg", "zVb"], w=["zVb"])
                for pq in range(8):
                    for i2 in range(2):
                        bk = pq * 2 + i2
                        t8, a_ = bk // 2, bk % 2
                        for g in range(2):
                            P.op("pe", lambda e, t8=t8, a_=a_, g=g, i2=i2, pq=pq: e.transpose(out=pb[2 + pq % 2][:, (i2 * 2 + g) * 128:(i2 * 2 + g + 1) * 128], in_=zkg4[:, a_, t8, g * 128:(g + 1) * 128], identity=ident),
                                 r=["zkg", "ident"], w=[PB[2 + pq % 2]])
                    evac("act" if pq % 2 else "dve", zKT[:, pq * 512:(pq + 1) * 512], pb[2 + pq % 2], [PB[2 + pq % 2]], ["zKT"])
                for pg in range(NPG):
                    for g in range(2):
                        P.op("pe", lambda e, pg=pg, g=g, s_=s_: e.matmul(out=pb[4][:, pg * 4 + g * 2:pg * 4 + g * 2 + 2], lhsT=zKT4[:, pg, g, :], rhs=zqT.rearrange("p (h s) -> p h s", h=4)[:, g * 2:(g + 1) * 2, s_], start=True, stop=True),
                             r=["zKT", "zqT"], w=[PB[4]])
                for g in range(2):
                    P.op("pe", lambda e, g=g, s_=s_: e.matmul(out=pb[5][0:1, g * 2:g * 2 + 2], lhsT=zkTn.rearrange("p (g s) -> p g s", g=2)[:, g, s_:s_ + 1], rhs=zqT.rearrange("p (h s) -> p h s", h=4)[:, g * 2:(g + 1) * 2, s_], start=True, stop=True),
                         r=["zkTn", "zqT"], w=[PB[5]])
                P.op("dve", lambda e: e.tensor_reduce(out=zsm[:, 0:1], in_=pb[4][:, 0:NPG * 4], axis=AX.X, op=ALU.max), r=[PB[4]], w=["zsm0"])
                P.op("pe", lambda e: e.transpose(out=pb[6][0:1, 0:128], in_=zsm[:, 0:1], identity=ident), r=["zsm0", "ident"], w=[PB[6]])
                evac("dve", zcr[0:1, :], pb[6][0:1, 0:128], [PB[6]], ["zcr"])
                P.op("dve", lambda e: e.tensor_reduce(out=zsm[0:1, 1:2], in_=zcr[0:1, :], axis=AX.X, op=ALU.max), r=["zcr"], w=["zsm1"])
                P.op("dve", lambda e: e.tensor_reduce(out=zsm[0:1, 2:3], in_=pb[5][0:1, 0:4], axis=AX.X, op=ALU.max), r=[PB[5]], w=["zsm2"])
                P.op("dve", lambda e: e.tensor_tensor(out=zsm[0:1, 1:2], in0=zsm[0:1, 1:2], in1=zsm[0:1, 2:3], op=ALU.max), r=["zsm1", "zsm2"], w=["zsm1"])
                P.op("dve", lambda e: e.tensor_scalar(out=zsm[0:1, 1:2], in0=zsm[0:1, 1:2], scalar1=-zSCALE, scalar2=None, op0=ALU.mult), r=["zsm1"], w=["zsm1"])
                P.op("pe", lambda e: e.matmul(out=pb[6][:, 128:129], lhsT=zones[0:1, :], rhs=zsm[0:1, 1:2], start=True, stop=True), r=["zones", "zsm1"], w=[PB[6]])
                evac("dve", zsm[:, 3:4], pb[6][:, 128:129], [PB[6]], ["zsm3"])
                P.op("act", lambda e: e.activation(out=zP, in_=pb[4][:, 0:NPG * 4], func=AF.Exp, scale=zSCALE, bias=zsm[:, 3:4]), r=[PB[4], "zsm3"], w=["zP"])
                P.op("act", lambda e: e.activation(out=zPn[0:1, :], in_=pb[5][0:1, 0:4], func=AF.Exp, scale=zSCALE, bias=zsm[0:1, 3:4]), r=[PB[5], "zsm3"], w=["zPn"])
                for h in range(4):
                    P.op("dve", lambda e, h=h, s_=s_: e.tensor_tensor(out=zP3[:, :, h], in0=zP3[:, :, h], in1=zMT3[:, :, s_], op=ALU.mult), r=["zP", "zMT"], w=["zP"])
                P.op("dve", lambda e, s_=s_: e.tensor_scalar(out=zPn[0:1, :], in0=zPn[0:1, :], scalar1=zMn[0:1, s_:s_ + 1], scalar2=None, op0=ALU.mult), r=["zPn", "zMn"], w=["zPn"])
                for g in range(2):
                    for pg in range(NPG):
                        P.op("pe", lambda e, pg=pg, g=g: e.matmul(out=pb[g][0:2, 0:130], lhsT=zP3[:, pg, g * 2:(g + 1) * 2], rhs=zVb4[:, pg, g, :], start=(pg == 0), stop=False), r=["zP", "zVb"], w=[PB[g]])
                    P.op("pe", lambda e, g=g, s_=s_: e.matmul(out=pb[g][0:2, 0:130], lhsT=zPn[0:1, g * 2:(g + 1) * 2], rhs=zvn4[0:1, s_, g, :], start=False, stop=True), r=["zPn", "zvn"], w=[PB[g]])
                for g in range(2):
                    P.op("dve", lambda e, g=g: e.reciprocal(out=zsm[0:2, 4 + g:5 + g], in_=pb[g][0:2, 128:129]), r=[PB[g]], w=["zsm4"])
                    P.op("dve", lambda e, g=g: e.tensor_scalar(out=zo[0:2, g * 128:(g + 1) * 128], in0=pb[g][0:2, 0:128], scalar1=zsm[0:2, 4 + g:5 + g], scalar2=None, op0=ALU.mult), r=[PB[g], "zsm4"], w=["zo"])
                for g in range(2):
                    P.op("pe", lambda e, g=g: e.transpose(out=pb[7][:, g * 2:(g + 1) * 2], in_=zo[0:2, g * 128:(g + 1) * 128], identity=ident[0:2, 0:2]), r=["zo", "ident"], w=[PB[7]])
                P.op("dve", lambda e, s_=s_: e.tensor_copy(out=zoT3[:, :, s_], in_=pb[7][:, 0:4]), r=[PB[7]], w=["zoT"])
            P.dma("sp", S["cT"][NT][:, 4:8, 0:NS], zoT3, r=["zoT"], w=["cTas"])

        if not os.environ.get('MK_NOD'):
            P.barrier()
            A.reset(persist0)
            zt = A.bf16(1024)
            P.op("pool", lambda e: e.memset(zt, 0.0), w=["zt"])
            if True:
                for ti in range(NT + 1):
                    if (ti == NT and os.environ.get('MK_NOTS')) or (ti < NT and (os.environ.get('MK_NOT') or ti >= int(os.environ.get("MK_TNT", NT)))):
                        P.dma("sp", S["cT"][ti][:, 4:8, :], zt[:, 0:512].rearrange("p (k t) -> p k t", k=4), r=["zt"], w=[f"cT{ti}"])
                    if ti == NT:
                        P.dma("sp", S["cT"][ti][:, 0:4, 16:128], zt[:, 0:448].rearrange("p (k t) -> p k t", k=4), r=["zt"], w=[f"cT{ti}"])
            g2_bc = A.f32(D); ga1_bc = A.f32(D); a2_bc = A.f32(D); sh2_bc = A.f32(D)
            a2_s = A.f32(D)
            P.dma("sp", g2_bc, I["g2"].to_broadcast([128, D]), w=["g2_bc"])
            P.dma("sp", ga1_bc, S["mod"][16:17, 2 * D:3 * D].to_broadcast([128, D]), r=["mod_scr"], w=["ga1_bc"])
            P.dma("sp", sh2_bc, S["mod"][16:17, 3 * D:4 * D].to_broadcast([128, D]), r=["mod_scr"], w=["sh2_bc"])
            P.dma("sp", a2_bc, S["mod"][16:17, 4 * D:5 * D].to_broadcast([128, D]), r=["mod_scr"], w=["a2_bc"])
            P.op("dve", lambda e: e.scalar_tensor_tensor(out=a2_bc, in0=a2_bc, scalar=1.0, in1=g2_bc, op0=ALU.add, op1=ALU.mult),
                 r=["a2_bc", "g2_bc"], w=["a2_bc"])
            P.op("dve", lambda e: e.scalar_tensor_tensor(out=a2_s[0:NS, :], in0=mod_sb[0:NS, 4 * D:5 * D], scalar=1.0, in1=g2_bc[0:NS, :], op0=ALU.add, op1=ALU.mult),
                 r=["mod_sb", "g2_bc"], w=["a2_s"])
            persistD = A.off
            w_out_b = A.bf16(8 * D); w_out3 = w_out_b.rearrange("p (k n) -> p k n", k=8)
            w_f_b = A.bf16(8 * 2 * DFF); w_f3 = w_f_b.rearrange("p (k n) -> p k n", k=8)
            persistD1 = A.off
            wst = [A.f32(8 * 512), A.f32(8 * 512)]
            for cb in range(2 + 11):
                ws3 = wst[cb % 2].rearrange("p (k n) -> p k n", k=8)
                if cb < 2:
                    src = I["w_out"][:, cb * 512:(cb + 1) * 512]; dst = w_out3[:, :, cb * 512:(cb + 1) * 512]; dn = "w_out_b"
                else:
                    src = I["w_ffn_in"][:, (cb - 2) * 512:(cb - 1) * 512]; dst = w_f3[:, :, (cb - 2) * 512:(cb - 1) * 512]; dn = "w_f_b"
                P.dma("sp", ws3, src.rearrange("(k p) n -> p k n", p=128), w=[f"wst{cb % 2}"])
                P.op("pool" if cb % 2 else "dve", lambda e, ws3=ws3, dst=dst: e.tensor_copy(out=dst, in_=ws3), r=[f"wst{cb % 2}"], w=[dn])
            P.seed_after_staging()
            A.reset(persistD1)
            xt = [A.f32(D), A.f32(D)]
            cTt = [A.bf16(8 * 128), A.bf16(8 * 128)]
            x1t = A.f32(D); h2 = A.f32(D); small = A.f32(16)
            h2T = A.bf16(8 * 512); h2T3 = h2T.rearrange("p (k t) -> p k t", k=8)
            uT = A.bf16(22 * 512); uT3 = uT.rearrange("p (f t) -> p f t", f=22)
            gsb = [A.f32(512), A.f32(512)]

            def rms_mod(rows, xin, xin_n, a_t, a_n, sh_t, sh_n, out, out_n):
                ss, rs = small[0:rows, 0:1], small[0:rows, 1:2]
                P.op("act", lambda e: e.activation(out=out, in_=xin, func=AF.Square, accum_out=ss), r=[xin_n], w=[out_n, "ss"])
                P.op("dve", lambda e: e.tensor_scalar(out=rs, in0=ss, scalar1=1.0 / D, scalar2=1e-6, op0=ALU.mult, op1=ALU.add), r=["ss"], w=["rs"])
                P.op("act", lambda e: e.activation(out=rs, in_=rs, func=AF.Sqrt), r=["rs"], w=["rs"])
                P.op("dve", lambda e: e.reciprocal(out=rs, in_=rs), r=["rs"], w=["rs"])
                P.op("dve", lambda e: e.scalar_tensor_tensor(out=out, in0=xin, scalar=rs, in1=a_t, op0=ALU.mult, op1=ALU.mult), r=[xin_n, "rs", a_n], w=[out_n])
                if sh_t is not None:
                    P.op("pool", lambda e: e.tensor_tensor(out=out, in0=out, in1=sh_t, op=ALU.add), r=[out_n, sh_n], w=[out_n])

            groups = [(g4, [(g4 * 4 + j, 128) for j in range(4)]) for g4 in range(4)] + [(4, [(NT, NS)])]
            for g4, tiles in groups:
                ntok = sum(r_ for _, r_ in tiles)
                col = 0
                for (ti, rows) in tiles:
                    it = ti
                    xb = xt[it % 2]; xn = f"xt{it % 2}"; cb_ = cTt[it % 2]; cn = f"cTt{it % 2}"
                    cT3 = cb_.rearrange("p (k t) -> p k t", k=8)
                    smp = (rows == NS)
                    P.dma("sp", xb[0:rows, :], I["xs"] if smp else I["x_own"][ti * 128:(ti + 1) * 128, :], w=[xn])
                    P.dma("sp", cT3, S["cT"][ti], r=[f"cT{ti}"], w=[cn])
                    ga1_t = mod_sb[0:NS, 2 * D:3 * D] if smp else ga1_bc
                    for hb in range(2):
                        for k in range(8):
                            P.op("pe", lambda e, k=k, hb=hb, cT3=cT3, rows=rows: e.matmul(out=pb[hb][0:rows, :], lhsT=cT3[:, k, 0:rows], rhs=w_out3[:, k, hb * 512:(hb + 1) * 512], start=(k == 0), stop=(k == 7)),
                                 r=[cn, "w_out_b"], w=[PB[hb]])
                        P.op("dve", lambda e, hb=hb, rows=rows, ga1_t=ga1_t: e.tensor_tensor(out=x1t[0:rows, hb * 512:(hb + 1) * 512], in0=pb[hb][0:rows, :], in1=ga1_t[0:rows, hb * 512:(hb + 1) * 512], op=ALU.mult),
                             r=[PB[hb], "ga1_bc", "mod_sb"], w=["x1t"])
                    P.op("pool", lambda e, rows=rows, xb=xb: e.tensor_tensor(out=x1t[0:rows, :], in0=x1t[0:rows, :], in1=xb[0:rows, :], op=ALU.add), r=["x1t", xn], w=["x1t"])
                    P.dma("sp", S["x1"][ti, 0:rows, :], x1t[0:rows, :], r=["x1t"], w=[f"x1_{ti}"])
                    if smp:
                        rms_mod(rows, x1t[0:rows, :], "x1t", a2_s[0:rows, :], "a2_s", mod_sb[0:rows, 3 * D:4 * D], "mod_sb", h2[0:rows, :], "h2")
                    else:
                        rms_mod(rows, x1t, "x1t", a2_bc, "a2_bc", sh2_bc, "sh2_bc", h2, "h2")
                    for k in range(8):
                        bank = 2 + k // 4
                        P.op("pe", lambda e, k=k, bank=bank, rows=rows: e.transpose(out=pb[bank][:, (k % 4) * 128:(k % 4) * 128 + rows], in_=h2[0:rows, k * 128:(k + 1) * 128], identity=ident[0:rows, 0:rows]),
                             r=["h2", "ident"], w=[PB[bank]])
                    for half_ in range(2):
                        evac(alt(), h2T3[:, half_ * 4:(half_ + 1) * 4, col:col + rows], pb[2 + half_][:, :].rearrange("p (k t) -> p k t", k=4)[:, :, 0:rows], [PB[2 + half_]], ["h2T"])
                    col += rows
                for fb in range(22):
                    for which in range(2):
                        bank = 4 + which * 2 + fb % 2
                        c0 = which * DFF + fb * 128
                        for k in range(8):
                            P.op("pe", lambda e, k=k, bank=bank, c0=c0, ntok=ntok: e.matmul(out=pb[bank][:, 0:ntok], lhsT=w_f3[:, k, c0:c0 + 128], rhs=h2T3[:, k, 0:ntok], start=(k == 0), stop=(k == 7)),
                                 r=["h2T", "w_f_b"], w=[PB[bank]])
                    gs_ = gsb[fb % 2]; gn = f"gsb{fb % 2}"
                    P.op("act", lambda e, fb=fb, gs_=gs_, ntok=ntok: e.activation(out=gs_[:, 0:ntok], in_=pb[4 + fb % 2][:, 0:ntok], func=AF.Silu), r=[PB[4 + fb % 2]], w=[gn])
                    P.op("dve", lambda e, fb=fb, gs_=gs_, ntok=ntok: e.tensor_tensor(out=uT3[:, fb, 0:ntok], in0=gs_[:, 0:ntok], in1=pb[6 + fb % 2][:, 0:ntok], op=ALU.mult),
                         r=[gn, PB[6 + fb % 2]], w=["uT"])
                P.dma("sp", S["uT"][g4], uT3, r=["uT"], w=[f"uT{g4}"])

            P.barrier()
            A.reset(persistD)
            ga2_bc = A.f32(D); gf_bc = A.f32(D)
            P.dma("sp", gf_bc, I["g_final"].to_broadcast([128, D]), w=["gf_bc"])
            P.dma("sp", ga2_bc, S["mod"][16:17, 5 * D:6 * D].to_broadcast([128, D]), r=["mod_scr"], w=["ga2_bc"])
            w_o_b = A.bf16(22 * D); w_o3 = w_o_b.rearrange("p (k n) -> p k n", k=22)
            persistD2 = A.off
            wst = [A.f32(22 * 256), A.f32(22 * 256)]
            for cb in range(4):
                ws3 = wst[cb % 2].rearrange("p (k n) -> p k n", k=22)
                P.dma("sp", ws3, I["w_ffn_out"][:, cb * 256:(cb + 1) * 256].rearrange("(k p) n -> p k n", p=128), w=[f"wst{cb % 2}"])
                P.op("pool" if cb % 2 else "dve", lambda e, ws3=ws3, cb=cb: e.tensor_copy(out=w_o3[:, :, cb * 256:(cb + 1) * 256], in_=ws3), r=[f"wst{cb % 2}"], w=["w_o_b"])
            P.seed_after_staging()
            A.reset(persistD2)
            uTg = [A.bf16(22 * 512), A.bf16(22 * 512)]
            x1b = [A.f32(D), A.f32(D)]
            x2 = A.f32(D); small = A.f32(16)
            yb = [A.f32(D), A.f32(D)]
            for g4, tiles in groups:
                ug = uTg[g4 % 2]; un = f"uTg{g4 % 2}"
                ug3 = ug.rearrange("p (f t) -> p f t", f=22)
                P.dma("sp", ug3, S["uT"][g4], r=[f"uT{g4}"], w=[un])
                col = 0
                for (ti, rows) in tiles:
                    smp = (rows == NS)
                    xb = x1b[ti % 2]; xn = f"x1b{ti % 2}"; yo = yb[ti % 2]; yn = f"yb{ti % 2}"
                    P.dma("sp", xb[0:rows, :], S["x1"][ti, 0:rows, :], r=[f"x1_{ti}"], w=[xn])
                    ga2_t = mod_sb[0:NS, 5 * D:6 * D] if smp else ga2_bc
                    for hb in range(2):
                        for kf in range(22):
                            P.op("pe", lambda e, kf=kf, hb=hb, rows=rows, col=col, ug3=ug3: e.matmul(out=pb[hb][0:rows, :], lhsT=ug3[:, kf, col:col + rows], rhs=w_o3[:, kf, hb * 512:(hb + 1) * 512], start=(kf == 0), stop=(kf == 21)),
                                 r=[un, "w_o_b"], w=[PB[hb]])
                        P.op("dve", lambda e, hb=hb, rows=rows, ga2_t=ga2_t: e.tensor_tensor(out=x2[0:rows, hb * 512:(hb + 1) * 512], in0=pb[hb][0:rows, :], in1=ga2_t[0:rows, hb * 512:(hb + 1) * 512], op=ALU.mult),
                             r=[PB[hb], "ga2_bc", "mod_sb"], w=["x2"])
                    P.op("pool", lambda e, rows=rows, xb=xb: e.tensor_tensor(out=x2[0:rows, :], in0=x2[0:rows, :], in1=xb[0:rows, :], op=ALU.add), r=["x2", xn], w=["x2"])
                    rms_mod(rows, x2[0:rows, :], "x2", gf_bc[0:rows, :], "gf_bc", None, None, yo[0:rows, :], yn)
                    P.dma("sp", O["y_s"] if smp else O["y_own"][ti * 128:(ti + 1) * 128, :], yo[0:rows, :], r=[yn])
                    col += rows

        P.build(ctx)
        global LAST_PROG
        LAST_PROG = P
    return nc


def rope_table(pos):
    pos = np.asarray(pos, np.float64)[:, None]
    invA = 500000.0 ** (-np.arange(16, dtype=np.float64) / 16)
    invI = 500000.0 ** (-np.arange(8, dtype=np.float64) / 8)
    angA = (pos.astype(np.float32) * invA.astype(np.float32)[None, :]).astype(np.float32)
    angI = (pos.astype(np.float32) * invI.astype(np.float32)[None, :]).astype(np.float32)
    t = np.concatenate([np.tile(np.cos(angA), (1, 4)), np.tile(np.sin(angA), (1, 4)),
                        np.tile(np.cos(angI), (1, 8)), np.tile(np.sin(angI), (1, 8))], axis=1)
    return np.ascontiguousarray(t.astype(np.float32))


_NC_CACHE = {}


def kernel(x_prompt, x_sample, c_prompt, c_sample, cache_k, cache_v, cache_idx_k, page_table, state_conv, state_ssm,
           w_ada, b_ada, g_norm1, w_in, w_conv, a_log, dt_bias, g_gdn_norm, w_out, g_norm2, w_ffn_in, w_ffn_out, g_final):
    f = lambda a: np.ascontiguousarray(np.asarray(a, dtype=np.float32))
    x_prompt = f(x_prompt); x_sample = f(x_sample)
    w_in_p = np.ascontiguousarray(f(w_in)[0][:, PERM])
    wc_p = np.ascontiguousarray(np.concatenate([f(w_conv)[0][:, 512:1536], f(w_conv)[0][:, 0:512]], axis=1))
    jj, cc = np.meshgrid(np.arange(128), np.arange(128), indexing="ij")
    GCONST = np.ascontiguousarray(np.concatenate([
        (jj <= cc).astype(np.float32),
        np.ones((128, 128), np.float32),
        np.where(cc >= jj, 1e4, 0.0).astype(np.float32),
        np.where(cc < jj, -1e4, 0.0).astype(np.float32),
        np.tile(np.eye(128, dtype=np.float32), (1, 4))], axis=1))
    CAUS = np.ascontiguousarray(np.where(np.arange(128)[None, :] > np.arange(128)[:, None], -30000.0, 0.0).astype(np.float32))
    pp = np.arange(128)
    TSEL = np.zeros((128, 257), np.float32)
    TSEL[:, :256] = np.tile((np.arange(8)[None, :] == (pp // 16)[:, None]).astype(np.float32), (1, 32))
    TSEL[:, 256] = pp % 16
    cik2 = f(cache_idx_k).reshape(-1, 64); ck2 = f(cache_k).reshape(-1, 256); cv2 = f(cache_v).reshape(-1, 256)
    EYE16 = np.ascontiguousarray(np.tile(np.eye(16, dtype=np.float32).reshape(1, 256), (128, 1)))
    in_maps = []
    for c in range(8):
        b, s = c // 2, c % 2
        own = slice(s * HALF, (s + 1) * HALF); oth = slice((1 - s) * HALF, (2 - s) * HALF)
        m = {
            "x_own": f(x_prompt[b, own]), "x_oth": f(x_prompt[b, oth]),
            "cin": f(np.concatenate([np.asarray(c_sample)[c * NS:(c + 1) * NS], np.asarray(c_prompt)[b:b + 1]], 0)),
            "xs": f(x_sample[c * NS:(c + 1) * NS, 0]),
            "w_ada": f(w_ada)[0], "b_ada": f(b_ada), "g1": f(g_norm1), "w_in": w_in_p, "w_conv": f(w_conv)[0],
            "a_log": f(a_log), "dt_bias": f(dt_bias), "g_gdn": f(g_gdn_norm), "w_out": f(w_out)[0], "g2": f(g_norm2),
            "w_ffn_in": f(w_ffn_in)[0], "w_ffn_out": f(w_ffn_out)[0], "g_final": f(g_final)[None, :],
            "tab_own": rope_table(np.arange(s * HALF, (s + 1) * HALF)), "tab_oth": rope_table(np.arange((1 - s) * HALF, (2 - s) * HALF)),
            "tab_s": rope_table(np.full(NS, 2048)),
            "flags": np.tile(np.array([[float(s), (s - 1) * 30000.0, 0, 0]], np.float32), (128, 1)),
            "ident": np.eye(128, dtype=np.float32),
            "state_conv": f(np.asarray(state_conv)[0, c * NS:(c + 1) * NS]),
            "gconst": GCONST, "wc_p": wc_p,
            "state_ssm": f(np.asarray(state_ssm)[0, c * NS:(c + 1) * NS]), "eye16": EYE16, "caus": CAUS,
            "pt": np.ascontiguousarray(np.asarray(page_table, np.int32)[c * NS:(c + 1) * NS].reshape(1, NS * 16)), "tsel": TSEL,
            "cache_ik": cik2, "cache_k": ck2, "cache_v": cv2,
        }
        if os.environ.get('MK_NOTS'):
            for k_ in ("cache_ik", "cache_k", "cache_v"):
                m.pop(k_)
        in_maps.append(m)
    if "nc" not in _NC_CACHE:
        _NC_CACHE["nc"] = build_program()
    res = run_bass_kernel_spmd(_NC_CACHE["nc"], in_maps, core_ids=list(range(8)))
    R = res.results
    B = 4
    y_prompt = np.zeros((B, T, D), np.float32); nk = np.zeros((1, B, T, 2, 128), np.float32); nv = np.zeros_like(nk)
    nik = np.zeros((1, B, T, 64), np.float32); nconv = np.zeros((1, B, 3, 1536), np.float32); nssm = np.zeros((1, B, 4, 128, 128), np.float32)
    y_s = np.zeros((128, 1, D), np.float32); ks = np.zeros((1, 128, 1, 2, 128), np.float32); vs = np.zeros_like(ks)
    iks = np.zeros((1, 128, 1, 64), np.float32); convs = np.zeros((1, 128, 3, 1536), np.float32); ssms = np.zeros((1, 128, 4, 128, 128), np.float32)
    for c in range(8):
        b, s = c // 2, c % 2
        own = slice(s * HALF, (s + 1) * HALF)
        r = R[c]
        y_prompt[b, own] = r["y_own"]; nk[0, b, own] = r["k_own"].reshape(HALF, 2, 128); nv[0, b, own] = r["v_own"].reshape(HALF, 2, 128)
        nik[0, b, own] = r["ik_own"]
        if s == 1:
            nconv[0, b] = r["conv_tail"]; nssm[0, b] = r["ssm_fin"]
        sl = slice(c * NS, (c + 1) * NS)
        y_s[sl, 0] = r["y_s"]; ks[0, sl, 0] = r["k_s"].reshape(NS, 2, 128); vs[0, sl, 0] = r["v_s"].reshape(NS, 2, 128)
        iks[0, sl, 0] = r["ik_s"]; convs[0, sl] = r["conv_s"]; ssms[0, sl] = r["ssm_s"]
    return (y_prompt, y_s, nk, nv, nik, nconv, nssm, ks, vs, iks, convs, ssms)
```

```python
import numpy as np
from contextlib import ExitStack
import concourse.bass as bass
import concourse.mybir as mybir
from concourse.bass_utils import run_bass_kernel_spmd

F32 = mybir.dt.float32
BF16 = mybir.dt.bfloat16
I32 = mybir.dt.int32
U32 = mybir.dt.uint32
AF = mybir.ActivationFunctionType
ALU = mybir.AluOpType
AX = mybir.AxisListType

import os
G_NT = int(os.environ.get("MK_GNT", "16"))
GSTOP = int(os.environ.get("MK_GSTOP", "99"))
NOS = int(os.environ.get("MK_NOS", "0"))
NOG = int(os.environ.get("MK_NOG", "0"))
ENGS = ["pe", "act", "dve", "pool", "sp"]
EPOCH = 12000
N_DMA_SEMS = 40
N_SW_SEMS = 12

D = 1024
T = 4096
HALF = 2048
NT = 16
NS = 16
INC = 3664
DFF = 2816
C_K, C_V, C_B, C_A, C_AK, C_AV, C_IK = 0, 512, 1024, 1028, 1032, 1288, 1544
N_OTH = 1608
C_Q, C_Z, C_AQ, C_IQ, C_IW = 1608, 2120, 2632, 3144, 3656
PERM = np.concatenate([np.arange(512, 1024), np.arange(1024, 1536), np.arange(2048, 2056),
                       np.arange(2568, 2824), np.arange(2824, 3080), np.arange(3592, 3656),
                       np.arange(0, 512), np.arange(1536, 2048), np.arange(2056, 2568),
                       np.arange(3080, 3592), np.arange(3656, 3664)])
GS_W = 2056
TABW = 256


class Prog:
    def __init__(self, nc):
        self.nc = nc
        self.ops = {e: [] for e in ENGS}
        self.res = {}
        self.seed = []
        self.dma_cum = [0] * (N_DMA_SEMS + N_SW_SEMS)
        self.dma_rr = 0
        self.sw_rr = 0

    def _deps(self, r, w):
        deps = []
        for name in r:
            st = self.res.get(name)
            if st and st[0] is not None:
                deps.append(st[0])
        for name in w:
            st = self.res.get(name)
            if st:
                if st[0] is not None:
                    deps.append(st[0])
                deps.extend(st[1])
            else:
                deps.extend(self.seed)
        return deps

    @staticmethod
    def _tkey(t):
        return (t[0], t[1])

    def _commit(self, tok, r, w):
        k = self._tkey(tok)
        for name in r:
            st = self.res.setdefault(name, [None, []])
            if not os.environ.get("MK_NOPRUNE"):
                st[1] = [t for t in st[1] if self._tkey(t) != k]
            st[1].append(tok)
        for name in w:
            self.res[name] = [tok, []]

    def op(self, eng, fn, r=(), w=()):
        deps = self._deps(r, w)
        tok = ("e", eng, len(self.ops[eng]))
        self.ops[eng].append({"deps": deps, "fn": fn, "dma": None, "sig": False})
        self._commit(tok, r, w)
        return tok

    def dma_raw(self, q, fn, r=(), w=(), sw=False):
        deps = self._deps(r, w)
        if sw:
            s = N_DMA_SEMS + self.sw_rr
            self.sw_rr = (self.sw_rr + 1) % N_SW_SEMS
        else:
            s = self.dma_rr
            self.dma_rr = (self.dma_rr + 1) % N_DMA_SEMS
        if self.dma_cum[s] > 0:
            deps.append(("d", s, self.dma_cum[s]))
        self.dma_cum[s] += 16
        tok = ("d", s, self.dma_cum[s])
        self.ops[q].append({"deps": deps, "fn": fn, "dma": (s, self.dma_cum[s]), "sig": False})
        self._commit(tok, r, w)
        return tok

    def dma(self, q, out, in_, r=(), w=(), **kw):
        return self.dma_raw(q, lambda e, out=out, in_=in_, kw=kw: e.dma_start(out=out, in_=in_, **kw), r, w)

    def seed_after_staging(self, names=("wst0", "wst1")):
        toks = list(self.seed)
        for n in names:
            st = self.res.get(n)
            if st:
                if st[0] is not None:
                    toks.append(st[0])
                toks.extend(st[1])
        self.seed = toks

    def barrier(self):
        deps_all = []
        for st in self.res.values():
            if st[0] is not None:
                deps_all.append(st[0])
            deps_all.extend(st[1])
        best = {}
        for t in ([] if os.environ.get("MK_NOPRUNE") else deps_all):
            k = self._tkey(t)
            if k not in best or best[k][2] < t[2]:
                best[k] = t
        for e in ENGS:
            for i in range(len(self.ops[e]) - 1, -1, -1):
                if self.ops[e][i]["dma"] is None:
                    best[("e", e)] = ("e", e, i)
                    break
        if not os.environ.get("MK_NOPRUNE"):
            deps_all = list(best.values())
        toks = []
        for e in ENGS:
            toks.append(("e", e, len(self.ops[e])))
            self.ops[e].append({"deps": list(deps_all), "fn": None, "dma": None, "sig": False})
        for e in ENGS:
            self.ops[e].append({"deps": list(toks), "fn": None, "dma": None, "sig": False})
        self.res = {}
        self.seed = []

    def build(self, ctx):
        nc = self.nc
        fin = [("d", s, c) for s, c in enumerate(self.dma_cum) if c > 0]
        self.ops["sp"].append({"deps": fin, "fn": None, "dma": None, "sig": False})
        for e in ENGS:
            for o in self.ops[e]:
                for d in o["deps"]:
                    if d[0] == "e":
                        if d[1] == "pe" and e == "pe":
                            continue
                        self.ops[d[1]][d[2]]["sig"] = True
        signo = {}
        nsig = {}
        for e in ENGS:
            c = 0
            for i, o in enumerate(self.ops[e]):
                if o["sig"]:
                    c += 1
                    signo[(e, i)] = c
            nsig[e] = c
        esem = {e: [ctx.enter_context(nc.semaphore(f"s_{e}_{k}")) for k in range(max(1, (nsig[e] + EPOCH - 1) // EPOCH))]
                for e in ENGS}
        dsem = [ctx.enter_context(nc.semaphore(f"s_dma_{k}")) for k in range(N_DMA_SEMS + N_SW_SEMS)]
        block = ctx.enter_context(nc.Block())
        eobj = {"pe": nc.tensor, "act": nc.scalar, "dve": nc.vector, "pool": nc.gpsimd, "sp": nc.sync}

        self.trace = {e: [] for e in ENGS}

        def body_for(e):
            def body(eng):
                known = {}
                tr = self.trace[e]
                for i, o in enumerate(self.ops[e]):
                    need = {}
                    for d in o["deps"]:
                        if d[0] == "e":
                            if d[1] == "pe" and e == "pe":
                                continue
                            key, val = ("e", d[1]), signo[(d[1], d[2])]
                        else:
                            key, val = ("d", d[1]), d[2]
                        if known.get(key, 0) >= val:
                            continue
                        if need.get(key, 0) < val:
                            need[key] = val
                    for key, val in need.items():
                        if key[0] == "e":
                            ep = (val - 1) // EPOCH
                            eng.wait_ge(esem[key[1]][ep], val - ep * EPOCH)
                            tr.append(("w", (key[1], ep), val - ep * EPOCH))
                        else:
                            eng.wait_ge(dsem[key[1]], val)
                            tr.append(("w", ("d", key[1]), val))
                        known[key] = val
                    if o["fn"] is None:
                        if o["sig"]:
                            sn = signo[(e, i)]
                            ep = (sn - 1) // EPOCH
                            eng.nop().then_inc(esem[e][ep], 1)
                            tr.append(("i", (e, ep), 1))
                        continue
                    ins = o["fn"](eng)
                    if o["dma"] is not None:
                        ins.then_inc(dsem[o["dma"][0]], 16)
                        tr.append(("i", ("d", o["dma"][0]), 16))
                    elif o["sig"]:
                        sn = signo[(e, i)]
                        ep = (sn - 1) // EPOCH
                        ins.then_inc(esem[e][ep], 1)
                        tr.append(("i", (e, ep), 1))
            return body

        block.tensor(body_for("pe"))
        block.scalar(body_for("act"))
        block.vector(body_for("dve"))
        block.gpsimd(body_for("pool"))
        block.sync(body_for("sp"))


class Arena:
    def __init__(self, t, n):
        self.t, self.n, self.off, self.uid = t, n, 0, 0

    def reset(self, to=0):
        self.off = to

    def f32(self, cols):
        a = self.t[:, self.off:self.off + cols]
        self.off += cols
        assert self.off <= self.n, ("arena overflow", self.off, self.n)
        return a

    def bf16(self, cols):
        c32 = (cols + 1) // 2
        a = self.t[:, self.off:self.off + c32].bitcast(BF16)
        self.off += c32
        assert self.off <= self.n, ("arena overflow", self.off, self.n)
        return a


def build_program(stage=99):
    nc = bass.Bass("TRN2", target_bir_lowering=False)
    dt_in = lambda name, shape, dt=F32: nc.dram_tensor(name, list(shape), dt, kind="ExternalInput").ap()
    dt_out = lambda name, shape, dt=F32: nc.dram_tensor(name, list(shape), dt, kind="ExternalOutput").ap()
    dt_scr = lambda name, shape, dt=F32: nc.dram_tensor(name, list(shape), dt, kind="Internal").ap()

    I = {}
    I["x_own"] = dt_in("x_own", [HALF, D]); I["x_oth"] = dt_in("x_oth", [HALF, D])
    I["cin"] = dt_in("cin", [17, D]); I["xs"] = dt_in("xs", [NS, D])
    I["w_ada"] = dt_in("w_ada", [D, 6 * D]); I["b_ada"] = dt_in("b_ada", [1, 6 * D])
    I["g1"] = dt_in("g1", [1, D]); I["w_in"] = dt_in("w_in", [D, INC])
    I["w_conv"] = dt_in("w_conv", [4, 1536]); I["a_log"] = dt_in("a_log", [1, 4]); I["dt_bias"] = dt_in("dt_bias", [1, 4])
    I["g_gdn"] = dt_in("g_gdn", [1, 128]); I["w_out"] = dt_in("w_out", [D, D]); I["g2"] = dt_in("g2", [1, D])
    I["w_ffn_in"] = dt_in("w_ffn_in", [D, 2 * DFF]); I["w_ffn_out"] = dt_in("w_ffn_out", [DFF, D]); I["g_final"] = dt_in("g_final", [1, D])
    I["tab_own"] = dt_in("tab_own", [HALF, TABW]); I["tab_oth"] = dt_in("tab_oth", [HALF, TABW]); I["tab_s"] = dt_in("tab_s", [NS, TABW])
    I["flags"] = dt_in("flags", [128, 4]); I["ident"] = dt_in("ident", [128, 128])
    I["state_conv"] = dt_in("state_conv", [NS, 3, 1536])
    I["gconst"] = dt_in("gconst", [128, 1024]); I["wc_p"] = dt_in("wc_p", [4, 1536])
    I["state_ssm"] = dt_in("state_ssm", [NS, 4, 128, 128]); I["eye16"] = dt_in("eye16", [128, 256])
    I["caus"] = dt_in("caus", [128, 128])
    I["pt"] = dt_in("pt", [1, NS * 16], I32); I["tsel"] = dt_in("tsel", [128, 257])
    NPHYS = 2560
    if not os.environ.get('MK_NOTS'):
        I["cache_ik"] = dt_in("cache_ik", [NPHYS * 128, 64]); I["cache_k"] = dt_in("cache_k", [NPHYS * 128, 256]); I["cache_v"] = dt_in("cache_v", [NPHYS * 128, 256])

    O = {}
    O["y_own"] = dt_out("y_own", [HALF, D]); O["k_own"] = dt_out("k_own", [HALF, 256]); O["v_own"] = dt_out("v_own", [HALF, 256])
    O["ik_own"] = dt_out("ik_own", [HALF, 64]); O["conv_tail"] = dt_out("conv_tail", [3, 1536]); O["ssm_fin"] = dt_out("ssm_fin", [4, 128, 128])
    O["y_s"] = dt_out("y_s", [NS, D]); O["k_s"] = dt_out("k_s", [NS, 256]); O["v_s"] = dt_out("v_s", [NS, 256]); O["ik_s"] = dt_out("ik_s", [NS, 64])
    O["conv_s"] = dt_out("conv_s", [NS, 3, 1536]); O["ssm_s"] = dt_out("ssm_s", [NS, 4, 128, 128])

    S = {}
    S["mod"] = dt_scr("mod_scr", [17, 6 * D])
    S["gs"] = dt_scr("gs_scr", [2, 3 + HALF, GS_W])
    S["qT"] = dt_scr("qT_scr", [NT, 128, 4, 128], BF16)
    S["iqT"] = dt_scr("iqT_scr", [NT, 128, 4, 128], BF16)
    S["cT"] = dt_scr("cT_scr", [NT + 1, 128, 8, 128], BF16)
    S["x1"] = dt_scr("x1_scr", [NT + 1, 128, D])
    S["uT"] = dt_scr("uT_scr", [5, 128, 22, 512], BF16)
    S["ps"] = dt_scr("ps_scr", [NS, INC])

    ctx = ExitStack()
    with ctx:
        ctx.enter_context(nc.allow_low_precision(reason="bf16 matmul operands, fp32 accumulate"))
        P = Prog(nc)
        ARN = 52800
        arena_t = ctx.enter_context(nc.sbuf_tensor("arena", [128, ARN], F32))
        A = Arena(arena_t, ARN)
        pb = [ctx.enter_context(nc.psum_tensor(f"pb{i}", [128, 512], F32))[:, :] for i in range(8)]
        PB = [f"pb{i}" for i in range(8)]
        cnt = [0]

        def alt():
            cnt[0] += 1
            return "act" if cnt[0] % 2 else "dve"

        def evac(eng, out, in_, r, w):
            if eng == "act":
                P.op("act", lambda e: e.activation(out=out, in_=in_, func=AF.Copy), r=r, w=w)
            else:
                P.op(eng, lambda e: e.tensor_copy(out=out, in_=in_), r=r, w=w)

        ident = A.f32(128)
        flags = A.f32(4)
        P.dma("sp", ident, I["ident"], w=["ident"])
        P.dma("sp", flags, I["flags"], w=["flags"])
        mod_sb = A.f32(6 * D)
        persist0 = A.off

        cs = A.f32(D)
        csT = A.f32(8 * 17)
        csT3 = csT.rearrange("p (k m) -> p k m", k=8)
        ones = A.f32(128)
        bstage = A.f32(512)
        P.op("dve", lambda e: e.memset(ones, 1.0), w=["ones"])
        P.dma("sp", cs[0:17, :], I["cin"], w=["cs"])
        P.op("act", lambda e: e.activation(out=cs[0:17, :], in_=cs[0:17, :], func=AF.Silu), r=["cs"], w=["cs"])
        for k in range(8):
            P.op("pe", lambda e, k=k: e.transpose(out=pb[0][:, k * 17:(k + 1) * 17], in_=cs[0:17, k * 128:(k + 1) * 128], identity=ident[0:17, 0:17]),
                 r=["cs", "ident"], w=[PB[0]])
        evac("dve", csT, pb[0][:, 0:8 * 17], [PB[0]], ["csT"])
        wst = [A.f32(8 * 512), A.f32(8 * 512)]
        for cb in range(12):
            ws = wst[cb % 2]
            ws3 = ws.rearrange("p (k n) -> p k n", k=8)
            P.dma("sp", ws3, I["w_ada"][:, cb * 512:(cb + 1) * 512].rearrange("(k p) n -> p k n", p=128), w=[f"wst{cb % 2}"])
            P.dma("sp", bstage[0:1, :], I["b_ada"][:, cb * 512:(cb + 1) * 512], w=["bstage"])
            bank = 1 + cb % 2
            for k in range(8):
                P.op("pe", lambda e, k=k, ws3=ws3, bank=bank: e.matmul(out=pb[bank][0:17, :], lhsT=csT3[:, k, :], rhs=ws3[:, k, :], start=(k == 0), stop=False),
                     r=["csT", f"wst{cb % 2}"], w=[PB[bank]])
            P.op("pe", lambda e, bank=bank: e.matmul(out=pb[bank][0:17, :], lhsT=ones[0:1, 0:17], rhs=bstage[0:1, :], start=False, stop=True),
                 r=["ones", "bstage"], w=[PB[bank]])
            evac("act", mod_sb[0:17, cb * 512:(cb + 1) * 512], pb[bank][0:17, :], [PB[bank]], ["mod_sb"])
        P.dma("sp", S["mod"], mod_sb[0:17, :], r=["mod_sb"], w=["mod_scr"])
        A.reset(persist0)
        def alloc_att():
            KT = A.bf16(2 * T); ikT = A.bf16(T); VA = A.bf16(32 * 2 * 130); iwabs = A.f32(NT * 8); iwsgn = A.f32(NT * 8)
            ksq_ = A.f32(64); qsq_ = A.f32(64)
            return KT, ikT, VA, iwabs, iwsgn, ksq_, qsq_
        OLDALLOC = bool(os.environ.get("MK_OLDALLOC"))
        if not OLDALLOC:
            KT, ikT, VA, iwabs, iwsgn, ksq, qsq = alloc_att()
        persist1 = A.off
        a1_bc = A.f32(D); sh1_bc = A.f32(D)
        g1_bc = A.f32(D); a1_s = A.f32(D)
        P.dma("sp", g1_bc, I["g1"].to_broadcast([128, D]), w=["g1_bc"])
        P.dma("sp", a1_bc, S["mod"][16:17, D:2 * D].to_broadcast([128, D]), r=["mod_scr"], w=["a1_bc"])
        P.dma("sp", sh1_bc, S["mod"][16:17, 0:D].to_broadcast([128, D]), r=["mod_scr"], w=["sh1_bc"])
        P.op("dve", lambda e: e.scalar_tensor_tensor(out=a1_bc, in0=a1_bc, scalar=1.0, in1=g1_bc, op0=ALU.add, op1=ALU.mult),
             r=["a1_bc", "g1_bc"], w=["a1_bc"])
        P.op("dve", lambda e: e.scalar_tensor_tensor(out=a1_s[0:NS, :], in0=mod_sb[0:NS, D:2 * D], scalar=1.0, in1=g1_bc[0:NS, :], op0=ALU.add, op1=ALU.mult),
             r=["mod_sb", "g1_bc"], w=["a1_s"])
        persistA = A.off

        w_in_b = A.bf16(8 * INC)
        w_in3 = w_in_b.rearrange("p (k n) -> p k n", k=8)
        if OLDALLOC:
            KT, ikT, VA, iwabs, iwsgn, ksq, qsq = alloc_att()
        KT3 = KT.rearrange("p (g s) -> p g s", g=2)
        VA4 = VA.rearrange("p (t g d) -> p t g d", t=32, g=2)
        persistB = A.off
        wst = [A.f32(8 * 512), A.f32(8 * 512)]
        nblk = (INC + 511) // 512
        for cb in range(nblk):
            c0, c1 = cb * 512, min(INC, cb * 512 + 512)
            ws3 = wst[cb % 2].rearrange("p (k n) -> p k n", k=8)
            P.dma("sp", ws3[:, :, 0:c1 - c0], I["w_in"][:, c0:c1].rearrange("(k p) n -> p k n", p=128), w=[f"wst{cb % 2}"])
            eng = "pool" if cb % 2 else "dve"
            P.op(eng, lambda e, ws3=ws3, c0=c0, c1=c1: e.tensor_copy(out=w_in3[:, :, c0:c1], in_=ws3[:, :, 0:c1 - c0]),
                 r=[f"wst{cb % 2}"], w=["w_in_b"])
        P.seed_after_staging()
        A.reset(persistB)
        xt = [A.f32(D), A.f32(D)]
        hh = A.f32(D)
        sq = A.f32(D)
        hT = [A.bf16(8 * 128), A.bf16(8 * 128)]
        Psb_l = [A.f32(INC), A.f32(INC)]
        tab = [A.f32(TABW), A.f32(TABW)]
        small = A.f32(16)
        rt = [A.f32(128) for _ in range(4)]
        ikd = A.f32(128)
        tstage = [A.bf16(4 * 128), A.bf16(4 * 128)]
        zrow = A.f32(GS_W)
        P.op("pool", lambda e: e.memset(zrow[0:3, :], 0.0), w=["zrow"])
        P.dma("sp", S["gs"][0, 0:3, :], zrow[0:3, :], r=["zrow"], w=["gs_pre0"])

        def rope(Psb, PN, rows, base, H, Dh, half, tb, coff, soff, tname):
            xv = Psb[0:rows, base:base + H * Dh].rearrange("p (h d) -> p h d", d=Dh)
            x1, x2 = xv[:, :, 0:half], xv[:, :, half:2 * half]
            cosv = tb[0:rows, coff:coff + H * half].rearrange("p (h i) -> p h i", i=half)
            sinv = tb[0:rows, soff:soff + H * half].rearrange("p (h i) -> p h i", i=half)
            t = [r_[0:rows, 0:H * half].rearrange("p (h i) -> p h i", i=half) for r_ in rt]
            P.op("dve", lambda e: e.tensor_tensor(out=t[0], in0=x1, in1=cosv, op=ALU.mult), r=[PN, tname], w=["rt0"])
            P.op("pool", lambda e: e.tensor_tensor(out=t[1], in0=x2, in1=sinv, op=ALU.mult), r=[PN, tname], w=["rt1"])
            P.op("dve", lambda e: e.tensor_tensor(out=t[2], in0=x2, in1=cosv, op=ALU.mult), r=[PN, tname], w=["rt2"])
            P.op("pool", lambda e: e.tensor_tensor(out=t[3], in0=x1, in1=sinv, op=ALU.mult), r=[PN, tname], w=["rt3"])
            P.op("dve", lambda e: e.tensor_tensor(out=x1, in0=t[0], in1=t[1], op=ALU.subtract), r=["rt0", "rt1", PN], w=[PN])
            P.op("dve", lambda e: e.tensor_tensor(out=x2, in0=t[2], in1=t[3], op=ALU.add), r=["rt2", "rt3", PN], w=[PN])

        itc = [0]

        def proj_tile(mode, ti):
            it = itc[0]; itc[0] += 1
            rows = NS if mode == 2 else 128
            Psb = Psb_l[it % 2]; PN = f"Psb{it % 2}"
            xb = xt[it % 2]; xn = f"xt{it % 2}"
            tb = tab[it % 2]; tn = f"tab{it % 2}"
            hTb = hT[it % 2]; hTn = f"hT{it % 2}"
            hT3 = hTb.rearrange("p (k t) -> p k t", k=8)
            if mode == 2:
                xsrc, tsrc = I["xs"], I["tab_s"]
                a_t, a_n, sh_t, sh_n = a1_s, "a1_s", mod_sb[:, 0:D], "mod_sb"
                ncols = INC
            else:
                xsrc = (I["x_oth"] if mode == 0 else I["x_own"])[ti * 128:(ti + 1) * 128, :]
                tsrc = (I["tab_oth"] if mode == 0 else I["tab_own"])[ti * 128:(ti + 1) * 128, :]
                a_t, a_n, sh_t, sh_n = a1_bc, "a1_bc", sh1_bc, "sh1_bc"
                ncols = INC if mode == 1 else (C_Q + 512 if ti == NT - 1 else N_OTH)
            P.dma("sp", xb[0:rows, :], xsrc, w=[xn])
            P.dma("sp", tb[0:rows, :], tsrc, w=[tn])
            ss, rs = small[0:rows, 0:1], small[0:rows, 1:2]
            P.op("act", lambda e: e.activation(out=sq[0:rows, :], in_=xb[0:rows, :], func=AF.Square, accum_out=ss), r=[xn], w=["sq", "ss"])
            P.op("dve", lambda e: e.tensor_scalar(out=rs, in0=ss, scalar1=1.0 / D, scalar2=1e-6, op0=ALU.mult, op1=ALU.add), r=["ss"], w=["rs"])
            P.op("act", lambda e: e.activation(out=rs, in_=rs, func=AF.Sqrt), r=["rs"], w=["rs"])
            P.op("dve", lambda e: e.reciprocal(out=rs, in_=rs), r=["rs"], w=["rs"])
            P.op("dve", lambda e: e.scalar_tensor_tensor(out=hh[0:rows, :], in0=xb[0:rows, :], scalar=rs, in1=a_t[0:rows, :], op0=ALU.mult, op1=ALU.mult),
                 r=[xn, "rs", a_n], w=["hh"])
            P.op("pool", lambda e: e.tensor_tensor(out=hh[0:rows, :], in0=hh[0:rows, :], in1=sh_t[0:rows, :], op=ALU.add), r=["hh", sh_n], w=["hh"])
            for k in range(8):
                bank = k // 4
                P.op("pe", lambda e, k=k, bank=bank: e.transpose(out=pb[bank][:, (k % 4) * 128:(k % 4) * 128 + rows], in_=hh[0:rows, k * 128:(k + 1) * 128], identity=ident[0:rows, 0:rows]),
                     r=["hh", "ident"], w=[PB[bank]])
            if rows == 128:
                evac("act", hTb[:, 0:512], pb[0], [PB[0]], [hTn])
                evac("dve", hTb[:, 512:1024], pb[1], [PB[1]], [hTn])
            else:
                for half_ in range(2):
                    evac("act" if half_ == 0 else "dve", hT3[:, half_ * 4:(half_ + 1) * 4, 0:rows], pb[half_].rearrange("p (k t) -> p k t", k=4)[:, :, 0:rows], [PB[half_]], [hTn])
            nb = (ncols + 511) // 512
            for cb in range(nb):
                c0, c1 = cb * 512, min(ncols, cb * 512 + 512)
                bank = 2 + cb % 4
                for k in range(8):
                    P.op("pe", lambda e, k=k, bank=bank, c0=c0, c1=c1: e.matmul(out=pb[bank][0:rows, 0:c1 - c0], lhsT=hT3[:, k, 0:rows], rhs=w_in3[:, k, c0:c1], start=(k == 0), stop=(k == 7)),
                         r=[hTn, "w_in_b"], w=[PB[bank]])
                evac(alt(), Psb[0:rows, c0:c1], pb[bank][0:rows, 0:c1 - c0], [PB[bank]], [PN])
            if mode == 2:
                cvs = sq[0:rows, :]
                for j in range(2):
                    for hh_ in range(2):
                        P.dma("sp", sq[0:rows, 0:768], I["state_conv"][:, 1 + j, hh_ * 768:(hh_ + 1) * 768], w=["sq"])
                        P.dma("sp", O["conv_s"][:, j, hh_ * 768:(hh_ + 1) * 768], sq[0:rows, 0:768], r=["sq"])
                P.dma("sp", O["conv_s"][:, 2, 0:512], Psb[0:rows, C_Q:C_Q + 512], r=[PN])
                P.dma("sp", O["conv_s"][:, 2, 512:1536], Psb[0:rows, 0:1024], r=[PN])
            else:
                row0 = 3 + ti * 128
                P.dma("sp", S["gs"][mode, row0:row0 + 128, 0:1032], Psb[:, 0:1032], r=[PN], w=[f"gs{mode}_{ti}"])
                if mode == 1:
                    P.dma("sp", S["gs"][mode, row0:row0 + 128, 1032:2056], Psb[:, C_Q:C_Q + 1024], r=[PN], w=[f"gs{mode}_{ti}"])
                elif ti == NT - 1:
                    P.dma("sp", S["gs"][mode, row0:row0 + 128, 1032:1544], Psb[:, C_Q:C_Q + 512], r=[PN], w=[f"gs{mode}_{ti}"])
            rope(Psb, PN, rows, C_AK, 2, 128, 16, tb, 0, 64, tn)
            rope(Psb, PN, rows, C_IK, 1, 64, 8, tb, 128, 192, tn)
            if mode >= 1:
                rope(Psb, PN, rows, C_AQ, 4, 128, 16, tb, 0, 64, tn)
                rope(Psb, PN, rows, C_IQ, 8, 64, 8, tb, 128, 192, tn)
            if mode == 2:
                P.dma("sp", O["k_s"], Psb[0:rows, C_AK:C_AK + 256], r=[PN])
                P.dma("sp", O["v_s"], Psb[0:rows, C_AV:C_AV + 256], r=[PN])
                P.dma("sp", O["ik_s"], Psb[0:rows, C_IK:C_IK + 64], r=[PN])
                P.dma("sp", S["ps"], Psb[0:rows, :], r=[PN], w=["ps_scr"])
                return
            if mode == 1:
                P.dma("sp", O["k_own"][ti * 128:(ti + 1) * 128, :], Psb[:, C_AK:C_AK + 256], r=[PN])
                P.dma("sp", O["v_own"][ti * 128:(ti + 1) * 128, :], Psb[:, C_AV:C_AV + 256], r=[PN])
                P.dma("sp", O["ik_own"][ti * 128:(ti + 1) * 128, :], Psb[:, C_IK:C_IK + 64], r=[PN])
            slot = (0 if mode == 0 else 16) + ti
            for g in range(2):
                P.op("act", lambda e, g=g: e.activation(out=sq[:, 0:128], in_=Psb[:, C_AK + g * 128:C_AK + (g + 1) * 128], func=AF.Square, accum_out=ksq[:, slot * 2 + g:slot * 2 + g + 1]),
                     r=[PN, "a1_bc"], w=["sq", "ksq"])
            if mode == 1:
                for h in range(4):
                    P.op("act", lambda e, h=h: e.activation(out=sq[:, 0:128], in_=Psb[:, C_AQ + h * 128:C_AQ + (h + 1) * 128], func=AF.Square, accum_out=qsq[:, ti * 4 + h:ti * 4 + h + 1]),
                         r=[PN, "a1_bc"], w=["sq", "qsq"])
            P.op("dve", lambda e: e.tensor_copy(out=ikd[:, 0:64], in_=Psb[:, C_IK:C_IK + 64]), r=[PN], w=["ikd"])
            P.op("pool", lambda e: e.tensor_copy(out=ikd[:, 64:128], in_=Psb[:, C_IK:C_IK + 64]), r=[PN], w=["ikd"])
            for g in range(2):
                P.op("pe", lambda e, g=g: e.transpose(out=pb[6][:, g * 128:(g + 1) * 128], in_=Psb[:, C_AK + g * 128:C_AK + (g + 1) * 128], identity=ident),
                     r=[PN, "ident"], w=[PB[6]])
            P.op("pe", lambda e: e.transpose(out=pb[6][:, 256:384], in_=ikd, identity=ident), r=["ikd", "ident"], w=[PB[6]])
            for g in range(2):
                evac(alt(), KT3[:, g, slot * 128:(slot + 1) * 128], pb[6][:, g * 128:(g + 1) * 128], [PB[6]], ["KT"])
            evac(alt(), ikT[:, slot * 128:(slot + 1) * 128], pb[6][:, 256:384], [PB[6]], ["ikT"])
            P.op("pool", lambda e: e.memset(VA4[:, slot, :, 128:130], 1.0), r=["a1_bc"], w=["VA"])
            P.op("act", lambda e: e.activation(out=VA4[:, slot, :, 0:128], in_=Psb[:, C_AV:C_AV + 256].rearrange("p (g d) -> p g d", g=2), func=AF.Copy),
                 r=[PN], w=["VA"])
            if mode == 1:
                ts_ = tstage[0]; tsn = "tstage0"
                for h in range(4):
                    P.op("pe", lambda e, h=h: e.transpose(out=pb[7][:, h * 128:(h + 1) * 128], in_=Psb[:, C_AQ + h * 128:C_AQ + (h + 1) * 128], identity=ident),
                         r=[PN, "ident"], w=[PB[7]])
                evac(alt(), ts_, pb[7], [PB[7]], [tsn])
                P.dma("sp", S["qT"][ti], ts_.rearrange("p (h t) -> p h t", h=4), r=[tsn], w=[f"qT{ti}"])
                ts2 = tstage[1]; tsn2 = "tstage1"
                for h in range(4):
                    P.op("pe", lambda e, h=h: e.transpose(out=pb[7][:, h * 128:(h + 1) * 128], in_=Psb[:, C_IQ + h * 128:C_IQ + (h + 1) * 128], identity=ident),
                         r=[PN, "ident"], w=[PB[7]])
                evac(alt(), ts2, pb[7], [PB[7]], [tsn2])
                P.dma("sp", S["iqT"][ti], ts2.rearrange("p (h t) -> p h t", h=4), r=[tsn2], w=[f"iqT{ti}"])
                P.op("act", lambda e: e.activation(out=iwabs[:, ti * 8:(ti + 1) * 8], in_=Psb[:, C_IW:C_IW + 8], func=AF.Abs, scale=8 ** -0.5),
                     r=[PN], w=["iwabs"])
                P.op("act", lambda e: e.activation(out=iwsgn[:, ti * 8:(ti + 1) * 8], in_=Psb[:, C_IW:C_IW + 8], func=AF.Sign), r=[PN], w=["iwsgn"])
                if ti == NT - 1:
                    P.dma("sp", O["conv_tail"][:, 0:512], S["gs"][1, 3 + HALF - 3:3 + HALF, 1032:1544], r=[f"gs1_{ti}"])
                    P.dma("sp", O["conv_tail"][:, 512:1536], S["gs"][1, 3 + HALF - 3:3 + HALF, 0:1024], r=[f"gs1_{ti}"])

        NA0 = int(os.environ.get("MK_NA0", NT)); NA1 = int(os.environ.get("MK_NA1", NT))
        for ti in range(NT - NA0, NT):
            proj_tile(0, ti)
        for ti in range(NA1):
            proj_tile(1, ti)
        if not NOS:
            proj_tile(2, 0)

        if not NOG:
            P.barrier()
            A.reset(persist1)
            cst = A.f32(1024)
            TRIU, ONESM, MASKL, MASKU = [cst[:, i * 128:(i + 1) * 128] for i in range(4)]
            ident4 = cst[:, 512:1024]
            P.dma("sp", cst, I["gconst"], w=["cst"])
            wc = [A.f32(1536) for _ in range(4)]
            for i in range(4):
                P.dma("sp", wc[i], I["wc_p"][i:i + 1, :].to_broadcast([128, 1536]), w=[f"wc{i}"])
            dtb = A.f32(4); negA = A.f32(4); ggd = A.f32(512)
            P.dma("sp", dtb, I["dt_bias"].to_broadcast([128, 4]), w=["dtb"])
            P.dma("sp", negA, I["a_log"].to_broadcast([128, 4]), w=["negA"])
            for h in range(4):
                P.dma("sp", ggd[:, h * 128:(h + 1) * 128], I["g_gdn"].to_broadcast([128, 128]), w=["ggd"])
            P.op("act", lambda e: e.activation(out=negA, in_=negA, func=AF.Exp), r=["negA"], w=["negA"])
            P.op("dve", lambda e: e.tensor_scalar(out=negA, in0=negA, scalar1=-1.0, scalar2=None, op0=ALU.mult), r=["negA"], w=["negA"])
            gs_base = A.off
            Sst = A.f32(512)
            P.op("dve", lambda e: e.memset(Sst, 0.0), w=["S"])
            X = [A.f32(GS_W) for _ in range(4)]
            cv = A.f32(1536); tmpa = A.f32(1536); tmpb = A.f32(1536)
            sm = A.f32(64)
            kn = A.f32(512); kt = A.f32(512); vb = A.f32(512); qn = A.f32(512); qt = A.f32(512)
            knT = A.f32(512); qnT = A.f32(512); qtT = A.f32(512)
            Dg = A.f32(512); dec = A.f32(512); decT = A.f32(512)
            Pbuf = [A.f32(512), A.f32(512)]; PTbuf = [A.f32(512), A.f32(512)]
            Wm = A.f32(512); ATm = A.f32(512); Rm = A.f32(512); vnew = A.f32(512); o_sb = A.f32(512); og = A.f32(512); szb = A.f32(512)
            cTst = A.bf16(512)
            pre = A.f32(GS_W)
            H4 = lambda ap, h: ap[:, h * 128:(h + 1) * 128]
            sc = lambda lo, h: sm[:, lo + h:lo + h + 1]

            def ts_mul(eng, out, in0, scal, r, w):
                if eng == "pool":
                    P.op("pool", lambda e: e.tensor_scalar(out=out, in0=in0, scalar1=scal, scalar2=1.0, op0=ALU.mult, op1=ALU.mult), r=r, w=w)
                else:
                    P.op("dve", lambda e: e.tensor_scalar(out=out, in0=in0, scalar1=scal, scalar2=None, op0=ALU.mult), r=r, w=w)

            for hf in range(2):
                own = (hf == 1)
                if own:
                    P.op("dve", lambda e: e.tensor_scalar(out=Sst, in0=Sst, scalar1=flags[:, 0:1], scalar2=None, op0=ALU.mult), r=["S", "flags"], w=["S"])
                    P.op("dve", lambda e: e.memset(pre[0:3, :], 0.0), w=["pre"])
                    P.dma("sp", pre[0:3, 0:1544], S["gs"][0, HALF:HALF + 3, 0:1544], w=["pre"])
                    P.op("dve", lambda e: e.tensor_scalar(out=pre[0:3, :], in0=pre[0:3, :], scalar1=flags[0:3, 0:1], scalar2=None, op0=ALU.mult), r=["pre", "flags"], w=["pre"])
                    P.dma("sp", S["gs"][1, 0:3, :], pre[0:3, :], r=["pre"], w=["gs_pre1"])
                segs = [(0, 1024, 0), (1024, 1536, 1032)] if own else [(0, 1024, 0)]
                nsc = 8 if own else 4
                for ti in range(G_NT):
                    W_ = GS_W if own else 1032
                    for i in range(4):
                        P.dma("sp", X[i][:, 0:W_], S["gs"][hf, ti * 128 + i:ti * 128 + i + 128, 0:W_], r=(["gs_pre1"] if (own and ti == 0) else []), w=[f"X{i}"])
                    for (d0, d1, s0) in segs:
                        n = d1 - d0
                        P.op("dve", lambda e, d0=d0, d1=d1, s0=s0, n=n: e.tensor_tensor(out=cv[:, d0:d1], in0=X[0][:, s0:s0 + n], in1=wc[0][:, d0:d1], op=ALU.mult), r=["X0", "wc0"], w=["cv"])
                        for i in range(1, 4):
                            tb_, tbn = (tmpa, "tmpa") if i % 2 else (tmpb, "tmpb")
                            P.op("pool", lambda e, i=i, d0=d0, d1=d1, s0=s0, n=n, tb_=tb_: e.tensor_tensor(out=tb_[:, d0:d1], in0=X[i][:, s0:s0 + n], in1=wc[i][:, d0:d1], op=ALU.mult), r=[f"X{i}", f"wc{i}"], w=[tbn])
                            P.op("dve", lambda e, d0=d0, d1=d1, tb_=tb_: e.tensor_tensor(out=cv[:, d0:d1], in0=cv[:, d0:d1], in1=tb_[:, d0:d1], op=ALU.add), r=["cv", tbn], w=["cv"])
                        P.op("act", lambda e, d0=d0, d1=d1: e.activation(out=cv[:, d0:d1], in_=cv[:, d0:d1], func=AF.Silu), r=["cv"], w=["cv"])
                    if GSTOP <= 1:
                        continue
                    P.op("pool", lambda e: e.tensor_tensor(out=tmpa[:, 0:512], in0=cv[:, 0:512], in1=cv[:, 0:512], op=ALU.mult), r=["cv"], w=["tmpa"])
                    P.op("dve", lambda e: e.tensor_reduce(out=sm[:, 0:4], in_=tmpa[:, 0:512].rearrange("p (h d) -> p h d", h=4), axis=AX.X, op=ALU.add), r=["tmpa"], w=["sm_ss"])
                    if own:
                        P.op("pool", lambda e: e.tensor_tensor(out=tmpb[:, 0:512], in0=cv[:, 1024:1536], in1=cv[:, 1024:1536], op=ALU.mult), r=["cv"], w=["tmpb"])
                        P.op("dve", lambda e: e.tensor_reduce(out=sm[:, 4:8], in_=tmpb[:, 0:512].rearrange("p (h d) -> p h d", h=4), axis=AX.X, op=ALU.add), r=["tmpb"], w=["sm_ss"])
                    P.op("dve", lambda e, nsc=nsc: e.tensor_scalar(out=sm[:, 0:nsc], in0=sm[:, 0:nsc], scalar1=1e-6, scalar2=None, op0=ALU.add), r=["sm_ss"], w=["sm_ss"])
                    P.op("act", lambda e, nsc=nsc: e.activation(out=sm[:, 0:nsc], in_=sm[:, 0:nsc], func=AF.Sqrt), r=["sm_ss"], w=["sm_ss"])
                    P.op("dve", lambda e, nsc=nsc: e.reciprocal(out=sm[:, 0:nsc], in_=sm[:, 0:nsc]), r=["sm_ss"], w=["sm_ss"])
                    if own:
                        P.op("dve", lambda e: e.tensor_scalar(out=sm[:, 4:8], in0=sm[:, 4:8], scalar1=128 ** -0.5, scalar2=None, op0=ALU.mult), r=["sm_ss"], w=["sm_ss"])
                    P.op("act", lambda e: e.activation(out=sm[:, 8:12], in_=X[3][:, 1024:1028], func=AF.Sigmoid), r=["X3"], w=["sm_b"])
                    P.op("dve", lambda e: e.tensor_tensor(out=sm[:, 12:16], in0=X[3][:, 1028:1032], in1=dtb, op=ALU.add), r=["X3", "dtb"], w=["sm_g"])
                    P.op("act", lambda e: e.activation(out=sm[:, 12:16], in_=sm[:, 12:16], func=AF.Exp), r=["sm_g"], w=["sm_g"])
                    P.op("act", lambda e: e.activation(out=sm[:, 12:16], in_=sm[:, 12:16], func=AF.Ln, bias=1.0), r=["sm_g"], w=["sm_g"])
                    P.op("dve", lambda e: e.tensor_tensor(out=sm[:, 12:16], in0=sm[:, 12:16], in1=negA, op=ALU.mult), r=["sm_g", "negA"], w=["sm_g"])
                    P.op("pe", lambda e: e.matmul(out=pb[3][:, 0:4], lhsT=TRIU, rhs=sm[:, 12:16], start=True, stop=True), r=["cst", "sm_g"], w=[PB[3]])
                    P.op("pe", lambda e: e.matmul(out=pb[3][:, 4:8], lhsT=ONESM, rhs=sm[:, 12:16], start=True, stop=True), r=["cst", "sm_g"], w=[PB[3]])
                    P.op("dve", lambda e: e.tensor_copy(out=sm[:, 16:24], in_=pb[3][:, 0:8]), r=[PB[3]], w=["sm_gc"])
                    P.op("dve", lambda e: e.tensor_copy(out=sm[:, 24:28], in_=sm[:, 16:20]), r=["sm_gc"], w=["sm_e"])
                    P.op("dve", lambda e: e.tensor_tensor(out=sm[:, 28:32], in0=sm[:, 20:24], in1=sm[:, 16:20], op=ALU.subtract), r=["sm_gc"], w=["sm_e"])
                    P.op("dve", lambda e: e.tensor_copy(out=sm[:, 32:36], in_=sm[:, 20:24]), r=["sm_gc"], w=["sm_e"])
                    P.op("act", lambda e: e.activation(out=sm[:, 24:36], in_=sm[:, 24:36], func=AF.Exp), r=["sm_e"], w=["sm_e"])
                    P.op("dve", lambda e: e.scalar_tensor_tensor(out=sm[:, 36:40], in0=sm[:, 8:12], scalar=-1.0, in1=sm[:, 24:28], op0=ALU.mult, op1=ALU.mult), r=["sm_b", "sm_e"], w=["sm_x"])
                    P.op("dve", lambda e: e.tensor_scalar(out=sm[:, 40:44], in0=sm[:, 16:20], scalar1=-1.0, scalar2=None, op0=ALU.mult), r=["sm_gc"], w=["sm_x"])
                    P.op("dve", lambda e: e.tensor_scalar(out=sm[:, 44:48], in0=sm[:, 8:12], scalar1=-1.0, scalar2=None, op0=ALU.mult), r=["sm_b"], w=["sm_x"])
                    if own:
                        P.op("dve", lambda e: e.tensor_tensor(out=sm[:, 48:52], in0=sm[:, 4:8], in1=sm[:, 24:28], op=ALU.mult), r=["sm_ss", "sm_e"], w=["sm_x"])
                    if GSTOP <= 2:
                        continue
                    for h in range(4):
                        ts_mul("dve", H4(kn, h), cv[:, h * 128:(h + 1) * 128], sc(0, h), ["cv", "sm_ss"], ["kn"])
                        ts_mul("pool", H4(kt, h), H4(kn, h), sc(28, h), ["kn", "sm_e"], ["kt"])
                        ts_mul("pool", H4(vb, h), cv[:, 512 + h * 128:512 + (h + 1) * 128], sc(8, h), ["cv", "sm_b"], ["vb"])
                        if own:
                            ts_mul("dve", H4(qn, h), cv[:, 1024 + h * 128:1024 + (h + 1) * 128], sc(4, h), ["cv", "sm_ss"], ["qn"])
                            ts_mul("pool", H4(qt, h), cv[:, 1024 + h * 128:1024 + (h + 1) * 128], sc(48, h), ["cv", "sm_x"], ["qt"])
                    for (src, sn, bank, dst, dn, eng) in ([(kn, "kn", 0, knT, "knT", "act")] + ([(qn, "qn", 1, qnT, "qnT", "dve"), (qt, "qt", 2, qtT, "qtT", "act")] if own else [])):
                        for h in range(4):
                            P.op("pe", lambda e, h=h, src=src, bank=bank: e.transpose(out=H4(pb[bank], h), in_=H4(src, h), identity=ident), r=[sn, "ident"], w=[PB[bank]])
                        evac(eng, dst, pb[bank], [PB[bank]], [dn])
                    if GSTOP <= 3:
                        continue
                    for h in range(4):
                        P.op("pe", lambda e, h=h: e.matmul(out=H4(pb[0], h), lhsT=H4(knT, h), rhs=H4(knT, h), start=True, stop=True), r=["knT"], w=[PB[0]])
                        P.op("pool", lambda e, h=h: e.tensor_scalar(out=H4(Dg, h), in0=ident, scalar1=sc(16, h), scalar2=1.0, op0=ALU.mult, op1=ALU.mult), r=["ident", "sm_gc"], w=["Dg"])
                    for h in range(4):
                        P.op("pe", lambda e, h=h: e.matmul(out=H4(pb[1], h), lhsT=ONESM, rhs=H4(Dg, h), start=True, stop=False), r=["cst", "Dg"], w=[PB[1]])
                        P.op("pe", lambda e, h=h: e.matmul(out=H4(pb[1], h), lhsT=ident, rhs=MASKL, start=False, stop=True), r=["cst", "ident"], w=[PB[1]])
                        P.op("act", lambda e, h=h: e.activation(out=H4(dec, h), in_=H4(pb[1], h), func=AF.Exp, scale=-1.0, bias=sc(16, h)), r=[PB[1], "sm_gc"], w=["dec"])
                        P.op("dve", lambda e, h=h: e.scalar_tensor_tensor(out=H4(Pbuf[0], h), in0=H4(pb[0], h), scalar=sc(44, h), in1=H4(dec, h), op0=ALU.mult, op1=ALU.mult),
                             r=[PB[0], "sm_x", "dec"], w=["P0"])
                    if own:
                        for h in range(4):
                            P.op("pe", lambda e, h=h: e.matmul(out=H4(pb[2], h), lhsT=ONESM, rhs=H4(Dg, h), start=True, stop=False), r=["cst", "Dg"], w=[PB[2]])
                            P.op("pe", lambda e, h=h: e.matmul(out=H4(pb[2], h), lhsT=ident, rhs=MASKU, start=False, stop=True), r=["cst", "ident"], w=[PB[2]])
                            P.op("act", lambda e, h=h: e.activation(out=H4(decT, h), in_=H4(pb[2], h), func=AF.Exp, scale=1.0, bias=sc(40, h)), r=[PB[2], "sm_x"], w=["decT"])
                            P.op("pe", lambda e, h=h: e.matmul(out=H4(pb[3], h), lhsT=H4(knT, h), rhs=H4(qnT, h), start=True, stop=True), r=["knT", "qnT"], w=[PB[3]])
                            P.op("dve", lambda e, h=h: e.tensor_tensor(out=H4(ATm, h), in0=H4(pb[3], h), in1=H4(decT, h), op=ALU.mult), r=[PB[3], "decT"], w=["ATm"])
                    if GSTOP <= 4:
                        continue
                    for h in range(4):
                        P.op("pe", lambda e, h=h: e.transpose(out=H4(pb[4], h), in_=H4(Pbuf[0], h), identity=ident), r=["P0", "ident"], w=[PB[4]])
                    evac("act", PTbuf[0], pb[4], [PB[4]], ["PT0"])
                    P.op("dve", lambda e: e.tensor_tensor(out=Wm, in0=PTbuf[0], in1=ident4, op=ALU.add), r=["PT0", "cst"], w=["Wm"])
                    for l in range(1, 7):
                        pc, ptc, pn_, ptn = Pbuf[(l - 1) % 2], PTbuf[(l - 1) % 2], Pbuf[l % 2], PTbuf[l % 2]
                        pcn, ptcn, pnn, ptnn = f"P{(l - 1) % 2}", f"PT{(l - 1) % 2}", f"P{l % 2}", f"PT{l % 2}"
                        for h in range(4):
                            P.op("pe", lambda e, h=h, pc=pc, ptc=ptc: e.matmul(out=H4(pb[4], h), lhsT=H4(ptc, h), rhs=H4(pc, h), start=True, stop=True), r=[pcn, ptcn], w=[PB[4]])
                        if l < 6:
                            for h in range(4):
                                P.op("pe", lambda e, h=h, pc=pc, ptc=ptc: e.matmul(out=H4(pb[5], h), lhsT=H4(pc, h), rhs=H4(ptc, h), start=True, stop=True), r=[pcn, ptcn], w=[PB[5]])
                        evac("act", pn_, pb[4], [PB[4]], [pnn])
                        if l < 6:
                            evac("dve", ptn, pb[5], [PB[5]], [ptnn])
                        for h in range(4):
                            P.op("pe", lambda e, h=h, pn_=pn_: e.matmul(out=H4(pb[6], h), lhsT=H4(pn_, h), rhs=H4(Wm, h), start=True, stop=True), r=[pnn, "Wm"], w=[PB[6]])
                        P.op("dve", lambda e: e.tensor_tensor(out=Wm, in0=Wm, in1=pb[6], op=ALU.add), r=["Wm", PB[6]], w=["Wm"])
                    if GSTOP <= 5:
                        continue
                    for h in range(4):
                        P.op("pe", lambda e, h=h: e.matmul(out=H4(pb[7], h), lhsT=H4(knT, h), rhs=H4(Sst, h), start=True, stop=True), r=["knT", "S"], w=[PB[7]])
                    for h in range(4):
                        P.op("dve", lambda e, h=h: e.scalar_tensor_tensor(out=H4(Rm, h), in0=H4(pb[7], h), scalar=sc(36, h), in1=H4(vb, h), op0=ALU.mult, op1=ALU.add),
                             r=[PB[7], "sm_x", "vb"], w=["Rm"])
                    for h in range(4):
                        P.op("pe", lambda e, h=h: e.matmul(out=H4(pb[0], h), lhsT=H4(Wm, h), rhs=H4(Rm, h), start=True, stop=True), r=["Wm", "Rm"], w=[PB[0]])
                    evac("act", vnew, pb[0], [PB[0]], ["vnew"])
                    if own:
                        for h in range(4):
                            P.op("pe", lambda e, h=h: e.matmul(out=H4(pb[1], h), lhsT=H4(qtT, h), rhs=H4(Sst, h), start=True, stop=False), r=["qtT", "S"], w=[PB[1]])
                            P.op("pe", lambda e, h=h: e.matmul(out=H4(pb[1], h), lhsT=H4(ATm, h), rhs=H4(vnew, h), start=False, stop=True), r=["ATm", "vnew"], w=[PB[1]])
                        evac("act", o_sb, pb[1], [PB[1]], ["o_sb"])
                    for h in range(4):
                        P.op("pe", lambda e, h=h: e.matmul(out=H4(pb[2], h), lhsT=H4(kt, h), rhs=H4(vnew, h), start=True, stop=True), r=["kt", "vnew"], w=[PB[2]])
                    for h in range(4):
                        P.op("dve", lambda e, h=h: e.scalar_tensor_tensor(out=H4(Sst, h), in0=H4(Sst, h), scalar=sc(32, h), in1=H4(pb[2], h), op0=ALU.mult, op1=ALU.add),
                             r=["S", "sm_e", PB[2]], w=["S"])
                    if own:
                        P.op("pool", lambda e: e.tensor_tensor(out=tmpa[:, 0:512], in0=o_sb, in1=o_sb, op=ALU.mult), r=["o_sb"], w=["tmpa"])
                        P.op("dve", lambda e: e.tensor_reduce(out=sm[:, 52:56], in_=tmpa[:, 0:512].rearrange("p (h d) -> p h d", h=4), axis=AX.X, op=ALU.add), r=["tmpa"], w=["sm_o"])
                        P.op("dve", lambda e: e.tensor_scalar(out=sm[:, 52:56], in0=sm[:, 52:56], scalar1=1.0 / 128, scalar2=1e-6, op0=ALU.mult, op1=ALU.add), r=["sm_o"], w=["sm_o"])
                        P.op("act", lambda e: e.activation(out=sm[:, 52:56], in_=sm[:, 52:56], func=AF.Sqrt), r=["sm_o"], w=["sm_o"])
                        P.op("dve", lambda e: e.reciprocal(out=sm[:, 52:56], in_=sm[:, 52:56]), r=["sm_o"], w=["sm_o"])
                        P.op("act", lambda e: e.activation(out=szb, in_=X[3][:, 1544:2056], func=AF.Silu), r=["X3"], w=["szb"])
                        for h in range(4):
                            P.op("dve", lambda e, h=h: e.scalar_tensor_tensor(out=H4(og, h), in0=H4(o_sb, h), scalar=sc(52, h), in1=H4(ggd, h), op0=ALU.mult, op1=ALU.mult),
                                 r=["o_sb", "sm_o", "ggd"], w=["og"])
                        P.op("pool", lambda e: e.tensor_tensor(out=og, in0=og, in1=szb, op=ALU.mult), r=["og", "szb"], w=["og"])
                        for h in range(4):
                            P.op("pe", lambda e, h=h: e.transpose(out=H4(pb[3], h), in_=H4(og, h), identity=ident), r=["og", "ident"], w=[PB[3]])
                        evac("act", cTst, pb[3], [PB[3]], ["cTst"])
                        P.dma("sp", S["cT"][ti][:, 0:4, :], cTst.rearrange("p (h t) -> p h t", h=4), r=["cTst"], w=[f"cTg{ti}"])
            P.dma("sp", O["ssm_fin"].rearrange("h d e -> d h e"), Sst.rearrange("p (h e) -> p h e", h=4), r=["S"])

            P.barrier()
            A.reset(gs_base)
            eye16 = A.f32(256)
            P.dma("sp", eye16, I["eye16"], w=["eye16"])
            Sall = A.f32(NS * 512); Sall4 = Sall.rearrange("p (s h e) -> p s h e", s=NS, h=4)
            for s_ in range(NS):
                P.dma("sp", Sall4[:, s_], I["state_ssm"][s_].rearrange("h d e -> d h e"), w=[f"S{s_}"])
            Ps2 = A.f32(INC); scv = A.f32(3 * 1536); scv3 = scv.rearrange("p (i c) -> p i c", i=3)
            P.dma("sp", Ps2[0:NS, :], S["ps"], w=["Ps2"])
            P.dma("sp", scv3[0:NS], I["state_conv"], w=["scv"])
            cvs = A.f32(1536); tms = A.f32(1536); sm2 = A.f32(64)
            kn2 = A.f32(512); qn2 = A.f32(512)
            knT2 = A.f32(64); qnT2 = A.f32(64)
            KTm = A.f32(1024); QTm = A.f32(1024)
            Km = [A.f32(512), A.f32(512)]
            EgD = A.f32(64); EGB = A.f32(64); Dl = A.f32(512); o_s = A.f32(512); og_s = A.f32(512); sz_s = A.f32(512)
            cTs = A.bf16(64)
            R = slice(0, NS)
            for (d0, d1, sc0, p0) in [(0, 1024, 512, 0), (1024, 1536, 0, C_Q)]:
                n = d1 - d0
                P.op("dve", lambda e, d0=d0, d1=d1, sc0=sc0, n=n: e.tensor_tensor(out=cvs[R, d0:d1], in0=scv3[R, 0, sc0:sc0 + n], in1=wc[0][R, d0:d1], op=ALU.mult), r=["scv", "wc0"], w=["cvs"])
                for i in range(1, 4):
                    src = (lambda i=i, sc0=sc0, n=n, p0=p0: scv3[R, i, sc0:sc0 + n] if i < 3 else Ps2[R, p0:p0 + n])()
                    P.op("pool", lambda e, i=i, d0=d0, d1=d1, src=src: e.tensor_tensor(out=tms[R, d0:d1], in0=src, in1=wc[i][R, d0:d1], op=ALU.mult), r=["scv", "Ps2", f"wc{i}"], w=["tms"])
                    P.op("dve", lambda e, d0=d0, d1=d1: e.tensor_tensor(out=cvs[R, d0:d1], in0=cvs[R, d0:d1], in1=tms[R, d0:d1], op=ALU.add), r=["cvs", "tms"], w=["cvs"])
                P.op("act", lambda e, d0=d0, d1=d1: e.activation(out=cvs[R, d0:d1], in_=cvs[R, d0:d1], func=AF.Silu), r=["cvs"], w=["cvs"])
            for (c0, o0) in [(0, 0), (1024, 4)]:
                P.op("pool", lambda e, c0=c0: e.tensor_tensor(out=tms[R, 0:512], in0=cvs[R, c0:c0 + 512], in1=cvs[R, c0:c0 + 512], op=ALU.mult), r=["cvs"], w=["tms"])
                P.op("dve", lambda e, o0=o0: e.tensor_reduce(out=sm2[R, o0:o0 + 4], in_=tms[R, 0:512].rearrange("p (h d) -> p h d", h=4), axis=AX.X, op=ALU.add), r=["tms"], w=["sm2"])
            P.op("dve", lambda e: e.tensor_scalar(out=sm2[R, 0:8], in0=sm2[R, 0:8], scalar1=1e-6, scalar2=None, op0=ALU.add), r=["sm2"], w=["sm2"])
            P.op("act", lambda e: e.activation(out=sm2[R, 0:8], in_=sm2[R, 0:8], func=AF.Sqrt), r=["sm2"], w=["sm2"])
            P.op("dve", lambda e: e.reciprocal(out=sm2[R, 0:8], in_=sm2[R, 0:8]), r=["sm2"], w=["sm2"])
            P.op("dve", lambda e: e.tensor_scalar(out=sm2[R, 4:8], in0=sm2[R, 4:8], scalar1=128 ** -0.5, scalar2=None, op0=ALU.mult), r=["sm2"], w=["sm2"])
            P.op("act", lambda e: e.activation(out=sm2[R, 8:12], in_=Ps2[R, C_B:C_B + 4], func=AF.Sigmoid), r=["Ps2"], w=["sm2b"])
            P.op("dve", lambda e: e.tensor_tensor(out=sm2[R, 12:16], in0=Ps2[R, C_A:C_A + 4], in1=dtb[R, :], op=ALU.add), r=["Ps2", "dtb"], w=["sm2g"])
            P.op("act", lambda e: e.activation(out=sm2[R, 12:16], in_=sm2[R, 12:16], func=AF.Exp), r=["sm2g"], w=["sm2g"])
            P.op("act", lambda e: e.activation(out=sm2[R, 12:16], in_=sm2[R, 12:16], func=AF.Ln, bias=1.0), r=["sm2g"], w=["sm2g"])
            P.op("dve", lambda e: e.tensor_tensor(out=sm2[R, 12:16], in0=sm2[R, 12:16], in1=negA[R, :], op=ALU.mult), r=["sm2g", "negA"], w=["sm2g"])
            P.op("act", lambda e: e.activation(out=sm2[R, 12:16], in_=sm2[R, 12:16], func=AF.Exp), r=["sm2g"], w=["sm2g"])
            P.op("dve", lambda e: e.tensor_scalar(out=sm2[R, 16:20], in0=sm2[R, 12:16], scalar1=-1.0, scalar2=None, op0=ALU.mult), r=["sm2g"], w=["sm2n"])
            for h in range(4):
                P.op("dve", lambda e, h=h: e.tensor_scalar(out=kn2[R, h * 128:(h + 1) * 128], in0=cvs[R, h * 128:(h + 1) * 128], scalar1=sm2[R, h:h + 1], scalar2=None, op0=ALU.mult), r=["cvs", "sm2"], w=["kn2"])
                P.op("dve", lambda e, h=h: e.tensor_scalar(out=qn2[R, h * 128:(h + 1) * 128], in0=cvs[R, 1024 + h * 128:1024 + (h + 1) * 128], scalar1=sm2[R, 4 + h:5 + h], scalar2=None, op0=ALU.mult), r=["cvs", "sm2"], w=["qn2"])
            for h in range(4):
                P.op("pe", lambda e, h=h: e.transpose(out=pb[6][:, h * 16:(h + 1) * 16], in_=kn2[R, h * 128:(h + 1) * 128], identity=ident[R, R]), r=["kn2", "ident"], w=[PB[6]])
                P.op("pe", lambda e, h=h: e.transpose(out=pb[6][:, 64 + h * 16:64 + (h + 1) * 16], in_=qn2[R, h * 128:(h + 1) * 128], identity=ident[R, R]), r=["qn2", "ident"], w=[PB[6]])
            evac("act", knT2, pb[6][:, 0:64], [PB[6]], ["knT2"])
            evac("dve", qnT2, pb[6][:, 64:128], [PB[6]], ["qnT2"])
            eye3 = eye16.rearrange("p (s m) -> p s m", s=NS)
            for h in range(4):
                for s_ in range(NS):
                    j = h * NS + s_
                    P.op("dve", lambda e, j=j, s_=s_: e.tensor_scalar(out=KTm[:, j * 16:(j + 1) * 16], in0=eye3[:, s_, :], scalar1=knT2[:, j:j + 1], scalar2=None, op0=ALU.mult), r=["eye16", "knT2"], w=["KTm"])
                    P.op("pool", lambda e, j=j, s_=s_: e.tensor_scalar(out=QTm[:, j * 16:(j + 1) * 16], in0=eye3[:, s_, :], scalar1=qnT2[:, j:j + 1], scalar2=1.0, op0=ALU.mult, op1=ALU.mult), r=["eye16", "qnT2"], w=["QTm"])
            for h in range(4):
                for s_ in range(NS):
                    j = h * NS + s_
                    P.op("pe", lambda e, h=h, s_=s_, j=j: e.matmul(out=pb[h][R, 0:128], lhsT=KTm[:, j * 16:(j + 1) * 16], rhs=Sall4[:, s_, h, :], start=(s_ == 0), stop=(s_ == NS - 1)),
                         r=["KTm", f"S{s_}"], w=[PB[h]])
            for h in range(4):
                P.op("dve", lambda e, h=h: e.scalar_tensor_tensor(out=Dl[R, h * 128:(h + 1) * 128], in0=pb[h][R, 0:128], scalar=sm2[R, 16 + h:17 + h], in1=cvs[R, 512 + h * 128:512 + (h + 1) * 128], op0=ALU.mult, op1=ALU.add),
                     r=[PB[h], "sm2n", "cvs"], w=["Dl"])
                P.op("dve", lambda e, h=h: e.tensor_scalar(out=Dl[R, h * 128:(h + 1) * 128], in0=Dl[R, h * 128:(h + 1) * 128], scalar1=sm2[R, 8 + h:9 + h], scalar2=None, op0=ALU.mult), r=["Dl", "sm2b"], w=["Dl"])
            for s_ in range(NS):
                P.op("dve", lambda e, s_=s_: e.tensor_scalar(out=EgD[R, s_ * 4:(s_ + 1) * 4], in0=sm2[R, 12:16], scalar1=ident[R, s_:s_ + 1], scalar2=None, op0=ALU.mult), r=["sm2g", "ident"], w=["EgD"])
            P.op("pe", lambda e: e.matmul(out=pb[6][:, 0:64], lhsT=ONESM[R, :], rhs=EgD[R, :], start=True, stop=True), r=["cst", "EgD"], w=[PB[6]])
            evac("act", EGB, pb[6][:, 0:64], [PB[6]], ["EGB"])
            for s_ in range(NS):
                km = Km[s_ % 2]; kmn = f"Km{s_ % 2}"
                bank = 4 + s_ % 2
                P.op("pool", lambda e, s_=s_, km=km: e.tensor_scalar(out=km[R, :], in0=kn2[R, :], scalar1=ident[R, s_:s_ + 1], scalar2=1.0, op0=ALU.mult, op1=ALU.mult), r=["kn2", "ident"], w=[kmn])
                for h in range(4):
                    P.op("pe", lambda e, h=h, km=km, bank=bank: e.matmul(out=pb[bank][:, h * 128:(h + 1) * 128], lhsT=km[R, h * 128:(h + 1) * 128], rhs=Dl[R, h * 128:(h + 1) * 128], start=True, stop=True),
                         r=[kmn, "Dl"], w=[PB[bank]])
                for h in range(4):
                    P.op("dve", lambda e, h=h, s_=s_, bank=bank: e.scalar_tensor_tensor(out=Sall4[:, s_, h, :], in0=Sall4[:, s_, h, :], scalar=EGB[:, s_ * 4 + h:s_ * 4 + h + 1], in1=pb[bank][:, h * 128:(h + 1) * 128], op0=ALU.mult, op1=ALU.add),
                         r=[f"S{s_}", "EGB", PB[bank]], w=[f"S{s_}"])
                P.dma("sp", O["ssm_s"][s_].rearrange("h d e -> d h e"), Sall4[:, s_], r=[f"S{s_}"])
            for h in range(4):
                for s_ in range(NS):
                    j = h * NS + s_
                    P.op("pe", lambda e, h=h, s_=s_, j=j: e.matmul(out=pb[h][R, 0:128], lhsT=QTm[:, j * 16:(j + 1) * 16], rhs=Sall4[:, s_, h, :], start=(s_ == 0), stop=(s_ == NS - 1)),
                         r=["QTm", f"S{s_}"], w=[PB[h]])
            for h in range(4):
                evac("act", o_s[R, h * 128:(h + 1) * 128], pb[h][R, 0:128], [PB[h]], ["o_s"])
            P.op("pool", lambda e: e.tensor_tensor(out=tms[R, 0:512], in0=o_s[R, :], in1=o_s[R, :], op=ALU.mult), r=["o_s"], w=["tms"])
            P.op("dve", lambda e: e.tensor_reduce(out=sm2[R, 20:24], in_=tms[R, 0:512].rearrange("p (h d) -> p h d", h=4), axis=AX.X, op=ALU.add), r=["tms"], w=["sm2o"])
            P.op("dve", lambda e: e.tensor_scalar(out=sm2[R, 20:24], in0=sm2[R, 20:24], scalar1=1.0 / 128, scalar2=1e-6, op0=ALU.mult, op1=ALU.add), r=["sm2o"], w=["sm2o"])
            P.op("act", lambda e: e.activation(out=sm2[R, 20:24], in_=sm2[R, 20:24], func=AF.Sqrt), r=["sm2o"], w=["sm2o"])
            P.op("dve", lambda e: e.reciprocal(out=sm2[R, 20:24], in_=sm2[R, 20:24]), r=["sm2o"], w=["sm2o"])
            P.op("act", lambda e: e.activation(out=sz_s[R, :], in_=Ps2[R, C_Z:C_Z + 512], func=AF.Silu), r=["Ps2"], w=["sz_s"])
            for h in range(4):
                P.op("dve", lambda e, h=h: e.scalar_tensor_tensor(out=og_s[R, h * 128:(h + 1) * 128], in0=o_s[R, h * 128:(h + 1) * 128], scalar=sm2[R, 20 + h:21 + h], in1=ggd[R, h * 128:(h + 1) * 128], op0=ALU.mult, op1=ALU.mult),
                     r=["o_s", "sm2o", "ggd"], w=["og_s"])
            P.op("pool", lambda e: e.tensor_tensor(out=og_s[R, :], in0=og_s[R, :], in1=sz_s[R, :], op=ALU.mult), r=["og_s", "sz_s"], w=["og_s"])
            for h in range(4):
                P.op("pe", lambda e, h=h: e.transpose(out=pb[7][:, h * 16:(h + 1) * 16], in_=og_s[R, h * 128:(h + 1) * 128], identity=ident[R, R]), r=["og_s", "ident"], w=[PB[7]])
            evac("act", cTs, pb[7][:, 0:64], [PB[7]], ["cTs"])
            P.dma("sp", S["cT"][NT][:, 0:4, 0:NS], cTs.rearrange("p (h t) -> p h t", h=4), r=["cTs"], w=["cTgs"])

        if not os.environ.get('MK_NOT'):
            P.barrier()
            A.reset(persist1)
            SCALE = 128 ** -0.5
            NEG = -30000.0
            NBIS = 14
            caus = A.f32(128); ii2 = A.bf16(256); onesr = A.f32(128)
            P.dma("sp", caus, I["caus"], w=["caus"])
            P.op("dve", lambda e: e.tensor_copy(out=ii2[:, 0:128], in_=ident), r=["ident"], w=["ii2"])
            P.op("dve", lambda e: e.tensor_copy(out=ii2[:, 128:256], in_=ident), r=["ident"], w=["ii2"])
            P.op("dve", lambda e: e.memset(onesr, 1.0), w=["onesr"])
            tsm = A.f32(256)
            krow = A.f32(128)
            P.op("dve", lambda e: e.tensor_reduce(out=tsm[:, 0:1], in_=ksq, axis=AX.X, op=ALU.max), r=["ksq"], w=["tsm0"])
            P.op("pe", lambda e: e.transpose(out=pb[0][0:1, 0:128], in_=tsm[:, 0:1], identity=ident), r=["tsm0", "ident"], w=[PB[0]])
            evac("dve", krow[0:1, :], pb[0][0:1, 0:128], [PB[0]], ["krow"])
            P.op("dve", lambda e: e.tensor_reduce(out=krow[0:1, 0:1], in_=krow[0:1, :], axis=AX.X, op=ALU.max), r=["krow"], w=["krow"])
            P.op("pe", lambda e: e.matmul(out=pb[0][:, 0:1], lhsT=onesr[0:1, :], rhs=krow[0:1, 0:1], start=True, stop=True), r=["onesr", "krow"], w=[PB[0]])
            evac("dve", tsm[:, 1:2], pb[0][:, 0:1], [PB[0]], ["tsm1"])
            P.op("dve", lambda e: e.tensor_reduce(out=tsm[:, 16:32], in_=qsq.rearrange("p (t h) -> p t h", h=4), axis=AX.X, op=ALU.max), r=["qsq"], w=["tsmq"])
            P.op("dve", lambda e: e.tensor_scalar(out=tsm[:, 32:48], in0=tsm[:, 16:32], scalar1=tsm[:, 1:2], scalar2=None, op0=ALU.mult), r=["tsmq", "tsm1"], w=["negm"])
            P.op("act", lambda e: e.activation(out=tsm[:, 32:48], in_=tsm[:, 32:48], func=AF.Sqrt), r=["negm"], w=["negm"])
            P.op("dve", lambda e: e.tensor_scalar(out=tsm[:, 32:48], in0=tsm[:, 32:48], scalar1=-1.0, scalar2=None, op0=ALU.mult), r=["negm"], w=["negm"])
            Iscs = [A.f32(T), A.f32(T)]
            junk = A.bf16(T); MBs_ = [A.bf16(T), A.bf16(T)]
            qTt = [A.bf16(512), A.bf16(512), A.bf16(512)]; iqTt = [A.bf16(512), A.bf16(512)]
            oaccs = [A.f32(4 * 130), A.f32(4 * 130)]
            rh = [A.bf16(512) for _ in range(4)]
            Dhs = [A.bf16(8 * 128), A.bf16(8 * 128)]
            PT = [A.bf16(256), A.bf16(256)]
            oatt = A.f32(512); cTa = A.bf16(512)
            bss = [A.f32(16), A.f32(16)]
            T_NT = int(os.environ.get("MK_TNT", NT))

            def t_index(j):
                p = j % 2
                Isc = Iscs[p]; In = f"Isc{p}"; Dh = Dhs[p]; Dn = f"Dh{p}"
                qb = qTt[j % 3]; qn_ = f"qTt{j % 3}"; iqb = iqTt[p]; iqn = f"iqTt{p}"
                qb3 = qb.rearrange("p (h t) -> p h t", h=4); iqb3 = iqb.rearrange("p (h t) -> p h t", h=4)
                P.dma("sp", iqb3, S["iqT"][j], w=[iqn])
                P.dma("sp", qb3, S["qT"][j], w=[qn_])
                ncol = HALF + 128 * (j + 1)
                for h in range(8):
                    P.op("dve", lambda e, h=h: e.tensor_scalar(out=Dh[:, h * 128:(h + 1) * 128], in0=ident, scalar1=iwsgn[:, j * 8 + h:j * 8 + h + 1], scalar2=None, op0=ALU.mult),
                         r=["ident", "iwsgn"], w=[Dn])
                nblk = (ncol + 511) // 512
                for kb in range(nblk):
                    c0, c1 = kb * 512, min(ncol, kb * 512 + 512)
                    w_ = c1 - c0
                    accb = 2 + kb % 2

                    def emit_S(h, c0=c0, c1=c1, w_=w_):
                        p_, hf_ = h // 2, h % 2
                        ba = h % 2
                        rb = rh[h % 4]; rbn = f"rh{h % 4}"
                        P.op("pe", lambda e: e.matmul(out=pb[ba][:, 0:w_], lhsT=iqb3[hf_ * 64:(hf_ + 1) * 64, p_, :], rhs=ikT[hf_ * 64:(hf_ + 1) * 64, c0:c1], start=True, stop=True),
                             r=[iqn, "ikT"], w=[PB[ba]])
                        P.op("act", lambda e: e.activation(out=rb[:, 0:w_], in_=pb[ba][:, 0:w_], func=AF.Relu, scale=iwabs[:, j * 8 + h:j * 8 + h + 1]),
                             r=[PB[ba], "iwabs"], w=[rbn])

                    def emit_D(h, w_=w_, accb=accb):
                        rb = rh[h % 4]; rbn = f"rh{h % 4}"
                        P.op("pe", lambda e: e.matmul(out=pb[accb][:, 0:w_], lhsT=Dh[:, h * 128:(h + 1) * 128], rhs=rb[:, 0:w_], start=(h == 0), stop=(h == 7)),
                             r=[Dn, rbn], w=[PB[accb]])

                    emit_S(0); emit_S(1)
                    for h in range(8):
                        emit_D(h)
                        if h + 2 < 8:
                            emit_S(h + 2)
                    if c0 < HALF:
                        P.op("act", lambda e, c0=c0, c1=c1, w_=w_, accb=accb: e.activation(out=Isc[:, c0:c1], in_=pb[accb][:, 0:w_], func=AF.Identity, bias=flags[:, 1:2]), r=[PB[accb], "flags"], w=[In])
                    else:
                        P.op("act", lambda e, c0=c0, c1=c1, w_=w_, accb=accb: e.activation(out=Isc[:, c0:c1], in_=pb[accb][:, 0:w_], func=AF.Copy), r=[PB[accb]], w=[In])
                P.op("pool", lambda e: e.tensor_tensor(out=Isc[:, ncol - 128:ncol], in0=Isc[:, ncol - 128:ncol], in1=caus, op=ALU.add), r=[In, "caus"], w=[In])

            def t_bisect(j):
                p = j % 2
                Isc = Iscs[p]; In = f"Isc{p}"; MB = MBs_[p]; Mn = f"MB{p}"; bs = bss[p]
                B = lambda n: f"bs{p}_{n}"
                ncol = HALF + 128 * (j + 1)
                lo, rng, thr, cntc, mm = bs[:, 0:1], bs[:, 1:2], bs[:, 2:3], bs[:, 3:4], bs[:, 4:5]
                P.op("dve", lambda e: e.tensor_reduce(out=lo, in_=Isc[:, 0:ncol], axis=AX.X, op=ALU.min), r=[In], w=[B("lo")])
                P.op("dve", lambda e: e.tensor_reduce(out=rng, in_=Isc[:, 0:ncol], axis=AX.X, op=ALU.max), r=[In], w=[B("rng")])
                P.op("dve", lambda e: e.tensor_scalar(out=thr, in0=rng, scalar1=-128.0, scalar2=None, op0=ALU.add), r=[B("rng")], w=[B("thr")])
                P.op("dve", lambda e: e.tensor_tensor(out=lo, in0=lo, in1=thr, op=ALU.max), r=[B("lo"), B("thr")], w=[B("lo")])
                P.op("dve", lambda e: e.tensor_tensor(out=rng, in0=rng, in1=lo, op=ALU.subtract), r=[B("rng"), B("lo")], w=[B("rng")])
                base = bs[:, 5:6]
                P.op("dve", lambda e: e.scalar_tensor_tensor(out=thr, in0=rng, scalar=0.5, in1=lo, op0=ALU.mult, op1=ALU.add), r=[B("rng"), B("lo")], w=[B("thr")])
                for it in range(NBIS):
                    st = 2.0 ** -(it + 1)
                    P.op("dve", lambda e: e.tensor_scalar(out=junk[:, 0:ncol], in0=Isc[:, 0:ncol], scalar1=thr, scalar2=None, op0=ALU.is_ge, op1=ALU.add, accum_out=cntc),
                         r=[In, B("thr")], w=["junk", B("cnt")])
                    P.op("dve", lambda e, st=st: e.scalar_tensor_tensor(out=base, in0=rng, scalar=-0.5 * st, in1=thr, op0=ALU.mult, op1=ALU.add), r=[B("rng"), B("thr")], w=[B("base")])
                    P.op("dve", lambda e: e.tensor_scalar(out=mm, in0=cntc, scalar1=255.5, scalar2=rng, op0=ALU.is_ge, op1=ALU.mult), r=[B("cnt"), B("rng")], w=[B("m")])
                    P.op("dve", lambda e, st=st: e.scalar_tensor_tensor(out=thr, in0=mm, scalar=st, in1=base, op0=ALU.mult, op1=ALU.add), r=[B("m"), B("base")], w=[B("thr")])
                P.op("dve", lambda e: e.scalar_tensor_tensor(out=lo, in0=rng, scalar=-(2.0 ** -(NBIS + 1)), in1=thr, op0=ALU.mult, op1=ALU.add), r=[B("rng"), B("thr")], w=[B("lo")])
                P.op("dve", lambda e: e.tensor_scalar(out=MB[:, 0:ncol], in0=Isc[:, 0:ncol], scalar1=lo, scalar2=NEG, op0=ALU.is_lt, op1=ALU.mult), r=[In, B("lo")], w=[Mn])
                P.op("dve", lambda e: e.tensor_scalar(out=MB[:, 0:ncol], in0=MB[:, 0:ncol], scalar1=tsm[:, 32 + j:33 + j], scalar2=None, op0=ALU.add), r=[Mn, "negm"], w=[Mn])

            def t_attend(j):
                p = j % 2
                MB = MBs_[p]; Mn = f"MB{p}"; bs = bss[p]
                qb = qTt[j % 3]; qn_ = f"qTt{j % 3}"
                qb3 = qb.rearrange("p (h t) -> p h t", h=4)
                oacc = oaccs[p]; oan = f"oacc{p}"
                ntile = 16 + j + 1
                seq = [(g, t) for g in range(2) for t in range(ntile)]

                def emit_ST(i):
                    g, t = seq[i]
                    sb_ = i % 2
                    ptb = PT[i % 2]; ptn = f"PT{i % 2}"
                    P.op("pe", lambda e: e.matmul(out=pb[sb_][:, 0:256], lhsT=KT3[:, g, t * 128:(t + 1) * 128], rhs=qb3[:, g * 2:(g + 1) * 2, :], start=True, stop=False),
                         r=["KT", qn_], w=[PB[sb_]])
                    P.op("pe", lambda e: e.matmul(out=pb[sb_][:, 0:256], lhsT=MB[:, t * 128:(t + 1) * 128], rhs=ii2, start=False, stop=True),
                         r=[Mn, "ii2"], w=[PB[sb_]])
                    P.op("act", lambda e: e.activation(out=ptb, in_=pb[sb_][:, 0:256], func=AF.Exp, scale=SCALE), r=[PB[sb_]], w=[ptn])

                def emit_PV(i):
                    g, t = seq[i]
                    ptb = PT[i % 2]; ptn = f"PT{i % 2}"
                    for h2i in range(2):
                        P.op("pe", lambda e, h2i=h2i: e.matmul(out=pb[4 + g * 2 + h2i][:, 0:130], lhsT=ptb[:, h2i * 128:(h2i + 1) * 128], rhs=VA4[:, t, g, :], start=(t == 0), stop=(t == ntile - 1)),
                             r=[ptn, "VA"], w=[PB[4 + g * 2 + h2i]])

                emit_ST(0)
                if len(seq) > 1:
                    emit_ST(1)
                for i in range(len(seq)):
                    emit_PV(i)
                    if i + 2 < len(seq):
                        emit_ST(i + 2)
                for h in range(4):
                    P.op("act", lambda e, h=h: e.activation(out=oacc[:, h * 130:(h + 1) * 130], in_=pb[4 + h][:, 0:130], func=AF.Copy), r=[PB[4 + h]], w=[oan])

            def t_final(j):
                p = j % 2
                bs = bss[p]; oacc = oaccs[p]; oan = f"oacc{p}"
                for h in range(4):
                    P.op("dve", lambda e, h=h: e.reciprocal(out=bs[:, 8 + h:9 + h], in_=oacc[:, h * 130 + 128:h * 130 + 129]), r=[oan], w=[f"bs{p}_r"])
                    P.op("dve", lambda e, h=h: e.tensor_scalar(out=oatt[:, h * 128:(h + 1) * 128], in0=oacc[:, h * 130:h * 130 + 128], scalar1=bs[:, 8 + h:9 + h], scalar2=None, op0=ALU.mult),
                         r=[oan, f"bs{p}_r"], w=["oatt"])
                for h in range(4):
                    P.op("pe", lambda e, h=h: e.transpose(out=pb[3][:, h * 128:(h + 1) * 128], in_=oatt[:, h * 128:(h + 1) * 128], identity=ident), r=["oatt", "ident"], w=[PB[3]])
                evac("act", cTa, pb[3], [PB[3]], ["cTa"])
                P.dma("sp", S["cT"][j][:, 4:8, :], cTa.rearrange("p (h t) -> p h t", h=4), r=["cTa"], w=[f"cTa{j}"])

            if T_NT > 0:
                t_index(0)
            if T_NT > 1:
                t_index(1)
            if T_NT > 0:
                t_bisect(0)
            for j in range(T_NT):
                t_attend(j)
                if j + 2 < T_NT:
                    t_index(j + 2)
                if j + 1 < T_NT:
                    t_bisect(j + 1)
                t_final(j)

        if not os.environ.get('MK_NOTS'):
            P.barrier()
            A.reset(persist0)
            zSCALE = 128 ** -0.5
            NPG = 16
            zidx_i = A.f32(32).bitcast(I32)
            zpt_i = A.f32(256).bitcast(I32)
            zidx_f = A.f32(256); zsel = A.f32(257); zix = A.f32(32)
            P.dma("sp", zpt_i[:, :], I["pt"].to_broadcast([128, 256]), w=["zpt_i"])
            P.dma("sp", zsel, I["tsel"], w=["zsel"])
            P.op("dve", lambda e: e.tensor_copy(out=zidx_f, in_=zpt_i[:, :]), r=["zpt_i"], w=["zidx_f"])
            P.op("dve", lambda e: e.tensor_tensor(out=zidx_f, in0=zidx_f, in1=zsel[:, 0:256], op=ALU.mult), r=["zidx_f", "zsel"], w=["zidx_f"])
            P.op("dve", lambda e: e.tensor_reduce(out=zix, in_=zidx_f.rearrange("p (q j) -> p q j", j=8), axis=AX.X, op=ALU.add), r=["zidx_f"], w=["zix"])
            P.op("dve", lambda e: e.tensor_scalar(out=zix, in0=zix, scalar1=16.0, scalar2=zsel[:, 256:257], op0=ALU.mult, op1=ALU.add), r=["zix", "zsel"], w=["zix"])
            P.op("dve", lambda e: e.tensor_copy(out=zidx_i[:, :], in_=zix), r=["zix"], w=["zidx_i"])
            zPs = A.f32(INC)
            P.dma("sp", zPs[0:NS, :], S["ps"], w=["zPs"])
            zones = A.f32(128)
            P.op("dve", lambda e: e.memset(zones, 1.0), w=["zones"])
            ziqT = A.f32(NS * 8)
            ziwT = A.f32(NS)
            zikn = A.f32(NS)
            zqT = A.bf16(4 * NS)
            zkTn = A.bf16(2 * NS)
            ziqT3 = ziqT.rearrange("p (s h) -> p s h", h=8)
            R = slice(0, NS)
            for h in range(8):
                P.op("pe", lambda e, h=h: e.transpose(out=pb[0][0:64, h * NS:(h + 1) * NS], in_=zPs[R, C_IQ + h * 64:C_IQ + (h + 1) * 64], identity=ident[R, R]), r=["zPs", "ident"], w=[PB[0]])
            P.op("dve", lambda e: e.tensor_copy(out=ziqT3[0:64], in_=pb[0][0:64, 0:8 * NS].rearrange("p (h s) -> p s h", h=8)), r=[PB[0]], w=["ziqT"])
            P.op("pe", lambda e: e.transpose(out=pb[1][0:8, 0:NS], in_=zPs[R, C_IW:C_IW + 8], identity=ident[R, R]), r=["zPs", "ident"], w=[PB[1]])
            P.op("dve", lambda e: e.tensor_scalar(out=ziwT[0:8, :], in0=pb[1][0:8, 0:NS], scalar1=8 ** -0.5, scalar2=None, op0=ALU.mult), r=[PB[1]], w=["ziwT"])
            P.op("pe", lambda e: e.transpose(out=pb[1][0:64, 64:64 + NS], in_=zPs[R, C_IK:C_IK + 64], identity=ident[R, R]), r=["zPs", "ident"], w=[PB[1]])
            P.op("dve", lambda e: e.tensor_copy(out=zikn[0:64, :], in_=pb[1][0:64, 64:64 + NS]), r=[PB[1]], w=["zikn"])
            for h in range(4):
                P.op("pe", lambda e, h=h: e.transpose(out=pb[2][:, h * NS:(h + 1) * NS], in_=zPs[R, C_AQ + h * 128:C_AQ + (h + 1) * 128], identity=ident[R, R]), r=["zPs", "ident"], w=[PB[2]])
            for g in range(2):
                P.op("pe", lambda e, g=g: e.transpose(out=pb[2][:, 64 + g * NS:64 + (g + 1) * NS], in_=zPs[R, C_AK + g * 128:C_AK + (g + 1) * 128], identity=ident[R, R]), r=["zPs", "ident"], w=[PB[2]])
            P.op("dve", lambda e: e.tensor_copy(out=zqT, in_=pb[2][:, 0:4 * NS]), r=[PB[2]], w=["zqT"])
            P.op("dve", lambda e: e.tensor_copy(out=zkTn, in_=pb[2][:, 64:64 + 2 * NS]), r=[PB[2]], w=["zkTn"])
            zvn_f = A.f32(NS * 256); zvn = A.bf16(NS * 2 * 130)
            zvn4 = zvn.rearrange("p (s g d) -> p s g d", s=NS, g=2)
            P.dma("sp", zvn_f[0:1, :].rearrange("p (s c) -> p s c", s=NS), S["ps"][:, C_AV:C_AV + 256].rearrange("(o s) c -> o s c", o=1), w=["zvn_f"])
            P.op("dve", lambda e: e.memset(zvn[0:1, :], 1.0), w=["zvn"])
            P.op("dve", lambda e: e.tensor_copy(out=zvn4[0:1, :, :, 0:128], in_=zvn_f[0:1, :].rearrange("p (s g d) -> p s g d", s=NS, g=2)), r=["zvn_f", "zvn"], w=["zvn"])
            NKS = 2049
            zIall = A.f32(NKS + 3); zikg = A.f32(NPG * 64); zikT = A.f32(NKS + 3)
            zik_rows = I["cache_ik"].rearrange("(r t) d -> r (t d)", t=8)
            zk_rows = I["cache_k"].rearrange("(r t) d -> r (t d)", t=8)
            zv_rows = I["cache_v"].rearrange("(r t) d -> r (t d)", t=8)
            zikg4 = zikg.rearrange("p (a t d) -> p a t d", a=2, t=8)
            zr8 = [A.f32(512), A.f32(512)]; zrw = A.f32(NKS + 3)
            zikg3 = zikg.rearrange("p (g d) -> p g d", g=NPG)
            for s_ in range(NS):
                for a_ in range(2):
                    col = s_ * 2 + a_
                    P.dma_raw("pool", lambda e, a_=a_, col=col: e.indirect_dma_start(out=zikg4[:, a_].rearrange("p t d -> p (t d)"), out_offset=None, in_=zik_rows, in_offset=bass.IndirectOffsetOnAxis(ap=zidx_i[:, col:col + 1], axis=0)),
                              r=["zidx_i"], w=["zikg"], sw=True)
                for q4 in range(4):
                    for i4 in range(4):
                        bk = q4 * 4 + i4
                        t8, a_ = bk // 2, bk % 2
                        P.op("pe", lambda e, t8=t8, a_=a_, i4=i4: e.transpose(out=pb[3][0:64, i4 * 128:(i4 + 1) * 128], in_=zikg4[:, a_, t8, :], identity=ident), r=["zikg", "ident"], w=[PB[3]])
                    evac("act" if q4 % 2 else "dve", zikT[0:64, q4 * 512:(q4 + 1) * 512], pb[3][0:64, :], [PB[3]], ["zikT"])
                P.op("dve", lambda e, s_=s_: e.tensor_copy(out=zikT[0:64, 2048:2049], in_=zikn[0:64, s_:s_ + 1]), r=["zikn"], w=["zikT"])
                for kb in range(5):
                    c0, c1 = kb * 512, min(NKS, kb * 512 + 512)
                    w_ = c1 - c0
                    rb = zr8[kb % 2]; rbn = f"zr8{kb % 2}"
                    P.op("pe", lambda e, s_=s_, c0=c0, c1=c1, w_=w_, kb=kb: e.matmul(out=pb[4 + kb % 2][0:8, 0:w_], lhsT=ziqT3[0:64, s_, :], rhs=zikT[0:64, c0:c1], start=True, stop=True), r=["ziqT", "zikT"], w=[PB[4 + kb % 2]])
                    P.op("dve", lambda e, s_=s_, w_=w_, kb=kb, rb=rb: e.tensor_scalar(out=rb[0:8, 0:w_], in0=pb[4 + kb % 2][0:8, 0:w_], scalar1=0.0, scalar2=ziwT[0:8, s_:s_ + 1], op0=ALU.max, op1=ALU.mult),
                         r=[PB[4 + kb % 2], "ziwT"], w=[rbn])
                    P.op("pe", lambda e, w_=w_, kb=kb, rb=rb: e.matmul(out=pb[6 + kb % 2][0:1, 0:w_], lhsT=zones[0:8, 0:1], rhs=rb[0:8, 0:w_], start=True, stop=True), r=["zones", rbn], w=[PB[6 + kb % 2]])
                    evac("act", zrw[0:1, c0:c1], pb[6 + kb % 2][0:1, 0:w_], [PB[6 + kb % 2]], ["zrw"])
                P.dma("sp", zIall[s_:s_ + 1, 0:NKS], zrw[0:1, 0:NKS], r=["zrw"], w=["zIall"])
            zbs = A.f32(16); zjunk = A.bf16(NKS + 3); zMB = A.f32(NKS + 3)
            zlo, zrng, zthr, zcnt, zmm = zbs[R, 0:1], zbs[R, 1:2], zbs[R, 2:3], zbs[R, 3:4], zbs[R, 4:5]
            P.op("dve", lambda e: e.tensor_reduce(out=zlo, in_=zIall[R, 0:NKS], axis=AX.X, op=ALU.min), r=["zIall"], w=["zlo"])
            P.op("dve", lambda e: e.tensor_reduce(out=zrng, in_=zIall[R, 0:NKS], axis=AX.X, op=ALU.max), r=["zIall"], w=["zrng"])
            P.op("dve", lambda e: e.tensor_tensor(out=zrng, in0=zrng, in1=zlo, op=ALU.subtract), r=["zrng", "zlo"], w=["zrng"])
            for it in range(24):
                st = 2.0 ** -(it + 1)
                P.op("dve", lambda e, st=st: e.scalar_tensor_tensor(out=zthr, in0=zrng, scalar=st, in1=zlo, op0=ALU.mult, op1=ALU.add), r=["zrng", "zlo"], w=["zthr"])
                P.op("dve", lambda e: e.tensor_scalar(out=zjunk[R, 0:NKS], in0=zIall[R, 0:NKS], scalar1=zthr, scalar2=None, op0=ALU.is_ge, op1=ALU.add, accum_out=zcnt), r=["zIall", "zthr"], w=["zjunk", "zcnt"])
                P.op("dve", lambda e: e.tensor_scalar(out=zmm, in0=zcnt, scalar1=255.5, scalar2=zrng, op0=ALU.is_ge, op1=ALU.mult), r=["zcnt", "zrng"], w=["zmm"])
                P.op("dve", lambda e, st=st: e.scalar_tensor_tensor(out=zlo, in0=zmm, scalar=st, in1=zlo, op0=ALU.mult, op1=ALU.add), r=["zmm", "zlo"], w=["zlo"])
            P.op("dve", lambda e: e.tensor_scalar(out=zMB[R, 0:NKS], in0=zIall[R, 0:NKS], scalar1=zlo, scalar2=None, op0=ALU.is_ge), r=["zIall", "zlo"], w=["zMB"])
            zMT = A.f32(NPG * NS); zMT3 = zMT.rearrange("p (g s) -> p g s", g=NPG); zMn = A.f32(NS)
            for pg in range(NPG):
                P.op("pe", lambda e, pg=pg: e.transpose(out=pb[0][:, pg * NS:(pg + 1) * NS], in_=zMB[R, pg * 128:(pg + 1) * 128], identity=ident[R, R]), r=["zMB", "ident"], w=[PB[0]])
            evac("dve", zMT, pb[0][:, 0:NPG * NS], [PB[0]], ["zMT"])
            P.op("pe", lambda e: e.transpose(out=pb[1][0:1, 0:NS], in_=zMB[R, 2048:2049], identity=ident[R, R]), r=["zMB", "ident"], w=[PB[1]])
            evac("dve", zMn[0:1, :], pb[1][0:1, 0:NS], [PB[1]], ["zMn"])
            zkg = A.f32(NPG * 256); zvg = A.f32(NPG * 256)
            zkg4 = zkg.rearrange("p (a t c) -> p a t c", a=2, t=8); zvg4 = zvg.rearrange("p (a t c) -> p a t c", a=2, t=8)
            zKT = A.bf16(NPG * 256); zKT4 = zKT.rearrange("p (g k c) -> p g k c", g=NPG, k=2)
            zVb = A.bf16(NPG * 2 * 130); zVb4 = zVb.rearrange("p (g k d) -> p g k d", g=NPG, k=2)
            zP = A.bf16(NPG * 4); zP3 = zP.rearrange("p (g h) -> p g h", g=NPG); zPn = A.bf16(4)
            zsm = A.f32(16); zcr = A.f32(128)
            zo = A.f32(256); zoT = A.bf16(4 * NS); zoT3 = zoT.rearrange("p (h s) -> p h s", h=4)
            P.op("dve", lambda e: e.memset(zVb, 1.0), w=["zVb"])
            for s_ in range(NS):
                for a_ in range(2):
                    col = s_ * 2 + a_
                    P.dma_raw("pool", lambda e, a_=a_, col=col: e.indirect_dma_start(out=zkg4[:, a_].rearrange("p t c -> p (t c)"), out_offset=None, in_=zk_rows, in_offset=bass.IndirectOffsetOnAxis(ap=zidx_i[:, col:col + 1], axis=0)),
                              r=["zidx_i"], w=["zkg"], sw=True)
                    P.dma_raw("pool", lambda e, a_=a_, col=col: e.indirect_dma_start(out=zvg4[:, a_].rearrange("p t c -> p (t c)"), out_offset=None, in_=zv_rows, in_offset=bass.IndirectOffsetOnAxis(ap=zidx_i[:, col:col + 1], axis=0)),
                              r=["zidx_i"], w=["zvg"], sw=True)
                for a_ in range(2):
                    P.op("act", lambda e, a_=a_: e.activation(out=zVb.rearrange("p (t a k d) -> p a t k d", t=8, a=2, k=2)[:, a_, :, :, 0:128], in_=zvg4[:, a_].rearrange("p t (k d) -> p t k d", k=2), func=AF.Copy),
                         r=["zvg", "zVb"], w=["zVb"])
                for pq in range(8):
                    for i2 in range(2):
                        bk = pq * 2 + i2
                        t8, a_ = bk // 2, bk % 2
                        for g in range(2):
                            P.op("pe", lambda e, t8=t8, a_=a_, g=g, i2=i2, pq=pq: e.transpose(out=pb[2 + pq % 2][:, (i2 * 2 + g) * 128:(i2 * 2 + g + 1) * 128], in_=zkg4[:, a_, t8, g * 128:(g + 1) * 128], identity=ident),
                                 r=["zkg", "ident"], w=[PB[2 + pq % 2]])
                    evac("act" if pq % 2 else "dve", zKT[:, pq * 512:(pq + 1) * 512], pb[2 + pq % 2], [PB[2 + pq % 2]], ["zKT"])
                for pg in range(NPG):
                    for g in range(2):
                        P.op("pe", lambda e, pg=pg, g=g, s_=s_: e.matmul(out=pb[4][:, pg * 4 + g * 2:pg * 4 + g * 2 + 2], lhsT=zKT4[:, pg, g, :], rhs=zqT.rearrange("p (h s) -> p h s", h=4)[:, g * 2:(g + 1) * 2, s_], start=True, stop=True),
                             r=["zKT", "zqT"], w=[PB[4]])
                for g in range(2):
                    P.op("pe", lambda e, g=g, s_=s_: e.matmul(out=pb[5][0:1, g * 2:g * 2 + 2], lhsT=zkTn.rearrange("p (g s) -> p g s", g=2)[:, g, s_:s_ + 1], rhs=zqT.rearrange("p (h s) -> p h s", h=4)[:, g * 2:(g + 1) * 2, s_], start=True, stop=True),
                         r=["zkTn", "zqT"], w=[PB[5]])
                P.op("dve", lambda e: e.tensor_reduce(out=zsm[:, 0:1], in_=pb[4][:, 0:NPG * 4], axis=AX.X, op=ALU.max), r=[PB[4]], w=["zsm0"])
                P.op("pe", lambda e: e.transpose(out=pb[6][0:1, 0:128], in_=zsm[:, 0:1], identity=ident), r=["zsm0", "ident"], w=[PB[6]])
                evac("dve", zcr[0:1, :], pb[6][0:1, 0:128], [PB[6]], ["zcr"])
                P.op("dve", lambda e: e.tensor_reduce(out=zsm[0:1, 1:2], in_=zcr[0:1, :], axis=AX.X, op=ALU.max), r=["zcr"], w=["zsm1"])
                P.op("dve", lambda e: e.tensor_reduce(out=zsm[0:1, 2:3], in_=pb[5][0:1, 0:4], axis=AX.X, op=ALU.max), r=[PB[5]], w=["zsm2"])
                P.op("dve", lambda e: e.tensor_tensor(out=zsm[0:1, 1:2], in0=zsm[0:1, 1:2], in1=zsm[0:1, 2:3], op=ALU.max), r=["zsm1", "zsm2"], w=["zsm1"])
                P.op("dve", lambda e: e.tensor_scalar(out=zsm[0:1, 1:2], in0=zsm[0:1, 1:2], scalar1=-zSCALE, scalar2=None, op0=ALU.mult), r=["zsm1"], w=["zsm1"])
                P.op("pe", lambda e: e.matmul(out=pb[6][:, 128:129], lhsT=zones[0:1, :], rhs=zsm[0:1, 1:2], start=True, stop=True), r=["zones", "zsm1"], w=[PB[6]])
                evac("dve", zsm[:, 3:4], pb[6][:, 128:129], [PB[6]], ["zsm3"])
                P.op("act", lambda e: e.activation(out=zP, in_=pb[4][:, 0:NPG * 4], func=AF.Exp, scale=zSCALE, bias=zsm[:, 3:4]), r=[PB[4], "zsm3"], w=["zP"])
                P.op("act", lambda e: e.activation(out=zPn[0:1, :], in_=pb[5][0:1, 0:4], func=AF.Exp, scale=zSCALE, bias=zsm[0:1, 3:4]), r=[PB[5], "zsm3"], w=["zPn"])
                for h in range(4):
                    P.op("dve", lambda e, h=h, s_=s_: e.tensor_tensor(out=zP3[:, :, h], in0=zP3[:, :, h], in1=zMT3[:, :, s_], op=ALU.mult), r=["zP", "zMT"], w=["zP"])
                P.op("dve", lambda e, s_=s_: e.tensor_scalar(out=zPn[0:1, :], in0=zPn[0:1, :], scalar1=zMn[0:1, s_:s_ + 1], scalar2=None, op0=ALU.mult), r=["zPn", "zMn"], w=["zPn"])
                for g in range(2):
                    for pg in range(NPG):
                        P.op("pe", lambda e, pg=pg, g=g: e.matmul(out=pb[g][0:2, 0:130], lhsT=zP3[:, pg, g * 2:(g + 1) * 2], rhs=zVb4[:, pg, g, :], start=(pg == 0), stop=False), r=["zP", "zVb"], w=[PB[g]])
                    P.op("pe", lambda e, g=g, s_=s_: e.matmul(out=pb[g][0:2, 0:130], lhsT=zPn[0:1, g * 2:(g + 1) * 2], rhs=zvn4[0:1, s_, g, :], start=False, stop=True), r=["zPn", "zvn"], w=[PB[g]])
                for g in range(2):
                    P.op("dve", lambda e, g=g: e.reciprocal(out=zsm[0:2, 4 + g:5 + g], in_=pb[g][0:2, 128:129]), r=[PB[g]], w=["zsm4"])
                    P.op("dve", lambda e, g=g: e.tensor_scalar(out=zo[0:2, g * 128:(g + 1) * 128], in0=pb[g][0:2, 0:128], scalar1=zsm[0:2, 4 + g:5 + g], scalar2=None, op0=ALU.mult), r=[PB[g], "zsm4"], w=["zo"])
                for g in range(2):
                    P.op("pe", lambda e, g=g: e.transpose(out=pb[7][:, g * 2:(g + 1) * 2], in_=zo[0:2, g * 128:(g + 1) * 128], identity=ident[0:2, 0:2]), r=["zo", "ident"], w=[PB[7]])
                P.op("dve", lambda e, s_=s_: e.tensor_copy(out=zoT3[:, :, s_], in_=pb[7][:, 0:4]), r=[PB[7]], w=["zoT"])
            P.dma("sp", S["cT"][NT][:, 4:8, 0:NS], zoT3, r=["zoT"], w=["cTas"])

        if not os.environ.get('MK_NOD'):
            P.barrier()
            A.reset(persist0)
            zt = A.bf16(1024)
            P.op("pool", lambda e: e.memset(zt, 0.0), w=["zt"])
            if True:
                for ti in range(NT + 1):
                    if (ti == NT and os.environ.get('MK_NOTS')) or (ti < NT and (os.environ.get('MK_NOT') or ti >= int(os.environ.get("MK_TNT", NT)))):
                        P.dma("sp", S["cT"][ti][:, 4:8, :], zt[:, 0:512].rearrange("p (k t) -> p k t", k=4), r=["zt"], w=[f"cT{ti}"])
                    if ti == NT:
                        P.dma("sp", S["cT"][ti][:, 0:4, 16:128], zt[:, 0:448].rearrange("p (k t) -> p k t", k=4), r=["zt"], w=[f"cT{ti}"])
            g2_bc = A.f32(D); ga1_bc = A.f32(D); a2_bc = A.f32(D); sh2_bc = A.f32(D)
            a2_s = A.f32(D)
            P.dma("sp", g2_bc, I["g2"].to_broadcast([128, D]), w=["g2_bc"])
            P.dma("sp", ga1_bc, S["mod"][16:17, 2 * D:3 * D].to_broadcast([128, D]), r=["mod_scr"], w=["ga1_bc"])
            P.dma("sp", sh2_bc, S["mod"][16:17, 3 * D:4 * D].to_broadcast([128, D]), r=["mod_scr"], w=["sh2_bc"])
            P.dma("sp", a2_bc, S["mod"][16:17, 4 * D:5 * D].to_broadcast([128, D]), r=["mod_scr"], w=["a2_bc"])
            P.op("dve", lambda e: e.scalar_tensor_tensor(out=a2_bc, in0=a2_bc, scalar=1.0, in1=g2_bc, op0=ALU.add, op1=ALU.mult),
                 r=["a2_bc", "g2_bc"], w=["a2_bc"])
            P.op("dve", lambda e: e.scalar_tensor_tensor(out=a2_s[0:NS, :], in0=mod_sb[0:NS, 4 * D:5 * D], scalar=1.0, in1=g2_bc[0:NS, :], op0=ALU.add, op1=ALU.mult),
                 r=["mod_sb", "g2_bc"], w=["a2_s"])
            persistD = A.off
            w_out_b = A.bf16(8 * D); w_out3 = w_out_b.rearrange("p (k n) -> p k n", k=8)
            w_f_b = A.bf16(8 * 2 * DFF); w_f3 = w_f_b.rearrange("p (k n) -> p k n", k=8)
            persistD1 = A.off
            wst = [A.f32(8 * 512), A.f32(8 * 512)]
            for cb in range(2 + 11):
                ws3 = wst[cb % 2].rearrange("p (k n) -> p k n", k=8)
                if cb < 2:
                    src = I["w_out"][:, cb * 512:(cb + 1) * 512]; dst = w_out3[:, :, cb * 512:(cb + 1) * 512]; dn = "w_out_b"
                else:
                    src = I["w_ffn_in"][:, (cb - 2) * 512:(cb - 1) * 512]; dst = w_f3[:, :, (cb - 2) * 512:(cb - 1) * 512]; dn = "w_f_b"
                P.dma("sp", ws3, src.rearrange("(k p) n -> p k n", p=128), w=[f"wst{cb % 2}"])
                P.op("pool" if cb % 2 else "dve", lambda e, ws3=ws3, dst=dst: e.tensor_copy(out=dst, in_=ws3), r=[f"wst{cb % 2}"], w=[dn])
            P.seed_after_staging()
            A.reset(persistD1)
            xt = [A.f32(D), A.f32(D)]
            cTt = [A.bf16(8 * 128), A.bf16(8 * 128)]
            x1t = A.f32(D); h2 = A.f32(D); small = A.f32(16)
            h2T = A.bf16(8 * 512); h2T3 = h2T.rearrange("p (k t) -> p k t", k=8)
            uT = A.bf16(22 * 512); uT3 = uT.rearrange("p (f t) -> p f t", f=22)
            gsb = [A.f32(512), A.f32(512)]

            def rms_mod(rows, xin, xin_n, a_t, a_n, sh_t, sh_n, out, out_n):
                ss, rs = small[0:rows, 0:1], small[0:rows, 1:2]
                P.op("act", lambda e: e.activation(out=out, in_=xin, func=AF.Square, accum_out=ss), r=[xin_n], w=[out_n, "ss"])
                P.op("dve", lambda e: e.tensor_scalar(out=rs, in0=ss, scalar1=1.0 / D, scalar2=1e-6, op0=ALU.mult, op1=ALU.add), r=["ss"], w=["rs"])
                P.op("act", lambda e: e.activation(out=rs, in_=rs, func=AF.Sqrt), r=["rs"], w=["rs"])
                P.op("dve", lambda e: e.reciprocal(out=rs, in_=rs), r=["rs"], w=["rs"])
                P.op("dve", lambda e: e.scalar_tensor_tensor(out=out, in0=xin, scalar=rs, in1=a_t, op0=ALU.mult, op1=ALU.mult), r=[xin_n, "rs", a_n], w=[out_n])
                if sh_t is not None:
                    P.op("pool", lambda e: e.tensor_tensor(out=out, in0=out, in1=sh_t, op=ALU.add), r=[out_n, sh_n], w=[out_n])

            groups = [(g4, [(g4 * 4 + j, 128) for j in range(4)]) for g4 in range(4)] + [(4, [(NT, NS)])]
            for g4, tiles in groups:
                ntok = sum(r_ for _, r_ in tiles)
                col = 0
                for (ti, rows) in tiles:
                    it = ti
                    xb = xt[it % 2]; xn = f"xt{it % 2}"; cb_ = cTt[it % 2]; cn = f"cTt{it % 2}"
                    cT3 = cb_.rearrange("p (k t) -> p k t", k=8)
                    smp = (rows == NS)
                    P.dma("sp", xb[0:rows, :], I["xs"] if smp else I["x_own"][ti * 128:(ti + 1) * 128, :], w=[xn])
                    P.dma("sp", cT3, S["cT"][ti], r=[f"cT{ti}"], w=[cn])
                    ga1_t = mod_sb[0:NS, 2 * D:3 * D] if smp else ga1_bc
                    for hb in range(2):
                        for k in range(8):
                            P.op("pe", lambda e, k=k, hb=hb, cT3=cT3, rows=rows: e.matmul(out=pb[hb][0:rows, :], lhsT=cT3[:, k, 0:rows], rhs=w_out3[:, k, hb * 512:(hb + 1) * 512], start=(k == 0), stop=(k == 7)),
                                 r=[cn, "w_out_b"], w=[PB[hb]])
                        P.op("dve", lambda e, hb=hb, rows=rows, ga1_t=ga1_t: e.tensor_tensor(out=x1t[0:rows, hb * 512:(hb + 1) * 512], in0=pb[hb][0:rows, :], in1=ga1_t[0:rows, hb * 512:(hb + 1) * 512], op=ALU.mult),
                             r=[PB[hb], "ga1_bc", "mod_sb"], w=["x1t"])
                    P.op("pool", lambda e, rows=rows, xb=xb: e.tensor_tensor(out=x1t[0:rows, :], in0=x1t[0:rows, :], in1=xb[0:rows, :], op=ALU.add), r=["x1t", xn], w=["x1t"])
                    P.dma("sp", S["x1"][ti, 0:rows, :], x1t[0:rows, :], r=["x1t"], w=[f"x1_{ti}"])
                    if smp:
                        rms_mod(rows, x1t[0:rows, :], "x1t", a2_s[0:rows, :], "a2_s", mod_sb[0:rows, 3 * D:4 * D], "mod_sb", h2[0:rows, :], "h2")
                    else:
                        rms_mod(rows, x1t, "x1t", a2_bc, "a2_bc", sh2_bc, "sh2_bc", h2, "h2")
                    for k in range(8):
                        bank = 2 + k // 4
                        P.op("pe", lambda e, k=k, bank=bank, rows=rows: e.transpose(out=pb[bank][:, (k % 4) * 128:(k % 4) * 128 + rows], in_=h2[0:rows, k * 128:(k + 1) * 128], identity=ident[0:rows, 0:rows]),
                             r=["h2", "ident"], w=[PB[bank]])
                    for half_ in range(2):
                        evac(alt(), h2T3[:, half_ * 4:(half_ + 1) * 4, col:col + rows], pb[2 + half_][:, :].rearrange("p (k t) -> p k t", k=4)[:, :, 0:rows], [PB[2 + half_]], ["h2T"])
                    col += rows
                for fb in range(22):
                    for which in range(2):
                        bank = 4 + which * 2 + fb % 2
                        c0 = which * DFF + fb * 128
                        for k in range(8):
                            P.op("pe", lambda e, k=k, bank=bank, c0=c0, ntok=ntok: e.matmul(out=pb[bank][:, 0:ntok], lhsT=w_f3[:, k, c0:c0 + 128], rhs=h2T3[:, k, 0:ntok], start=(k == 0), stop=(k == 7)),
                                 r=["h2T", "w_f_b"], w=[PB[bank]])
                    gs_ = gsb[fb % 2]; gn = f"gsb{fb % 2}"
                    P.op("act", lambda e, fb=fb, gs_=gs_, ntok=ntok: e.activation(out=gs_[:, 0:ntok], in_=pb[4 + fb % 2][:, 0:ntok], func=AF.Silu), r=[PB[4 + fb % 2]], w=[gn])
                    P.op("dve", lambda e, fb=fb, gs_=gs_, ntok=ntok: e.tensor_tensor(out=uT3[:, fb, 0:ntok], in0=gs_[:, 0:ntok], in1=pb[6 + fb % 2][:, 0:ntok], op=ALU.mult),
                         r=[gn, PB[6 + fb % 2]], w=["uT"])
                P.dma("sp", S["uT"][g4], uT3, r=["uT"], w=[f"uT{g4}"])

            P.barrier()
            A.reset(persistD)
            ga2_bc = A.f32(D); gf_bc = A.f32(D)
            P.dma("sp", gf_bc, I["g_final"].to_broadcast([128, D]), w=["gf_bc"])
            P.dma("sp", ga2_bc, S["mod"][16:17, 5 * D:6 * D].to_broadcast([128, D]), r=["mod_scr"], w=["ga2_bc"])
            w_o_b = A.bf16(22 * D); w_o3 = w_o_b.rearrange("p (k n) -> p k n", k=22)
            persistD2 = A.off
            wst = [A.f32(22 * 256), A.f32(22 * 256)]
            for cb in range(4):
                ws3 = wst[cb % 2].rearrange("p (k n) -> p k n", k=22)
                P.dma("sp", ws3, I["w_ffn_out"][:, cb * 256:(cb + 1) * 256].rearrange("(k p) n -> p k n", p=128), w=[f"wst{cb % 2}"])
                P.op("pool" if cb % 2 else "dve", lambda e, ws3=ws3, cb=cb: e.tensor_copy(out=w_o3[:, :, cb * 256:(cb + 1) * 256], in_=ws3), r=[f"wst{cb % 2}"], w=["w_o_b"])
            P.seed_after_staging()
            A.reset(persistD2)
            uTg = [A.bf16(22 * 512), A.bf16(22 * 512)]
            x1b = [A.f32(D), A.f32(D)]
            x2 = A.f32(D); small = A.f32(16)
            yb = [A.f32(D), A.f32(D)]
            for g4, tiles in groups:
                ug = uTg[g4 % 2]; un = f"uTg{g4 % 2}"
                ug3 = ug.rearrange("p (f t) -> p f t", f=22)
                P.dma("sp", ug3, S["uT"][g4], r=[f"uT{g4}"], w=[un])
                col = 0
                for (ti, rows) in tiles:
                    smp = (rows == NS)
                    xb = x1b[ti % 2]; xn = f"x1b{ti % 2}"; yo = yb[ti % 2]; yn = f"yb{ti % 2}"
                    P.dma("sp", xb[0:rows, :], S["x1"][ti, 0:rows, :], r=[f"x1_{ti}"], w=[xn])
                    ga2_t = mod_sb[0:NS, 5 * D:6 * D] if smp else ga2_bc
                    for hb in range(2):
                        for kf in range(22):
                            P.op("pe", lambda e, kf=kf, hb=hb, rows=rows, col=col, ug3=ug3: e.matmul(out=pb[hb][0:rows, :], lhsT=ug3[:, kf, col:col + rows], rhs=w_o3[:, kf, hb * 512:(hb + 1) * 512], start=(kf == 0), stop=(kf == 21)),
                                 r=[un, "w_o_b"], w=[PB[hb]])
                        P.op("dve", lambda e, hb=hb, rows=rows, ga2_t=ga2_t: e.tensor_tensor(out=x2[0:rows, hb * 512:(hb + 1) * 512], in0=pb[hb][0:rows, :], in1=ga2_t[0:rows, hb * 512:(hb + 1) * 512], op=ALU.mult),
                             r=[PB[hb], "ga2_bc", "mod_sb"], w=["x2"])
                    P.op("pool", lambda e, rows=rows, xb=xb: e.tensor_tensor(out=x2[0:rows, :], in0=x2[0:rows, :], in1=xb[0:rows, :], op=ALU.add), r=["x2", xn], w=["x2"])
                    rms_mod(rows, x2[0:rows, :], "x2", gf_bc[0:rows, :], "gf_bc", None, None, yo[0:rows, :], yn)
                    P.dma("sp", O["y_s"] if smp else O["y_own"][ti * 128:(ti + 1) * 128, :], yo[0:rows, :], r=[yn])
                    col += rows

        P.build(ctx)
        global LAST_PROG
        LAST_PROG = P
    return nc


def rope_table(pos):
    pos = np.asarray(pos, np.float64)[:, None]
    invA = 500000.0 ** (-np.arange(16, dtype=np.float64) / 16)
    invI = 500000.0 ** (-np.arange(8, dtype=np.float64) / 8)
    angA = (pos.astype(np.float32) * invA.astype(np.float32)[None, :]).astype(np.float32)
    angI = (pos.astype(np.float32) * invI.astype(np.float32)[None, :]).astype(np.float32)
    t = np.concatenate([np.tile(np.cos(angA), (1, 4)), np.tile(np.sin(angA), (1, 4)),
                        np.tile(np.cos(angI), (1, 8)), np.tile(np.sin(angI), (1, 8))], axis=1)
    return np.ascontiguousarray(t.astype(np.float32))


_NC_CACHE = {}


def kernel(x_prompt, x_sample, c_prompt, c_sample, cache_k, cache_v, cache_idx_k, page_table, state_conv, state_ssm,
           w_ada, b_ada, g_norm1, w_in, w_conv, a_log, dt_bias, g_gdn_norm, w_out, g_norm2, w_ffn_in, w_ffn_out, g_final):
    f = lambda a: np.ascontiguousarray(np.asarray(a, dtype=np.float32))
    x_prompt = f(x_prompt); x_sample = f(x_sample)
    w_in_p = np.ascontiguousarray(f(w_in)[0][:, PERM])
    wc_p = np.ascontiguousarray(np.concatenate([f(w_conv)[0][:, 512:1536], f(w_conv)[0][:, 0:512]], axis=1))
    jj, cc = np.meshgrid(np.arange(128), np.arange(128), indexing="ij")
    GCONST = np.ascontiguousarray(np.concatenate([
        (jj <= cc).astype(np.float32),
        np.ones((128, 128), np.float32),
        np.where(cc >= jj, 1e4, 0.0).astype(np.float32),
        np.where(cc < jj, -1e4, 0.0).astype(np.float32),
        np.tile(np.eye(128, dtype=np.float32), (1, 4))], axis=1))
    CAUS = np.ascontiguousarray(np.where(np.arange(128)[None, :] > np.arange(128)[:, None], -30000.0, 0.0).astype(np.float32))
    pp = np.arange(128)
    TSEL = np.zeros((128, 257), np.float32)
    TSEL[:, :256] = np.tile((np.arange(8)[None, :] == (pp // 16)[:, None]).astype(np.float32), (1, 32))
    TSEL[:, 256] = pp % 16
    cik2 = f(cache_idx_k).reshape(-1, 64); ck2 = f(cache_k).reshape(-1, 256); cv2 = f(cache_v).reshape(-1, 256)
    EYE16 = np.ascontiguousarray(np.tile(np.eye(16, dtype=np.float32).reshape(1, 256), (128, 1)))
    in_maps = []
    for c in range(8):
        b, s = c // 2, c % 2
        own = slice(s * HALF, (s + 1) * HALF); oth = slice((1 - s) * HALF, (2 - s) * HALF)
        m = {
            "x_own": f(x_prompt[b, own]), "x_oth": f(x_prompt[b, oth]),
            "cin": f(np.concatenate([np.asarray(c_sample)[c * NS:(c + 1) * NS], np.asarray(c_prompt)[b:b + 1]], 0)),
            "xs": f(x_sample[c * NS:(c + 1) * NS, 0]),
            "w_ada": f(w_ada)[0], "b_ada": f(b_ada), "g1": f(g_norm1), "w_in": w_in_p, "w_conv": f(w_conv)[0],
            "a_log": f(a_log), "dt_bias": f(dt_bias), "g_gdn": f(g_gdn_norm), "w_out": f(w_out)[0], "g2": f(g_norm2),
            "w_ffn_in": f(w_ffn_in)[0], "w_ffn_out": f(w_ffn_out)[0], "g_final": f(g_final)[None, :],
            "tab_own": rope_table(np.arange(s * HALF, (s + 1) * HALF)), "tab_oth": rope_table(np.arange((1 - s) * HALF, (2 - s) * HALF)),
            "tab_s": rope_table(np.full(NS, 2048)),
            "flags": np.tile(np.array([[float(s), (s - 1) * 30000.0, 0, 0]], np.float32), (128, 1)),
            "ident": np.eye(128, dtype=np.float32),
            "state_conv": f(np.asarray(state_conv)[0, c * NS:(c + 1) * NS]),
            "gconst": GCONST, "wc_p": wc_p,
            "state_ssm": f(np.asarray(state_ssm)[0, c * NS:(c + 1) * NS]), "eye16": EYE16, "caus": CAUS,
            "pt": np.ascontiguousarray(np.asarray(page_table, np.int32)[c * NS:(c + 1) * NS].reshape(1, NS * 16)), "tsel": TSEL,
            "cache_ik": cik2, "cache_k": ck2, "cache_v": cv2,
        }
        if os.environ.get('MK_NOTS'):
            for k_ in ("cache_ik", "cache_k", "cache_v"):
                m.pop(k_)
        in_maps.append(m)
    if "nc" not in _NC_CACHE:
        _NC_CACHE["nc"] = build_program()
    res = run_bass_kernel_spmd(_NC_CACHE["nc"], in_maps, core_ids=list(range(8)))
    R = res.results
    B = 4
    y_prompt = np.zeros((B, T, D), np.float32); nk = np.zeros((1, B, T, 2, 128), np.float32); nv = np.zeros_like(nk)
    nik = np.zeros((1, B, T, 64), np.float32); nconv = np.zeros((1, B, 3, 1536), np.float32); nssm = np.zeros((1, B, 4, 128, 128), np.float32)
    y_s = np.zeros((128, 1, D), np.float32); ks = np.zeros((1, 128, 1, 2, 128), np.float32); vs = np.zeros_like(ks)
    iks = np.zeros((1, 128, 1, 64), np.float32); convs = np.zeros((1, 128, 3, 1536), np.float32); ssms = np.zeros((1, 128, 4, 128, 128), np.float32)
    for c in range(8):
        b, s = c // 2, c % 2
        own = slice(s * HALF, (s + 1) * HALF)
        r = R[c]
        y_prompt[b, own] = r["y_own"]; nk[0, b, own] = r["k_own"].reshape(HALF, 2, 128); nv[0, b, own] = r["v_own"].reshape(HALF, 2, 128)
        nik[0, b, own] = r["ik_own"]
        if s == 1:
            nconv[0, b] = r["conv_tail"]; nssm[0, b] = r["ssm_fin"]
        sl = slice(c * NS, (c + 1) * NS)
        y_s[sl, 0] = r["y_s"]; ks[0, sl, 0] = r["k_s"].reshape(NS, 2, 128); vs[0, sl, 0] = r["v_s"].reshape(NS, 2, 128)
        iks[0, sl, 0] = r["ik_s"]; convs[0, sl] = r["conv_s"]; ssms[0, sl] = r["ssm_s"]
    return (y_prompt, y_s, nk, nv, nik, nconv, nssm, ks, vs, iks, convs, ssms)
```

```python
import numpy as np
from contextlib import ExitStack
import concourse.bass as bass
import concourse.mybir as mybir
from concourse.bass_utils import run_bass_kernel_spmd

F32 = mybir.dt.float32
BF16 = mybir.dt.bfloat16
I32 = mybir.dt.int32
U32 = mybir.dt.uint32
AF = mybir.ActivationFunctionType
ALU = mybir.AluOpType
AX = mybir.AxisListType

import os
G_NT = int(os.environ.get("MK_GNT", "16"))
GSTOP = int(os.environ.get("MK_GSTOP", "99"))
NOS = int(os.environ.get("MK_NOS", "0"))
NOG = int(os.environ.get("MK_NOG", "0"))
ENGS = ["pe", "act", "dve", "pool", "sp"]
EPOCH = 12000
N_DMA_SEMS = 40
N_SW_SEMS = 12

D = 1024
T = 4096
HALF = 2048
NT = 16
NS = 16
INC = 3664
DFF = 2816
C_K, C_V, C_B, C_A, C_AK, C_AV, C_IK = 0, 512, 1024, 1028, 1032, 1288, 1544
N_OTH = 1608
C_Q, C_Z, C_AQ, C_IQ, C_IW = 1608, 2120, 2632, 3144, 3656
PERM = np.concatenate([np.arange(512, 1024), np.arange(1024, 1536), np.arange(2048, 2056),
                       np.arange(2568, 2824), np.arange(2824, 3080), np.arange(3592, 3656),
                       np.arange(0, 512), np.arange(1536, 2048), np.arange(2056, 2568),
                       np.arange(3080, 3592), np.arange(3656, 3664)])
GS_W = 2056
TABW = 256


class Prog:
    def __init__(self, nc):
        self.nc = nc
        self.ops = {e: [] for e in ENGS}
        self.res = {}
        self.seed = []
        self.dma_cum = [0] * (N_DMA_SEMS + N_SW_SEMS)
        self.dma_rr = 0
        self.sw_rr = 0

    def _deps(self, r, w):
        deps = []
        for name in r:
            st = self.res.get(name)
            if st and st[0] is not None:
                deps.append(st[0])
        for name in w:
            st = self.res.get(name)
            if st:
                if st[0] is not None:
                    deps.append(st[0])
                deps.extend(st[1])
            else:
                deps.extend(self.seed)
        return deps

    @staticmethod
    def _tkey(t):
        return (t[0], t[1])

    def _commit(self, tok, r, w):
        k = self._tkey(tok)
        for name in r:
            st = self.res.setdefault(name, [None, []])
            if not os.environ.get("MK_NOPRUNE"):
                st[1] = [t for t in st[1] if self._tkey(t) != k]
            st[1].append(tok)
        for name in w:
            self.res[name] = [tok, []]

    def op(self, eng, fn, r=(), w=()):
        deps = self._deps(r, w)
        tok = ("e", eng, len(self.ops[eng]))
        self.ops[eng].append({"deps": deps, "fn": fn, "dma": None, "sig": False})
        self._commit(tok, r, w)
        return tok

    def dma_raw(self, q, fn, r=(), w=(), sw=False):
        deps = self._deps(r, w)
        if sw:
            s = N_DMA_SEMS + self.sw_rr
            self.sw_rr = (self.sw_rr + 1) % N_SW_SEMS
        else:
            s = self.dma_rr
            self.dma_rr = (self.dma_rr + 1) % N_DMA_SEMS
        if self.dma_cum[s] > 0:
            deps.append(("d", s, self.dma_cum[s]))
        self.dma_cum[s] += 16
        tok = ("d", s, self.dma_cum[s])
        self.ops[q].append({"deps": deps, "fn": fn, "dma": (s, self.dma_cum[s]), "sig": False})
        self._commit(tok, r, w)
        return tok

    def dma(self, q, out, in_, r=(), w=(), **kw):
        return self.dma_raw(q, lambda e, out=out, in_=in_, kw=kw: e.dma_start(out=out, in_=in_, **kw), r, w)

    def seed_after_staging(self, names=("wst0", "wst1")):
        toks = list(self.seed)
        for n in names:
            st = self.res.get(n)
            if st:
                if st[0] is not None:
                    toks.append(st[0])
                toks.extend(st[1])
        self.seed = toks

    def barrier(self):
        deps_all = []
        for st in self.res.values():
            if st[0] is not None:
                deps_all.append(st[0])
            deps_all.extend(st[1])
        best = {}
        for t in ([] if os.environ.get("MK_NOPRUNE") else deps_all):
            k = self._tkey(t)
            if k not in best or best[k][2] < t[2]:
                best[k] = t
        for e in ENGS:
            for i in range(len(self.ops[e]) - 1, -1, -1):
                if self.ops[e][i]["dma"] is None:
                    best[("e", e)] = ("e", e, i)
                    break
        if not os.environ.get("MK_NOPRUNE"):
            deps_all = list(best.values())
        toks = []
        for e in ENGS:
            toks.append(("e", e, len(self.ops[e])))
            self.ops[e].append({"deps": list(deps_all), "fn": None, "dma": None, "sig": False})
        for e in ENGS:
            self.ops[e].append({"deps": list(toks), "fn": None, "dma": None, "sig": False})
        self.res = {}
        self.seed = []

    def build(self, ctx):
        nc = self.nc
        fin = [("d", s, c) for s, c in enumerate(self.dma_cum) if c > 0]
        self.ops["sp"].append({"deps": fin, "fn": None, "dma": None, "sig": False})
        for e in ENGS:
            for o in self.ops[e]:
                for d in o["deps"]:
                    if d[0] == "e":
                        if d[1] == "pe" and e == "pe":
                            continue
                        self.ops[d[1]][d[2]]["sig"] = True
        signo = {}
        nsig = {}
        for e in ENGS:
            c = 0
            for i, o in enumerate(self.ops[e]):
                if o["sig"]:
                    c += 1
                    signo[(e, i)] = c
            nsig[e] = c
        esem = {e: [ctx.enter_context(nc.semaphore(f"s_{e}_{k}")) for k in range(max(1, (nsig[e] + EPOCH - 1) // EPOCH))]
                for e in ENGS}
        dsem = [ctx.enter_context(nc.semaphore(f"s_dma_{k}")) for k in range(N_DMA_SEMS + N_SW_SEMS)]
        block = ctx.enter_context(nc.Block())
        eobj = {"pe": nc.tensor, "act": nc.scalar, "dve": nc.vector, "pool": nc.gpsimd, "sp": nc.sync}

        self.trace = {e: [] for e in ENGS}

        def body_for(e):
            def body(eng):
                known = {}
                tr = self.trace[e]
                for i, o in enumerate(self.ops[e]):
                    need = {}
                    for d in o["deps"]:
                        if d[0] == "e":
                            if d[1] == "pe" and e == "pe":
                                continue
                            key, val = ("e", d[1]), signo[(d[1], d[2])]
                        else:
                            key, val = ("d", d[1]), d[2]
                        if known.get(key, 0) >= val:
                            continue
                        if need.get(key, 0) < val:
                            need[key] = val
                    for key, val in need.items():
                        if key[0] == "e":
                            ep = (val - 1) // EPOCH
                            eng.wait_ge(esem[key[1]][ep], val - ep * EPOCH)
                            tr.append(("w", (key[1], ep), val - ep * EPOCH))
                        else:
                            eng.wait_ge(dsem[key[1]], val)
                            tr.append(("w", ("d", key[1]), val))
                        known[key] = val
                    if o["fn"] is None:
                        if o["sig"]:
                            sn = signo[(e, i)]
                            ep = (sn - 1) // EPOCH
                            eng.nop().then_inc(esem[e][ep], 1)
                            tr.append(("i", (e, ep), 1))
                        continue
                    ins = o["fn"](eng)
                    if o["dma"] is not None:
                        ins.then_inc(dsem[o["dma"][0]], 16)
                        tr.append(("i", ("d", o["dma"][0]), 16))
                    elif o["sig"]:
                        sn = signo[(e, i)]
                        ep = (sn - 1) // EPOCH
                        ins.then_inc(esem[e][ep], 1)
                        tr.append(("i", (e, ep), 1))
            return body

        block.tensor(body_for("pe"))
        block.scalar(body_for("act"))
        block.vector(body_for("dve"))
        block.gpsimd(body_for("pool"))
        block.sync(body_for("sp"))


class Arena:
    def __init__(self, t, n):
        self.t, self.n, self.off, self.uid = t, n, 0, 0

    def reset(self, to=0):
        self.off = to

    def f32(self, cols):
        a = self.t[:, self.off:self.off + cols]
        self.off += cols
        assert self.off <= self.n, ("arena overflow", self.off, self.n)
        return a

    def bf16(self, cols):
        c32 = (cols + 1) // 2
        a = self.t[:, self.off:self.off + c32].bitcast(BF16)
        self.off += c32
        assert self.off <= self.n, ("arena overflow", self.off, self.n)
        return a


def build_program(stage=99):
    nc = bass.Bass("TRN2", target_bir_lowering=False)
    dt_in = lambda name, shape, dt=F32: nc.dram_tensor(name, list(shape), dt, kind="ExternalInput").ap()
    dt_out = lambda name, shape, dt=F32: nc.dram_tensor(name, list(shape), dt, kind="ExternalOutput").ap()
    dt_scr = lambda name, shape, dt=F32: nc.dram_tensor(name, list(shape), dt, kind="Internal").ap()

    I = {}
    I["x_own"] = dt_in("x_own", [HALF, D]); I["x_oth"] = dt_in("x_oth", [HALF, D])
    I["cin"] = dt_in("cin", [17, D]); I["xs"] = dt_in("xs", [NS, D])
    I["w_ada"] = dt_in("w_ada", [D, 6 * D]); I["b_ada"] = dt_in("b_ada", [1, 6 * D])
    I["g1"] = dt_in("g1", [1, D]); I["w_in"] = dt_in("w_in", [D, INC])
    I["w_conv"] = dt_in("w_conv", [4, 1536]); I["a_log"] = dt_in("a_log", [1, 4]); I["dt_bias"] = dt_in("dt_bias", [1, 4])
    I["g_gdn"] = dt_in("g_gdn", [1, 128]); I["w_out"] = dt_in("w_out", [D, D]); I["g2"] = dt_in("g2", [1, D])
    I["w_ffn_in"] = dt_in("w_ffn_in", [D, 2 * DFF]); I["w_ffn_out"] = dt_in("w_ffn_out", [DFF, D]); I["g_final"] = dt_in("g_final", [1, D])
    I["tab_own"] = dt_in("tab_own", [HALF, TABW]); I["tab_oth"] = dt_in("tab_oth", [HALF, TABW]); I["tab_s"] = dt_in("tab_s", [NS, TABW])
    I["flags"] = dt_in("flags", [128, 4]); I["ident"] = dt_in("ident", [128, 128])
    I["state_conv"] = dt_in("state_conv", [NS, 3, 1536])
    I["gconst"] = dt_in("gconst", [128, 1024]); I["wc_p"] = dt_in("wc_p", [4, 1536])
    I["state_ssm"] = dt_in("state_ssm", [NS, 4, 128, 128]); I["eye16"] = dt_in("eye16", [128, 256])
    I["caus"] = dt_in("caus", [128, 128])
    I["pt"] = dt_in("pt", [1, NS * 16], I32); I["tsel"] = dt_in("tsel", [128, 257])
    NPHYS = 2560
    if not os.environ.get('MK_NOTS'):
        I["cache_ik"] = dt_in("cache_ik", [NPHYS * 128, 64]); I["cache_k"] = dt_in("cache_k", [NPHYS * 128, 256]); I["cache_v"] = dt_in("cache_v", [NPHYS * 128, 256])

    O = {}
    O["y_own"] = dt_out("y_own", [HALF, D]); O["k_own"] = dt_out("k_own", [HALF, 256]); O["v_own"] = dt_out("v_own", [HALF, 256])
    O["ik_own"] = dt_out("ik_own", [HALF, 64]); O["conv_tail"] = dt_out("conv_tail", [3, 1536]); O["ssm_fin"] = dt_out("ssm_fin", [4, 128, 128])
    O["y_s"] = dt_out("y_s", [NS, D]); O["k_s"] = dt_out("k_s", [NS, 256]); O["v_s"] = dt_out("v_s", [NS, 256]); O["ik_s"] = dt_out("ik_s", [NS, 64])
    O["conv_s"] = dt_out("conv_s", [NS, 3, 1536]); O["ssm_s"] = dt_out("ssm_s", [NS, 4, 128, 128])

    S = {}
    S["mod"] = dt_scr("mod_scr", [17, 6 * D])
    S["gs"] = dt_scr("gs_scr", [2, 3 + HALF, GS_W])
    S["qT"] = dt_scr("qT_scr", [NT, 128, 4, 128], BF16)
    S["iqT"] = dt_scr("iqT_scr", [NT, 128, 4, 128], BF16)
    S["cT"] = dt_scr("cT_scr", [NT + 1, 128, 8, 128], BF16)
    S["x1"] = dt_scr("x1_scr", [NT + 1, 128, D])
    S["uT"] = dt_scr("uT_scr", [5, 128, 22, 512], BF16)
    S["ps"] = dt_scr("ps_scr", [NS, INC])

    ctx = ExitStack()
    with ctx:
        ctx.enter_context(nc.allow_low_precision(reason="bf16 matmul operands, fp32 accumulate"))
        P = Prog(nc)
        ARN = 52800
        arena_t = ctx.enter_context(nc.sbuf_tensor("arena", [128, ARN], F32))
        A = Arena(arena_t, ARN)
        pb = [ctx.enter_context(nc.psum_tensor(f"pb{i}", [128, 512], F32))[:, :] for i in range(8)]
        PB = [f"pb{i}" for i in range(8)]
        cnt = [0]

        def alt():
            cnt[0] += 1
            return "act" if cnt[0] % 2 else "dve"

        def evac(eng, out, in_, r, w):
            if eng == "act":
                P.op("act", lambda e: e.activation(out=out, in_=in_, func=AF.Copy), r=r, w=w)
            else:
                P.op(eng, lambda e: e.tensor_copy(out=out, in_=in_), r=r, w=w)

        ident = A.f32(128)
        flags = A.f32(4)
        P.dma("sp", ident, I["ident"], w=["ident"])
        P.dma("sp", flags, I["flags"], w=["flags"])
        mod_sb = A.f32(6 * D)
        persist0 = A.off

        cs = A.f32(D)
        csT = A.f32(8 * 17)
        csT3 = csT.rearrange("p (k m) -> p k m", k=8)
        ones = A.f32(128)
        bstage = A.f32(512)
        P.op("dve", lambda e: e.memset(ones, 1.0), w=["ones"])
        P.dma("sp", cs[0:17, :], I["cin"], w=["cs"])
        P.op("act", lambda e: e.activation(out=cs[0:17, :], in_=cs[0:17, :], func=AF.Silu), r=["cs"], w=["cs"])
        for k in range(8):
            P.op("pe", lambda e, k=k: e.transpose(out=pb[0][:, k * 17:(k + 1) * 17], in_=cs[0:17, k * 128:(k + 1) * 128], identity=ident[0:17, 0:17]),
                 r=["cs", "ident"], w=[PB[0]])
        evac("dve", csT, pb[0][:, 0:8 * 17], [PB[0]], ["csT"])
        wst = [A.f32(8 * 512), A.f32(8 * 512)]
        for cb in range(12):
            ws = wst[cb % 2]
            ws3 = ws.rearrange("p (k n) -> p k n", k=8)
            P.dma("sp", ws3, I["w_ada"][:, cb * 512:(cb + 1) * 512].rearrange("(k p) n -> p k n", p=128), w=[f"wst{cb % 2}"])
            P.dma("sp", bstage[0:1, :], I["b_ada"][:, cb * 512:(cb + 1) * 512], w=["bstage"])
            bank = 1 + cb % 2
            for k in range(8):
                P.op("pe", lambda e, k=k, ws3=ws3, bank=bank: e.matmul(out=pb[bank][0:17, :], lhsT=csT3[:, k, :], rhs=ws3[:, k, :], start=(k == 0), stop=False),
                     r=["csT", f"wst{cb % 2}"], w=[PB[bank]])
            P.op("pe", lambda e, bank=bank: e.matmul(out=pb[bank][0:17, :], lhsT=ones[0:1, 0:17], rhs=bstage[0:1, :], start=False, stop=True),
                 r=["ones", "bstage"], w=[PB[bank]])
            evac("act", mod_sb[0:17, cb * 512:(cb + 1) * 512], pb[bank][0:17, :], [PB[bank]], ["mod_sb"])
        P.dma("sp", S["mod"], mod_sb[0:17, :], r=["mod_sb"], w=["mod_scr"])
        A.reset(persist0)
        def alloc_att():
            KT = A.bf16(2 * T); ikT = A.bf16(T); VA = A.bf16(32 * 2 * 130); iwabs = A.f32(NT * 8); iwsgn = A.f32(NT * 8)
            ksq_ = A.f32(64); qsq_ = A.f32(64)
            return KT, ikT, VA, iwabs, iwsgn, ksq_, qsq_
        OLDALLOC = bool(os.environ.get("MK_OLDALLOC"))
        if not OLDALLOC:
            KT, ikT, VA, iwabs, iwsgn, ksq, qsq = alloc_att()
        persist1 = A.off
        a1_bc = A.f32(D); sh1_bc = A.f32(D)
        g1_bc = A.f32(D); a1_s = A.f32(D)
        P.dma("sp", g1_bc, I["g1"].to_broadcast([128, D]), w=["g1_bc"])
        P.dma("sp", a1_bc, S["mod"][16:17, D:2 * D].to_broadcast([128, D]), r=["mod_scr"], w=["a1_bc"])
        P.dma("sp", sh1_bc, S["mod"][16:17, 0:D].to_broadcast([128, D]), r=["mod_scr"], w=["sh1_bc"])
        P.op("dve", lambda e: e.scalar_tensor_tensor(out=a1_bc, in0=a1_bc, scalar=1.0, in1=g1_bc, op0=ALU.add, op1=ALU.mult),
             r=["a1_bc", "g1_bc"], w=["a1_bc"])
        P.op("dve", lambda e: e.scalar_tensor_tensor(out=a1_s[0:NS, :], in0=mod_sb[0:NS, D:2 * D], scalar=1.0, in1=g1_bc[0:NS, :], op0=ALU.add, op1=ALU.mult),
             r=["mod_sb", "g1_bc"], w=["a1_s"])
        persistA = A.off

        w_in_b = A.bf16(8 * INC)
        w_in3 = w_in_b.rearrange("p (k n) -> p k n", k=8)
        if OLDALLOC:
            KT, ikT, VA, iwabs, iwsgn, ksq, qsq = alloc_att()
        KT3 = KT.rearrange("p (g s) -> p g s", g=2)
        VA4 = VA.rearrange("p (t g d) -> p t g d", t=32, g=2)
        persistB = A.off
        wst = [A.f32(8 * 512), A.f32(8 * 512)]
        nblk = (INC + 511) // 512
        for cb in range(nblk):
            c0, c1 = cb * 512, min(INC, cb * 512 + 512)
            ws3 = wst[cb % 2].rearrange("p (k n) -> p k n", k=8)
            P.dma("sp", ws3[:, :, 0:c1 - c0], I["w_in"][:, c0:c1].rearrange("(k p) n -> p k n", p=128), w=[f"wst{cb % 2}"])
            eng = "pool" if cb % 2 else "dve"
            P.op(eng, lambda e, ws3=ws3, c0=c0, c1=c1: e.tensor_copy(out=w_in3[:, :, c0:c1], in_=ws3[:, :, 0:c1 - c0]),
                 r=[f"wst{cb % 2}"], w=["w_in_b"])
        P.seed_after_staging()
        A.reset(persistB)
        xt = [A.f32(D), A.f32(D)]
        hh = A.f32(D)
        sq = A.f32(D)
        hT = [A.bf16(8 * 128), A.bf16(8 * 128)]
        Psb_l = [A.f32(INC), A.f32(INC)]
        tab = [A.f32(TABW), A.f32(TABW)]
        small = A.f32(16)
        rt = [A.f32(128) for _ in range(4)]
        ikd = A.f32(128)
        tstage = [A.bf16(4 * 128), A.bf16(4 * 128)]
        zrow = A.f32(GS_W)
        P.op("pool", lambda e: e.memset(zrow[0:3, :], 0.0), w=["zrow"])
        P.dma("sp", S["gs"][0, 0:3, :], zrow[0:3, :], r=["zrow"], w=["gs_pre0"])

        def rope(Psb, PN, rows, base, H, Dh, half, tb, coff, soff, tname):
            xv = Psb[0:rows, base:base + H * Dh].rearrange("p (h d) -> p h d", d=Dh)
            x1, x2 = xv[:, :, 0:half], xv[:, :, half:2 * half]
            cosv = tb[0:rows, coff:coff + H * half].rearrange("p (h i) -> p h i", i=half)
            sinv = tb[0:rows, soff:soff + H * half].rearrange("p (h i) -> p h i", i=half)
            t = [r_[0:rows, 0:H * half].rearrange("p (h i) -> p h i", i=half) for r_ in rt]
            P.op("dve", lambda e: e.tensor_tensor(out=t[0], in0=x1, in1=cosv, op=ALU.mult), r=[PN, tname], w=["rt0"])
            P.op("pool", lambda e: e.tensor_tensor(out=t[1], in0=x2, in1=sinv, op=ALU.mult), r=[PN, tname], w=["rt1"])
            P.op("dve", lambda e: e.tensor_tensor(out=t[2], in0=x2, in1=cosv, op=ALU.mult), r=[PN, tname], w=["rt2"])
            P.op("pool", lambda e: e.tensor_tensor(out=t[3], in0=x1, in1=sinv, op=ALU.mult), r=[PN, tname], w=["rt3"])
            P.op("dve", lambda e: e.tensor_tensor(out=x1, in0=t[0], in1=t[1], op=ALU.subtract), r=["rt0", "rt1", PN], w=[PN])
            P.op("dve", lambda e: e.tensor_tensor(out=x2, in0=t[2], in1=t[3], op=ALU.add), r=["rt2", "rt3", PN], w=[PN])

        itc = [0]

        def proj_tile(mode, ti):
            it = itc[0]; itc[0] += 1
            rows = NS if mode == 2 else 128
            Psb = Psb_l[it % 2]; PN = f"Psb{it % 2}"
            xb = xt[it % 2]; xn = f"xt{it % 2}"
            tb = tab[it % 2]; tn = f"tab{it % 2}"
            hTb = hT[it % 2]; hTn = f"hT{it % 2}"
            hT3 = hTb.rearrange("p (k t) -> p k t", k=8)
            if mode == 2:
                xsrc, tsrc = I["xs"], I["tab_s"]
                a_t, a_n, sh_t, sh_n = a1_s, "a1_s", mod_sb[:, 0:D], "mod_sb"
                ncols = INC
            else:
                xsrc = (I["x_oth"] if mode == 0 else I["x_own"])[ti * 128:(ti + 1) * 128, :]
                tsrc = (I["tab_oth"] if mode == 0 else I["tab_own"])[ti * 128:(ti + 1) * 128, :]
                a_t, a_n, sh_t, sh_n = a1_bc, "a1_bc", sh1_bc, "sh1_bc"
                ncols = INC if mode == 1 else (C_Q + 512 if ti == NT - 1 else N_OTH)
            P.dma("sp", xb[0:rows, :], xsrc, w=[xn])
            P.dma("sp", tb[0:rows, :], tsrc, w=[tn])
            ss, rs = small[0:rows, 0:1], small[0:rows, 1:2]
            P.op("act", lambda e: e.activation(out=sq[0:rows, :], in_=xb[0:rows, :], func=AF.Square, accum_out=ss), r=[xn], w=["sq", "ss"])
            P.op("dve", lambda e: e.tensor_scalar(out=rs, in0=ss, scalar1=1.0 / D, scalar2=1e-6, op0=ALU.mult, op1=ALU.add), r=["ss"], w=["rs"])
            P.op("act", lambda e: e.activation(out=rs, in_=rs, func=AF.Sqrt), r=["rs"], w=["rs"])
            P.op("dve", lambda e: e.reciprocal(out=rs, in_=rs), r=["rs"], w=["rs"])
            P.op("dve", lambda e: e.scalar_tensor_tensor(out=hh[0:rows, :], in0=xb[0:rows, :], scalar=rs, in1=a_t[0:rows, :], op0=ALU.mult, op1=ALU.mult),
                 r=[xn, "rs", a_n], w=["hh"])
            P.op("pool", lambda e: e.tensor_tensor(out=hh[0:rows, :], in0=hh[0:rows, :], in1=sh_t[0:rows, :], op=ALU.add), r=["hh", sh_n], w=["hh"])
            for k in range(8):
                bank = k // 4
                P.op("pe", lambda e, k=k, bank=bank: e.transpose(out=pb[bank][:, (k % 4) * 128:(k % 4) * 128 + rows], in_=hh[0:rows, k * 128:(k + 1) * 128], identity=ident[0:rows, 0:rows]),
                     r=["hh", "ident"], w=[PB[bank]])
            if rows == 128:
                evac("act", hTb[:, 0:512], pb[0], [PB[0]], [hTn])
                evac("dve", hTb[:, 512:1024], pb[1], [PB[1]], [hTn])
            else:
                for half_ in range(2):
                    evac("act" if half_ == 0 else "dve", hT3[:, half_ * 4:(half_ + 1) * 4, 0:rows], pb[half_].rearrange("p (k t) -> p k t", k=4)[:, :, 0:rows], [PB[half_]], [hTn])
            nb = (ncols + 511) // 512
            for cb in range(nb):
                c0, c1 = cb * 512, min(ncols, cb * 512 + 512)
                bank = 2 + cb % 4
                for k in range(8):
                    P.op("pe", lambda e, k=k, bank=bank, c0=c0, c1=c1: e.matmul(out=pb[bank][0:rows, 0:c1 - c0], lhsT=hT3[:, k, 0:rows], rhs=w_in3[:, k, c0:c1], start=(k == 0), stop=(k == 7)),
                         r=[hTn, "w_in_b"], w=[PB[bank]])
                evac(alt(), Psb[0:rows, c0:c1], pb[bank][0:rows, 0:c1 - c0], [PB[bank]], [PN])
            def stage2():
                if mode == 2:
                    cvs = sq[0:rows, :]
                    for j in range(2):
                        for hh_ in range(2):
                            P.dma("sp", sq[0:rows, 0:768], I["state_conv"][:, 1 + j, hh_ * 768:(hh_ + 1) * 768], w=["sq"])
                            P.dma("sp", O["conv_s"][:, j, hh_ * 768:(hh_ + 1) * 768], sq[0:rows, 0:768], r=["sq"])
                    P.dma("sp", O["conv_s"][:, 2, 0:512], Psb[0:rows, C_Q:C_Q + 512], r=[PN])
                    P.dma("sp", O["conv_s"][:, 2, 512:1536], Psb[0:rows, 0:1024], r=[PN])
                else:
                    row0 = 3 + ti * 128
                    P.dma("sp", S["gs"][mode, row0:row0 + 128, 0:1032], Psb[:, 0:1032], r=[PN], w=[f"gs{mode}_{ti}"])
                    if mode == 1:
                        P.dma("sp", S["gs"][mode, row0:row0 + 128, 1032:2056], Psb[:, C_Q:C_Q + 1024], r=[PN], w=[f"gs{mode}_{ti}"])
                    elif ti == NT - 1:
                        P.dma("sp", S["gs"][mode, row0:row0 + 128, 1032:1544], Psb[:, C_Q:C_Q + 512], r=[PN], w=[f"gs{mode}_{ti}"])
                rope(Psb, PN, rows, C_AK, 2, 128, 16, tb, 0, 64, tn)
                rope(Psb, PN, rows, C_IK, 1, 64, 8, tb, 128, 192, tn)
                if mode >= 1:
                    rope(Psb, PN, rows, C_AQ, 4, 128, 16, tb, 0, 64, tn)
                    rope(Psb, PN, rows, C_IQ, 8, 64, 8, tb, 128, 192, tn)
                if mode == 2:
                    P.dma("sp", O["k_s"], Psb[0:rows, C_AK:C_AK + 256], r=[PN])
                    P.dma("sp", O["v_s"], Psb[0:rows, C_AV:C_AV + 256], r=[PN])
                    P.dma("sp", O["ik_s"], Psb[0:rows, C_IK:C_IK + 64], r=[PN])
                    P.dma("sp", S["ps"], Psb[0:rows, :], r=[PN], w=["ps_scr"])
                    return
                if mode == 1:
                    P.dma("sp", O["k_own"][ti * 128:(ti + 1) * 128, :], Psb[:, C_AK:C_AK + 256], r=[PN])
                    P.dma("sp", O["v_own"][ti * 128:(ti + 1) * 128, :], Psb[:, C_AV:C_AV + 256], r=[PN])
                    P.dma("sp", O["ik_own"][ti * 128:(ti + 1) * 128, :], Psb[:, C_IK:C_IK + 64], r=[PN])
                slot = (0 if mode == 0 else 16) + ti
                for g in range(2):
                    P.op("act", lambda e, g=g: e.activation(out=sq[:, 0:128], in_=Psb[:, C_AK + g * 128:C_AK + (g + 1) * 128], func=AF.Square, accum_out=ksq[:, slot * 2 + g:slot * 2 + g + 1]),
                         r=[PN, "a1_bc"], w=["sq", "ksq"])
                if mode == 1:
                    for h in range(4):
                        P.op("act", lambda e, h=h: e.activation(out=sq[:, 0:128], in_=Psb[:, C_AQ + h * 128:C_AQ + (h + 1) * 128], func=AF.Square, accum_out=qsq[:, ti * 4 + h:ti * 4 + h + 1]),
                             r=[PN, "a1_bc"], w=["sq", "qsq"])
                P.op("dve", lambda e: e.tensor_copy(out=ikd[:, 0:64], in_=Psb[:, C_IK:C_IK + 64]), r=[PN], w=["ikd"])
                P.op("pool", lambda e: e.tensor_copy(out=ikd[:, 64:128], in_=Psb[:, C_IK:C_IK + 64]), r=[PN], w=["ikd"])
                for g in range(2):
                    P.op("pe", lambda e, g=g: e.transpose(out=pb[6][:, g * 128:(g + 1) * 128], in_=Psb[:, C_AK + g * 128:C_AK + (g + 1) * 128], identity=ident),
                         r=[PN, "ident"], w=[PB[6]])
                P.op("pe", lambda e: e.transpose(out=pb[6][:, 256:384], in_=ikd, identity=ident), r=["ikd", "ident"], w=[PB[6]])
                for g in range(2):
                    evac(alt(), KT3[:, g, slot * 128:(slot + 1) * 128], pb[6][:, g * 128:(g + 1) * 128], [PB[6]], ["KT"])
                evac(alt(), ikT[:, slot * 128:(slot + 1) * 128], pb[6][:, 256:384], [PB[6]], ["ikT"])
                P.op("pool", lambda e: e.memset(VA4[:, slot, :, 128:130], 1.0), r=["a1_bc"], w=["VA"])
                P.op("act", lambda e: e.activation(out=VA4[:, slot, :, 0:128], in_=Psb[:, C_AV:C_AV + 256].rearrange("p (g d) -> p g d", g=2), func=AF.Copy),
                     r=[PN], w=["VA"])
                if mode == 1:
                    ts_ = tstage[0]; tsn = "tstage0"
                    for h in range(4):
                        P.op("pe", lambda e, h=h: e.transpose(out=pb[7][:, h * 128:(h + 1) * 128], in_=Psb[:, C_AQ + h * 128:C_AQ + (h + 1) * 128], identity=ident),
                             r=[PN, "ident"], w=[PB[7]])
                    evac(alt(), ts_, pb[7], [PB[7]], [tsn])
                    P.dma("sp", S["qT"][ti], ts_.rearrange("p (h t) -> p h t", h=4), r=[tsn], w=[f"qT{ti}"])
                    ts2 = tstage[1]; tsn2 = "tstage1"
                    for h in range(4):
                        P.op("pe", lambda e, h=h: e.transpose(out=pb[7][:, h * 128:(h + 1) * 128], in_=Psb[:, C_IQ + h * 128:C_IQ + (h + 1) * 128], identity=ident),
                             r=[PN, "ident"], w=[PB[7]])
                    evac(alt(), ts2, pb[7], [PB[7]], [tsn2])
                    P.dma("sp", S["iqT"][ti], ts2.rearrange("p (h t) -> p h t", h=4), r=[tsn2], w=[f"iqT{ti}"])
                    P.op("act", lambda e: e.activation(out=iwabs[:, ti * 8:(ti + 1) * 8], in_=Psb[:, C_IW:C_IW + 8], func=AF.Abs, scale=8 ** -0.5),
                         r=[PN], w=["iwabs"])
                    P.op("act", lambda e: e.activation(out=iwsgn[:, ti * 8:(ti + 1) * 8], in_=Psb[:, C_IW:C_IW + 8], func=AF.Sign), r=[PN], w=["iwsgn"])
                    if ti == NT - 1:
                        P.dma("sp", O["conv_tail"][:, 0:512], S["gs"][1, 3 + HALF - 3:3 + HALF, 1032:1544], r=[f"gs1_{ti}"])
                        P.dma("sp", O["conv_tail"][:, 512:1536], S["gs"][1, 3 + HALF - 3:3 + HALF, 0:1024], r=[f"gs1_{ti}"])

            return stage2

        NA0 = int(os.environ.get("MK_NA0", NT)); NA1 = int(os.environ.get("MK_NA1", NT))
        tiles_a = [(0, ti) for ti in range(NT - NA0, NT)] + [(1, ti) for ti in range(NA1)] + ([] if NOS else [(2, 0)])
        pend = None
        for (m_, ti) in tiles_a:
            nxt = proj_tile(m_, ti)
            if pend is not None:
                pend()
            pend = nxt
        if pend is not None:
            pend()

        if not NOG:
            P.barrier()
            A.reset(persist1)
            cst = A.f32(1024)
            TRIU, ONESM, MASKL, MASKU = [cst[:, i * 128:(i + 1) * 128] for i in range(4)]
            ident4 = cst[:, 512:1024]
            P.dma("sp", cst, I["gconst"], w=["cst"])
            wc = [A.f32(1536) for _ in range(4)]
            for i in range(4):
                P.dma("sp", wc[i], I["wc_p"][i:i + 1, :].to_broadcast([128, 1536]), w=[f"wc{i}"])
            dtb = A.f32(4); negA = A.f32(4); ggd = A.f32(512)
            P.dma("sp", dtb, I["dt_bias"].to_broadcast([128, 4]), w=["dtb"])
            P.dma("sp", negA, I["a_log"].to_broadcast([128, 4]), w=["negA"])
            for h in range(4):
                P.dma("sp", ggd[:, h * 128:(h + 1) * 128], I["g_gdn"].to_broadcast([128, 128]), w=["ggd"])
            P.op("act", lambda e: e.activation(out=negA, in_=negA, func=AF.Exp), r=["negA"], w=["negA"])
            P.op("dve", lambda e: e.tensor_scalar(out=negA, in0=negA, scalar1=-1.0, scalar2=None, op0=ALU.mult), r=["negA"], w=["negA"])
            gs_base = A.off
            Sst = A.f32(512)
            P.op("dve", lambda e: e.memset(Sst, 0.0), w=["S"])
            X = [A.f32(GS_W) for _ in range(4)]
            cv = A.f32(1536); tmpa = A.f32(1536); tmpb = A.f32(1536)
            sm = A.f32(64)
            kn = A.f32(512); kt = A.f32(512); vb = A.f32(512); qn = A.f32(512); qt = A.f32(512)
            knT = A.f32(512); qnT = A.f32(512); qtT = A.f32(512)
            Dg = A.f32(512); dec = A.f32(512); decT = A.f32(512)
            Pbuf = [A.f32(512), A.f32(512)]; PTbuf = [A.f32(512), A.f32(512)]
            Wm = A.f32(512); ATm = A.f32(512); Rm = A.f32(512); vnew = A.f32(512); o_sb = A.f32(512); og = A.f32(512); szb = A.f32(512)
            cTst = A.bf16(512)
            pre = A.f32(GS_W)
            H4 = lambda ap, h: ap[:, h * 128:(h + 1) * 128]
            sc = lambda lo, h: sm[:, lo + h:lo + h + 1]

            def ts_mul(eng, out, in0, scal, r, w):
                if eng == "pool":
                    P.op("pool", lambda e: e.tensor_scalar(out=out, in0=in0, scalar1=scal, scalar2=1.0, op0=ALU.mult, op1=ALU.mult), r=r, w=w)
                else:
                    P.op("dve", lambda e: e.tensor_scalar(out=out, in0=in0, scalar1=scal, scalar2=None, op0=ALU.mult), r=r, w=w)

            for hf in range(2):
                own = (hf == 1)
                if own:
                    P.op("dve", lambda e: e.tensor_scalar(out=Sst, in0=Sst, scalar1=flags[:, 0:1], scalar2=None, op0=ALU.mult), r=["S", "flags"], w=["S"])
                    P.op("dve", lambda e: e.memset(pre[0:3, :], 0.0), w=["pre"])
                    P.dma("sp", pre[0:3, 0:1544], S["gs"][0, HALF:HALF + 3, 0:1544], w=["pre"])
                    P.op("dve", lambda e: e.tensor_scalar(out=pre[0:3, :], in0=pre[0:3, :], scalar1=flags[0:3, 0:1], scalar2=None, op0=ALU.mult), r=["pre", "flags"], w=["pre"])
                    P.dma("sp", S["gs"][1, 0:3, :], pre[0:3, :], r=["pre"], w=["gs_pre1"])
                segs = [(0, 1024, 0), (1024, 1536, 1032)] if own else [(0, 1024, 0)]
                nsc = 8 if own else 4
                for ti in range(G_NT):
                    W_ = GS_W if own else 1032
                    for i in range(4):
                        P.dma("sp", X[i][:, 0:W_], S["gs"][hf, ti * 128 + i:ti * 128 + i + 128, 0:W_], r=(["gs_pre1"] if (own and ti == 0) else []), w=[f"X{i}"])
                    for (d0, d1, s0) in segs:
                        n = d1 - d0
                        P.op("dve", lambda e, d0=d0, d1=d1, s0=s0, n=n: e.tensor_tensor(out=cv[:, d0:d1], in0=X[0][:, s0:s0 + n], in1=wc[0][:, d0:d1], op=ALU.mult), r=["X0", "wc0"], w=["cv"])
                        for i in range(1, 4):
                            tb_, tbn = (tmpa, "tmpa") if i % 2 else (tmpb, "tmpb")
                            P.op("pool", lambda e, i=i, d0=d0, d1=d1, s0=s0, n=n, tb_=tb_: e.tensor_tensor(out=tb_[:, d0:d1], in0=X[i][:, s0:s0 + n], in1=wc[i][:, d0:d1], op=ALU.mult), r=[f"X{i}", f"wc{i}"], w=[tbn])
                            P.op("dve", lambda e, d0=d0, d1=d1, tb_=tb_: e.tensor_tensor(out=cv[:, d0:d1], in0=cv[:, d0:d1], in1=tb_[:, d0:d1], op=ALU.add), r=["cv", tbn], w=["cv"])
                        P.op("act", lambda e, d0=d0, d1=d1: e.activation(out=cv[:, d0:d1], in_=cv[:, d0:d1], func=AF.Silu), r=["cv"], w=["cv"])
                    if GSTOP <= 1:
                        continue
                    P.op("pool", lambda e: e.tensor_tensor(out=tmpa[:, 0:512], in0=cv[:, 0:512], in1=cv[:, 0:512], op=ALU.mult), r=["cv"], w=["tmpa"])
                    P.op("dve", lambda e: e.tensor_reduce(out=sm[:, 0:4], in_=tmpa[:, 0:512].rearrange("p (h d) -> p h d", h=4), axis=AX.X, op=ALU.add), r=["tmpa"], w=["sm_ss"])
                    if own:
                        P.op("pool", lambda e: e.tensor_tensor(out=tmpb[:, 0:512], in0=cv[:, 1024:1536], in1=cv[:, 1024:1536], op=ALU.mult), r=["cv"], w=["tmpb"])
                        P.op("dve", lambda e: e.tensor_reduce(out=sm[:, 4:8], in_=tmpb[:, 0:512].rearrange("p (h d) -> p h d", h=4), axis=AX.X, op=ALU.add), r=["tmpb"], w=["sm_ss"])
                    P.op("dve", lambda e, nsc=nsc: e.tensor_scalar(out=sm[:, 0:nsc], in0=sm[:, 0:nsc], scalar1=1e-6, scalar2=None, op0=ALU.add), r=["sm_ss"], w=["sm_ss"])
                    P.op("act", lambda e, nsc=nsc: e.activation(out=sm[:, 0:nsc], in_=sm[:, 0:nsc], func=AF.Sqrt), r=["sm_ss"], w=["sm_ss"])
                    P.op("dve", lambda e, nsc=nsc: e.reciprocal(out=sm[:, 0:nsc], in_=sm[:, 0:nsc]), r=["sm_ss"], w=["sm_ss"])
                    if own:
                        P.op("dve", lambda e: e.tensor_scalar(out=sm[:, 4:8], in0=sm[:, 4:8], scalar1=128 ** -0.5, scalar2=None, op0=ALU.mult), r=["sm_ss"], w=["sm_ss"])
                    P.op("act", lambda e: e.activation(out=sm[:, 8:12], in_=X[3][:, 1024:1028], func=AF.Sigmoid), r=["X3"], w=["sm_b"])
                    P.op("dve", lambda e: e.tensor_tensor(out=sm[:, 12:16], in0=X[3][:, 1028:1032], in1=dtb, op=ALU.add), r=["X3", "dtb"], w=["sm_g"])
                    P.op("act", lambda e: e.activation(out=sm[:, 12:16], in_=sm[:, 12:16], func=AF.Exp), r=["sm_g"], w=["sm_g"])
                    P.op("act", lambda e: e.activation(out=sm[:, 12:16], in_=sm[:, 12:16], func=AF.Ln, bias=1.0), r=["sm_g"], w=["sm_g"])
                    P.op("dve", lambda e: e.tensor_tensor(out=sm[:, 12:16], in0=sm[:, 12:16], in1=negA, op=ALU.mult), r=["sm_g", "negA"], w=["sm_g"])
                    P.op("pe", lambda e: e.matmul(out=pb[3][:, 0:4], lhsT=TRIU, rhs=sm[:, 12:16], start=True, stop=True), r=["cst", "sm_g"], w=[PB[3]])
                    P.op("pe", lambda e: e.matmul(out=pb[3][:, 4:8], lhsT=ONESM, rhs=sm[:, 12:16], start=True, stop=True), r=["cst", "sm_g"], w=[PB[3]])
                    P.op("dve", lambda e: e.tensor_copy(out=sm[:, 16:24], in_=pb[3][:, 0:8]), r=[PB[3]], w=["sm_gc"])
                    P.op("dve", lambda e: e.tensor_copy(out=sm[:, 24:28], in_=sm[:, 16:20]), r=["sm_gc"], w=["sm_e"])
                    P.op("dve", lambda e: e.tensor_tensor(out=sm[:, 28:32], in0=sm[:, 20:24], in1=sm[:, 16:20], op=ALU.subtract), r=["sm_gc"], w=["sm_e"])
                    P.op("dve", lambda e: e.tensor_copy(out=sm[:, 32:36], in_=sm[:, 20:24]), r=["sm_gc"], w=["sm_e"])
                    P.op("act", lambda e: e.activation(out=sm[:, 24:36], in_=sm[:, 24:36], func=AF.Exp), r=["sm_e"], w=["sm_e"])
                    P.op("dve", lambda e: e.scalar_tensor_tensor(out=sm[:, 36:40], in0=sm[:, 8:12], scalar=-1.0, in1=sm[:, 24:28], op0=ALU.mult, op1=ALU.mult), r=["sm_b", "sm_e"], w=["sm_x"])
                    P.op("dve", lambda e: e.tensor_scalar(out=sm[:, 40:44], in0=sm[:, 16:20], scalar1=-1.0, scalar2=None, op0=ALU.mult), r=["sm_gc"], w=["sm_x"])
                    P.op("dve", lambda e: e.tensor_scalar(out=sm[:, 44:48], in0=sm[:, 8:12], scalar1=-1.0, scalar2=None, op0=ALU.mult), r=["sm_b"], w=["sm_x"])
                    if own:
                        P.op("dve", lambda e: e.tensor_tensor(out=sm[:, 48:52], in0=sm[:, 4:8], in1=sm[:, 24:28], op=ALU.mult), r=["sm_ss", "sm_e"], w=["sm_x"])
                    if GSTOP <= 2:
                        continue
                    for h in range(4):
                        ts_mul("dve", H4(kn, h), cv[:, h * 128:(h + 1) * 128], sc(0, h), ["cv", "sm_ss"], ["kn"])
                        ts_mul("pool", H4(kt, h), H4(kn, h), sc(28, h), ["kn", "sm_e"], ["kt"])
                        ts_mul("pool", H4(vb, h), cv[:, 512 + h * 128:512 + (h + 1) * 128], sc(8, h), ["cv", "sm_b"], ["vb"])
                        if own:
                            ts_mul("dve", H4(qn, h), cv[:, 1024 + h * 128:1024 + (h + 1) * 128], sc(4, h), ["cv", "sm_ss"], ["qn"])
                            ts_mul("pool", H4(qt, h), cv[:, 1024 + h * 128:1024 + (h + 1) * 128], sc(48, h), ["cv", "sm_x"], ["qt"])
                    for (src, sn, bank, dst, dn, eng) in ([(kn, "kn", 0, knT, "knT", "act")] + ([(qn, "qn", 1, qnT, "qnT", "dve"), (qt, "qt", 2, qtT, "qtT", "act")] if own else [])):
                        for h in range(4):
                            P.op("pe", lambda e, h=h, src=src, bank=bank: e.transpose(out=H4(pb[bank], h), in_=H4(src, h), identity=ident), r=[sn, "ident"], w=[PB[bank]])
                        evac(eng, dst, pb[bank], [PB[bank]], [dn])
                    if GSTOP <= 3:
                        continue
                    for h in range(4):
                        P.op("pe", lambda e, h=h: e.matmul(out=H4(pb[0], h), lhsT=H4(knT, h), rhs=H4(knT, h), start=True, stop=True), r=["knT"], w=[PB[0]])
                        P.op("pool", lambda e, h=h: e.tensor_scalar(out=H4(Dg, h), in0=ident, scalar1=sc(16, h), scalar2=1.0, op0=ALU.mult, op1=ALU.mult), r=["ident", "sm_gc"], w=["Dg"])
                    for h in range(4):
                        P.op("pe", lambda e, h=h: e.matmul(out=H4(pb[1], h), lhsT=ONESM, rhs=H4(Dg, h), start=True, stop=False), r=["cst", "Dg"], w=[PB[1]])
                        P.op("pe", lambda e, h=h: e.matmul(out=H4(pb[1], h), lhsT=ident, rhs=MASKL, start=False, stop=True), r=["cst", "ident"], w=[PB[1]])
                        P.op("act", lambda e, h=h: e.activation(out=H4(dec, h), in_=H4(pb[1], h), func=AF.Exp, scale=-1.0, bias=sc(16, h)), r=[PB[1], "sm_gc"], w=["dec"])
                        P.op("dve", lambda e, h=h: e.scalar_tensor_tensor(out=H4(Pbuf[0], h), in0=H4(pb[0], h), scalar=sc(44, h), in1=H4(dec, h), op0=ALU.mult, op1=ALU.mult),
                             r=[PB[0], "sm_x", "dec"], w=["P0"])
                    if own:
                        for h in range(4):
                            P.op("pe", lambda e, h=h: e.matmul(out=H4(pb[2], h), lhsT=ONESM, rhs=H4(Dg, h), start=True, stop=False), r=["cst", "Dg"], w=[PB[2]])
                            P.op("pe", lambda e, h=h: e.matmul(out=H4(pb[2], h), lhsT=ident, rhs=MASKU, start=False, stop=True), r=["cst", "ident"], w=[PB[2]])
                            P.op("act", lambda e, h=h: e.activation(out=H4(decT, h), in_=H4(pb[2], h), func=AF.Exp, scale=1.0, bias=sc(40, h)), r=[PB[2], "sm_x"], w=["decT"])
                            P.op("pe", lambda e, h=h: e.matmul(out=H4(pb[3], h), lhsT=H4(knT, h), rhs=H4(qnT, h), start=True, stop=True), r=["knT", "qnT"], w=[PB[3]])
                            P.op("dve", lambda e, h=h: e.tensor_tensor(out=H4(ATm, h), in0=H4(pb[3], h), in1=H4(decT, h), op=ALU.mult), r=[PB[3], "decT"], w=["ATm"])
                    if GSTOP <= 4:
                        continue
                    for h in range(4):
                        P.op("pe", lambda e, h=h: e.transpose(out=H4(pb[4], h), in_=H4(Pbuf[0], h), identity=ident), r=["P0", "ident"], w=[PB[4]])
                    evac("act", PTbuf[0], pb[4], [PB[4]], ["PT0"])
                    P.op("dve", lambda e: e.tensor_tensor(out=Wm, in0=PTbuf[0], in1=ident4, op=ALU.add), r=["PT0", "cst"], w=["Wm"])
                    for l in range(1, 7):
                        pc, ptc, pn_, ptn = Pbuf[(l - 1) % 2], PTbuf[(l - 1) % 2], Pbuf[l % 2], PTbuf[l % 2]
                        pcn, ptcn, pnn, ptnn = f"P{(l - 1) % 2}", f"PT{(l - 1) % 2}", f"P{l % 2}", f"PT{l % 2}"
                        for h in range(4):
                            P.op("pe", lambda e, h=h, pc=pc, ptc=ptc: e.matmul(out=H4(pb[4], h), lhsT=H4(ptc, h), rhs=H4(pc, h), start=True, stop=True), r=[pcn, ptcn], w=[PB[4]])
                        if l < 6:
                            for h in range(4):
                                P.op("pe", lambda e, h=h, pc=pc, ptc=ptc: e.matmul(out=H4(pb[5], h), lhsT=H4(pc, h), rhs=H4(ptc, h), start=True, stop=True), r=[pcn, ptcn], w=[PB[5]])
                        evac("act", pn_, pb[4], [PB[4]], [pnn])
                        if l < 6:
                            evac("dve", ptn, pb[5], [PB[5]], [ptnn])
                        for h in range(4):
                            P.op("pe", lambda e, h=h, pn_=pn_: e.matmul(out=H4(pb[6], h), lhsT=H4(pn_, h), rhs=H4(Wm, h), start=True, stop=True), r=[pnn, "Wm"], w=[PB[6]])
                        P.op("dve", lambda e: e.tensor_tensor(out=Wm, in0=Wm, in1=pb[6], op=ALU.add), r=["Wm", PB[6]], w=["Wm"])
                    if GSTOP <= 5:
                        continue
                    for h in range(4):
                        P.op("pe", lambda e, h=h: e.matmul(out=H4(pb[7], h), lhsT=H4(knT, h), rhs=H4(Sst, h), start=True, stop=True), r=["knT", "S"], w=[PB[7]])
                    for h in range(4):
                        P.op("dve", lambda e, h=h: e.scalar_tensor_tensor(out=H4(Rm, h), in0=H4(pb[7], h), scalar=sc(36, h), in1=H4(vb, h), op0=ALU.mult, op1=ALU.add),
                             r=[PB[7], "sm_x", "vb"], w=["Rm"])
                    for h in range(4):
                        P.op("pe", lambda e, h=h: e.matmul(out=H4(pb[0], h), lhsT=H4(Wm, h), rhs=H4(Rm, h), start=True, stop=True), r=["Wm", "Rm"], w=[PB[0]])
                    evac("act", vnew, pb[0], [PB[0]], ["vnew"])
                    if own:
                        for h in range(4):
                            P.op("pe", lambda e, h=h: e.matmul(out=H4(pb[1], h), lhsT=H4(qtT, h), rhs=H4(Sst, h), start=True, stop=False), r=["qtT", "S"], w=[PB[1]])
                            P.op("pe", lambda e, h=h: e.matmul(out=H4(pb[1], h), lhsT=H4(ATm, h), rhs=H4(vnew, h), start=False, stop=True), r=["ATm", "vnew"], w=[PB[1]])
                        evac("act", o_sb, pb[1], [PB[1]], ["o_sb"])
                    for h in range(4):
                        P.op("pe", lambda e, h=h: e.matmul(out=H4(pb[2], h), lhsT=H4(kt, h), rhs=H4(vnew, h), start=True, stop=True), r=["kt", "vnew"], w=[PB[2]])
                    for h in range(4):
                        P.op("dve", lambda e, h=h: e.scalar_tensor_tensor(out=H4(Sst, h), in0=H4(Sst, h), scalar=sc(32, h), in1=H4(pb[2], h), op0=ALU.mult, op1=ALU.add),
                             r=["S", "sm_e", PB[2]], w=["S"])
                    if own:
                        P.op("pool", lambda e: e.tensor_tensor(out=tmpa[:, 0:512], in0=o_sb, in1=o_sb, op=ALU.mult), r=["o_sb"], w=["tmpa"])
                        P.op("dve", lambda e: e.tensor_reduce(out=sm[:, 52:56], in_=tmpa[:, 0:512].rearrange("p (h d) -> p h d", h=4), axis=AX.X, op=ALU.add), r=["tmpa"], w=["sm_o"])
                        P.op("dve", lambda e: e.tensor_scalar(out=sm[:, 52:56], in0=sm[:, 52:56], scalar1=1.0 / 128, scalar2=1e-6, op0=ALU.mult, op1=ALU.add), r=["sm_o"], w=["sm_o"])
                        P.op("act", lambda e: e.activation(out=sm[:, 52:56], in_=sm[:, 52:56], func=AF.Sqrt), r=["sm_o"], w=["sm_o"])
                        P.op("dve", lambda e: e.reciprocal(out=sm[:, 52:56], in_=sm[:, 52:56]), r=["sm_o"], w=["sm_o"])
                        P.op("act", lambda e: e.activation(out=szb, in_=X[3][:, 1544:2056], func=AF.Silu), r=["X3"], w=["szb"])
                        for h in range(4):
                            P.op("dve", lambda e, h=h: e.scalar_tensor_tensor(out=H4(og, h), in0=H4(o_sb, h), scalar=sc(52, h), in1=H4(ggd, h), op0=ALU.mult, op1=ALU.mult),
                                 r=["o_sb", "sm_o", "ggd"], w=["og"])
                        P.op("pool", lambda e: e.tensor_tensor(out=og, in0=og, in1=szb, op=ALU.mult), r=["og", "szb"], w=["og"])
                        for h in range(4):
                            P.op("pe", lambda e, h=h: e.transpose(out=H4(pb[3], h), in_=H4(og, h), identity=ident), r=["og", "ident"], w=[PB[3]])
                        evac("act", cTst, pb[3], [PB[3]], ["cTst"])
                        P.dma("sp", S["cT"][ti][:, 0:4, :], cTst.rearrange("p (h t) -> p h t", h=4), r=["cTst"], w=[f"cTg{ti}"])
            P.dma("sp", O["ssm_fin"].rearrange("h d e -> d h e"), Sst.rearrange("p (h e) -> p h e", h=4), r=["S"])

            P.barrier()
            A.reset(gs_base)
            eye16 = A.f32(256)
            P.dma("sp", eye16, I["eye16"], w=["eye16"])
            Sall = A.f32(NS * 512); Sall4 = Sall.rearrange("p (s h e) -> p s h e", s=NS, h=4)
            for s_ in range(NS):
                P.dma("sp", Sall4[:, s_], I["state_ssm"][s_].rearrange("h d e -> d h e"), w=[f"S{s_}"])
            Ps2 = A.f32(INC); scv = A.f32(3 * 1536); scv3 = scv.rearrange("p (i c) -> p i c", i=3)
            P.dma("sp", Ps2[0:NS, :], S["ps"], w=["Ps2"])
            P.dma("sp", scv3[0:NS], I["state_conv"], w=["scv"])
            cvs = A.f32(1536); tms = A.f32(1536); sm2 = A.f32(64)
            kn2 = A.f32(512); qn2 = A.f32(512)
            knT2 = A.f32(64); qnT2 = A.f32(64)
            KTm = A.f32(1024); QTm = A.f32(1024)
            Km = [A.f32(512), A.f32(512)]
            EgD = A.f32(64); EGB = A.f32(64); Dl = A.f32(512); o_s = A.f32(512); og_s = A.f32(512); sz_s = A.f32(512)
            cTs = A.bf16(64)
            R = slice(0, NS)
            for (d0, d1, sc0, p0) in [(0, 1024, 512, 0), (1024, 1536, 0, C_Q)]:
                n = d1 - d0
                P.op("dve", lambda e, d0=d0, d1=d1, sc0=sc0, n=n: e.tensor_tensor(out=cvs[R, d0:d1], in0=scv3[R, 0, sc0:sc0 + n], in1=wc[0][R, d0:d1], op=ALU.mult), r=["scv", "wc0"], w=["cvs"])
                for i in range(1, 4):
                    src = (lambda i=i, sc0=sc0, n=n, p0=p0: scv3[R, i, sc0:sc0 + n] if i < 3 else Ps2[R, p0:p0 + n])()
                    P.op("pool", lambda e, i=i, d0=d0, d1=d1, src=src: e.tensor_tensor(out=tms[R, d0:d1], in0=src, in1=wc[i][R, d0:d1], op=ALU.mult), r=["scv", "Ps2", f"wc{i}"], w=["tms"])
                    P.op("dve", lambda e, d0=d0, d1=d1: e.tensor_tensor(out=cvs[R, d0:d1], in0=cvs[R, d0:d1], in1=tms[R, d0:d1], op=ALU.add), r=["cvs", "tms"], w=["cvs"])
                P.op("act", lambda e, d0=d0, d1=d1: e.activation(out=cvs[R, d0:d1], in_=cvs[R, d0:d1], func=AF.Silu), r=["cvs"], w=["cvs"])
            for (c0, o0) in [(0, 0), (1024, 4)]:
                P.op("pool", lambda e, c0=c0: e.tensor_tensor(out=tms[R, 0:512], in0=cvs[R, c0:c0 + 512], in1=cvs[R, c0:c0 + 512], op=ALU.mult), r=["cvs"], w=["tms"])
                P.op("dve", lambda e, o0=o0: e.tensor_reduce(out=sm2[R, o0:o0 + 4], in_=tms[R, 0:512].rearrange("p (h d) -> p h d", h=4), axis=AX.X, op=ALU.add), r=["tms"], w=["sm2"])
            P.op("dve", lambda e: e.tensor_scalar(out=sm2[R, 0:8], in0=sm2[R, 0:8], scalar1=1e-6, scalar2=None, op0=ALU.add), r=["sm2"], w=["sm2"])
            P.op("act", lambda e: e.activation(out=sm2[R, 0:8], in_=sm2[R, 0:8], func=AF.Sqrt), r=["sm2"], w=["sm2"])
            P.op("dve", lambda e: e.reciprocal(out=sm2[R, 0:8], in_=sm2[R, 0:8]), r=["sm2"], w=["sm2"])
            P.op("dve", lambda e: e.tensor_scalar(out=sm2[R, 4:8], in0=sm2[R, 4:8], scalar1=128 ** -0.5, scalar2=None, op0=ALU.mult), r=["sm2"], w=["sm2"])
            P.op("act", lambda e: e.activation(out=sm2[R, 8:12], in_=Ps2[R, C_B:C_B + 4], func=AF.Sigmoid), r=["Ps2"], w=["sm2b"])
            P.op("dve", lambda e: e.tensor_tensor(out=sm2[R, 12:16], in0=Ps2[R, C_A:C_A + 4], in1=dtb[R, :], op=ALU.add), r=["Ps2", "dtb"], w=["sm2g"])
            P.op("act", lambda e: e.activation(out=sm2[R, 12:16], in_=sm2[R, 12:16], func=AF.Exp), r=["sm2g"], w=["sm2g"])
            P.op("act", lambda e: e.activation(out=sm2[R, 12:16], in_=sm2[R, 12:16], func=AF.Ln, bias=1.0), r=["sm2g"], w=["sm2g"])
            P.op("dve", lambda e: e.tensor_tensor(out=sm2[R, 12:16], in0=sm2[R, 12:16], in1=negA[R, :], op=ALU.mult), r=["sm2g", "negA"], w=["sm2g"])
            P.op("act", lambda e: e.activation(out=sm2[R, 12:16], in_=sm2[R, 12:16], func=AF.Exp), r=["sm2g"], w=["sm2g"])
            P.op("dve", lambda e: e.tensor_scalar(out=sm2[R, 16:20], in0=sm2[R, 12:16], scalar1=-1.0, scalar2=None, op0=ALU.mult), r=["sm2g"], w=["sm2n"])
            for h in range(4):
                P.op("dve", lambda e, h=h: e.tensor_scalar(out=kn2[R, h * 128:(h + 1) * 128], in0=cvs[R, h * 128:(h + 1) * 128], scalar1=sm2[R, h:h + 1], scalar2=None, op0=ALU.mult), r=["cvs", "sm2"], w=["kn2"])
                P.op("dve", lambda e, h=h: e.tensor_scalar(out=qn2[R, h * 128:(h + 1) * 128], in0=cvs[R, 1024 + h * 128:1024 + (h + 1) * 128], scalar1=sm2[R, 4 + h:5 + h], scalar2=None, op0=ALU.mult), r=["cvs", "sm2"], w=["qn2"])
            for h in range(4):
                P.op("pe", lambda e, h=h: e.transpose(out=pb[6][:, h * 16:(h + 1) * 16], in_=kn2[R, h * 128:(h + 1) * 128], identity=ident[R, R]), r=["kn2", "ident"], w=[PB[6]])
                P.op("pe", lambda e, h=h: e.transpose(out=pb[6][:, 64 + h * 16:64 + (h + 1) * 16], in_=qn2[R, h * 128:(h + 1) * 128], identity=ident[R, R]), r=["qn2", "ident"], w=[PB[6]])
            evac("act", knT2, pb[6][:, 0:64], [PB[6]], ["knT2"])
            evac("dve", qnT2, pb[6][:, 64:128], [PB[6]], ["qnT2"])
            eye3 = eye16.rearrange("p (s m) -> p s m", s=NS)
            for h in range(4):
                for s_ in range(NS):
                    j = h * NS + s_
                    P.op("dve", lambda e, j=j, s_=s_: e.tensor_scalar(out=KTm[:, j * 16:(j + 1) * 16], in0=eye3[:, s_, :], scalar1=knT2[:, j:j + 1], scalar2=None, op0=ALU.mult), r=["eye16", "knT2"], w=["KTm"])
                    P.op("pool", lambda e, j=j, s_=s_: e.tensor_scalar(out=QTm[:, j * 16:(j + 1) * 16], in0=eye3[:, s_, :], scalar1=qnT2[:, j:j + 1], scalar2=1.0, op0=ALU.mult, op1=ALU.mult), r=["eye16", "qnT2"], w=["QTm"])
            for h in range(4):
                for s_ in range(NS):
                    j = h * NS + s_
                    P.op("pe", lambda e, h=h, s_=s_, j=j: e.matmul(out=pb[h][R, 0:128], lhsT=KTm[:, j * 16:(j + 1) * 16], rhs=Sall4[:, s_, h, :], start=(s_ == 0), stop=(s_ == NS - 1)),
                         r=["KTm", f"S{s_}"], w=[PB[h]])
            for h in range(4):
                P.op("dve", lambda e, h=h: e.scalar_tensor_tensor(out=Dl[R, h * 128:(h + 1) * 128], in0=pb[h][R, 0:128], scalar=sm2[R, 16 + h:17 + h], in1=cvs[R, 512 + h * 128:512 + (h + 1) * 128], op0=ALU.mult, op1=ALU.add),
                     r=[PB[h], "sm2n", "cvs"], w=["Dl"])
                P.op("dve", lambda e, h=h: e.tensor_scalar(out=Dl[R, h * 128:(h + 1) * 128], in0=Dl[R, h * 128:(h + 1) * 128], scalar1=sm2[R, 8 + h:9 + h], scalar2=None, op0=ALU.mult), r=["Dl", "sm2b"], w=["Dl"])
            for s_ in range(NS):
                P.op("dve", lambda e, s_=s_: e.tensor_scalar(out=EgD[R, s_ * 4:(s_ + 1) * 4], in0=sm2[R, 12:16], scalar1=ident[R, s_:s_ + 1], scalar2=None, op0=ALU.mult), r=["sm2g", "ident"], w=["EgD"])
            P.op("pe", lambda e: e.matmul(out=pb[6][:, 0:64], lhsT=ONESM[R, :], rhs=EgD[R, :], start=True, stop=True), r=["cst", "EgD"], w=[PB[6]])
            evac("act", EGB, pb[6][:, 0:64], [PB[6]], ["EGB"])
            for s_ in range(NS):
                km = Km[s_ % 2]; kmn = f"Km{s_ % 2}"
                bank = 4 + s_ % 2
                P.op("pool", lambda e, s_=s_, km=km: e.tensor_scalar(out=km[R, :], in0=kn2[R, :], scalar1=ident[R, s_:s_ + 1], scalar2=1.0, op0=ALU.mult, op1=ALU.mult), r=["kn2", "ident"], w=[kmn])
                for h in range(4):
                    P.op("pe", lambda e, h=h, km=km, bank=bank: e.matmul(out=pb[bank][:, h * 128:(h + 1) * 128], lhsT=km[R, h * 128:(h + 1) * 128], rhs=Dl[R, h * 128:(h + 1) * 128], start=True, stop=True),
                         r=[kmn, "Dl"], w=[PB[bank]])
                for h in range(4):
                    P.op("dve", lambda e, h=h, s_=s_, bank=bank: e.scalar_tensor_tensor(out=Sall4[:, s_, h, :], in0=Sall4[:, s_, h, :], scalar=EGB[:, s_ * 4 + h:s_ * 4 + h + 1], in1=pb[bank][:, h * 128:(h + 1) * 128], op0=ALU.mult, op1=ALU.add),
                         r=[f"S{s_}", "EGB", PB[bank]], w=[f"S{s_}"])
                P.dma("sp", O["ssm_s"][s_].rearrange("h d e -> d h e"), Sall4[:, s_], r=[f"S{s_}"])
            for h in range(4):
                for s_ in range(NS):
                    j = h * NS + s_
                    P.op("pe", lambda e, h=h, s_=s_, j=j: e.matmul(out=pb[h][R, 0:128], lhsT=QTm[:, j * 16:(j + 1) * 16], rhs=Sall4[:, s_, h, :], start=(s_ == 0), stop=(s_ == NS - 1)),
                         r=["QTm", f"S{s_}"], w=[PB[h]])
            for h in range(4):
                evac("act", o_s[R, h * 128:(h + 1) * 128], pb[h][R, 0:128], [PB[h]], ["o_s"])
            P.op("pool", lambda e: e.tensor_tensor(out=tms[R, 0:512], in0=o_s[R, :], in1=o_s[R, :], op=ALU.mult), r=["o_s"], w=["tms"])
            P.op("dve", lambda e: e.tensor_reduce(out=sm2[R, 20:24], in_=tms[R, 0:512].rearrange("p (h d) -> p h d", h=4), axis=AX.X, op=ALU.add), r=["tms"], w=["sm2o"])
            P.op("dve", lambda e: e.tensor_scalar(out=sm2[R, 20:24], in0=sm2[R, 20:24], scalar1=1.0 / 128, scalar2=1e-6, op0=ALU.mult, op1=ALU.add), r=["sm2o"], w=["sm2o"])
            P.op("act", lambda e: e.activation(out=sm2[R, 20:24], in_=sm2[R, 20:24], func=AF.Sqrt), r=["sm2o"], w=["sm2o"])
            P.op("dve", lambda e: e.reciprocal(out=sm2[R, 20:24], in_=sm2[R, 20:24]), r=["sm2o"], w=["sm2o"])
            P.op("act", lambda e: e.activation(out=sz_s[R, :], in_=Ps2[R, C_Z:C_Z + 512], func=AF.Silu), r=["Ps2"], w=["sz_s"])
            for h in range(4):
                P.op("dve", lambda e, h=h: e.scalar_tensor_tensor(out=og_s[R, h * 128:(h + 1) * 128], in0=o_s[R, h * 128:(h + 1) * 128], scalar=sm2[R, 20 + h:21 + h], in1=ggd[R, h * 128:(h + 1) * 128], op0=ALU.mult, op1=ALU.mult),
                     r=["o_s", "sm2o", "ggd"], w=["og_s"])
            P.op("pool", lambda e: e.tensor_tensor(out=og_s[R, :], in0=og_s[R, :], in1=sz_s[R, :], op=ALU.mult), r=["og_s", "sz_s"], w=["og_s"])
            for h in range(4):
                P.op("pe", lambda e, h=h: e.transpose(out=pb[7][:, h * 16:(h + 1) * 16], in_=og_s[R, h * 128:(h + 1) * 128], identity=ident[R, R]), r=["og_s", "ident"], w=[PB[7]])
            evac("act", cTs, pb[7][:, 0:64], [PB[7]], ["cTs"])
            P.dma("sp", S["cT"][NT][:, 0:4, 0:NS], cTs.rearrange("p (h t) -> p h t", h=4), r=["cTs"], w=["cTgs"])

        if not os.environ.get('MK_NOT'):
            P.barrier()
            A.reset(persist1)
            SCALE = 128 ** -0.5
            NEG = -30000.0
            NBIS = 14
            caus = A.f32(128); ii2 = A.bf16(256); onesr = A.f32(128)
            P.dma("sp", caus, I["caus"], w=["caus"])
            P.op("dve", lambda e: e.tensor_copy(out=ii2[:, 0:128], in_=ident), r=["ident"], w=["ii2"])
            P.op("dve", lambda e: e.tensor_copy(out=ii2[:, 128:256], in_=ident), r=["ident"], w=["ii2"])
            P.op("dve", lambda e: e.memset(onesr, 1.0), w=["onesr"])
            tsm = A.f32(256)
            krow = A.f32(128)
            P.op("dve", lambda e: e.tensor_reduce(out=tsm[:, 0:1], in_=ksq, axis=AX.X, op=ALU.max), r=["ksq"], w=["tsm0"])
            P.op("pe", lambda e: e.transpose(out=pb[0][0:1, 0:128], in_=tsm[:, 0:1], identity=ident), r=["tsm0", "ident"], w=[PB[0]])
            evac("dve", krow[0:1, :], pb[0][0:1, 0:128], [PB[0]], ["krow"])
            P.op("dve", lambda e: e.tensor_reduce(out=krow[0:1, 0:1], in_=krow[0:1, :], axis=AX.X, op=ALU.max), r=["krow"], w=["krow"])
            P.op("pe", lambda e: e.matmul(out=pb[0][:, 0:1], lhsT=onesr[0:1, :], rhs=krow[0:1, 0:1], start=True, stop=True), r=["onesr", "krow"], w=[PB[0]])
            evac("dve", tsm[:, 1:2], pb[0][:, 0:1], [PB[0]], ["tsm1"])
            P.op("dve", lambda e: e.tensor_reduce(out=tsm[:, 16:32], in_=qsq.rearrange("p (t h) -> p t h", h=4), axis=AX.X, op=ALU.max), r=["qsq"], w=["tsmq"])
            P.op("dve", lambda e: e.tensor_scalar(out=tsm[:, 32:48], in0=tsm[:, 16:32], scalar1=tsm[:, 1:2], scalar2=None, op0=ALU.mult), r=["tsmq", "tsm1"], w=["negm"])
            P.op("act", lambda e: e.activation(out=tsm[:, 32:48], in_=tsm[:, 32:48], func=AF.Sqrt), r=["negm"], w=["negm"])
            P.op("dve", lambda e: e.tensor_scalar(out=tsm[:, 32:48], in0=tsm[:, 32:48], scalar1=-1.0, scalar2=None, op0=ALU.mult), r=["negm"], w=["negm"])
            Iscs = [A.f32(T), A.f32(T)]
            junk = A.bf16(T); MBs_ = [A.bf16(T), A.bf16(T)]
            qTt = [A.bf16(512), A.bf16(512), A.bf16(512)]; iqTt = [A.bf16(512), A.bf16(512)]
            oaccs = [A.f32(4 * 130), A.f32(4 * 130)]
            rh = [A.bf16(512) for _ in range(4)]
            Dhs = [A.bf16(8 * 128), A.bf16(8 * 128)]
            PT = [A.bf16(256), A.bf16(256)]
            oatt = A.f32(512); cTa = A.bf16(512)
            bss = [A.f32(16), A.f32(16)]
            T_NT = int(os.environ.get("MK_TNT", NT))

            def t_index(j):
                p = j % 2
                Isc = Iscs[p]; In = f"Isc{p}"; Dh = Dhs[p]; Dn = f"Dh{p}"
                qb = qTt[j % 3]; qn_ = f"qTt{j % 3}"; iqb = iqTt[p]; iqn = f"iqTt{p}"
                qb3 = qb.rearrange("p (h t) -> p h t", h=4); iqb3 = iqb.rearrange("p (h t) -> p h t", h=4)
                P.dma("sp", iqb3, S["iqT"][j], w=[iqn])
                P.dma("sp", qb3, S["qT"][j], w=[qn_])
                ncol = HALF + 128 * (j + 1)
                for h in range(8):
                    P.op("dve", lambda e, h=h: e.tensor_scalar(out=Dh[:, h * 128:(h + 1) * 128], in0=ident, scalar1=iwsgn[:, j * 8 + h:j * 8 + h + 1], scalar2=None, op0=ALU.mult),
                         r=["ident", "iwsgn"], w=[Dn])
                nblk = (ncol + 511) // 512
                for kb in range(nblk):
                    c0, c1 = kb * 512, min(ncol, kb * 512 + 512)
                    w_ = c1 - c0
                    accb = 2 + kb % 2

                    def emit_S(h, c0=c0, c1=c1, w_=w_):
                        p_, hf_ = h // 2, h % 2
                        ba = h % 2
                        rb = rh[h % 4]; rbn = f"rh{h % 4}"
                        P.op("pe", lambda e: e.matmul(out=pb[ba][:, 0:w_], lhsT=iqb3[hf_ * 64:(hf_ + 1) * 64, p_, :], rhs=ikT[hf_ * 64:(hf_ + 1) * 64, c0:c1], start=True, stop=True),
                             r=[iqn, "ikT"], w=[PB[ba]])
                        P.op("act", lambda e: e.activation(out=rb[:, 0:w_], in_=pb[ba][:, 0:w_], func=AF.Relu, scale=iwabs[:, j * 8 + h:j * 8 + h + 1]),
                             r=[PB[ba], "iwabs"], w=[rbn])

                    def emit_D(h, w_=w_, accb=accb):
                        rb = rh[h % 4]; rbn = f"rh{h % 4}"
                        P.op("pe", lambda e: e.matmul(out=pb[accb][:, 0:w_], lhsT=Dh[:, h * 128:(h + 1) * 128], rhs=rb[:, 0:w_], start=(h == 0), stop=(h == 7)),
                             r=[Dn, rbn], w=[PB[accb]])

                    emit_S(0); emit_S(1)
                    for h in range(8):
                        emit_D(h)
                        if h + 2 < 8:
                            emit_S(h + 2)
                    if c0 < HALF:
                        P.op("act", lambda e, c0=c0, c1=c1, w_=w_, accb=accb: e.activation(out=Isc[:, c0:c1], in_=pb[accb][:, 0:w_], func=AF.Identity, bias=flags[:, 1:2]), r=[PB[accb], "flags"], w=[In])
                    else:
                        P.op("act", lambda e, c0=c0, c1=c1, w_=w_, accb=accb: e.activation(out=Isc[:, c0:c1], in_=pb[accb][:, 0:w_], func=AF.Copy), r=[PB[accb]], w=[In])
                P.op("pool", lambda e: e.tensor_tensor(out=Isc[:, ncol - 128:ncol], in0=Isc[:, ncol - 128:ncol], in1=caus, op=ALU.add), r=[In, "caus"], w=[In])

            def t_bisect(j):
                p = j % 2
                Isc = Iscs[p]; In = f"Isc{p}"; MB = MBs_[p]; Mn = f"MB{p}"; bs = bss[p]
                B = lambda n: f"bs{p}_{n}"
                ncol = HALF + 128 * (j + 1)
                lo, rng, thr, cntc, mm = bs[:, 0:1], bs[:, 1:2], bs[:, 2:3], bs[:, 3:4], bs[:, 4:5]
                P.op("dve", lambda e: e.tensor_reduce(out=lo, in_=Isc[:, 0:ncol], axis=AX.X, op=ALU.min), r=[In], w=[B("lo")])
                P.op("dve", lambda e: e.tensor_reduce(out=rng, in_=Isc[:, 0:ncol], axis=AX.X, op=ALU.max), r=[In], w=[B("rng")])
                P.op("dve", lambda e: e.tensor_scalar(out=thr, in0=rng, scalar1=-128.0, scalar2=None, op0=ALU.add), r=[B("rng")], w=[B("thr")])
                P.op("dve", lambda e: e.tensor_tensor(out=lo, in0=lo, in1=thr, op=ALU.max), r=[B("lo"), B("thr")], w=[B("lo")])
                P.op("dve", lambda e: e.tensor_tensor(out=rng, in0=rng, in1=lo, op=ALU.subtract), r=[B("rng"), B("lo")], w=[B("rng")])
                base = bs[:, 5:6]
                P.op("dve", lambda e: e.scalar_tensor_tensor(out=thr, in0=rng, scalar=0.5, in1=lo, op0=ALU.mult, op1=ALU.add), r=[B("rng"), B("lo")], w=[B("thr")])
                for it in range(NBIS):
                    st = 2.0 ** -(it + 1)
                    P.op("dve", lambda e: e.tensor_scalar(out=junk[:, 0:ncol], in0=Isc[:, 0:ncol], scalar1=thr, scalar2=None, op0=ALU.is_ge, op1=ALU.add, accum_out=cntc),
                         r=[In, B("thr")], w=["junk", B("cnt")])
                    P.op("dve", lambda e, st=st: e.scalar_tensor_tensor(out=base, in0=rng, scalar=-0.5 * st, in1=thr, op0=ALU.mult, op1=ALU.add), r=[B("rng"), B("thr")], w=[B("base")])
                    P.op("dve", lambda e: e.tensor_scalar(out=mm, in0=cntc, scalar1=255.5, scalar2=rng, op0=ALU.is_ge, op1=ALU.mult), r=[B("cnt"), B("rng")], w=[B("m")])
                    P.op("dve", lambda e, st=st: e.scalar_tensor_tensor(out=thr, in0=mm, scalar=st, in1=base, op0=ALU.mult, op1=ALU.add), r=[B("m"), B("base")], w=[B("thr")])
                P.op("dve", lambda e: e.scalar_tensor_tensor(out=lo, in0=rng, scalar=-(2.0 ** -(NBIS + 1)), in1=thr, op0=ALU.mult, op1=ALU.add), r=[B("rng"), B("thr")], w=[B("lo")])
                P.op("dve", lambda e: e.tensor_scalar(out=MB[:, 0:ncol], in0=Isc[:, 0:ncol], scalar1=lo, scalar2=NEG, op0=ALU.is_lt, op1=ALU.mult), r=[In, B("lo")], w=[Mn])
                P.op("dve", lambda e: e.tensor_scalar(out=MB[:, 0:ncol], in0=MB[:, 0:ncol], scalar1=tsm[:, 32 + j:33 + j], scalar2=None, op0=ALU.add), r=[Mn, "negm"], w=[Mn])

            def t_attend(j):
                p = j % 2
                MB = MBs_[p]; Mn = f"MB{p}"; bs = bss[p]
                qb = qTt[j % 3]; qn_ = f"qTt{j % 3}"
                qb3 = qb.rearrange("p (h t) -> p h t", h=4)
                oacc = oaccs[p]; oan = f"oacc{p}"
                ntile = 16 + j + 1
                seq = [(g, t) for g in range(2) for t in range(ntile)]

                def emit_ST(i):
                    g, t = seq[i]
                    sb_ = i % 2
                    ptb = PT[i % 2]; ptn = f"PT{i % 2}"
                    P.op("pe", lambda e: e.matmul(out=pb[sb_][:, 0:256], lhsT=KT3[:, g, t * 128:(t + 1) * 128], rhs=qb3[:, g * 2:(g + 1) * 2, :], start=True, stop=False),
                         r=["KT", qn_], w=[PB[sb_]])
                    P.op("pe", lambda e: e.matmul(out=pb[sb_][:, 0:256], lhsT=MB[:, t * 128:(t + 1) * 128], rhs=ii2, start=False, stop=True),
                         r=[Mn, "ii2"], w=[PB[sb_]])
                    P.op("act", lambda e: e.activation(out=ptb, in_=pb[sb_][:, 0:256], func=AF.Exp, scale=SCALE), r=[PB[sb_]], w=[ptn])

                def emit_PV(i):
                    g, t = seq[i]
                    ptb = PT[i % 2]; ptn = f"PT{i % 2}"
                    for h2i in range(2):
                        P.op("pe", lambda e, h2i=h2i: e.matmul(out=pb[4 + g * 2 + h2i][:, 0:130], lhsT=ptb[:, h2i * 128:(h2i + 1) * 128], rhs=VA4[:, t, g, :], start=(t == 0), stop=(t == ntile - 1)),
                             r=[ptn, "VA"], w=[PB[4 + g * 2 + h2i]])

                emit_ST(0)
                if len(seq) > 1:
                    emit_ST(1)
                for i in range(len(seq)):
                    emit_PV(i)
                    if i + 2 < len(seq):
                        emit_ST(i + 2)
                for h in range(4):
                    P.op("act", lambda e, h=h: e.activation(out=oacc[:, h * 130:(h + 1) * 130], in_=pb[4 + h][:, 0:130], func=AF.Copy), r=[PB[4 + h]], w=[oan])

            def t_final(j):
                p = j % 2
                bs = bss[p]; oacc = oaccs[p]; oan = f"oacc{p}"
                for h in range(4):
                    P.op("dve", lambda e, h=h: e.reciprocal(out=bs[:, 8 + h:9 + h], in_=oacc[:, h * 130 + 128:h * 130 + 129]), r=[oan], w=[f"bs{p}_r"])
                    P.op("dve", lambda e, h=h: e.tensor_scalar(out=oatt[:, h * 128:(h + 1) * 128], in0=oacc[:, h * 130:h * 130 + 128], scalar1=bs[:, 8 + h:9 + h], scalar2=None, op0=ALU.mult),
                         r=[oan, f"bs{p}_r"], w=["oatt"])
                for h in range(4):
                    P.op("pe", lambda e, h=h: e.transpose(out=pb[3][:, h * 128:(h + 1) * 128], in_=oatt[:, h * 128:(h + 1) * 128], identity=ident), r=["oatt", "ident"], w=[PB[3]])
                evac("act", cTa, pb[3], [PB[3]], ["cTa"])
                P.dma("sp", S["cT"][j][:, 4:8, :], cTa.rearrange("p (h t) -> p h t", h=4), r=["cTa"], w=[f"cTa{j}"])

            if T_NT > 0:
                t_index(0)
            if T_NT > 1:
                t_index(1)
            if T_NT > 0:
                t_bisect(0)
            for j in range(T_NT):
                t_attend(j)
                if j + 2 < T_NT:
                    t_index(j + 2)
                if j + 1 < T_NT:
                    t_bisect(j + 1)
                t_final(j)

        if not os.environ.get('MK_NOTS'):
            P.barrier()
            A.reset(persist0)
            zSCALE = 128 ** -0.5
            NPG = 16
            zidx_i = A.f32(32).bitcast(I32)
            zpt_i = A.f32(256).bitcast(I32)
            zidx_f = A.f32(256); zsel = A.f32(257); zix = A.f32(32)
            P.dma("sp", zpt_i[:, :], I["pt"].to_broadcast([128, 256]), w=["zpt_i"])
            P.dma("sp", zsel, I["tsel"], w=["zsel"])
            P.op("dve", lambda e: e.tensor_copy(out=zidx_f, in_=zpt_i[:, :]), r=["zpt_i"], w=["zidx_f"])
            P.op("dve", lambda e: e.tensor_tensor(out=zidx_f, in0=zidx_f, in1=zsel[:, 0:256], op=ALU.mult), r=["zidx_f", "zsel"], w=["zidx_f"])
            P.op("dve", lambda e: e.tensor_reduce(out=zix, in_=zidx_f.rearrange("p (q j) -> p q j", j=8), axis=AX.X, op=ALU.add), r=["zidx_f"], w=["zix"])
            P.op("dve", lambda e: e.tensor_scalar(out=zix, in0=zix, scalar1=16.0, scalar2=zsel[:, 256:257], op0=ALU.mult, op1=ALU.add), r=["zix", "zsel"], w=["zix"])
            P.op("dve", lambda e: e.tensor_copy(out=zidx_i[:, :], in_=zix), r=["zix"], w=["zidx_i"])
            zPs = A.f32(INC)
            P.dma("sp", zPs[0:NS, :], S["ps"], w=["zPs"])
            zones = A.f32(128)
            P.op("dve", lambda e: e.memset(zones, 1.0), w=["zones"])
            ziqT = A.f32(NS * 8)
            ziwT = A.f32(NS)
            zikn = A.f32(NS)
            zqT = A.bf16(4 * NS)
            zkTn = A.bf16(2 * NS)
            ziqT3 = ziqT.rearrange("p (s h) -> p s h", h=8)
            R = slice(0, NS)
            for h in range(8):
                P.op("pe", lambda e, h=h: e.transpose(out=pb[0][0:64, h * NS:(h + 1) * NS], in_=zPs[R, C_IQ + h * 64:C_IQ + (h + 1) * 64], identity=ident[R, R]), r=["zPs", "ident"], w=[PB[0]])
            P.op("dve", lambda e: e.tensor_copy(out=ziqT3[0:64], in_=pb[0][0:64, 0:8 * NS].rearrange("p (h s) -> p s h", h=8)), r=[PB[0]], w=["ziqT"])
            P.op("pe", lambda e: e.transpose(out=pb[1][0:8, 0:NS], in_=zPs[R, C_IW:C_IW + 8], identity=ident[R, R]), r=["zPs", "ident"], w=[PB[1]])
            P.op("dve", lambda e: e.tensor_scalar(out=ziwT[0:8, :], in0=pb[1][0:8, 0:NS], scalar1=8 ** -0.5, scalar2=None, op0=ALU.mult), r=[PB[1]], w=["ziwT"])
            P.op("pe", lambda e: e.transpose(out=pb[1][0:64, 64:64 + NS], in_=zPs[R, C_IK:C_IK + 64], identity=ident[R, R]), r=["zPs", "ident"], w=[PB[1]])
            P.op("dve", lambda e: e.tensor_copy(out=zikn[0:64, :], in_=pb[1][0:64, 64:64 + NS]), r=[PB[1]], w=["zikn"])
            for h in range(4):
                P.op("pe", lambda e, h=h: e.transpose(out=pb[2][:, h * NS:(h + 1) * NS], in_=zPs[R, C_AQ + h * 128:C_AQ + (h + 1) * 128], identity=ident[R, R]), r=["zPs", "ident"], w=[PB[2]])
            for g in range(2):
                P.op("pe", lambda e, g=g: e.transpose(out=pb[2][:, 64 + g * NS:64 + (g + 1) * NS], in_=zPs[R, C_AK + g * 128:C_AK + (g + 1) * 128], identity=ident[R, R]), r=["zPs", "ident"], w=[PB[2]])
            P.op("dve", lambda e: e.tensor_copy(out=zqT, in_=pb[2][:, 0:4 * NS]), r=[PB[2]], w=["zqT"])
            P.op("dve", lambda e: e.tensor_copy(out=zkTn, in_=pb[2][:, 64:64 + 2 * NS]), r=[PB[2]], w=["zkTn"])
            zvn_f = A.f32(NS * 256); zvn = A.bf16(NS * 2 * 130)
            zvn4 = zvn.rearrange("p (s g d) -> p s g d", s=NS, g=2)
            P.dma("sp", zvn_f[0:1, :].rearrange("p (s c) -> p s c", s=NS), S["ps"][:, C_AV:C_AV + 256].rearrange("(o s) c -> o s c", o=1), w=["zvn_f"])
            P.op("dve", lambda e: e.memset(zvn[0:1, :], 1.0), w=["zvn"])
            P.op("dve", lambda e: e.tensor_copy(out=zvn4[0:1, :, :, 0:128], in_=zvn_f[0:1, :].rearrange("p (s g d) -> p s g d", s=NS, g=2)), r=["zvn_f", "zvn"], w=["zvn"])
            NKS = 2049
            zIall = A.f32(NKS + 3); zikg = A.f32(NPG * 64); zikT = A.f32(NKS + 3)
            zik_rows = I["cache_ik"].rearrange("(r t) d -> r (t d)", t=8)
            zk_rows = I["cache_k"].rearrange("(r t) d -> r (t d)", t=8)
            zv_rows = I["cache_v"].rearrange("(r t) d -> r (t d)", t=8)
            zikg4 = zikg.rearrange("p (a t d) -> p a t d", a=2, t=8)
            zr8 = [A.f32(512), A.f32(512)]; zrw = A.f32(NKS + 3)
            zikg3 = zikg.rearrange("p (g d) -> p g d", g=NPG)
            for s_ in range(NS):
                for a_ in range(2):
                    col = s_ * 2 + a_
                    P.dma_raw("pool", lambda e, a_=a_, col=col: e.indirect_dma_start(out=zikg4[:, a_].rearrange("p t d -> p (t d)"), out_offset=None, in_=zik_rows, in_offset=bass.IndirectOffsetOnAxis(ap=zidx_i[:, col:col + 1], axis=0)),
                              r=["zidx_i"], w=["zikg"], sw=True)
                for q4 in range(4):
                    for i4 in range(4):
                        bk = q4 * 4 + i4
                        t8, a_ = bk // 2, bk % 2
                        P.op("pe", lambda e, t8=t8, a_=a_, i4=i4: e.transpose(out=pb[3][0:64, i4 * 128:(i4 + 1) * 128], in_=zikg4[:, a_, t8, :], identity=ident), r=["zikg", "ident"], w=[PB[3]])
                    evac("act" if q4 % 2 else "dve", zikT[0:64, q4 * 512:(q4 + 1) * 512], pb[3][0:64, :], [PB[3]], ["zikT"])
                P.op("dve", lambda e, s_=s_: e.tensor_copy(out=zikT[0:64, 2048:2049], in_=zikn[0:64, s_:s_ + 1]), r=["zikn"], w=["zikT"])
                for kb in range(5):
                    c0, c1 = kb * 512, min(NKS, kb * 512 + 512)
                    w_ = c1 - c0
                    rb = zr8[kb % 2]; rbn = f"zr8{kb % 2}"
                    P.op("pe", lambda e, s_=s_, c0=c0, c1=c1, w_=w_, kb=kb: e.matmul(out=pb[4 + kb % 2][0:8, 0:w_], lhsT=ziqT3[0:64, s_, :], rhs=zikT[0:64, c0:c1], start=True, stop=True), r=["ziqT", "zikT"], w=[PB[4 + kb % 2]])
                    P.op("dve", lambda e, s_=s_, w_=w_, kb=kb, rb=rb: e.tensor_scalar(out=rb[0:8, 0:w_], in0=pb[4 + kb % 2][0:8, 0:w_], scalar1=0.0, scalar2=ziwT[0:8, s_:s_ + 1], op0=ALU.max, op1=ALU.mult),
                         r=[PB[4 + kb % 2], "ziwT"], w=[rbn])
                    P.op("pe", lambda e, w_=w_, kb=kb, rb=rb: e.matmul(out=pb[6 + kb % 2][0:1, 0:w_], lhsT=zones[0:8, 0:1], rhs=rb[0:8, 0:w_], start=True, stop=True), r=["zones", rbn], w=[PB[6 + kb % 2]])
                    evac("act", zrw[0:1, c0:c1], pb[6 + kb % 2][0:1, 0:w_], [PB[6 + kb % 2]], ["zrw"])
                P.dma("sp", zIall[s_:s_ + 1, 0:NKS], zrw[0:1, 0:NKS], r=["zrw"], w=["zIall"])
            zbs = A.f32(16); zjunk = A.bf16(NKS + 3); zMB = A.f32(NKS + 3)
            zlo, zrng, zthr, zcnt, zmm = zbs[R, 0:1], zbs[R, 1:2], zbs[R, 2:3], zbs[R, 3:4], zbs[R, 4:5]
            P.op("dve", lambda e: e.tensor_reduce(out=zlo, in_=zIall[R, 0:NKS], axis=AX.X, op=ALU.min), r=["zIall"], w=["zlo"])
            P.op("dve", lambda e: e.tensor_reduce(out=zrng, in_=zIall[R, 0:NKS], axis=AX.X, op=ALU.max), r=["zIall"], w=["zrng"])
            P.op("dve", lambda e: e.tensor_tensor(out=zrng, in0=zrng, in1=zlo, op=ALU.subtract), r=["zrng", "zlo"], w=["zrng"])
            for it in range(24):
                st = 2.0 ** -(it + 1)
                P.op("dve", lambda e, st=st: e.scalar_tensor_tensor(out=zthr, in0=zrng, scalar=st, in1=zlo, op0=ALU.mult, op1=ALU.add), r=["zrng", "zlo"], w=["zthr"])
                P.op("dve", lambda e: e.tensor_scalar(out=zjunk[R, 0:NKS], in0=zIall[R, 0:NKS], scalar1=zthr, scalar2=None, op0=ALU.is_ge, op1=ALU.add, accum_out=zcnt), r=["zIall", "zthr"], w=["zjunk", "zcnt"])
                P.op("dve", lambda e: e.tensor_scalar(out=zmm, in0=zcnt, scalar1=255.5, scalar2=zrng, op0=ALU.is_ge, op1=ALU.mult), r=["zcnt", "zrng"], w=["zmm"])
                P.op("dve", lambda e, st=st: e.scalar_tensor_tensor(out=zlo, in0=zmm, scalar=st, in1=zlo, op0=ALU.mult, op1=ALU.add), r=["zmm", "zlo"], w=["zlo"])
            P.op("dve", lambda e: e.tensor_scalar(out=zMB[R, 0:NKS], in0=zIall[R, 0:NKS], scalar1=zlo, scalar2=None, op0=ALU.is_ge), r=["zIall", "zlo"], w=["zMB"])
            zMT = A.f32(NPG * NS); zMT3 = zMT.rearrange("p (g s) -> p g s", g=NPG); zMn = A.f32(NS)
            for pg in range(NPG):
                P.op("pe", lambda e, pg=pg: e.transpose(out=pb[0][:, pg * NS:(pg + 1) * NS], in_=zMB[R, pg * 128:(pg + 1) * 128], identity=ident[R, R]), r=["zMB", "ident"], w=[PB[0]])
            evac("dve", zMT, pb[0][:, 0:NPG * NS], [PB[0]], ["zMT"])
            P.op("pe", lambda e: e.transpose(out=pb[1][0:1, 0:NS], in_=zMB[R, 2048:2049], identity=ident[R, R]), r=["zMB", "ident"], w=[PB[1]])
            evac("dve", zMn[0:1, :], pb[1][0:1, 0:NS], [PB[1]], ["zMn"])
            zkg = A.f32(NPG * 256); zvg = A.f32(NPG * 256)
            zkg4 = zkg.rearrange("p (a t c) -> p a t c", a=2, t=8); zvg4 = zvg.rearrange("p (a t c) -> p a t c", a=2, t=8)
            zKT = A.bf16(NPG * 256); zKT4 = zKT.rearrange("p (g k c) -> p g k c", g=NPG, k=2)
            zVb = A.bf16(NPG * 2 * 130); zVb4 = zVb.rearrange("p (g k d) -> p g k d", g=NPG, k=2)
            zP = A.bf16(NPG * 4); zP3 = zP.rearrange("p (g h) -> p g h", g=NPG); zPn = A.bf16(4)
            zsm = A.f32(16); zcr = A.f32(128)
            zo = A.f32(256); zoT = A.bf16(4 * NS); zoT3 = zoT.rearrange("p (h s) -> p h s", h=4)
            P.op("dve", lambda e: e.memset(zVb, 1.0), w=["zVb"])
            for s_ in range(NS):
                for a_ in range(2):
                    col = s_ * 2 + a_
                    P.dma_raw("pool", lambda e, a_=a_, col=col: e.indirect_dma_start(out=zkg4[:, a_].rearrange("p t c -> p (t c)"), out_offset=None, in_=zk_rows, in_offset=bass.IndirectOffsetOnAxis(ap=zidx_i[:, col:col + 1], axis=0)),
                              r=["zidx_i"], w=["zkg"], sw=True)
                    P.dma_raw("pool", lambda e, a_=a_, col=col: e.indirect_dma_start(out=zvg4[:, a_].rearrange("p t c -> p (t c)"), out_offset=None, in_=zv_rows, in_offset=bass.IndirectOffsetOnAxis(ap=zidx_i[:, col:col + 1], axis=0)),
                              r=["zidx_i"], w=["zvg"], sw=True)
                for a_ in range(2):
                    P.op("act", lambda e, a_=a_: e.activation(out=zVb.rearrange("p (t a k d) -> p a t k d", t=8, a=2, k=2)[:, a_, :, :, 0:128], in_=zvg4[:, a_].rearrange("p t (k d) -> p t k d", k=2), func=AF.Copy),
                         r=["zvg", "zVb"], w=["zVb"])
                for pq in range(8):
                    for i2 in range(2):
                        bk = pq * 2 + i2
                        t8, a_ = bk // 2, bk % 2
                        for g in range(2):
                            P.op("pe", lambda e, t8=t8, a_=a_, g=g, i2=i2, pq=pq: e.transpose(out=pb[2 + pq % 2][:, (i2 * 2 + g) * 128:(i2 * 2 + g + 1) * 128], in_=zkg4[:, a_, t8, g * 128:(g + 1) * 128], identity=ident),
                                 r=["zkg", "ident"], w=[PB[2 + pq % 2]])
                    evac("act" if pq % 2 else "dve", zKT[:, pq * 512:(pq + 1) * 512], pb[2 + pq % 2], [PB[2 + pq % 2]], ["zKT"])
                for pg in range(NPG):
                    for g in range(2):
                        P.op("pe", lambda e, pg=pg, g=g, s_=s_: e.matmul(out=pb[4][:, pg * 4 + g * 2:pg * 4 + g * 2 + 2], lhsT=zKT4[:, pg, g, :], rhs=zqT.rearrange("p (h s) -> p h s", h=4)[:, g * 2:(g + 1) * 2, s_], start=True, stop=True),
                             r=["zKT", "zqT"], w=[PB[4]])
                for g in range(2):
                    P.op("pe", lambda e, g=g, s_=s_: e.matmul(out=pb[5][0:1, g * 2:g * 2 + 2], lhsT=zkTn.rearrange("p (g s) -> p g s", g=2)[:, g, s_:s_ + 1], rhs=zqT.rearrange("p (h s) -> p h s", h=4)[:, g * 2:(g + 1) * 2, s_], start=True, stop=True),
                         r=["zkTn", "zqT"], w=[PB[5]])
                P.op("dve", lambda e: e.tensor_reduce(out=zsm[:, 0:1], in_=pb[4][:, 0:NPG * 4], axis=AX.X, op=ALU.max), r=[PB[4]], w=["zsm0"])
                P.op("pe", lambda e: e.transpose(out=pb[6][0:1, 0:128], in_=zsm[:, 0:1], identity=ident), r=["zsm0", "ident"], w=[PB[6]])
                evac("dve", zcr[0:1, :], pb[6][0:1, 0:128], [PB[6]], ["zcr"])
                P.op("dve", lambda e: e.tensor_reduce(out=zsm[0:1, 1:2], in_=zcr[0:1, :], axis=AX.X, op=ALU.max), r=["zcr"], w=["zsm1"])
                P.op("dve", lambda e: e.tensor_reduce(out=zsm[0:1, 2:3], in_=pb[5][0:1, 0:4], axis=AX.X, op=ALU.max), r=[PB[5]], w=["zsm2"])
                P.op("dve", lambda e: e.tensor_tensor(out=zsm[0:1, 1:2], in0=zsm[0:1, 1:2], in1=zsm[0:1, 2:3], op=ALU.max), r=["zsm1", "zsm2"], w=["zsm1"])
                P.op("dve", lambda e: e.tensor_scalar(out=zsm[0:1, 1:2], in0=zsm[0:1, 1:2], scalar1=-zSCALE, scalar2=None, op0=ALU.mult), r=["zsm1"], w=["zsm1"])
                P.op("pe", lambda e: e.matmul(out=pb[6][:, 128:129], lhsT=zones[0:1, :], rhs=zsm[0:1, 1:2], start=True, stop=True), r=["zones", "zsm1"], w=[PB[6]])
                evac("dve", zsm[:, 3:4], pb[6][:, 128:129], [PB[6]], ["zsm3"])
                P.op("act", lambda e: e.activation(out=zP, in_=pb[4][:, 0:NPG * 4], func=AF.Exp, scale=zSCALE, bias=zsm[:, 3:4]), r=[PB[4], "zsm3"], w=["zP"])
                P.op("act", lambda e: e.activation(out=zPn[0:1, :], in_=pb[5][0:1, 0:4], func=AF.Exp, scale=zSCALE, bias=zsm[0:1, 3:4]), r=[PB[5], "zsm3"], w=["zPn"])
                for h in range(4):
                    P.op("dve", lambda e, h=h, s_=s_: e.tensor_tensor(out=zP3[:, :, h], in0=zP3[:, :, h], in1=zMT3[:, :, s_], op=ALU.mult), r=["zP", "zMT"], w=["zP"])
                P.op("dve", lambda e, s_=s_: e.tensor_scalar(out=zPn[0:1, :], in0=zPn[0:1, :], scalar1=zMn[0:1, s_:s_ + 1], scalar2=None, op0=ALU.mult), r=["zPn", "zMn"], w=["zPn"])
                for g in range(2):
                    for pg in range(NPG):
                        P.op("pe", lambda e, pg=pg, g=g: e.matmul(out=pb[g][0:2, 0:130], lhsT=zP3[:, pg, g * 2:(g + 1) * 2], rhs=zVb4[:, pg, g, :], start=(pg == 0), stop=False), r=["zP", "zVb"], w=[PB[g]])
                    P.op("pe", lambda e, g=g, s_=s_: e.matmul(out=pb[g][0:2, 0:130], lhsT=zPn[0:1, g * 2:(g + 1) * 2], rhs=zvn4[0:1, s_, g, :], start=False, stop=True), r=["zPn", "zvn"], w=[PB[g]])
                for g in range(2):
                    P.op("dve", lambda e, g=g: e.reciprocal(out=zsm[0:2, 4 + g:5 + g], in_=pb[g][0:2, 128:129]), r=[PB[g]], w=["zsm4"])
                    P.op("dve", lambda e, g=g: e.tensor_scalar(out=zo[0:2, g * 128:(g + 1) * 128], in0=pb[g][0:2, 0:128], scalar1=zsm[0:2, 4 + g:5 + g], scalar2=None, op0=ALU.mult), r=[PB[g], "zsm4"], w=["zo"])
                for g in range(2):
                    P.op("pe", lambda e, g=g: e.transpose(out=pb[7][:, g * 2:(g + 1) * 2], in_=zo[0:2, g * 128:(g + 1) * 128], identity=ident[0:2, 0:2]), r=["zo", "ident"], w=[PB[7]])
                P.op("dve", lambda e, s_=s_: e.tensor_copy(out=zoT3[:, :, s_], in_=pb[7][:, 0:4]), r=[PB[7]], w=["zoT"])
            P.dma("sp", S["cT"][NT][:, 4:8, 0:NS], zoT3, r=["zoT"], w=["cTas"])

        if not os.environ.get('MK_NOD'):
            P.barrier()
            A.reset(persist0)
            zt = A.bf16(1024)
            P.op("pool", lambda e: e.memset(zt, 0.0), w=["zt"])
            if True:
                for ti in range(NT + 1):
                    if (ti == NT and os.environ.get('MK_NOTS')) or (ti < NT and (os.environ.get('MK_NOT') or ti >= int(os.environ.get("MK_TNT", NT)))):
                        P.dma("sp", S["cT"][ti][:, 4:8, :], zt[:, 0:512].rearrange("p (k t) -> p k t", k=4), r=["zt"], w=[f"cT{ti}"])
                    if ti == NT:
                        P.dma("sp", S["cT"][ti][:, 0:4, 16:128], zt[:, 0:448].rearrange("p (k t) -> p k t", k=4), r=["zt"], w=[f"cT{ti}"])
            g2_bc = A.f32(D); ga1_bc = A.f32(D); a2_bc = A.f32(D); sh2_bc = A.f32(D)
            a2_s = A.f32(D)
            P.dma("sp", g2_bc, I["g2"].to_broadcast([128, D]), w=["g2_bc"])
            P.dma("sp", ga1_bc, S["mod"][16:17, 2 * D:3 * D].to_broadcast([128, D]), r=["mod_scr"], w=["ga1_bc"])
            P.dma("sp", sh2_bc, S["mod"][16:17, 3 * D:4 * D].to_broadcast([128, D]), r=["mod_scr"], w=["sh2_bc"])
            P.dma("sp", a2_bc, S["mod"][16:17, 4 * D:5 * D].to_broadcast([128, D]), r=["mod_scr"], w=["a2_bc"])
            P.op("dve", lambda e: e.scalar_tensor_tensor(out=a2_bc, in0=a2_bc, scalar=1.0, in1=g2_bc, op0=ALU.add, op1=ALU.mult),
                 r=["a2_bc", "g2_bc"], w=["a2_bc"])
            P.op("dve", lambda e: e.scalar_tensor_tensor(out=a2_s[0:NS, :], in0=mod_sb[0:NS, 4 * D:5 * D], scalar=1.0, in1=g2_bc[0:NS, :], op0=ALU.add, op1=ALU.mult),
                 r=["mod_sb", "g2_bc"], w=["a2_s"])
            persistD = A.off
            w_out_b = A.bf16(8 * D); w_out3 = w_out_b.rearrange("p (k n) -> p k n", k=8)
            w_f_b = A.bf16(8 * 2 * DFF); w_f3 = w_f_b.rearrange("p (k n) -> p k n", k=8)
            persistD1 = A.off
            wst = [A.f32(8 * 512), A.f32(8 * 512)]
            for cb in range(2 + 11):
                ws3 = wst[cb % 2].rearrange("p (k n) -> p k n", k=8)
                if cb < 2:
                    src = I["w_out"][:, cb * 512:(cb + 1) * 512]; dst = w_out3[:, :, cb * 512:(cb + 1) * 512]; dn = "w_out_b"
                else:
                    src = I["w_ffn_in"][:, (cb - 2) * 512:(cb - 1) * 512]; dst = w_f3[:, :, (cb - 2) * 512:(cb - 1) * 512]; dn = "w_f_b"
                P.dma("sp", ws3, src.rearrange("(k p) n -> p k n", p=128), w=[f"wst{cb % 2}"])
                P.op("pool" if cb % 2 else "dve", lambda e, ws3=ws3, dst=dst: e.tensor_copy(out=dst, in_=ws3), r=[f"wst{cb % 2}"], w=[dn])
            P.seed_after_staging()
            A.reset(persistD1)
            xt = [A.f32(D), A.f32(D)]
            cTt = [A.bf16(8 * 128), A.bf16(8 * 128)]
            x1t = A.f32(D); h2 = A.f32(D); small = A.f32(16)
            h2T = A.bf16(8 * 512); h2T3 = h2T.rearrange("p (k t) -> p k t", k=8)
            uT = A.bf16(22 * 512); uT3 = uT.rearrange("p (f t) -> p f t", f=22)
            gsb = [A.f32(512), A.f32(512)]

            def rms_mod(rows, xin, xin_n, a_t, a_n, sh_t, sh_n, out, out_n):
                ss, rs = small[0:rows, 0:1], small[0:rows, 1:2]
                P.op("act", lambda e: e.activation(out=out, in_=xin, func=AF.Square, accum_out=ss), r=[xin_n], w=[out_n, "ss"])
                P.op("dve", lambda e: e.tensor_scalar(out=rs, in0=ss, scalar1=1.0 / D, scalar2=1e-6, op0=ALU.mult, op1=ALU.add), r=["ss"], w=["rs"])
                P.op("act", lambda e: e.activation(out=rs, in_=rs, func=AF.Sqrt), r=["rs"], w=["rs"])
                P.op("dve", lambda e: e.reciprocal(out=rs, in_=rs), r=["rs"], w=["rs"])
                P.op("dve", lambda e: e.scalar_tensor_tensor(out=out, in0=xin, scalar=rs, in1=a_t, op0=ALU.mult, op1=ALU.mult), r=[xin_n, "rs", a_n], w=[out_n])
                if sh_t is not None:
                    P.op("pool", lambda e: e.tensor_tensor(out=out, in0=out, in1=sh_t, op=ALU.add), r=[out_n, sh_n], w=[out_n])

            groups = [(g4, [(g4 * 4 + j, 128) for j in range(4)]) for g4 in range(4)] + [(4, [(NT, NS)])]
            for g4, tiles in groups:
                ntok = sum(r_ for _, r_ in tiles)
                col = 0
                for (ti, rows) in tiles:
                    it = ti
                    xb = xt[it % 2]; xn = f"xt{it % 2}"; cb_ = cTt[it % 2]; cn = f"cTt{it % 2}"
                    cT3 = cb_.rearrange("p (k t) -> p k t", k=8)
                    smp = (rows == NS)
                    P.dma("sp", xb[0:rows, :], I["xs"] if smp else I["x_own"][ti * 128:(ti + 1) * 128, :], w=[xn])
                    P.dma("sp", cT3, S["cT"][ti], r=[f"cT{ti}"], w=[cn])
                    ga1_t = mod_sb[0:NS, 2 * D:3 * D] if smp else ga1_bc
                    for hb in range(2):
                        for k in range(8):
                            P.op("pe", lambda e, k=k, hb=hb, cT3=cT3, rows=rows: e.matmul(out=pb[hb][0:rows, :], lhsT=cT3[:, k, 0:rows], rhs=w_out3[:, k, hb * 512:(hb + 1) * 512], start=(k == 0), stop=(k == 7)),
                                 r=[cn, "w_out_b"], w=[PB[hb]])
                        P.op("dve", lambda e, hb=hb, rows=rows, ga1_t=ga1_t: e.tensor_tensor(out=x1t[0:rows, hb * 512:(hb + 1) * 512], in0=pb[hb][0:rows, :], in1=ga1_t[0:rows, hb * 512:(hb + 1) * 512], op=ALU.mult),
                             r=[PB[hb], "ga1_bc", "mod_sb"], w=["x1t"])
                    P.op("pool", lambda e, rows=rows, xb=xb: e.tensor_tensor(out=x1t[0:rows, :], in0=x1t[0:rows, :], in1=xb[0:rows, :], op=ALU.add), r=["x1t", xn], w=["x1t"])
                    P.dma("sp", S["x1"][ti, 0:rows, :], x1t[0:rows, :], r=["x1t"], w=[f"x1_{ti}"])
                    if smp:
                        rms_mod(rows, x1t[0:rows, :], "x1t", a2_s[0:rows, :], "a2_s", mod_sb[0:rows, 3 * D:4 * D], "mod_sb", h2[0:rows, :], "h2")
                    else:
                        rms_mod(rows, x1t, "x1t", a2_bc, "a2_bc", sh2_bc, "sh2_bc", h2, "h2")
                    for k in range(8):
                        bank = 2 + k // 4
                        P.op("pe", lambda e, k=k, bank=bank, rows=rows: e.transpose(out=pb[bank][:, (k % 4) * 128:(k % 4) * 128 + rows], in_=h2[0:rows, k * 128:(k + 1) * 128], identity=ident[0:rows, 0:rows]),
                             r=["h2", "ident"], w=[PB[bank]])
                    for half_ in range(2):
                        evac(alt(), h2T3[:, half_ * 4:(half_ + 1) * 4, col:col + rows], pb[2 + half_][:, :].rearrange("p (k t) -> p k t", k=4)[:, :, 0:rows], [PB[2 + half_]], ["h2T"])
                    col += rows
                for fb in range(22):
                    for which in range(2):
                        bank = 4 + which * 2 + fb % 2
                        c0 = which * DFF + fb * 128
                        for k in range(8):
                            P.op("pe", lambda e, k=k, bank=bank, c0=c0, ntok=ntok: e.matmul(out=pb[bank][:, 0:ntok], lhsT=w_f3[:, k, c0:c0 + 128], rhs=h2T3[:, k, 0:ntok], start=(k == 0), stop=(k == 7)),
                                 r=["h2T", "w_f_b"], w=[PB[bank]])
                    gs_ = gsb[fb % 2]; gn = f"gsb{fb % 2}"
                    P.op("act", lambda e, fb=fb, gs_=gs_, ntok=ntok: e.activation(out=gs_[:, 0:ntok], in_=pb[4 + fb % 2][:, 0:ntok], func=AF.Silu), r=[PB[4 + fb % 2]], w=[gn])
                    P.op("dve", lambda e, fb=fb, gs_=gs_, ntok=ntok: e.tensor_tensor(out=uT3[:, fb, 0:ntok], in0=gs_[:, 0:ntok], in1=pb[6 + fb % 2][:, 0:ntok], op=ALU.mult),
                         r=[gn, PB[6 + fb % 2]], w=["uT"])
                P.dma("sp", S["uT"][g4], uT3, r=["uT"], w=[f"uT{g4}"])

            P.barrier()
            A.reset(persistD)
            ga2_bc = A.f32(D); gf_bc = A.f32(D)
            P.dma("sp", gf_bc, I["g_final"].to_broadcast([128, D]), w=["gf_bc"])
            P.dma("sp", ga2_bc, S["mod"][16:17, 5 * D:6 * D].to_broadcast([128, D]), r=["mod_scr"], w=["ga2_bc"])
            w_o_b = A.bf16(22 * D); w_o3 = w_o_b.rearrange("p (k n) -> p k n", k=22)
            persistD2 = A.off
            wst = [A.f32(22 * 256), A.f32(22 * 256)]
            for cb in range(4):
                ws3 = wst[cb % 2].rearrange("p (k n) -> p k n", k=22)
                P.dma("sp", ws3, I["w_ffn_out"][:, cb * 256:(cb + 1) * 256].rearrange("(k p) n -> p k n", p=128), w=[f"wst{cb % 2}"])
                P.op("pool" if cb % 2 else "dve", lambda e, ws3=ws3, cb=cb: e.tensor_copy(out=w_o3[:, :, cb * 256:(cb + 1) * 256], in_=ws3), r=[f"wst{cb % 2}"], w=["w_o_b"])
            P.seed_after_staging()
            A.reset(persistD2)
            uTg = [A.bf16(22 * 512), A.bf16(22 * 512)]
            x1b = [A.f32(D), A.f32(D)]
            x2 = A.f32(D); small = A.f32(16)
            yb = [A.f32(D), A.f32(D)]
            for g4, tiles in groups:
                ug = uTg[g4 % 2]; un = f"uTg{g4 % 2}"
                ug3 = ug.rearrange("p (f t) -> p f t", f=22)
                P.dma("sp", ug3, S["uT"][g4], r=[f"uT{g4}"], w=[un])
                col = 0
                for (ti, rows) in tiles:
                    smp = (rows == NS)
                    xb = x1b[ti % 2]; xn = f"x1b{ti % 2}"; yo = yb[ti % 2]; yn = f"yb{ti % 2}"
                    P.dma("sp", xb[0:rows, :], S["x1"][ti, 0:rows, :], r=[f"x1_{ti}"], w=[xn])
                    ga2_t = mod_sb[0:NS, 5 * D:6 * D] if smp else ga2_bc
                    for hb in range(2):
                        for kf in range(22):
                            P.op("pe", lambda e, kf=kf, hb=hb, rows=rows, col=col, ug3=ug3: e.matmul(out=pb[hb][0:rows, :], lhsT=ug3[:, kf, col:col + rows], rhs=w_o3[:, kf, hb * 512:(hb + 1) * 512], start=(kf == 0), stop=(kf == 21)),
                                 r=[un, "w_o_b"], w=[PB[hb]])
                        P.op("dve", lambda e, hb=hb, rows=rows, ga2_t=ga2_t: e.tensor_tensor(out=x2[0:rows, hb * 512:(hb + 1) * 512], in0=pb[hb][0:rows, :], in1=ga2_t[0:rows, hb * 512:(hb + 1) * 512], op=ALU.mult),
                             r=[PB[hb], "ga2_bc", "mod_sb"], w=["x2"])
                    P.op("pool", lambda e, rows=rows, xb=xb: e.tensor_tensor(out=x2[0:rows, :], in0=x2[0:rows, :], in1=xb[0:rows, :], op=ALU.add), r=["x2", xn], w=["x2"])
                    rms_mod(rows, x2[0:rows, :], "x2", gf_bc[0:rows, :], "gf_bc", None, None, yo[0:rows, :], yn)
                    P.dma("sp", O["y_s"] if smp else O["y_own"][ti * 128:(ti + 1) * 128, :], yo[0:rows, :], r=[yn])
                    col += rows

        P.build(ctx)
        global LAST_PROG
        LAST_PROG = P
    return nc


def rope_table(pos):
    pos = np.asarray(pos, np.float64)[:, None]
    invA = 500000.0 ** (-np.arange(16, dtype=np.float64) / 16)
    invI = 500000.0 ** (-np.arange(8, dtype=np.float64) / 8)
    angA = (pos.astype(np.float32) * invA.astype(np.float32)[None, :]).astype(np.float32)
    angI = (pos.astype(np.float32) * invI.astype(np.float32)[None, :]).astype(np.float32)
    t = np.concatenate([np.tile(np.cos(angA), (1, 4)), np.tile(np.sin(angA), (1, 4)),
                        np.tile(np.cos(angI), (1, 8)), np.tile(np.sin(angI), (1, 8))], axis=1)
    return np.ascontiguousarray(t.astype(np.float32))


_NC_CACHE = {}


def kernel(x_prompt, x_sample, c_prompt, c_sample, cache_k, cache_v, cache_idx_k, page_table, state_conv, state_ssm,
           w_ada, b_ada, g_norm1, w_in, w_conv, a_log, dt_bias, g_gdn_norm, w_out, g_norm2, w_ffn_in, w_ffn_out, g_final):
    f = lambda a: np.ascontiguousarray(np.asarray(a, dtype=np.float32))
    x_prompt = f(x_prompt); x_sample = f(x_sample)
    w_in_p = np.ascontiguousarray(f(w_in)[0][:, PERM])
    wc_p = np.ascontiguousarray(np.concatenate([f(w_conv)[0][:, 512:1536], f(w_conv)[0][:, 0:512]], axis=1))
    jj, cc = np.meshgrid(np.arange(128), np.arange(128), indexing="ij")
    GCONST = np.ascontiguousarray(np.concatenate([
        (jj <= cc).astype(np.float32),
        np.ones((128, 128), np.float32),
        np.where(cc >= jj, 1e4, 0.0).astype(np.float32),
        np.where(cc < jj, -1e4, 0.0).astype(np.float32),
        np.tile(np.eye(128, dtype=np.float32), (1, 4))], axis=1))
    CAUS = np.ascontiguousarray(np.where(np.arange(128)[None, :] > np.arange(128)[:, None], -30000.0, 0.0).astype(np.float32))
    pp = np.arange(128)
    TSEL = np.zeros((128, 257), np.float32)
    TSEL[:, :256] = np.tile((np.arange(8)[None, :] == (pp // 16)[:, None]).astype(np.float32), (1, 32))
    TSEL[:, 256] = pp % 16
    cik2 = f(cache_idx_k).reshape(-1, 64); ck2 = f(cache_k).reshape(-1, 256); cv2 = f(cache_v).reshape(-1, 256)
    EYE16 = np.ascontiguousarray(np.tile(np.eye(16, dtype=np.float32).reshape(1, 256), (128, 1)))
    in_maps = []
    for c in range(8):
        b, s = c // 2, c % 2
        own = slice(s * HALF, (s + 1) * HALF); oth = slice((1 - s) * HALF, (2 - s) * HALF)
        m = {
            "x_own": f(x_prompt[b, own]), "x_oth": f(x_prompt[b, oth]),
            "cin": f(np.concatenate([np.asarray(c_sample)[c * NS:(c + 1) * NS], np.asarray(c_prompt)[b:b + 1]], 0)),
            "xs": f(x_sample[c * NS:(c + 1) * NS, 0]),
            "w_ada": f(w_ada)[0], "b_ada": f(b_ada), "g1": f(g_norm1), "w_in": w_in_p, "w_conv": f(w_conv)[0],
            "a_log": f(a_log), "dt_bias": f(dt_bias), "g_gdn": f(g_gdn_norm), "w_out": f(w_out)[0], "g2": f(g_norm2),
            "w_ffn_in": f(w_ffn_in)[0], "w_ffn_out": f(w_ffn_out)[0], "g_final": f(g_final)[None, :],
            "tab_own": rope_table(np.arange(s * HALF, (s + 1) * HALF)), "tab_oth": rope_table(np.arange((1 - s) * HALF, (2 - s) * HALF)),
            "tab_s": rope_table(np.full(NS, 2048)),
            "flags": np.tile(np.array([[float(s), (s - 1) * 30000.0, 0, 0]], np.float32), (128, 1)),
            "ident": np.eye(128, dtype=np.float32),
            "state_conv": f(np.asarray(state_conv)[0, c * NS:(c + 1) * NS]),
            "gconst": GCONST, "wc_p": wc_p,
            "state_ssm": f(np.asarray(state_ssm)[0, c * NS:(c + 1) * NS]), "eye16": EYE16, "caus": CAUS,
            "pt": np.ascontiguousarray(np.asarray(page_table, np.int32)[c * NS:(c + 1) * NS].reshape(1, NS * 16)), "tsel": TSEL,
            "cache_ik": cik2, "cache_k": ck2, "cache_v": cv2,
        }
        if os.environ.get('MK_NOTS'):
            for k_ in ("cache_ik", "cache_k", "cache_v"):
                m.pop(k_)
        in_maps.append(m)
    if "nc" not in _NC_CACHE:
        _NC_CACHE["nc"] = build_program()
    res = run_bass_kernel_spmd(_NC_CACHE["nc"], in_maps, core_ids=list(range(8)))
    R = res.results
    B = 4
    y_prompt = np.zeros((B, T, D), np.float32); nk = np.zeros((1, B, T, 2, 128), np.float32); nv = np.zeros_like(nk)
    nik = np.zeros((1, B, T, 64), np.float32); nconv = np.zeros((1, B, 3, 1536), np.float32); nssm = np.zeros((1, B, 4, 128, 128), np.float32)
    y_s = np.zeros((128, 1, D), np.float32); ks = np.zeros((1, 128, 1, 2, 128), np.float32); vs = np.zeros_like(ks)
    iks = np.zeros((1, 128, 1, 64), np.float32); convs = np.zeros((1, 128, 3, 1536), np.float32); ssms = np.zeros((1, 128, 4, 128, 128), np.float32)
    for c in range(8):
        b, s = c // 2, c % 2
        own = slice(s * HALF, (s + 1) * HALF)
        r = R[c]
        y_prompt[b, own] = r["y_own"]; nk[0, b, own] = r["k_own"].reshape(HALF, 2, 128); nv[0, b, own] = r["v_own"].reshape(HALF, 2, 128)
        nik[0, b, own] = r["ik_own"]
        if s == 1:
            nconv[0, b] = r["conv_tail"]; nssm[0, b] = r["ssm_fin"]
        sl = slice(c * NS, (c + 1) * NS)
        y_s[sl, 0] = r["y_s"]; ks[0, sl, 0] = r["k_s"].reshape(NS, 2, 128); vs[0, sl, 0] = r["v_s"].reshape(NS, 2, 128)
        iks[0, sl, 0] = r["ik_s"]; convs[0, sl] = r["conv_s"]; ssms[0, sl] = r["ssm_s"]
    return (y_prompt, y_s, nk, nv, nik, nconv, nssm, ks, vs, iks, convs, ssms)
```

```python
import numpy as np
from contextlib import ExitStack
import concourse.bass as bass
import concourse.mybir as mybir
from concourse.bass_utils import run_bass_kernel_spmd

F32 = mybir.dt.float32
BF16 = mybir.dt.bfloat16
I32 = mybir.dt.int32
U32 = mybir.dt.uint32
AF = mybir.ActivationFunctionType
ALU = mybir.AluOpType
AX = mybir.AxisListType

import os
G_NT = int(os.environ.get("MK_GNT", "16"))
GSTOP = int(os.environ.get("MK_GSTOP", "99"))
NOS = int(os.environ.get("MK_NOS", "0"))
NOG = int(os.environ.get("MK_NOG", "0"))
ENGS = ["pe", "act", "dve", "pool", "sp"]
EPOCH = 12000
N_DMA_SEMS = 40
N_SW_SEMS = 12

D = 1024
T = 4096
HALF = 2048
NT = 16
NS = 16
INC = 3664
DFF = 2816
C_K, C_V, C_B, C_A, C_AK, C_AV, C_IK = 0, 512, 1024, 1028, 1032, 1288, 1544
N_OTH = 1608
C_Q, C_Z, C_AQ, C_IQ, C_IW = 1608, 2120, 2632, 3144, 3656
PERM = np.concatenate([np.arange(512, 1024), np.arange(1024, 1536), np.arange(2048, 2056),
                       np.arange(2568, 2824), np.arange(2824, 3080), np.arange(3592, 3656),
                       np.arange(0, 512), np.arange(1536, 2048), np.arange(2056, 2568),
                       np.arange(3080, 3592), np.arange(3656, 3664)])
GS_W = 2056
TABW = 256


class Prog:
    def __init__(self, nc):
        self.nc = nc
        self.ops = {e: [] for e in ENGS}
        self.res = {}
        self.seed = []
        self.dma_cum = [0] * (N_DMA_SEMS + N_SW_SEMS)
        self.dma_rr = 0
        self.sw_rr = 0

    def _deps(self, r, w):
        deps = []
        for name in r:
            st = self.res.get(name)
            if st and st[0] is not None:
                deps.append(st[0])
        for name in w:
            st = self.res.get(name)
            if st:
                if st[0] is not None:
                    deps.append(st[0])
                deps.extend(st[1])
            else:
                deps.extend(self.seed)
        return deps

    @staticmethod
    def _tkey(t):
        return (t[0], t[1])

    def _commit(self, tok, r, w):
        k = self._tkey(tok)
        for name in r:
            st = self.res.setdefault(name, [None, []])
            if not os.environ.get("MK_NOPRUNE"):
                st[1] = [t for t in st[1] if self._tkey(t) != k]
            st[1].append(tok)
        for name in w:
            self.res[name] = [tok, []]

    def op(self, eng, fn, r=(), w=()):
        deps = self._deps(r, w)
        tok = ("e", eng, len(self.ops[eng]))
        self.ops[eng].append({"deps": deps, "fn": fn, "dma": None, "sig": False})
        self._commit(tok, r, w)
        return tok

    def dma_raw(self, q, fn, r=(), w=(), sw=False):
        deps = self._deps(r, w)
        if sw:
            s = N_DMA_SEMS + self.sw_rr
            self.sw_rr = (self.sw_rr + 1) % N_SW_SEMS
        else:
            s = self.dma_rr
            self.dma_rr = (self.dma_rr + 1) % N_DMA_SEMS
        if self.dma_cum[s] > 0:
            deps.append(("d", s, self.dma_cum[s]))
        self.dma_cum[s] += 16
        tok = ("d", s, self.dma_cum[s])
        self.ops[q].append({"deps": deps, "fn": fn, "dma": (s, self.dma_cum[s]), "sig": False})
        self._commit(tok, r, w)
        return tok

    def dma(self, q, out, in_, r=(), w=(), **kw):
        return self.dma_raw(q, lambda e, out=out, in_=in_, kw=kw: e.dma_start(out=out, in_=in_, **kw), r, w)

    def seed_after_staging(self, names=("wst0", "wst1")):
        toks = list(self.seed)
        for n in names:
            st = self.res.get(n)
            if st:
                if st[0] is not None:
                    toks.append(st[0])
                toks.extend(st[1])
        self.seed = toks

    def barrier(self):
        deps_all = []
        for st in self.res.values():
            if st[0] is not None:
                deps_all.append(st[0])
            deps_all.extend(st[1])
        best = {}
        for t in ([] if os.environ.get("MK_NOPRUNE") else deps_all):
            k = self._tkey(t)
            if k not in best or best[k][2] < t[2]:
                best[k] = t
        for e in ENGS:
            for i in range(len(self.ops[e]) - 1, -1, -1):
                if self.ops[e][i]["dma"] is None:
                    best[("e", e)] = ("e", e, i)
                    break
        if not os.environ.get("MK_NOPRUNE"):
            deps_all = list(best.values())
        toks = []
        for e in ENGS:
            toks.append(("e", e, len(self.ops[e])))
            self.ops[e].append({"deps": list(deps_all), "fn": None, "dma": None, "sig": False})
        for e in ENGS:
            self.ops[e].append({"deps": list(toks), "fn": None, "dma": None, "sig": False})
        self.res = {}
        self.seed = []

    def build(self, ctx):
        nc = self.nc
        fin = [("d", s, c) for s, c in enumerate(self.dma_cum) if c > 0]
        self.ops["sp"].append({"deps": fin, "fn": None, "dma": None, "sig": False})
        for e in ENGS:
            for o in self.ops[e]:
                for d in o["deps"]:
                    if d[0] == "e":
                        if d[1] == "pe" and e == "pe":
                            continue
                        self.ops[d[1]][d[2]]["sig"] = True
        signo = {}
        nsig = {}
        for e in ENGS:
            c = 0
            for i, o in enumerate(self.ops[e]):
                if o["sig"]:
                    c += 1
                    signo[(e, i)] = c
            nsig[e] = c
        esem = {e: [ctx.enter_context(nc.semaphore(f"s_{e}_{k}")) for k in range(max(1, (nsig[e] + EPOCH - 1) // EPOCH))]
                for e in ENGS}
        dsem = [ctx.enter_context(nc.semaphore(f"s_dma_{k}")) for k in range(N_DMA_SEMS + N_SW_SEMS)]
        block = ctx.enter_context(nc.Block())
        eobj = {"pe": nc.tensor, "act": nc.scalar, "dve": nc.vector, "pool": nc.gpsimd, "sp": nc.sync}

        self.trace = {e: [] for e in ENGS}

        def body_for(e):
            def body(eng):
                known = {}
                tr = self.trace[e]
                for i, o in enumerate(self.ops[e]):
                    need = {}
                    for d in o["deps"]:
                        if d[0] == "e":
                            if d[1] == "pe" and e == "pe":
                                continue
                            key, val = ("e", d[1]), signo[(d[1], d[2])]
                        else:
                            key, val = ("d", d[1]), d[2]
                        if known.get(key, 0) >= val:
                            continue
                        if need.get(key, 0) < val:
                            need[key] = val
                    for key, val in need.items():
                        if key[0] == "e":
                            ep = (val - 1) // EPOCH
                            eng.wait_ge(esem[key[1]][ep], val - ep * EPOCH)
                            tr.append(("w", (key[1], ep), val - ep * EPOCH))
                        else:
                            eng.wait_ge(dsem[key[1]], val)
                            tr.append(("w", ("d", key[1]), val))
                        known[key] = val
                    if o["fn"] is None:
                        if o["sig"]:
                            sn = signo[(e, i)]
                            ep = (sn - 1) // EPOCH
                            eng.nop().then_inc(esem[e][ep], 1)
                            tr.append(("i", (e, ep), 1))
                        continue
                    ins = o["fn"](eng)
                    if o["dma"] is not None:
                        ins.then_inc(dsem[o["dma"][0]], 16)
                        tr.append(("i", ("d", o["dma"][0]), 16))
                    elif o["sig"]:
                        sn = signo[(e, i)]
                        ep = (sn - 1) // EPOCH
                        ins.then_inc(esem[e][ep], 1)
                        tr.append(("i", (e, ep), 1))
            return body

        block.tensor(body_for("pe"))
        block.scalar(body_for("act"))
        block.vector(body_for("dve"))
        block.gpsimd(body_for("pool"))
        block.sync(body_for("sp"))


class Arena:
    def __init__(self, t, n):
        self.t, self.n, self.off, self.uid = t, n, 0, 0

    def reset(self, to=0):
        self.off = to

    def f32(self, cols):
        a = self.t[:, self.off:self.off + cols]
        self.off += cols
        assert self.off <= self.n, ("arena overflow", self.off, self.n)
        return a

    def bf16(self, cols):
        c32 = (cols + 1) // 2
        a = self.t[:, self.off:self.off + c32].bitcast(BF16)
        self.off += c32
        assert self.off <= self.n, ("arena overflow", self.off, self.n)
        return a


def build_program(stage=99):
    nc = bass.Bass("TRN2", target_bir_lowering=False)
    dt_in = lambda name, shape, dt=F32: nc.dram_tensor(name, list(shape), dt, kind="ExternalInput").ap()
    dt_out = lambda name, shape, dt=F32: nc.dram_tensor(name, list(shape), dt, kind="ExternalOutput").ap()
    dt_scr = lambda name, shape, dt=F32: nc.dram_tensor(name, list(shape), dt, kind="Internal").ap()

    I = {}
    I["x_own"] = dt_in("x_own", [HALF, D]); I["x_oth"] = dt_in("x_oth", [HALF, D])
    I["cin"] = dt_in("cin", [17, D]); I["xs"] = dt_in("xs", [NS, D])
    I["w_ada"] = dt_in("w_ada", [D, 6 * D]); I["b_ada"] = dt_in("b_ada", [1, 6 * D])
    I["g1"] = dt_in("g1", [1, D]); I["w_in"] = dt_in("w_in", [D, INC])
    I["w_conv"] = dt_in("w_conv", [4, 1536]); I["a_log"] = dt_in("a_log", [1, 4]); I["dt_bias"] = dt_in("dt_bias", [1, 4])
    I["g_gdn"] = dt_in("g_gdn", [1, 128]); I["w_out"] = dt_in("w_out", [D, D]); I["g2"] = dt_in("g2", [1, D])
    I["w_ffn_in"] = dt_in("w_ffn_in", [D, 2 * DFF]); I["w_ffn_out"] = dt_in("w_ffn_out", [DFF, D]); I["g_final"] = dt_in("g_final", [1, D])
    I["tab_own"] = dt_in("tab_own", [HALF, TABW]); I["tab_oth"] = dt_in("tab_oth", [HALF, TABW]); I["tab_s"] = dt_in("tab_s", [NS, TABW])
    I["flags"] = dt_in("flags", [128, 4]); I["ident"] = dt_in("ident", [128, 128])
    I["state_conv"] = dt_in("state_conv", [NS, 3, 1536])
    I["gconst"] = dt_in("gconst", [128, 1024]); I["wc_p"] = dt_in("wc_p", [4, 1536])
    I["state_ssm"] = dt_in("state_ssm", [NS, 4, 128, 128]); I["eye16"] = dt_in("eye16", [128, 256])
    I["caus"] = dt_in("caus", [128, 128])
    I["pt"] = dt_in("pt", [1, NS * 16], I32); I["tsel"] = dt_in("tsel", [128, 257])
    NPHYS = 2560
    if not os.environ.get('MK_NOTS'):
        I["cache_ik"] = dt_in("cache_ik", [NPHYS * 128, 64]); I["cache_k"] = dt_in("cache_k", [NPHYS * 128, 256]); I["cache_v"] = dt_in("cache_v", [NPHYS * 128, 256])

    O = {}
    O["y_own"] = dt_out("y_own", [HALF, D]); O["k_own"] = dt_out("k_own", [HALF, 256]); O["v_own"] = dt_out("v_own", [HALF, 256])
    O["ik_own"] = dt_out("ik_own", [HALF, 64]); O["conv_tail"] = dt_out("conv_tail", [3, 1536]); O["ssm_fin"] = dt_out("ssm_fin", [4, 128, 128])
    O["y_s"] = dt_out("y_s", [NS, D]); O["k_s"] = dt_out("k_s", [NS, 256]); O["v_s"] = dt_out("v_s", [NS, 256]); O["ik_s"] = dt_out("ik_s", [NS, 64])
    O["conv_s"] = dt_out("conv_s", [NS, 3, 1536]); O["ssm_s"] = dt_out("ssm_s", [NS, 4, 128, 128])

    S = {}
    S["mod"] = dt_scr("mod_scr", [17, 6 * D])
    S["gs"] = dt_scr("gs_scr", [2, 3 + HALF, GS_W])
    S["qT"] = dt_scr("qT_scr", [NT, 128, 4, 128], BF16)
    S["iqT"] = dt_scr("iqT_scr", [NT, 128, 4, 128], BF16)
    S["cT"] = dt_scr("cT_scr", [NT + 1, 128, 8, 128], BF16)
    S["x1"] = dt_scr("x1_scr", [NT + 1, 128, D])
    S["uT"] = dt_scr("uT_scr", [5, 128, 22, 512], BF16)
    S["ps"] = dt_scr("ps_scr", [NS, INC])

    ctx = ExitStack()
    with ctx:
        ctx.enter_context(nc.allow_low_precision(reason="bf16 matmul operands, fp32 accumulate"))
        P = Prog(nc)
        ARN = 52800
        arena_t = ctx.enter_context(nc.sbuf_tensor("arena", [128, ARN], F32))
        A = Arena(arena_t, ARN)
        pb = [ctx.enter_context(nc.psum_tensor(f"pb{i}", [128, 512], F32))[:, :] for i in range(8)]
        PB = [f"pb{i}" for i in range(8)]
        cnt = [0]

        def alt():
            cnt[0] += 1
            return "act" if cnt[0] % 2 else "dve"

        def evac(eng, out, in_, r, w):
            if eng == "act":
                P.op("act", lambda e: e.activation(out=out, in_=in_, func=AF.Copy), r=r, w=w)
            else:
                P.op(eng, lambda e: e.tensor_copy(out=out, in_=in_), r=r, w=w)

        ident = A.f32(128)
        flags = A.f32(4)
        P.dma("sp", ident, I["ident"], w=["ident"])
        P.dma("sp", flags, I["flags"], w=["flags"])
        mod_sb = A.f32(6 * D)
        persist0 = A.off

        cs = A.f32(D)
        csT = A.f32(8 * 17)
        csT3 = csT.rearrange("p (k m) -> p k m", k=8)
        ones = A.f32(128)
        bstage = A.f32(512)
        P.op("dve", lambda e: e.memset(ones, 1.0), w=["ones"])
        P.dma("sp", cs[0:17, :], I["cin"], w=["cs"])
        P.op("act", lambda e: e.activation(out=cs[0:17, :], in_=cs[0:17, :], func=AF.Silu), r=["cs"], w=["cs"])
        for k in range(8):
            P.op("pe", lambda e, k=k: e.transpose(out=pb[0][:, k * 17:(k + 1) * 17], in_=cs[0:17, k * 128:(k + 1) * 128], identity=ident[0:17, 0:17]),
                 r=["cs", "ident"], w=[PB[0]])
        evac("dve", csT, pb[0][:, 0:8 * 17], [PB[0]], ["csT"])
        wst = [A.f32(8 * 512), A.f32(8 * 512)]
        for cb in range(12):
            ws = wst[cb % 2]
            ws3 = ws.rearrange("p (k n) -> p k n", k=8)
            P.dma("sp", ws3, I["w_ada"][:, cb * 512:(cb + 1) * 512].rearrange("(k p) n -> p k n", p=128), w=[f"wst{cb % 2}"])
            P.dma("sp", bstage[0:1, :], I["b_ada"][:, cb * 512:(cb + 1) * 512], w=["bstage"])
            bank = 1 + cb % 2
            for k in range(8):
                P.op("pe", lambda e, k=k, ws3=ws3, bank=bank: e.matmul(out=pb[bank][0:17, :], lhsT=csT3[:, k, :], rhs=ws3[:, k, :], start=(k == 0), stop=False),
                     r=["csT", f"wst{cb % 2}"], w=[PB[bank]])
            P.op("pe", lambda e, bank=bank: e.matmul(out=pb[bank][0:17, :], lhsT=ones[0:1, 0:17], rhs=bstage[0:1, :], start=False, stop=True),
                 r=["ones", "bstage"], w=[PB[bank]])
            evac("act", mod_sb[0:17, cb * 512:(cb + 1) * 512], pb[bank][0:17, :], [PB[bank]], ["mod_sb"])
        P.dma("sp", S["mod"], mod_sb[0:17, :], r=["mod_sb"], w=["mod_scr"])
        A.reset(persist0)
        def alloc_att():
            KT = A.bf16(2 * T); ikT = A.bf16(T); VA = A.bf16(32 * 2 * 130); iwabs = A.f32(NT * 8); iwsgn = A.f32(NT * 8)
            ksq_ = A.f32(64); qsq_ = A.f32(64)
            return KT, ikT, VA, iwabs, iwsgn, ksq_, qsq_
        OLDALLOC = bool(os.environ.get("MK_OLDALLOC"))
        if not OLDALLOC:
            KT, ikT, VA, iwabs, iwsgn, ksq, qsq = alloc_att()
        persist1 = A.off
        a1_bc = A.f32(D); sh1_bc = A.f32(D)
        g1_bc = A.f32(D); a1_s = A.f32(D)
        P.dma("sp", g1_bc, I["g1"].to_broadcast([128, D]), w=["g1_bc"])
        P.dma("sp", a1_bc, S["mod"][16:17, D:2 * D].to_broadcast([128, D]), r=["mod_scr"], w=["a1_bc"])
        P.dma("sp", sh1_bc, S["mod"][16:17, 0:D].to_broadcast([128, D]), r=["mod_scr"], w=["sh1_bc"])
        P.op("dve", lambda e: e.scalar_tensor_tensor(out=a1_bc, in0=a1_bc, scalar=1.0, in1=g1_bc, op0=ALU.add, op1=ALU.mult),
             r=["a1_bc", "g1_bc"], w=["a1_bc"])
        P.op("dve", lambda e: e.scalar_tensor_tensor(out=a1_s[0:NS, :], in0=mod_sb[0:NS, D:2 * D], scalar=1.0, in1=g1_bc[0:NS, :], op0=ALU.add, op1=ALU.mult),
             r=["mod_sb", "g1_bc"], w=["a1_s"])
        persistA = A.off

        w_in_b = A.bf16(8 * INC)
        w_in3 = w_in_b.rearrange("p (k n) -> p k n", k=8)
        if OLDALLOC:
            KT, ikT, VA, iwabs, iwsgn, ksq, qsq = alloc_att()
        KT3 = KT.rearrange("p (g s) -> p g s", g=2)
        VA4 = VA.rearrange("p (t g d) -> p t g d", t=32, g=2)
        persistB = A.off
        wst = [A.f32(8 * 512), A.f32(8 * 512)]
        nblk = (INC + 511) // 512
        for cb in range(nblk):
            c0, c1 = cb * 512, min(INC, cb * 512 + 512)
            ws3 = wst[cb % 2].rearrange("p (k n) -> p k n", k=8)
            P.dma("sp", ws3[:, :, 0:c1 - c0], I["w_in"][:, c0:c1].rearrange("(k p) n -> p k n", p=128), w=[f"wst{cb % 2}"])
            eng = "pool" if cb % 2 else "dve"
            P.op(eng, lambda e, ws3=ws3, c0=c0, c1=c1: e.tensor_copy(out=w_in3[:, :, c0:c1], in_=ws3[:, :, 0:c1 - c0]),
                 r=[f"wst{cb % 2}"], w=["w_in_b"])
        P.seed_after_staging()
        A.reset(persistB)
        xt = [A.f32(D), A.f32(D)]
        hh = A.f32(D)
        sq = A.f32(D)
        hT = [A.bf16(8 * 128), A.bf16(8 * 128)]
        Psb_l = [A.f32(INC), A.f32(INC)]
        tab = [A.f32(TABW), A.f32(TABW)]
        small = A.f32(16)
        rt = [A.f32(128) for _ in range(4)]
        ikd = A.f32(128)
        tstage = [A.bf16(4 * 128), A.bf16(4 * 128)]
        zrow = A.f32(GS_W)
        P.op("pool", lambda e: e.memset(zrow[0:3, :], 0.0), w=["zrow"])
        P.dma("sp", S["gs"][0, 0:3, :], zrow[0:3, :], r=["zrow"], w=["gs_pre0"])

        def rope(Psb, PN, rows, base, H, Dh, half, tb, coff, soff, tname):
            xv = Psb[0:rows, base:base + H * Dh].rearrange("p (h d) -> p h d", d=Dh)
            x1, x2 = xv[:, :, 0:half], xv[:, :, half:2 * half]
            cosv = tb[0:rows, coff:coff + H * half].rearrange("p (h i) -> p h i", i=half)
            sinv = tb[0:rows, soff:soff + H * half].rearrange("p (h i) -> p h i", i=half)
            t = [r_[0:rows, 0:H * half].rearrange("p (h i) -> p h i", i=half) for r_ in rt]
            P.op("dve", lambda e: e.tensor_tensor(out=t[0], in0=x1, in1=cosv, op=ALU.mult), r=[PN, tname], w=["rt0"])
            P.op("pool", lambda e: e.tensor_tensor(out=t[1], in0=x2, in1=sinv, op=ALU.mult), r=[PN, tname], w=["rt1"])
            P.op("dve", lambda e: e.tensor_tensor(out=t[2], in0=x2, in1=cosv, op=ALU.mult), r=[PN, tname], w=["rt2"])
            P.op("pool", lambda e: e.tensor_tensor(out=t[3], in0=x1, in1=sinv, op=ALU.mult), r=[PN, tname], w=["rt3"])
            P.op("dve", lambda e: e.tensor_tensor(out=x1, in0=t[0], in1=t[1], op=ALU.subtract), r=["rt0", "rt1", PN], w=[PN])
            P.op("dve", lambda e: e.tensor_tensor(out=x2, in0=t[2], in1=t[3], op=ALU.add), r=["rt2", "rt3", PN], w=[PN])

        itc = [0]

        def proj_tile(mode, ti):
            it = itc[0]; itc[0] += 1
            rows = NS if mode == 2 else 128
            Psb = Psb_l[it % 2]; PN = f"Psb{it % 2}"
            xb = xt[it % 2]; xn = f"xt{it % 2}"
            tb = tab[it % 2]; tn = f"tab{it % 2}"
            hTb = hT[it % 2]; hTn = f"hT{it % 2}"
            hT3 = hTb.rearrange("p (k t) -> p k t", k=8)
            if mode == 2:
                xsrc, tsrc = I["xs"], I["tab_s"]
                a_t, a_n, sh_t, sh_n = a1_s, "a1_s", mod_sb[:, 0:D], "mod_sb"
                ncols = INC
            else:
                xsrc = (I["x_oth"] if mode == 0 else I["x_own"])[ti * 128:(ti + 1) * 128, :]
                tsrc = (I["tab_oth"] if mode == 0 else I["tab_own"])[ti * 128:(ti + 1) * 128, :]
                a_t, a_n, sh_t, sh_n = a1_bc, "a1_bc", sh1_bc, "sh1_bc"
                ncols = INC if mode == 1 else (C_Q + 512 if ti == NT - 1 else N_OTH)
            P.dma("sp", xb[0:rows, :], xsrc, w=[xn])
            P.dma("sp", tb[0:rows, :], tsrc, w=[tn])
            ss, rs = small[0:rows, 0:1], small[0:rows, 1:2]
            P.op("act", lambda e: e.activation(out=sq[0:rows, :], in_=xb[0:rows, :], func=AF.Square, accum_out=ss), r=[xn], w=["sq", "ss"])
            P.op("dve", lambda e: e.tensor_scalar(out=rs, in0=ss, scalar1=1.0 / D, scalar2=1e-6, op0=ALU.mult, op1=ALU.add), r=["ss"], w=["rs"])
            P.op("act", lambda e: e.activation(out=rs, in_=rs, func=AF.Sqrt), r=["rs"], w=["rs"])
            P.op("dve", lambda e: e.reciprocal(out=rs, in_=rs), r=["rs"], w=["rs"])
            P.op("dve", lambda e: e.scalar_tensor_tensor(out=hh[0:rows, :], in0=xb[0:rows, :], scalar=rs, in1=a_t[0:rows, :], op0=ALU.mult, op1=ALU.mult),
                 r=[xn, "rs", a_n], w=["hh"])
            P.op("pool", lambda e: e.tensor_tensor(out=hh[0:rows, :], in0=hh[0:rows, :], in1=sh_t[0:rows, :], op=ALU.add), r=["hh", sh_n], w=["hh"])
            for k in range(8):
                bank = k // 4
                P.op("pe", lambda e, k=k, bank=bank: e.transpose(out=pb[bank][:, (k % 4) * 128:(k % 4) * 128 + rows], in_=hh[0:rows, k * 128:(k + 1) * 128], identity=ident[0:rows, 0:rows]),
                     r=["hh", "ident"], w=[PB[bank]])
            if rows == 128:
                evac("act", hTb[:, 0:512], pb[0], [PB[0]], [hTn])
                evac("dve", hTb[:, 512:1024], pb[1], [PB[1]], [hTn])
            else:
                for half_ in range(2):
                    evac("act" if half_ == 0 else "dve", hT3[:, half_ * 4:(half_ + 1) * 4, 0:rows], pb[half_].rearrange("p (k t) -> p k t", k=4)[:, :, 0:rows], [PB[half_]], [hTn])
            nb = (ncols + 511) // 512
            for cb in range(nb):
                c0, c1 = cb * 512, min(ncols, cb * 512 + 512)
                bank = 2 + cb % 4
                for k in range(8):
                    P.op("pe", lambda e, k=k, bank=bank, c0=c0, c1=c1: e.matmul(out=pb[bank][0:rows, 0:c1 - c0], lhsT=hT3[:, k, 0:rows], rhs=w_in3[:, k, c0:c1], start=(k == 0), stop=(k == 7)),
                         r=[hTn, "w_in_b"], w=[PB[bank]])
                evac(alt(), Psb[0:rows, c0:c1], pb[bank][0:rows, 0:c1 - c0], [PB[bank]], [PN])
            def stage2():
                if mode == 2:
                    cvs = sq[0:rows, :]
                    for j in range(2):
                        for hh_ in range(2):
                            P.dma("sp", sq[0:rows, 0:768], I["state_conv"][:, 1 + j, hh_ * 768:(hh_ + 1) * 768], w=["sq"])
                            P.dma("sp", O["conv_s"][:, j, hh_ * 768:(hh_ + 1) * 768], sq[0:rows, 0:768], r=["sq"])
                    P.dma("sp", O["conv_s"][:, 2, 0:512], Psb[0:rows, C_Q:C_Q + 512], r=[PN])
                    P.dma("sp", O["conv_s"][:, 2, 512:1536], Psb[0:rows, 0:1024], r=[PN])
                else:
                    row0 = 3 + ti * 128
                    P.dma("sp", S["gs"][mode, row0:row0 + 128, 0:1032], Psb[:, 0:1032], r=[PN], w=[f"gs{mode}_{ti}"])
                    if mode == 1:
                        P.dma("sp", S["gs"][mode, row0:row0 + 128, 1032:2056], Psb[:, C_Q:C_Q + 1024], r=[PN], w=[f"gs{mode}_{ti}"])
                    elif ti == NT - 1:
                        P.dma("sp", S["gs"][mode, row0:row0 + 128, 1032:1544], Psb[:, C_Q:C_Q + 512], r=[PN], w=[f"gs{mode}_{ti}"])
                rope(Psb, PN, rows, C_AK, 2, 128, 16, tb, 0, 64, tn)
                rope(Psb, PN, rows, C_IK, 1, 64, 8, tb, 128, 192, tn)
                if mode >= 1:
                    rope(Psb, PN, rows, C_AQ, 4, 128, 16, tb, 0, 64, tn)
                    rope(Psb, PN, rows, C_IQ, 8, 64, 8, tb, 128, 192, tn)
                if mode == 2:
                    P.dma("sp", O["k_s"], Psb[0:rows, C_AK:C_AK + 256], r=[PN])
                    P.dma("sp", O["v_s"], Psb[0:rows, C_AV:C_AV + 256], r=[PN])
                    P.dma("sp", O["ik_s"], Psb[0:rows, C_IK:C_IK + 64], r=[PN])
                    P.dma("sp", S["ps"], Psb[0:rows, :], r=[PN], w=["ps_scr"])
                    return
                if mode == 1:
                    P.dma("sp", O["k_own"][ti * 128:(ti + 1) * 128, :], Psb[:, C_AK:C_AK + 256], r=[PN])
                    P.dma("sp", O["v_own"][ti * 128:(ti + 1) * 128, :], Psb[:, C_AV:C_AV + 256], r=[PN])
                    P.dma("sp", O["ik_own"][ti * 128:(ti + 1) * 128, :], Psb[:, C_IK:C_IK + 64], r=[PN])
                slot = (0 if mode == 0 else 16) + ti
                for g in range(2):
                    P.op("act", lambda e, g=g: e.activation(out=sq[:, 0:128], in_=Psb[:, C_AK + g * 128:C_AK + (g + 1) * 128], func=AF.Square, accum_out=ksq[:, slot * 2 + g:slot * 2 + g + 1]),
                         r=[PN, "a1_bc"], w=["sq", "ksq"])
                if mode == 1:
                    for h in range(4):
                        P.op("act", lambda e, h=h: e.activation(out=sq[:, 0:128], in_=Psb[:, C_AQ + h * 128:C_AQ + (h + 1) * 128], func=AF.Square, accum_out=qsq[:, ti * 4 + h:ti * 4 + h + 1]),
                             r=[PN, "a1_bc"], w=["sq", "qsq"])
                P.op("dve", lambda e: e.tensor_copy(out=ikd[:, 0:64], in_=Psb[:, C_IK:C_IK + 64]), r=[PN], w=["ikd"])
                P.op("pool", lambda e: e.tensor_copy(out=ikd[:, 64:128], in_=Psb[:, C_IK:C_IK + 64]), r=[PN], w=["ikd"])
                for g in range(2):
                    P.op("pe", lambda e, g=g: e.transpose(out=pb[6][:, g * 128:(g + 1) * 128], in_=Psb[:, C_AK + g * 128:C_AK + (g + 1) * 128], identity=ident),
                         r=[PN, "ident"], w=[PB[6]])
                P.op("pe", lambda e: e.transpose(out=pb[6][:, 256:384], in_=ikd, identity=ident), r=["ikd", "ident"], w=[PB[6]])
                for g in range(2):
                    evac(alt(), KT3[:, g, slot * 128:(slot + 1) * 128], pb[6][:, g * 128:(g + 1) * 128], [PB[6]], ["KT"])
                evac(alt(), ikT[:, slot * 128:(slot + 1) * 128], pb[6][:, 256:384], [PB[6]], ["ikT"])
                P.op("pool", lambda e: e.memset(VA4[:, slot, :, 128:130], 1.0), r=["a1_bc"], w=["VA"])
                P.op("act", lambda e: e.activation(out=VA4[:, slot, :, 0:128], in_=Psb[:, C_AV:C_AV + 256].rearrange("p (g d) -> p g d", g=2), func=AF.Copy),
                     r=[PN], w=["VA"])
                if mode == 1:
                    ts_ = tstage[0]; tsn = "tstage0"
                    for h in range(4):
                        P.op("pe", lambda e, h=h: e.transpose(out=pb[7][:, h * 128:(h + 1) * 128], in_=Psb[:, C_AQ + h * 128:C_AQ + (h + 1) * 128], identity=ident),
                             r=[PN, "ident"], w=[PB[7]])
                    evac(alt(), ts_, pb[7], [PB[7]], [tsn])
                    P.dma("sp", S["qT"][ti], ts_.rearrange("p (h t) -> p h t", h=4), r=[tsn], w=[f"qT{ti}"])
                    ts2 = tstage[1]; tsn2 = "tstage1"
                    for h in range(4):
                        P.op("pe", lambda e, h=h: e.transpose(out=pb[7][:, h * 128:(h + 1) * 128], in_=Psb[:, C_IQ + h * 128:C_IQ + (h + 1) * 128], identity=ident),
                             r=[PN, "ident"], w=[PB[7]])
                    evac(alt(), ts2, pb[7], [PB[7]], [tsn2])
                    P.dma("sp", S["iqT"][ti], ts2.rearrange("p (h t) -> p h t", h=4), r=[tsn2], w=[f"iqT{ti}"])
                    P.op("act", lambda e: e.activation(out=iwabs[:, ti * 8:(ti + 1) * 8], in_=Psb[:, C_IW:C_IW + 8], func=AF.Abs, scale=8 ** -0.5),
                         r=[PN], w=["iwabs"])
                    P.op("act", lambda e: e.activation(out=iwsgn[:, ti * 8:(ti + 1) * 8], in_=Psb[:, C_IW:C_IW + 8], func=AF.Sign), r=[PN], w=["iwsgn"])
                    if ti == NT - 1:
                        P.dma("sp", O["conv_tail"][:, 0:512], S["gs"][1, 3 + HALF - 3:3 + HALF, 1032:1544], r=[f"gs1_{ti}"])
                        P.dma("sp", O["conv_tail"][:, 512:1536], S["gs"][1, 3 + HALF - 3:3 + HALF, 0:1024], r=[f"gs1_{ti}"])

            return stage2

        NA0 = int(os.environ.get("MK_NA0", NT)); NA1 = int(os.environ.get("MK_NA1", NT))
        tiles_a = [(0, ti) for ti in range(NT - NA0, NT)] + [(1, ti) for ti in range(NA1)] + ([] if NOS else [(2, 0)])
        pend = None
        for (m_, ti) in tiles_a:
            nxt = proj_tile(m_, ti)
            if pend is not None:
                pend()
            pend = nxt
        if pend is not None:
            pend()

        if not NOG:
            P.barrier()
            A.reset(persist1)
            cst = A.f32(1024)
            TRIU, ONESM, MASKL, MASKU = [cst[:, i * 128:(i + 1) * 128] for i in range(4)]
            ident4 = cst[:, 512:1024]
            P.dma("sp", cst, I["gconst"], w=["cst"])
            wc = [A.f32(1536) for _ in range(4)]
            for i in range(4):
                P.dma("sp", wc[i], I["wc_p"][i:i + 1, :].to_broadcast([128, 1536]), w=[f"wc{i}"])
            dtb = A.f32(4); negA = A.f32(4); ggd = A.f32(512)
            P.dma("sp", dtb, I["dt_bias"].to_broadcast([128, 4]), w=["dtb"])
            P.dma("sp", negA, I["a_log"].to_broadcast([128, 4]), w=["negA"])
            for h in range(4):
                P.dma("sp", ggd[:, h * 128:(h + 1) * 128], I["g_gdn"].to_broadcast([128, 128]), w=["ggd"])
            P.op("act", lambda e: e.activation(out=negA, in_=negA, func=AF.Exp), r=["negA"], w=["negA"])
            P.op("dve", lambda e: e.tensor_scalar(out=negA, in0=negA, scalar1=-1.0, scalar2=None, op0=ALU.mult), r=["negA"], w=["negA"])
            gs_base = A.off
            Sst = A.f32(512)
            P.op("dve", lambda e: e.memset(Sst, 0.0), w=["S"])
            X = [A.f32(GS_W) for _ in range(4)]
            cv = A.f32(1536); tmpa = A.f32(1536); tmpb = A.f32(1536)
            sm = A.f32(64)
            kn = A.f32(512); kt = A.f32(512); vb = A.f32(512); qn = A.f32(512); qt = A.f32(512)
            knT = A.f32(512); qnT = A.f32(512); qtT = A.f32(512)
            Dg = A.f32(512); dec = A.f32(512); decT = A.f32(512)
            Pbuf = [A.f32(512), A.f32(512)]; PTbuf = [A.f32(512), A.f32(512)]
            Wm = A.f32(512); ATm = A.f32(512); Rm = A.f32(512); vnew = A.f32(512); o_sb = A.f32(512); og = A.f32(512); szb = A.f32(512)
            cTst = A.bf16(512)
            pre = A.f32(GS_W)
            H4 = lambda ap, h: ap[:, h * 128:(h + 1) * 128]
            sc = lambda lo, h: sm[:, lo + h:lo + h + 1]

            def ts_mul(eng, out, in0, scal, r, w):
                if eng == "pool":
                    P.op("pool", lambda e: e.tensor_scalar(out=out, in0=in0, scalar1=scal, scalar2=1.0, op0=ALU.mult, op1=ALU.mult), r=r, w=w)
                else:
                    P.op("dve", lambda e: e.tensor_scalar(out=out, in0=in0, scalar1=scal, scalar2=None, op0=ALU.mult), r=r, w=w)

            for hf in range(2):
                own = (hf == 1)
                if own:
                    P.op("dve", lambda e: e.tensor_scalar(out=Sst, in0=Sst, scalar1=flags[:, 0:1], scalar2=None, op0=ALU.mult), r=["S", "flags"], w=["S"])
                    P.op("dve", lambda e: e.memset(pre[0:3, :], 0.0), w=["pre"])
                    P.dma("sp", pre[0:3, 0:1544], S["gs"][0, HALF:HALF + 3, 0:1544], w=["pre"])
                    P.op("dve", lambda e: e.tensor_scalar(out=pre[0:3, :], in0=pre[0:3, :], scalar1=flags[0:3, 0:1], scalar2=None, op0=ALU.mult), r=["pre", "flags"], w=["pre"])
                    P.dma("sp", S["gs"][1, 0:3, :], pre[0:3, :], r=["pre"], w=["gs_pre1"])
                segs = [(0, 1024, 0), (1024, 1536, 1032)] if own else [(0, 1024, 0)]
                nsc = 8 if own else 4
                for ti in range(G_NT):
                    W_ = GS_W if own else 1032
                    for i in range(4):
                        P.dma("sp", X[i][:, 0:W_], S["gs"][hf, ti * 128 + i:ti * 128 + i + 128, 0:W_], r=(["gs_pre1"] if (own and ti == 0) else []), w=[f"X{i}"])
                    for (d0, d1, s0) in segs:
                        n = d1 - d0
                        P.op("dve", lambda e, d0=d0, d1=d1, s0=s0, n=n: e.tensor_tensor(out=cv[:, d0:d1], in0=X[0][:, s0:s0 + n], in1=wc[0][:, d0:d1], op=ALU.mult), r=["X0", "wc0"], w=["cv"])
                        for i in range(1, 4):
                            tb_, tbn = (tmpa, "tmpa") if i % 2 else (tmpb, "tmpb")
                            P.op("pool", lambda e, i=i, d0=d0, d1=d1, s0=s0, n=n, tb_=tb_: e.tensor_tensor(out=tb_[:, d0:d1], in0=X[i][:, s0:s0 + n], in1=wc[i][:, d0:d1], op=ALU.mult), r=[f"X{i}", f"wc{i}"], w=[tbn])
                            P.op("dve", lambda e, d0=d0, d1=d1, tb_=tb_: e.tensor_tensor(out=cv[:, d0:d1], in0=cv[:, d0:d1], in1=tb_[:, d0:d1], op=ALU.add), r=["cv", tbn], w=["cv"])
                        P.op("act", lambda e, d0=d0, d1=d1: e.activation(out=cv[:, d0:d1], in_=cv[:, d0:d1], func=AF.Silu), r=["cv"], w=["cv"])
                    if GSTOP <= 1:
                        continue
                    P.op("pool", lambda e: e.tensor_tensor(out=tmpa[:, 0:512], in0=cv[:, 0:512], in1=cv[:, 0:512], op=ALU.mult), r=["cv"], w=["tmpa"])
                    P.op("dve", lambda e: e.tensor_reduce(out=sm[:, 0:4], in_=tmpa[:, 0:512].rearrange("p (h d) -> p h d", h=4), axis=AX.X, op=ALU.add), r=["tmpa"], w=["sm_ss"])
                    if own:
                        P.op("pool", lambda e: e.tensor_tensor(out=tmpb[:, 0:512], in0=cv[:, 1024:1536], in1=cv[:, 1024:1536], op=ALU.mult), r=["cv"], w=["tmpb"])
                        P.op("dve", lambda e: e.tensor_reduce(out=sm[:, 4:8], in_=tmpb[:, 0:512].rearrange("p (h d) -> p h d", h=4), axis=AX.X, op=ALU.add), r=["tmpb"], w=["sm_ss"])
                    P.op("dve", lambda e, nsc=nsc: e.tensor_scalar(out=sm[:, 0:nsc], in0=sm[:, 0:nsc], scalar1=1e-6, scalar2=None, op0=ALU.add), r=["sm_ss"], w=["sm_ss"])
                    P.op("act", lambda e, nsc=nsc: e.activation(out=sm[:, 0:nsc], in_=sm[:, 0:nsc], func=AF.Sqrt), r=["sm_ss"], w=["sm_ss"])
                    P.op("dve", lambda e, nsc=nsc: e.reciprocal(out=sm[:, 0:nsc], in_=sm[:, 0:nsc]), r=["sm_ss"], w=["sm_ss"])
                    if own:
                        P.op("dve", lambda e: e.tensor_scalar(out=sm[:, 4:8], in0=sm[:, 4:8], scalar1=128 ** -0.5, scalar2=None, op0=ALU.mult), r=["sm_ss"], w=["sm_ss"])
                    P.op("act", lambda e: e.activation(out=sm[:, 8:12], in_=X[3][:, 1024:1028], func=AF.Sigmoid), r=["X3"], w=["sm_b"])
                    P.op("dve", lambda e: e.tensor_tensor(out=sm[:, 12:16], in0=X[3][:, 1028:1032], in1=dtb, op=ALU.add), r=["X3", "dtb"], w=["sm_g"])
                    P.op("act", lambda e: e.activation(out=sm[:, 12:16], in_=sm[:, 12:16], func=AF.Exp), r=["sm_g"], w=["sm_g"])
                    P.op("act", lambda e: e.activation(out=sm[:, 12:16], in_=sm[:, 12:16], func=AF.Ln, bias=1.0), r=["sm_g"], w=["sm_g"])
                    P.op("dve", lambda e: e.tensor_tensor(out=sm[:, 12:16], in0=sm[:, 12:16], in1=negA, op=ALU.mult), r=["sm_g", "negA"], w=["sm_g"])
                    P.op("pe", lambda e: e.matmul(out=pb[3][:, 0:4], lhsT=TRIU, rhs=sm[:, 12:16], start=True, stop=True), r=["cst", "sm_g"], w=[PB[3]])
                    P.op("pe", lambda e: e.matmul(out=pb[3][:, 4:8], lhsT=ONESM, rhs=sm[:, 12:16], start=True, stop=True), r=["cst", "sm_g"], w=[PB[3]])
                    P.op("dve", lambda e: e.tensor_copy(out=sm[:, 16:24], in_=pb[3][:, 0:8]), r=[PB[3]], w=["sm_gc"])
                    P.op("dve", lambda e: e.tensor_copy(out=sm[:, 24:28], in_=sm[:, 16:20]), r=["sm_gc"], w=["sm_e"])
                    P.op("dve", lambda e: e.tensor_tensor(out=sm[:, 28:32], in0=sm[:, 20:24], in1=sm[:, 16:20], op=ALU.subtract), r=["sm_gc"], w=["sm_e"])
                    P.op("dve", lambda e: e.tensor_copy(out=sm[:, 32:36], in_=sm[:, 20:24]), r=["sm_gc"], w=["sm_e"])
                    P.op("act", lambda e: e.activation(out=sm[:, 24:36], in_=sm[:, 24:36], func=AF.Exp), r=["sm_e"], w=["sm_e"])
                    P.op("dve", lambda e: e.scalar_tensor_tensor(out=sm[:, 36:40], in0=sm[:, 8:12], scalar=-1.0, in1=sm[:, 24:28], op0=ALU.mult, op1=ALU.mult), r=["sm_b", "sm_e"], w=["sm_x"])
                    P.op("dve", lambda e: e.tensor_scalar(out=sm[:, 40:44], in0=sm[:, 16:20], scalar1=-1.0, scalar2=None, op0=ALU.mult), r=["sm_gc"], w=["sm_x"])
                    P.op("dve", lambda e: e.tensor_scalar(out=sm[:, 44:48], in0=sm[:, 8:12], scalar1=-1.0, scalar2=None, op0=ALU.mult), r=["sm_b"], w=["sm_x"])
                    if own:
                        P.op("dve", lambda e: e.tensor_tensor(out=sm[:, 48:52], in0=sm[:, 4:8], in1=sm[:, 24:28], op=ALU.mult), r=["sm_ss", "sm_e"], w=["sm_x"])
                    if GSTOP <= 2:
                        continue
                    for h in range(4):
                        ts_mul("dve", H4(kn, h), cv[:, h * 128:(h + 1) * 128], sc(0, h), ["cv", "sm_ss"], ["kn"])
                        ts_mul("pool", H4(kt, h), H4(kn, h), sc(28, h), ["kn", "sm_e"], ["kt"])
                        ts_mul("pool", H4(vb, h), cv[:, 512 + h * 128:512 + (h + 1) * 128], sc(8, h), ["cv", "sm_b"], ["vb"])
                        if own:
                            ts_mul("dve", H4(qn, h), cv[:, 1024 + h * 128:1024 + (h + 1) * 128], sc(4, h), ["cv", "sm_ss"], ["qn"])
                            ts_mul("pool", H4(qt, h), cv[:, 1024 + h * 128:1024 + (h + 1) * 128], sc(48, h), ["cv", "sm_x"], ["qt"])
                    for (src, sn, bank, dst, dn, eng) in ([(kn, "kn", 0, knT, "knT", "act")] + ([(qn, "qn", 1, qnT, "qnT", "dve"), (qt, "qt", 2, qtT, "qtT", "act")] if own else [])):
                        for h in range(4):
                            P.op("pe", lambda e, h=h, src=src, bank=bank: e.transpose(out=H4(pb[bank], h), in_=H4(src, h), identity=ident), r=[sn, "ident"], w=[PB[bank]])
                        evac(eng, dst, pb[bank], [PB[bank]], [dn])
                    if GSTOP <= 3:
                        continue
                    for h in range(4):
                        P.op("pe", lambda e, h=h: e.matmul(out=H4(pb[0], h), lhsT=H4(knT, h), rhs=H4(knT, h), start=True, stop=True), r=["knT"], w=[PB[0]])
                        P.op("pool", lambda e, h=h: e.tensor_scalar(out=H4(Dg, h), in0=ident, scalar1=sc(16, h), scalar2=1.0, op0=ALU.mult, op1=ALU.mult), r=["ident", "sm_gc"], w=["Dg"])
                    for h in range(4):
                        P.op("pe", lambda e, h=h: e.matmul(out=H4(pb[1], h), lhsT=ONESM, rhs=H4(Dg, h), start=True, stop=False), r=["cst", "Dg"], w=[PB[1]])
                        P.op("pe", lambda e, h=h: e.matmul(out=H4(pb[1], h), lhsT=ident, rhs=MASKL, start=False, stop=True), r=["cst", "ident"], w=[PB[1]])
                        P.op("act", lambda e, h=h: e.activation(out=H4(dec, h), in_=H4(pb[1], h), func=AF.Exp, scale=-1.0, bias=sc(16, h)), r=[PB[1], "sm_gc"], w=["dec"])
                        P.op("dve", lambda e, h=h: e.scalar_tensor_tensor(out=H4(Pbuf[0], h), in0=H4(pb[0], h), scalar=sc(44, h), in1=H4(dec, h), op0=ALU.mult, op1=ALU.mult),
                             r=[PB[0], "sm_x", "dec"], w=["P0"])
                    if own:
                        for h in range(4):
                            P.op("pe", lambda e, h=h: e.matmul(out=H4(pb[2], h), lhsT=ONESM, rhs=H4(Dg, h), start=True, stop=False), r=["cst", "Dg"], w=[PB[2]])
                            P.op("pe", lambda e, h=h: e.matmul(out=H4(pb[2], h), lhsT=ident, rhs=MASKU, start=False, stop=True), r=["cst", "ident"], w=[PB[2]])
                            P.op("act", lambda e, h=h: e.activation(out=H4(decT, h), in_=H4(pb[2], h), func=AF.Exp, scale=1.0, bias=sc(40, h)), r=[PB[2], "sm_x"], w=["decT"])
                            P.op("pe", lambda e, h=h: e.matmul(out=H4(pb[3], h), lhsT=H4(knT, h), rhs=H4(qnT, h), start=True, stop=True), r=["knT", "qnT"], w=[PB[3]])
                            P.op("dve", lambda e, h=h: e.tensor_tensor(out=H4(ATm, h), in0=H4(pb[3], h), in1=H4(decT, h), op=ALU.mult), r=[PB[3], "decT"], w=["ATm"])
                    if GSTOP <= 4:
                        continue
                    for h in range(4):
                        P.op("pe", lambda e, h=h: e.transpose(out=H4(pb[4], h), in_=H4(Pbuf[0], h), identity=ident), r=["P0", "ident"], w=[PB[4]])
                    evac("act", PTbuf[0], pb[4], [PB[4]], ["PT0"])
                    P.op("dve", lambda e: e.tensor_tensor(out=Wm, in0=PTbuf[0], in1=ident4, op=ALU.add), r=["PT0", "cst"], w=["Wm"])
                    for l in range(1, 7):
                        pc, ptc, pn_, ptn = Pbuf[(l - 1) % 2], PTbuf[(l - 1) % 2], Pbuf[l % 2], PTbuf[l % 2]
                        pcn, ptcn, pnn, ptnn = f"P{(l - 1) % 2}", f"PT{(l - 1) % 2}", f"P{l % 2}", f"PT{l % 2}"
                        for h in range(4):
                            P.op("pe", lambda e, h=h, pc=pc, ptc=ptc: e.matmul(out=H4(pb[4], h), lhsT=H4(ptc, h), rhs=H4(pc, h), start=True, stop=True), r=[pcn, ptcn], w=[PB[4]])
                        if l < 6:
                            for h in range(4):
                                P.op("pe", lambda e, h=h, pc=pc, ptc=ptc: e.matmul(out=H4(pb[5], h), lhsT=H4(pc, h), rhs=H4(ptc, h), start=True, stop=True), r=[pcn, ptcn], w=[PB[5]])
                        evac("act", pn_, pb[4], [PB[4]], [pnn])
                        if l < 6:
                            evac("dve", ptn, pb[5], [PB[5]], [ptnn])
                        for h in range(4):
                            P.op("pe", lambda e, h=h, pn_=pn_: e.matmul(out=H4(pb[6], h), lhsT=H4(pn_, h), rhs=H4(Wm, h), start=True, stop=True), r=[pnn, "Wm"], w=[PB[6]])
                        P.op("dve", lambda e: e.tensor_tensor(out=Wm, in0=Wm, in1=pb[6], op=ALU.add), r=["Wm", PB[6]], w=["Wm"])
                    if GSTOP <= 5:
                        continue
                    for h in range(4):
                        P.op("pe", lambda e, h=h: e.matmul(out=H4(pb[7], h), lhsT=H4(knT, h), rhs=H4(Sst, h), start=True, stop=True), r=["knT", "S"], w=[PB[7]])
                    for h in range(4):
                        P.op("dve", lambda e, h=h: e.scalar_tensor_tensor(out=H4(Rm, h), in0=H4(pb[7], h), scalar=sc(36, h), in1=H4(vb, h), op0=ALU.mult, op1=ALU.add),
                             r=[PB[7], "sm_x", "vb"], w=["Rm"])
                    for h in range(4):
                        P.op("pe", lambda e, h=h: e.matmul(out=H4(pb[0], h), lhsT=H4(Wm, h), rhs=H4(Rm, h), start=True, stop=True), r=["Wm", "Rm"], w=[PB[0]])
                    evac("act", vnew, pb[0], [PB[0]], ["vnew"])
                    if own:
                        for h in range(4):
                            P.op("pe", lambda e, h=h: e.matmul(out=H4(pb[1], h), lhsT=H4(qtT, h), rhs=H4(Sst, h), start=True, stop=False), r=["qtT", "S"], w=[PB[1]])
                            P.op("pe", lambda e, h=h: e.matmul(out=H4(pb[1], h), lhsT=H4(ATm, h), rhs=H4(vnew, h), start=False, stop=True), r=["ATm", "vnew"], w=[PB[1]])
                        evac("act", o_sb, pb[1], [PB[1]], ["o_sb"])
                    for h in range(4):
                        P.op("pe", lambda e, h=h: e.matmul(out=H4(pb[2], h), lhsT=H4(kt, h), rhs=H4(vnew, h), start=True, stop=True), r=["kt", "vnew"], w=[PB[2]])
                    for h in range(4):
                        P.op("dve", lambda e, h=h: e.scalar_tensor_tensor(out=H4(Sst, h), in0=H4(Sst, h), scalar=sc(32, h), in1=H4(pb[2], h), op0=ALU.mult, op1=ALU.add),
                             r=["S", "sm_e", PB[2]], w=["S"])
                    if own:
                        P.op("pool", lambda e: e.tensor_tensor(out=tmpa[:, 0:512], in0=o_sb, in1=o_sb, op=ALU.mult), r=["o_sb"], w=["tmpa"])
                        P.op("dve", lambda e: e.tensor_reduce(out=sm[:, 52:56], in_=tmpa[:, 0:512].rearrange("p (h d) -> p h d", h=4), axis=AX.X, op=ALU.add), r=["tmpa"], w=["sm_o"])
                        P.op("dve", lambda e: e.tensor_scalar(out=sm[:, 52:56], in0=sm[:, 52:56], scalar1=1.0 / 128, scalar2=1e-6, op0=ALU.mult, op1=ALU.add), r=["sm_o"], w=["sm_o"])
                        P.op("act", lambda e: e.activation(out=sm[:, 52:56], in_=sm[:, 52:56], func=AF.Sqrt), r=["sm_o"], w=["sm_o"])
                        P.op("dve", lambda e: e.reciprocal(out=sm[:, 52:56], in_=sm[:, 52:56]), r=["sm_o"], w=["sm_o"])
                        P.op("act", lambda e: e.activation(out=szb, in_=X[3][:, 1544:2056], func=AF.Silu), r=["X3"], w=["szb"])
                        for h in range(4):
                            P.op("dve", lambda e, h=h: e.scalar_tensor_tensor(out=H4(og, h), in0=H4(o_sb, h), scalar=sc(52, h), in1=H4(ggd, h), op0=ALU.mult, op1=ALU.mult),
                                 r=["o_sb", "sm_o", "ggd"], w=["og"])
                        P.op("pool", lambda e: e.tensor_tensor(out=og, in0=og, in1=szb, op=ALU.mult), r=["og", "szb"], w=["og"])
                        for h in range(4):
                            P.op("pe", lambda e, h=h: e.transpose(out=H4(pb[3], h), in_=H4(og, h), identity=ident), r=["og", "ident"], w=[PB[3]])
                        evac("act", cTst, pb[3], [PB[3]], ["cTst"])
                        P.dma("sp", S["cT"][ti][:, 0:4, :], cTst.rearrange("p (h t) -> p h t", h=4), r=["cTst"], w=[f"cTg{ti}"])
            P.dma("sp", O["ssm_fin"].rearrange("h d e -> d h e"), Sst.rearrange("p (h e) -> p h e", h=4), r=["S"])

            P.barrier()
            A.reset(gs_base)
            eye16 = A.f32(256)
            P.dma("sp", eye16, I["eye16"], w=["eye16"])
            Sall = A.f32(NS * 512); Sall4 = Sall.rearrange("p (s h e) -> p s h e", s=NS, h=4)
            for s_ in range(NS):
                P.dma("sp", Sall4[:, s_], I["state_ssm"][s_].rearrange("h d e -> d h e"), w=[f"S{s_}"])
            Ps2 = A.f32(INC); scv = A.f32(3 * 1536); scv3 = scv.rearrange("p (i c) -> p i c", i=3)
            P.dma("sp", Ps2[0:NS, :], S["ps"], w=["Ps2"])
            P.dma("sp", scv3[0:NS], I["state_conv"], w=["scv"])
            cvs = A.f32(1536); tms = A.f32(1536); sm2 = A.f32(64)
            kn2 = A.f32(512); qn2 = A.f32(512)
            knT2 = A.f32(64); qnT2 = A.f32(64)
            KTm = A.f32(1024); QTm = A.f32(1024)
            Km = [A.f32(512), A.f32(512)]
            EgD = A.f32(64); EGB = A.f32(64); Dl = A.f32(512); o_s = A.f32(512); og_s = A.f32(512); sz_s = A.f32(512)
            cTs = A.bf16(64)
            R = slice(0, NS)
            for (d0, d1, sc0, p0) in [(0, 1024, 512, 0), (1024, 1536, 0, C_Q)]:
                n = d1 - d0
                P.op("dve", lambda e, d0=d0, d1=d1, sc0=sc0, n=n: e.tensor_tensor(out=cvs[R, d0:d1], in0=scv3[R, 0, sc0:sc0 + n], in1=wc[0][R, d0:d1], op=ALU.mult), r=["scv", "wc0"], w=["cvs"])
                for i in range(1, 4):
                    src = (lambda i=i, sc0=sc0, n=n, p0=p0: scv3[R, i, sc0:sc0 + n] if i < 3 else Ps2[R, p0:p0 + n])()
                    P.op("pool", lambda e, i=i, d0=d0, d1=d1, src=src: e.tensor_tensor(out=tms[R, d0:d1], in0=src, in1=wc[i][R, d0:d1], op=ALU.mult), r=["scv", "Ps2", f"wc{i}"], w=["tms"])
                    P.op("dve", lambda e, d0=d0, d1=d1: e.tensor_tensor(out=cvs[R, d0:d1], in0=cvs[R, d0:d1], in1=tms[R, d0:d1], op=ALU.add), r=["cvs", "tms"], w=["cvs"])
                P.op("act", lambda e, d0=d0, d1=d1: e.activation(out=cvs[R, d0:d1], in_=cvs[R, d0:d1], func=AF.Silu), r=["cvs"], w=["cvs"])
            for (c0, o0) in [(0, 0), (1024, 4)]:
                P.op("pool", lambda e, c0=c0: e.tensor_tensor(out=tms[R, 0:512], in0=cvs[R, c0:c0 + 512], in1=cvs[R, c0:c0 + 512], op=ALU.mult), r=["cvs"], w=["tms"])
                P.op("dve", lambda e, o0=o0: e.tensor_reduce(out=sm2[R, o0:o0 + 4], in_=tms[R, 0:512].rearrange("p (h d) -> p h d", h=4), axis=AX.X, op=ALU.add), r=["tms"], w=["sm2"])
            P.op("dve", lambda e: e.tensor_scalar(out=sm2[R, 0:8], in0=sm2[R, 0:8], scalar1=1e-6, scalar2=None, op0=ALU.add), r=["sm2"], w=["sm2"])
            P.op("act", lambda e: e.activation(out=sm2[R, 0:8], in_=sm2[R, 0:8], func=AF.Sqrt), r=["sm2"], w=["sm2"])
            P.op("dve", lambda e: e.reciprocal(out=sm2[R, 0:8], in_=sm2[R, 0:8]), r=["sm2"], w=["sm2"])
            P.op("dve", lambda e: e.tensor_scalar(out=sm2[R, 4:8], in0=sm2[R, 4:8], scalar1=128 ** -0.5, scalar2=None, op0=ALU.mult), r=["sm2"], w=["sm2"])
            P.op("act", lambda e: e.activation(out=sm2[R, 8:12], in_=Ps2[R, C_B:C_B + 4], func=AF.Sigmoid), r=["Ps2"], w=["sm2b"])
            P.op("dve", lambda e: e.tensor_tensor(out=sm2[R, 12:16], in0=Ps2[R, C_A:C_A + 4], in1=dtb[R, :], op=ALU.add), r=["Ps2", "dtb"], w=["sm2g"])
            P.op("act", lambda e: e.activation(out=sm2[R, 12:16], in_=sm2[R, 12:16], func=AF.Exp), r=["sm2g"], w=["sm2g"])
            P.op("act", lambda e: e.activation(out=sm2[R, 12:16], in_=sm2[R, 12:16], func=AF.Ln, bias=1.0), r=["sm2g"], w=["sm2g"])
            P.op("dve", lambda e: e.tensor_tensor(out=sm2[R, 12:16], in0=sm2[R, 12:16], in1=negA[R, :], op=ALU.mult), r=["sm2g", "negA"], w=["sm2g"])
            P.op("act", lambda e: e.activation(out=sm2[R, 12:16], in_=sm2[R, 12:16], func=AF.Exp), r=["sm2g"], w=["sm2g"])
            P.op("dve", lambda e: e.tensor_scalar(out=sm2[R, 16:20], in0=sm2[R, 12:16], scalar1=-1.0, scalar2=None, op0=ALU.mult), r=["sm2g"], w=["sm2n"])
            for h in range(4):
                P.op("dve", lambda e, h=h: e.tensor_scalar(out=kn2[R, h * 128:(h + 1) * 128], in0=cvs[R, h * 128:(h + 1) * 128], scalar1=sm2[R, h:h + 1], scalar2=None, op0=ALU.mult), r=["cvs", "sm2"], w=["kn2"])
                P.op("dve", lambda e, h=h: e.tensor_scalar(out=qn2[R, h * 128:(h + 1) * 128], in0=cvs[R, 1024 + h * 128:1024 + (h + 1) * 128], scalar1=sm2[R, 4 + h:5 + h], scalar2=None, op0=ALU.mult), r=["cvs", "sm2"], w=["qn2"])
            for h in range(4):
                P.op("pe", lambda e, h=h: e.transpose(out=pb[6][:, h * 16:(h + 1) * 16], in_=kn2[R, h * 128:(h + 1) * 128], identity=ident[R, R]), r=["kn2", "ident"], w=[PB[6]])
                P.op("pe", lambda e, h=h: e.transpose(out=pb[6][:, 64 + h * 16:64 + (h + 1) * 16], in_=qn2[R, h * 128:(h + 1) * 128], identity=ident[R, R]), r=["qn2", "ident"], w=[PB[6]])
            evac("act", knT2, pb[6][:, 0:64], [PB[6]], ["knT2"])
            evac("dve", qnT2, pb[6][:, 64:128], [PB[6]], ["qnT2"])
            eye3 = eye16.rearrange("p (s m) -> p s m", s=NS)
            for h in range(4):
                for s_ in range(NS):
                    j = h * NS + s_
                    P.op("dve", lambda e, j=j, s_=s_: e.tensor_scalar(out=KTm[:, j * 16:(j + 1) * 16], in0=eye3[:, s_, :], scalar1=knT2[:, j:j + 1], scalar2=None, op0=ALU.mult), r=["eye16", "knT2"], w=["KTm"])
                    P.op("pool", lambda e, j=j, s_=s_: e.tensor_scalar(out=QTm[:, j * 16:(j + 1) * 16], in0=eye3[:, s_, :], scalar1=qnT2[:, j:j + 1], scalar2=1.0, op0=ALU.mult, op1=ALU.mult), r=["eye16", "qnT2"], w=["QTm"])
            for h in range(4):
                for s_ in range(NS):
                    j = h * NS + s_
                    P.op("pe", lambda e, h=h, s_=s_, j=j: e.matmul(out=pb[h][R, 0:128], lhsT=KTm[:, j * 16:(j + 1) * 16], rhs=Sall4[:, s_, h, :], start=(s_ == 0), stop=(s_ == NS - 1)),
                         r=["KTm", f"S{s_}"], w=[PB[h]])
            for h in range(4):
                P.op("dve", lambda e, h=h: e.scalar_tensor_tensor(out=Dl[R, h * 128:(h + 1) * 128], in0=pb[h][R, 0:128], scalar=sm2[R, 16 + h:17 + h], in1=cvs[R, 512 + h * 128:512 + (h + 1) * 128], op0=ALU.mult, op1=ALU.add),
                     r=[PB[h], "sm2n", "cvs"], w=["Dl"])
                P.op("dve", lambda e, h=h: e.tensor_scalar(out=Dl[R, h * 128:(h + 1) * 128], in0=Dl[R, h * 128:(h + 1) * 128], scalar1=sm2[R, 8 + h:9 + h], scalar2=None, op0=ALU.mult), r=["Dl", "sm2b"], w=["Dl"])
            for s_ in range(NS):
                P.op("dve", lambda e, s_=s_: e.tensor_scalar(out=EgD[R, s_ * 4:(s_ + 1) * 4], in0=sm2[R, 12:16], scalar1=ident[R, s_:s_ + 1], scalar2=None, op0=ALU.mult), r=["sm2g", "ident"], w=["EgD"])
            P.op("pe", lambda e: e.matmul(out=pb[6][:, 0:64], lhsT=ONESM[R, :], rhs=EgD[R, :], start=True, stop=True), r=["cst", "EgD"], w=[PB[6]])
            evac("act", EGB, pb[6][:, 0:64], [PB[6]], ["EGB"])
            for s_ in range(NS):
                km = Km[s_ % 2]; kmn = f"Km{s_ % 2}"
                bank = 4 + s_ % 2
                P.op("pool", lambda e, s_=s_, km=km: e.tensor_scalar(out=km[R, :], in0=kn2[R, :], scalar1=ident[R, s_:s_ + 1], scalar2=1.0, op0=ALU.mult, op1=ALU.mult), r=["kn2", "ident"], w=[kmn])
                for h in range(4):
                    P.op("pe", lambda e, h=h, km=km, bank=bank: e.matmul(out=pb[bank][:, h * 128:(h + 1) * 128], lhsT=km[R, h * 128:(h + 1) * 128], rhs=Dl[R, h * 128:(h + 1) * 128], start=True, stop=True),
                         r=[kmn, "Dl"], w=[PB[bank]])
                for h in range(4):
                    P.op("dve", lambda e, h=h, s_=s_, bank=bank: e.scalar_tensor_tensor(out=Sall4[:, s_, h, :], in0=Sall4[:, s_, h, :], scalar=EGB[:, s_ * 4 + h:s_ * 4 + h + 1], in1=pb[bank][:, h * 128:(h + 1) * 128], op0=ALU.mult, op1=ALU.add),
                         r=[f"S{s_}", "EGB", PB[bank]], w=[f"S{s_}"])
                P.dma("sp", O["ssm_s"][s_].rearrange("h d e -> d h e"), Sall4[:, s_], r=[f"S{s_}"])
            for h in range(4):
                for s_ in range(NS):
                    j = h * NS + s_
                    P.op("pe", lambda e, h=h, s_=s_, j=j: e.matmul(out=pb[h][R, 0:128], lhsT=QTm[:, j * 16:(j + 1) * 16], rhs=Sall4[:, s_, h, :], start=(s_ == 0), stop=(s_ == NS - 1)),
                         r=["QTm", f"S{s_}"], w=[PB[h]])
            for h in range(4):
                evac("act", o_s[R, h * 128:(h + 1) * 128], pb[h][R, 0:128], [PB[h]], ["o_s"])
            P.op("pool", lambda e: e.tensor_tensor(out=tms[R, 0:512], in0=o_s[R, :], in1=o_s[R, :], op=ALU.mult), r=["o_s"], w=["tms"])
            P.op("dve", lambda e: e.tensor_reduce(out=sm2[R, 20:24], in_=tms[R, 0:512].rearrange("p (h d) -> p h d", h=4), axis=AX.X, op=ALU.add), r=["tms"], w=["sm2o"])
            P.op("dve", lambda e: e.tensor_scalar(out=sm2[R, 20:24], in0=sm2[R, 20:24], scalar1=1.0 / 128, scalar2=1e-6, op0=ALU.mult, op1=ALU.add), r=["sm2o"], w=["sm2o"])
            P.op("act", lambda e: e.activation(out=sm2[R, 20:24], in_=sm2[R, 20:24], func=AF.Sqrt), r=["sm2o"], w=["sm2o"])
            P.op("dve", lambda e: e.reciprocal(out=sm2[R, 20:24], in_=sm2[R, 20:24]), r=["sm2o"], w=["sm2o"])
            P.op("act", lambda e: e.activation(out=sz_s[R, :], in_=Ps2[R, C_Z:C_Z + 512], func=AF.Silu), r=["Ps2"], w=["sz_s"])
            for h in range(4):
                P.op("dve", lambda e, h=h: e.scalar_tensor_tensor(out=og_s[R, h * 128:(h + 1) * 128], in0=o_s[R, h * 128:(h + 1) * 128], scalar=sm2[R, 20 + h:21 + h], in1=ggd[R, h * 128:(h + 1) * 128], op0=ALU.mult, op1=ALU.mult),
                     r=["o_s", "sm2o", "ggd"], w=["og_s"])
            P.op("pool", lambda e: e.tensor_tensor(out=og_s[R, :], in0=og_s[R, :], in1=sz_s[R, :], op=ALU.mult), r=["og_s", "sz_s"], w=["og_s"])
            for h in range(4):
                P.op("pe", lambda e, h=h: e.transpose(out=pb[7][:, h * 16:(h + 1) * 16], in_=og_s[R, h * 128:(h + 1) * 128], identity=ident[R, R]), r=["og_s", "ident"], w=[PB[7]])
            evac("act", cTs, pb[7][:, 0:64], [PB[7]], ["cTs"])
            P.dma("sp", S["cT"][NT][:, 0:4, 0:NS], cTs.rearrange("p (h t) -> p h t", h=4), r=["cTs"], w=["cTgs"])

        if not os.environ.get('MK_NOT'):
            P.barrier()
            A.reset(persist1)
            SCALE = 128 ** -0.5
            NEG = -30000.0
            NBIS = 14
            caus = A.f32(128); ii2 = A.bf16(256); onesr = A.f32(128)
            P.dma("sp", caus, I["caus"], w=["caus"])
            P.op("dve", lambda e: e.tensor_copy(out=ii2[:, 0:128], in_=ident), r=["ident"], w=["ii2"])
            P.op("dve", lambda e: e.tensor_copy(out=ii2[:, 128:256], in_=ident), r=["ident"], w=["ii2"])
            P.op("dve", lambda e: e.memset(onesr, 1.0), w=["onesr"])
            tsm = A.f32(256)
            krow = A.f32(128)
            P.op("dve", lambda e: e.tensor_reduce(out=tsm[:, 0:1], in_=ksq, axis=AX.X, op=ALU.max), r=["ksq"], w=["tsm0"])
            P.op("pe", lambda e: e.transpose(out=pb[0][0:1, 0:128], in_=tsm[:, 0:1], identity=ident), r=["tsm0", "ident"], w=[PB[0]])
            evac("dve", krow[0:1, :], pb[0][0:1, 0:128], [PB[0]], ["krow"])
            P.op("dve", lambda e: e.tensor_reduce(out=krow[0:1, 0:1], in_=krow[0:1, :], axis=AX.X, op=ALU.max), r=["krow"], w=["krow"])
            P.op("pe", lambda e: e.matmul(out=pb[0][:, 0:1], lhsT=onesr[0:1, :], rhs=krow[0:1, 0:1], start=True, stop=True), r=["onesr", "krow"], w=[PB[0]])
            evac("dve", tsm[:, 1:2], pb[0][:, 0:1], [PB[0]], ["tsm1"])
            P.op("dve", lambda e: e.tensor_reduce(out=tsm[:, 16:32], in_=qsq.rearrange("p (t h) -> p t h", h=4), axis=AX.X, op=ALU.max), r=["qsq"], w=["tsmq"])
            P.op("dve", lambda e: e.tensor_scalar(out=tsm[:, 32:48], in0=tsm[:, 16:32], scalar1=tsm[:, 1:2], scalar2=None, op0=ALU.mult), r=["tsmq", "tsm1"], w=["negm"])
            P.op("act", lambda e: e.activation(out=tsm[:, 32:48], in_=tsm[:, 32:48], func=AF.Sqrt), r=["negm"], w=["negm"])
            P.op("dve", lambda e: e.tensor_scalar(out=tsm[:, 32:48], in0=tsm[:, 32:48], scalar1=-1.0, scalar2=None, op0=ALU.mult), r=["negm"], w=["negm"])
            Iscs = [A.f32(T), A.f32(T)]
            junk = A.bf16(T); MBs_ = [A.bf16(T), A.bf16(T)]
            qTt = [A.bf16(512), A.bf16(512), A.bf16(512)]; iqTt = [A.bf16(512), A.bf16(512)]
            oaccs = [A.f32(4 * 130), A.f32(4 * 130)]
            rh = [A.bf16(512) for _ in range(4)]
            Dhs = [A.bf16(8 * 128), A.bf16(8 * 128)]
            PT = [A.bf16(256), A.bf16(256)]
            oatt = A.f32(512); cTa = A.bf16(512)
            bss = [A.f32(16), A.f32(16)]
            T_NT = int(os.environ.get("MK_TNT", NT))

            def t_index(j):
                p = j % 2
                Isc = Iscs[p]; In = f"Isc{p}"; Dh = Dhs[p]; Dn = f"Dh{p}"
                qb = qTt[j % 3]; qn_ = f"qTt{j % 3}"; iqb = iqTt[p]; iqn = f"iqTt{p}"
                qb3 = qb.rearrange("p (h t) -> p h t", h=4); iqb3 = iqb.rearrange("p (h t) -> p h t", h=4)
                P.dma("sp", iqb3, S["iqT"][j], w=[iqn])
                P.dma("sp", qb3, S["qT"][j], w=[qn_])
                ncol = HALF + 128 * (j + 1)
                for h in range(8):
                    P.op("dve", lambda e, h=h: e.tensor_scalar(out=Dh[:, h * 128:(h + 1) * 128], in0=ident, scalar1=iwsgn[:, j * 8 + h:j * 8 + h + 1], scalar2=None, op0=ALU.mult),
                         r=["ident", "iwsgn"], w=[Dn])
                nblk = (ncol + 511) // 512
                for kb in range(nblk):
                    c0, c1 = kb * 512, min(ncol, kb * 512 + 512)
                    w_ = c1 - c0
                    accb = 2 + kb % 2

                    def emit_S(h, c0=c0, c1=c1, w_=w_):
                        p_, hf_ = h // 2, h % 2
                        ba = h % 2
                        rb = rh[h % 4]; rbn = f"rh{h % 4}"
                        P.op("pe", lambda e: e.matmul(out=pb[ba][:, 0:w_], lhsT=iqb3[hf_ * 64:(hf_ + 1) * 64, p_, :], rhs=ikT[hf_ * 64:(hf_ + 1) * 64, c0:c1], start=True, stop=True),
                             r=[iqn, "ikT"], w=[PB[ba]])
                        P.op("act", lambda e: e.activation(out=rb[:, 0:w_], in_=pb[ba][:, 0:w_], func=AF.Relu, scale=iwabs[:, j * 8 + h:j * 8 + h + 1]),
                             r=[PB[ba], "iwabs"], w=[rbn])

                    def emit_D(h, w_=w_, accb=accb):
                        rb = rh[h % 4]; rbn = f"rh{h % 4}"
                        P.op("pe", lambda e: e.matmul(out=pb[accb][:, 0:w_], lhsT=Dh[:, h * 128:(h + 1) * 128], rhs=rb[:, 0:w_], start=(h == 0), stop=(h == 7)),
                             r=[Dn, rbn], w=[PB[accb]])

                    emit_S(0); emit_S(1)
                    for h in range(8):
                        emit_D(h)
                        if h + 2 < 8:
                            emit_S(h + 2)
                    if c0 < HALF:
                        P.op("act", lambda e, c0=c0, c1=c1, w_=w_, accb=accb: e.activation(out=Isc[:, c0:c1], in_=pb[accb][:, 0:w_], func=AF.Identity, bias=flags[:, 1:2]), r=[PB[accb], "flags"], w=[In])
                    else:
                        P.op("act", lambda e, c0=c0, c1=c1, w_=w_, accb=accb: e.activation(out=Isc[:, c0:c1], in_=pb[accb][:, 0:w_], func=AF.Copy), r=[PB[accb]], w=[In])
                P.op("pool", lambda e: e.tensor_tensor(out=Isc[:, ncol - 128:ncol], in0=Isc[:, ncol - 128:ncol], in1=caus, op=ALU.add), r=[In, "caus"], w=[In])

            def t_bisect(j):
                p = j % 2
                Isc = Iscs[p]; In = f"Isc{p}"; MB = MBs_[p]; Mn = f"MB{p}"; bs = bss[p]
                B = lambda n: f"bs{p}_{n}"
                ncol = HALF + 128 * (j + 1)
                lo, rng, thr, cntc, mm = bs[:, 0:1], bs[:, 1:2], bs[:, 2:3], bs[:, 3:4], bs[:, 4:5]
                P.op("dve", lambda e: e.tensor_reduce(out=lo, in_=Isc[:, 0:ncol], axis=AX.X, op=ALU.min), r=[In], w=[B("lo")])
                P.op("dve", lambda e: e.tensor_reduce(out=rng, in_=Isc[:, 0:ncol], axis=AX.X, op=ALU.max), r=[In], w=[B("rng")])
                P.op("dve", lambda e: e.tensor_scalar(out=thr, in0=rng, scalar1=-128.0, scalar2=None, op0=ALU.add), r=[B("rng")], w=[B("thr")])
                P.op("dve", lambda e: e.tensor_tensor(out=lo, in0=lo, in1=thr, op=ALU.max), r=[B("lo"), B("thr")], w=[B("lo")])
                P.op("dve", lambda e: e.tensor_tensor(out=rng, in0=rng, in1=lo, op=ALU.subtract), r=[B("rng"), B("lo")], w=[B("rng")])
                base = bs[:, 5:6]
                P.op("dve", lambda e: e.scalar_tensor_tensor(out=thr, in0=rng, scalar=0.5, in1=lo, op0=ALU.mult, op1=ALU.add), r=[B("rng"), B("lo")], w=[B("thr")])
                for it in range(NBIS):
                    st = 2.0 ** -(it + 1)
                    P.op("dve", lambda e: e.tensor_scalar(out=junk[:, 0:ncol], in0=Isc[:, 0:ncol], scalar1=thr, scalar2=None, op0=ALU.is_ge, op1=ALU.add, accum_out=cntc),
                         r=[In, B("thr")], w=["junk", B("cnt")])
                    P.op("dve", lambda e, st=st: e.scalar_tensor_tensor(out=base, in0=rng, scalar=-0.5 * st, in1=thr, op0=ALU.mult, op1=ALU.add), r=[B("rng"), B("thr")], w=[B("base")])
                    P.op("dve", lambda e: e.tensor_scalar(out=mm, in0=cntc, scalar1=255.5, scalar2=rng, op0=ALU.is_ge, op1=ALU.mult), r=[B("cnt"), B("rng")], w=[B("m")])
                    P.op("dve", lambda e, st=st: e.scalar_tensor_tensor(out=thr, in0=mm, scalar=st, in1=base, op0=ALU.mult, op1=ALU.add), r=[B("m"), B("base")], w=[B("thr")])
                P.op("dve", lambda e: e.scalar_tensor_tensor(out=lo, in0=rng, scalar=-(2.0 ** -(NBIS + 1)), in1=thr, op0=ALU.mult, op1=ALU.add), r=[B("rng"), B("thr")], w=[B("lo")])
                P.op("dve", lambda e: e.tensor_scalar(out=MB[:, 0:ncol], in0=Isc[:, 0:ncol], scalar1=lo, scalar2=NEG, op0=ALU.is_lt, op1=ALU.mult), r=[In, B("lo")], w=[Mn])
                P.op("dve", lambda e: e.tensor_scalar(out=MB[:, 0:ncol], in0=MB[:, 0:ncol], scalar1=tsm[:, 32 + j:33 + j], scalar2=None, op0=ALU.add), r=[Mn, "negm"], w=[Mn])

            def t_attend(j):
                p = j % 2
                MB = MBs_[p]; Mn = f"MB{p}"; bs = bss[p]
                qb = qTt[j % 3]; qn_ = f"qTt{j % 3}"
                qb3 = qb.rearrange("p (h t) -> p h t", h=4)
                oacc = oaccs[p]; oan = f"oacc{p}"
                ntile = 16 + j + 1
                seq = [(g, t) for g in range(2) for t in range(ntile)]

                def emit_ST(i):
                    g, t = seq[i]
                    sb_ = i % 2
                    ptb = PT[i % 2]; ptn = f"PT{i % 2}"
                    P.op("pe", lambda e: e.matmul(out=pb[sb_][:, 0:256], lhsT=KT3[:, g, t * 128:(t + 1) * 128], rhs=qb3[:, g * 2:(g + 1) * 2, :], start=True, stop=False),
                         r=["KT", qn_], w=[PB[sb_]])
                    P.op("pe", lambda e: e.matmul(out=pb[sb_][:, 0:256], lhsT=MB[:, t * 128:(t + 1) * 128], rhs=ii2, start=False, stop=True),
                         r=[Mn, "ii2"], w=[PB[sb_]])
                    P.op("act", lambda e: e.activation(out=ptb, in_=pb[sb_][:, 0:256], func=AF.Exp, scale=SCALE), r=[PB[sb_]], w=[ptn])

                def emit_PV(i):
                    g, t = seq[i]
                    ptb = PT[i % 2]; ptn = f"PT{i % 2}"
                    for h2i in range(2):
                        P.op("pe", lambda e, h2i=h2i: e.matmul(out=pb[4 + g * 2 + h2i][:, 0:130], lhsT=ptb[:, h2i * 128:(h2i + 1) * 128], rhs=VA4[:, t, g, :], start=(t == 0), stop=(t == ntile - 1)),
                             r=[ptn, "VA"], w=[PB[4 + g * 2 + h2i]])

                emit_ST(0)
                if len(seq) > 1:
                    emit_ST(1)
                for i in range(len(seq)):
                    emit_PV(i)
                    if i + 2 < len(seq):
                        emit_ST(i + 2)
                for h in range(4):
                    P.op("act", lambda e, h=h: e.activation(out=oacc[:, h * 130:(h + 1) * 130], in_=pb[4 + h][:, 0:130], func=AF.Copy), r=[PB[4 + h]], w=[oan])

            def t_final(j):
                p = j % 2
                bs = bss[p]; oacc = oaccs[p]; oan = f"oacc{p}"
                for h in range(4):
                    P.op("dve", lambda e, h=h: e.reciprocal(out=bs[:, 8 + h:9 + h], in_=oacc[:, h * 130 + 128:h * 130 + 129]), r=[oan], w=[f"bs{p}_r"])
                    P.op("dve", lambda e, h=h: e.tensor_scalar(out=oatt[:, h * 128:(h + 1) * 128], in0=oacc[:, h * 130:h * 130 + 128], scalar1=bs[:, 8 + h:9 + h], scalar2=None, op0=ALU.mult),
                         r=[oan, f"bs{p}_r"], w=["oatt"])
                for h in range(4):
                    P.op("pe", lambda e, h=h: e.transpose(out=pb[3][:, h * 128:(h + 1) * 128], in_=oatt[:, h * 128:(h + 1) * 128], identity=ident), r=["oatt", "ident"], w=[PB[3]])
                evac("act", cTa, pb[3], [PB[3]], ["cTa"])
                P.dma("sp", S["cT"][j][:, 4:8, :], cTa.rearrange("p (h t) -> p h t", h=4), r=["cTa"], w=[f"cTa{j}"])

            if T_NT > 0:
                t_index(0)
            if T_NT > 1:
                t_index(1)
            if T_NT > 0:
                t_bisect(0)
            for j in range(T_NT):
                t_attend(j)
                if j + 2 < T_NT:
                    t_index(j + 2)
                if j + 1 < T_NT:
                    t_bisect(j + 1)
                t_final(j)

        if not os.environ.get('MK_NOTS'):
            P.barrier()
            A.reset(persist0)
            zSCALE = 128 ** -0.5
            NPG = 16
            zidx_i = A.f32(32).bitcast(I32)
            zpt_i = A.f32(256).bitcast(I32)
            zidx_f = A.f32(256); zsel = A.f32(257); zix = A.f32(32)
            P.dma("sp", zpt_i[:, :], I["pt"].to_broadcast([128, 256]), w=["zpt_i"])
            P.dma("sp", zsel, I["tsel"], w=["zsel"])
            P.op("dve", lambda e: e.tensor_copy(out=zidx_f, in_=zpt_i[:, :]), r=["zpt_i"], w=["zidx_f"])
            P.op("dve", lambda e: e.tensor_tensor(out=zidx_f, in0=zidx_f, in1=zsel[:, 0:256], op=ALU.mult), r=["zidx_f", "zsel"], w=["zidx_f"])
            P.op("dve", lambda e: e.tensor_reduce(out=zix, in_=zidx_f.rearrange("p (q j) -> p q j", j=8), axis=AX.X, op=ALU.add), r=["zidx_f"], w=["zix"])
            P.op("dve", lambda e: e.tensor_scalar(out=zix, in0=zix, scalar1=16.0, scalar2=zsel[:, 256:257], op0=ALU.mult, op1=ALU.add), r=["zix", "zsel"], w=["zix"])
            P.op("dve", lambda e: e.tensor_copy(out=zidx_i[:, :], in_=zix), r=["zix"], w=["zidx_i"])
            zPs = A.f32(INC)
            P.dma("sp", zPs[0:NS, :], S["ps"], w=["zPs"])
            zones = A.f32(128)
            P.op("dve", lambda e: e.memset(zones, 1.0), w=["zones"])
            ziqT = A.f32(NS * 8)
            ziwT = A.f32(NS)
            zikn = A.f32(NS)
            zqT = A.bf16(4 * NS)
            zkTn = A.bf16(2 * NS)
            ziqT3 = ziqT.rearrange("p (s h) -> p s h", h=8)
            R = slice(0, NS)
            for h in range(8):
                P.op("pe", lambda e, h=h: e.transpose(out=pb[0][0:64, h * NS:(h + 1) * NS], in_=zPs[R, C_IQ + h * 64:C_IQ + (h + 1) * 64], identity=ident[R, R]), r=["zPs", "ident"], w=[PB[0]])
            P.op("dve", lambda e: e.tensor_copy(out=ziqT3[0:64], in_=pb[0][0:64, 0:8 * NS].rearrange("p (h s) -> p s h", h=8)), r=[PB[0]], w=["ziqT"])
            P.op("pe", lambda e: e.transpose(out=pb[1][0:8, 0:NS], in_=zPs[R, C_IW:C_IW + 8], identity=ident[R, R]), r=["zPs", "ident"], w=[PB[1]])
            P.op("dve", lambda e: e.tensor_scalar(out=ziwT[0:8, :], in0=pb[1][0:8, 0:NS], scalar1=8 ** -0.5, scalar2=None, op0=ALU.mult), r=[PB[1]], w=["ziwT"])
            P.op("pe", lambda e: e.transpose(out=pb[1][0:64, 64:64 + NS], in_=zPs[R, C_IK:C_IK + 64], identity=ident[R, R]), r=["zPs", "ident"], w=[PB[1]])
            P.op("dve", lambda e: e.tensor_copy(out=zikn[0:64, :], in_=pb[1][0:64, 64:64 + NS]), r=[PB[1]], w=["zikn"])
            for h in range(4):
                P.op("pe", lambda e, h=h: e.transpose(out=pb[2][:, h * NS:(h + 1) * NS], in_=zPs[R, C_AQ + h * 128:C_AQ + (h + 1) * 128], identity=ident[R, R]), r=["zPs", "ident"], w=[PB[2]])
            for g in range(2):
                P.op("pe", lambda e, g=g: e.transpose(out=pb[2][:, 64 + g * NS:64 + (g + 1) * NS], in_=zPs[R, C_AK + g * 128:C_AK + (g + 1) * 128], identity=ident[R, R]), r=["zPs", "ident"], w=[PB[2]])
            P.op("dve", lambda e: e.tensor_copy(out=zqT, in_=pb[2][:, 0:4 * NS]), r=[PB[2]], w=["zqT"])
            P.op("dve", lambda e: e.tensor_copy(out=zkTn, in_=pb[2][:, 64:64 + 2 * NS]), r=[PB[2]], w=["zkTn"])
            zvn_f = A.f32(NS * 256); zvn = A.bf16(NS * 2 * 130)
            zvn4 = zvn.rearrange("p (s g d) -> p s g d", s=NS, g=2)
            P.dma("sp", zvn_f[0:1, :].rearrange("p (s c) -> p s c", s=NS), S["ps"][:, C_AV:C_AV + 256].rearrange("(o s) c -> o s c", o=1), w=["zvn_f"])
            P.op("dve", lambda e: e.memset(zvn[0:1, :], 1.0), w=["zvn"])
            P.op("dve", lambda e: e.tensor_copy(out=zvn4[0:1, :, :, 0:128], in_=zvn_f[0:1, :].rearrange("p (s g d) -> p s g d", s=NS, g=2)), r=["zvn_f", "zvn"], w=["zvn"])
            NKS = 2049
            zIall = A.f32(NKS + 3); zikgs = [A.f32(NPG * 64), A.f32(NPG * 64)]; zikT = A.f32(NKS + 3)
            zik_rows = I["cache_ik"].rearrange("(r t) d -> r (t d)", t=8)
            zk_rows = I["cache_k"].rearrange("(r t) d -> r (t d)", t=8)
            zv_rows = I["cache_v"].rearrange("(r t) d -> r (t d)", t=8)
            zikg4s = [z_.rearrange("p (a t d) -> p a t d", a=2, t=8) for z_ in zikgs]

            def ts1_gather(s_):
                zg4 = zikg4s[s_ % 2]
                for a_ in range(2):
                    col = s_ * 2 + a_
                    P.dma_raw("pool", lambda e, a_=a_, col=col: e.indirect_dma_start(out=zg4[:, a_].rearrange("p t d -> p (t d)"), out_offset=None, in_=zik_rows, in_offset=bass.IndirectOffsetOnAxis(ap=zidx_i[:, col:col + 1], axis=0)),
                              r=["zidx_i"], w=[f"zikg{s_ % 2}"], sw=True)
            ts1_gather(0)
            zr8 = [A.f32(512), A.f32(512)]; zrw = A.f32(NKS + 3)
            for s_ in range(NS):
                if s_ + 1 < NS:
                    ts1_gather(s_ + 1)
                zikg4 = zikg4s[s_ % 2]; zikgn = f"zikg{s_ % 2}"
                for q4 in range(4):
                    for i4 in range(4):
                        bk = q4 * 4 + i4
                        t8, a_ = bk // 2, bk % 2
                        P.op("pe", lambda e, t8=t8, a_=a_, i4=i4, zikg4=zikg4: e.transpose(out=pb[3][0:64, i4 * 128:(i4 + 1) * 128], in_=zikg4[:, a_, t8, :], identity=ident), r=[zikgn, "ident"], w=[PB[3]])
                    evac("act" if q4 % 2 else "dve", zikT[0:64, q4 * 512:(q4 + 1) * 512], pb[3][0:64, :], [PB[3]], ["zikT"])
                P.op("dve", lambda e, s_=s_: e.tensor_copy(out=zikT[0:64, 2048:2049], in_=zikn[0:64, s_:s_ + 1]), r=["zikn"], w=["zikT"])
                for kb in range(5):
                    c0, c1 = kb * 512, min(NKS, kb * 512 + 512)
                    w_ = c1 - c0
                    rb = zr8[kb % 2]; rbn = f"zr8{kb % 2}"
                    P.op("pe", lambda e, s_=s_, c0=c0, c1=c1, w_=w_, kb=kb: e.matmul(out=pb[4 + kb % 2][0:8, 0:w_], lhsT=ziqT3[0:64, s_, :], rhs=zikT[0:64, c0:c1], start=True, stop=True), r=["ziqT", "zikT"], w=[PB[4 + kb % 2]])
                    P.op("dve", lambda e, s_=s_, w_=w_, kb=kb, rb=rb: e.tensor_scalar(out=rb[0:8, 0:w_], in0=pb[4 + kb % 2][0:8, 0:w_], scalar1=0.0, scalar2=ziwT[0:8, s_:s_ + 1], op0=ALU.max, op1=ALU.mult),
                         r=[PB[4 + kb % 2], "ziwT"], w=[rbn])
                    P.op("pe", lambda e, w_=w_, kb=kb, rb=rb: e.matmul(out=pb[6 + kb % 2][0:1, 0:w_], lhsT=zones[0:8, 0:1], rhs=rb[0:8, 0:w_], start=True, stop=True), r=["zones", rbn], w=[PB[6 + kb % 2]])
                    evac("act", zrw[0:1, c0:c1], pb[6 + kb % 2][0:1, 0:w_], [PB[6 + kb % 2]], ["zrw"])
                P.dma("sp", zIall[s_:s_ + 1, 0:NKS], zrw[0:1, 0:NKS], r=["zrw"], w=["zIall"])
            zbs = A.f32(16); zjunk = A.bf16(NKS + 3); zMB = A.f32(NKS + 3)
            zlo, zrng, zthr, zcnt, zmm = zbs[R, 0:1], zbs[R, 1:2], zbs[R, 2:3], zbs[R, 3:4], zbs[R, 4:5]
            P.op("dve", lambda e: e.tensor_reduce(out=zlo, in_=zIall[R, 0:NKS], axis=AX.X, op=ALU.min), r=["zIall"], w=["zlo"])
            P.op("dve", lambda e: e.tensor_reduce(out=zrng, in_=zIall[R, 0:NKS], axis=AX.X, op=ALU.max), r=["zIall"], w=["zrng"])
            P.op("dve", lambda e: e.tensor_tensor(out=zrng, in0=zrng, in1=zlo, op=ALU.subtract), r=["zrng", "zlo"], w=["zrng"])
            for it in range(24):
                st = 2.0 ** -(it + 1)
                P.op("dve", lambda e, st=st: e.scalar_tensor_tensor(out=zthr, in0=zrng, scalar=st, in1=zlo, op0=ALU.mult, op1=ALU.add), r=["zrng", "zlo"], w=["zthr"])
                P.op("dve", lambda e: e.tensor_scalar(out=zjunk[R, 0:NKS], in0=zIall[R, 0:NKS], scalar1=zthr, scalar2=None, op0=ALU.is_ge, op1=ALU.add, accum_out=zcnt), r=["zIall", "zthr"], w=["zjunk", "zcnt"])
                P.op("dve", lambda e: e.tensor_scalar(out=zmm, in0=zcnt, scalar1=255.5, scalar2=zrng, op0=ALU.is_ge, op1=ALU.mult), r=["zcnt", "zrng"], w=["zmm"])
                P.op("dve", lambda e, st=st: e.scalar_tensor_tensor(out=zlo, in0=zmm, scalar=st, in1=zlo, op0=ALU.mult, op1=ALU.add), r=["zmm", "zlo"], w=["zlo"])
            P.op("dve", lambda e: e.tensor_scalar(out=zMB[R, 0:NKS], in0=zIall[R, 0:NKS], scalar1=zlo, scalar2=None, op0=ALU.is_ge), r=["zIall", "zlo"], w=["zMB"])
            zMT = A.f32(NPG * NS); zMT3 = zMT.rearrange("p (g s) -> p g s", g=NPG); zMn = A.f32(NS)
            for pg in range(NPG):
                P.op("pe", lambda e, pg=pg: e.transpose(out=pb[0][:, pg * NS:(pg + 1) * NS], in_=zMB[R, pg * 128:(pg + 1) * 128], identity=ident[R, R]), r=["zMB", "ident"], w=[PB[0]])
            evac("dve", zMT, pb[0][:, 0:NPG * NS], [PB[0]], ["zMT"])
            P.op("pe", lambda e: e.transpose(out=pb[1][0:1, 0:NS], in_=zMB[R, 2048:2049], identity=ident[R, R]), r=["zMB", "ident"], w=[PB[1]])
            evac("dve", zMn[0:1, :], pb[1][0:1, 0:NS], [PB[1]], ["zMn"])
            zkgs = [A.f32(NPG * 256), A.f32(NPG * 256)]; zvgs = [A.f32(NPG * 256), A.f32(NPG * 256)]
            zkg4s = [z_.rearrange("p (a t c) -> p a t c", a=2, t=8) for z_ in zkgs]; zvg4s = [z_.rearrange("p (a t c) -> p a t c", a=2, t=8) for z_ in zvgs]

            def ts2_gather(s_):
                zk4 = zkg4s[s_ % 2]; zv4 = zvg4s[s_ % 2]
                for a_ in range(2):
                    col = s_ * 2 + a_
                    P.dma_raw("pool", lambda e, a_=a_, col=col: e.indirect_dma_start(out=zk4[:, a_].rearrange("p t c -> p (t c)"), out_offset=None, in_=zk_rows, in_offset=bass.IndirectOffsetOnAxis(ap=zidx_i[:, col:col + 1], axis=0)),
                              r=["zidx_i"], w=[f"zkg{s_ % 2}"], sw=True)
                    P.dma_raw("pool", lambda e, a_=a_, col=col: e.indirect_dma_start(out=zv4[:, a_].rearrange("p t c -> p (t c)"), out_offset=None, in_=zv_rows, in_offset=bass.IndirectOffsetOnAxis(ap=zidx_i[:, col:col + 1], axis=0)),
                              r=["zidx_i"], w=[f"zvg{s_ % 2}"], sw=True)
            ts2_gather(0)
            zKT = A.bf16(NPG * 256); zKT4 = zKT.rearrange("p (g k c) -> p g k c", g=NPG, k=2)
            zVb = A.bf16(NPG * 2 * 130); zVb4 = zVb.rearrange("p (g k d) -> p g k d", g=NPG, k=2)
            zP = A.bf16(NPG * 4); zP3 = zP.rearrange("p (g h) -> p g h", g=NPG); zPn = A.bf16(4)
            zsm = A.f32(16); zcr = A.f32(128)
            zo = A.f32(256); zoT = A.bf16(4 * NS); zoT3 = zoT.rearrange("p (h s) -> p h s", h=4)
            P.op("dve", lambda e: e.memset(zVb, 1.0), w=["zVb"])
            for s_ in range(NS):
                if s_ + 1 < NS:
                    ts2_gather(s_ + 1)
                zkg4 = zkg4s[s_ % 2]; zvg4 = zvg4s[s_ % 2]; zkgn = f"zkg{s_ % 2}"; zvgn = f"zvg{s_ % 2}"
                for a_ in range(2):
                    P.op("act", lambda e, a_=a_, zvg4=zvg4: e.activation(out=zVb.rearrange("p (t a k d) -> p a t k d", t=8, a=2, k=2)[:, a_, :, :, 0:128], in_=zvg4[:, a_].rearrange("p t (k d) -> p t k d", k=2), func=AF.Copy),
                         r=[zvgn, "zVb"], w=["zVb"])
                for pq in range(8):
                    for i2 in range(2):
                        bk = pq * 2 + i2
                        t8, a_ = bk // 2, bk % 2
                        for g in range(2):
                            P.op("pe", lambda e, t8=t8, a_=a_, g=g, i2=i2, pq=pq, zkg4=zkg4: e.transpose(out=pb[2 + pq % 2][:, (i2 * 2 + g) * 128:(i2 * 2 + g + 1) * 128], in_=zkg4[:, a_, t8, g * 128:(g + 1) * 128], identity=ident),
                                 r=[zkgn, "ident"], w=[PB[2 + pq % 2]])
                    evac("act" if pq % 2 else "dve", zKT[:, pq * 512:(pq + 1) * 512], pb[2 + pq % 2], [PB[2 + pq % 2]], ["zKT"])
                for pg in range(NPG):
                    for g in range(2):
                        P.op("pe", lambda e, pg=pg, g=g, s_=s_: e.matmul(out=pb[4][:, pg * 4 + g * 2:pg * 4 + g * 2 + 2], lhsT=zKT4[:, pg, g, :], rhs=zqT.rearrange("p (h s) -> p h s", h=4)[:, g * 2:(g + 1) * 2, s_], start=True, stop=True),
                             r=["zKT", "zqT"], w=[PB[4]])
                for g in range(2):
                    P.op("pe", lambda e, g=g, s_=s_: e.matmul(out=pb[5][0:1, g * 2:g * 2 + 2], lhsT=zkTn.rearrange("p (g s) -> p g s", g=2)[:, g, s_:s_ + 1], rhs=zqT.rearrange("p (h s) -> p h s", h=4)[:, g * 2:(g + 1) * 2, s_], start=True, stop=True),
                         r=["zkTn", "zqT"], w=[PB[5]])
                P.op("dve", lambda e: e.tensor_reduce(out=zsm[:, 0:1], in_=pb[4][:, 0:NPG * 4], axis=AX.X, op=ALU.max), r=[PB[4]], w=["zsm0"])
                P.op("pe", lambda e: e.transpose(out=pb[6][0:1, 0:128], in_=zsm[:, 0:1], identity=ident), r=["zsm0", "ident"], w=[PB[6]])
                evac("dve", zcr[0:1, :], pb[6][0:1, 0:128], [PB[6]], ["zcr"])
                P.op("dve", lambda e: e.tensor_reduce(out=zsm[0:1, 1:2], in_=zcr[0:1, :], axis=AX.X, op=ALU.max), r=["zcr"], w=["zsm1"])
                P.op("dve", lambda e: e.tensor_reduce(out=zsm[0:1, 2:3], in_=pb[5][0:1, 0:4], axis=AX.X, op=ALU.max), r=[PB[5]], w=["zsm2"])
                P.op("dve", lambda e: e.tensor_tensor(out=zsm[0:1, 1:2], in0=zsm[0:1, 1:2], in1=zsm[0:1, 2:3], op=ALU.max), r=["zsm1", "zsm2"], w=["zsm1"])
                P.op("dve", lambda e: e.tensor_scalar(out=zsm[0:1, 1:2], in0=zsm[0:1, 1:2], scalar1=-zSCALE, scalar2=None, op0=ALU.mult), r=["zsm1"], w=["zsm1"])
                P.op("pe", lambda e: e.matmul(out=pb[6][:, 128:129], lhsT=zones[0:1, :], rhs=zsm[0:1, 1:2], start=True, stop=True), r=["zones", "zsm1"], w=[PB[6]])
                evac("dve", zsm[:, 3:4], pb[6][:, 128:129], [PB[6]], ["zsm3"])
                P.op("act", lambda e: e.activation(out=zP, in_=pb[4][:, 0:NPG * 4], func=AF.Exp, scale=zSCALE, bias=zsm[:, 3:4]), r=[PB[4], "zsm3"], w=["zP"])
                P.op("act", lambda e: e.activation(out=zPn[0:1, :], in_=pb[5][0:1, 0:4], func=AF.Exp, scale=zSCALE, bias=zsm[0:1, 3:4]), r=[PB[5], "zsm3"], w=["zPn"])
                for h in range(4):
                    P.op("dve", lambda e, h=h, s_=s_: e.tensor_tensor(out=zP3[:, :, h], in0=zP3[:, :, h], in1=zMT3[:, :, s_], op=ALU.mult), r=["zP", "zMT"], w=["zP"])
                P.op("dve", lambda e, s_=s_: e.tensor_scalar(out=zPn[0:1, :], in0=zPn[0:1, :], scalar1=zMn[0:1, s_:s_ + 1], scalar2=None, op0=ALU.mult), r=["zPn", "zMn"], w=["zPn"])
                for g in range(2):
                    for pg in range(NPG):
                        P.op("pe", lambda e, pg=pg, g=g: e.matmul(out=pb[g][0:2, 0:130], lhsT=zP3[:, pg, g * 2:(g + 1) * 2], rhs=zVb4[:, pg, g, :], start=(pg == 0), stop=False), r=["zP", "zVb"], w=[PB[g]])
                    P.op("pe", lambda e, g=g, s_=s_: e.matmul(out=pb[g][0:2, 0:130], lhsT=zPn[0:1, g * 2:(g + 1) * 2], rhs=zvn4[0:1, s_, g, :], start=False, stop=True), r=["zPn", "zvn"], w=[PB[g]])
                for g in range(2):
                    P.op("dve", lambda e, g=g: e.reciprocal(out=zsm[0:2, 4 + g:5 + g], in_=pb[g][0:2, 128:129]), r=[PB[g]], w=["zsm4"])
                    P.op("dve", lambda e, g=g: e.tensor_scalar(out=zo[0:2, g * 128:(g + 1) * 128], in0=pb[g][0:2, 0:128], scalar1=zsm[0:2, 4 + g:5 + g], scalar2=None, op0=ALU.mult), r=[PB[g], "zsm4"], w=["zo"])
                for g in range(2):
                    P.op("pe", lambda e, g=g: e.transpose(out=pb[7][:, g * 2:(g + 1) * 2], in_=zo[0:2, g * 128:(g + 1) * 128], identity=ident[0:2, 0:2]), r=["zo", "ident"], w=[PB[7]])
                P.op("dve", lambda e, s_=s_: e.tensor_copy(out=zoT3[:, :, s_], in_=pb[7][:, 0:4]), r=[PB[7]], w=["zoT"])
            P.dma("sp", S["cT"][NT][:, 4:8, 0:NS], zoT3, r=["zoT"], w=["cTas"])

        if not os.environ.get('MK_NOD'):
            P.barrier()
            A.reset(persist0)
            zt = A.bf16(1024)
            P.op("pool", lambda e: e.memset(zt, 0.0), w=["zt"])
            if True:
                for ti in range(NT + 1):
                    if (ti == NT and os.environ.get('MK_NOTS')) or (ti < NT and (os.environ.get('MK_NOT') or ti >= int(os.environ.get("MK_TNT", NT)))):
                        P.dma("sp", S["cT"][ti][:, 4:8, :], zt[:, 0:512].rearrange("p (k t) -> p k t", k=4), r=["zt"], w=[f"cT{ti}"])
                    if ti == NT:
                        P.dma("sp", S["cT"][ti][:, 0:4, 16:128], zt[:, 0:448].rearrange("p (k t) -> p k t", k=4), r=["zt"], w=[f"cT{ti}"])
            g2_bc = A.f32(D); ga1_bc = A.f32(D); a2_bc = A.f32(D); sh2_bc = A.f32(D)
            a2_s = A.f32(D)
            P.dma("sp", g2_bc, I["g2"].to_broadcast([128, D]), w=["g2_bc"])
            P.dma("sp", ga1_bc, S["mod"][16:17, 2 * D:3 * D].to_broadcast([128, D]), r=["mod_scr"], w=["ga1_bc"])
            P.dma("sp", sh2_bc, S["mod"][16:17, 3 * D:4 * D].to_broadcast([128, D]), r=["mod_scr"], w=["sh2_bc"])
            P.dma("sp", a2_bc, S["mod"][16:17, 4 * D:5 * D].to_broadcast([128, D]), r=["mod_scr"], w=["a2_bc"])
            P.op("dve", lambda e: e.scalar_tensor_tensor(out=a2_bc, in0=a2_bc, scalar=1.0, in1=g2_bc, op0=ALU.add, op1=ALU.mult),
                 r=["a2_bc", "g2_bc"], w=["a2_bc"])
            P.op("dve", lambda e: e.scalar_tensor_tensor(out=a2_s[0:NS, :], in0=mod_sb[0:NS, 4 * D:5 * D], scalar=1.0, in1=g2_bc[0:NS, :], op0=ALU.add, op1=ALU.mult),
                 r=["mod_sb", "g2_bc"], w=["a2_s"])
            persistD = A.off
            w_out_b = A.bf16(8 * D); w_out3 = w_out_b.rearrange("p (k n) -> p k n", k=8)
            w_f_b = A.bf16(8 * 2 * DFF); w_f3 = w_f_b.rearrange("p (k n) -> p k n", k=8)
            persistD1 = A.off
            wst = [A.f32(8 * 512), A.f32(8 * 512)]
            for cb in range(2 + 11):
                ws3 = wst[cb % 2].rearrange("p (k n) -> p k n", k=8)
                if cb < 2:
                    src = I["w_out"][:, cb * 512:(cb + 1) * 512]; dst = w_out3[:, :, cb * 512:(cb + 1) * 512]; dn = "w_out_b"
                else:
                    src = I["w_ffn_in"][:, (cb - 2) * 512:(cb - 1) * 512]; dst = w_f3[:, :, (cb - 2) * 512:(cb - 1) * 512]; dn = "w_f_b"
                P.dma("sp", ws3, src.rearrange("(k p) n -> p k n", p=128), w=[f"wst{cb % 2}"])
                P.op("pool" if cb % 2 else "dve", lambda e, ws3=ws3, dst=dst: e.tensor_copy(out=dst, in_=ws3), r=[f"wst{cb % 2}"], w=[dn])
            P.seed_after_staging()
            A.reset(persistD1)
            xt = [A.f32(D), A.f32(D)]
            cTt = [A.bf16(8 * 128), A.bf16(8 * 128)]
            x1t = A.f32(D); h2 = A.f32(D); small = A.f32(16)
            h2T = A.bf16(8 * 512); h2T3 = h2T.rearrange("p (k t) -> p k t", k=8)
            uT = A.bf16(22 * 512); uT3 = uT.rearrange("p (f t) -> p f t", f=22)
            gsb = [A.f32(512), A.f32(512)]

            def rms_mod(rows, xin, xin_n, a_t, a_n, sh_t, sh_n, out, out_n):
                ss, rs = small[0:rows, 0:1], small[0:rows, 1:2]
                P.op("act", lambda e: e.activation(out=out, in_=xin, func=AF.Square, accum_out=ss), r=[xin_n], w=[out_n, "ss"])
                P.op("dve", lambda e: e.tensor_scalar(out=rs, in0=ss, scalar1=1.0 / D, scalar2=1e-6, op0=ALU.mult, op1=ALU.add), r=["ss"], w=["rs"])
                P.op("act", lambda e: e.activation(out=rs, in_=rs, func=AF.Sqrt), r=["rs"], w=["rs"])
                P.op("dve", lambda e: e.reciprocal(out=rs, in_=rs), r=["rs"], w=["rs"])
                P.op("dve", lambda e: e.scalar_tensor_tensor(out=out, in0=xin, scalar=rs, in1=a_t, op0=ALU.mult, op1=ALU.mult), r=[xin_n, "rs", a_n], w=[out_n])
                if sh_t is not None:
                    P.op("pool", lambda e: e.tensor_tensor(out=out, in0=out, in1=sh_t, op=ALU.add), r=[out_n, sh_n], w=[out_n])

            groups = [(g4, [(g4 * 4 + j, 128) for j in range(4)]) for g4 in range(4)] + [(4, [(NT, NS)])]
            for g4, tiles in groups:
                ntok = sum(r_ for _, r_ in tiles)
                col = 0
                for (ti, rows) in tiles:
                    it = ti
                    xb = xt[it % 2]; xn = f"xt{it % 2}"; cb_ = cTt[it % 2]; cn = f"cTt{it % 2}"
                    cT3 = cb_.rearrange("p (k t) -> p k t", k=8)
                    smp = (rows == NS)
                    P.dma("sp", xb[0:rows, :], I["xs"] if smp else I["x_own"][ti * 128:(ti + 1) * 128, :], w=[xn])
                    P.dma("sp", cT3, S["cT"][ti], r=[f"cT{ti}"], w=[cn])
                    ga1_t = mod_sb[0:NS, 2 * D:3 * D] if smp else ga1_bc
                    for hb in range(2):
                        for k in range(8):
                            P.op("pe", lambda e, k=k, hb=hb, cT3=cT3, rows=rows: e.matmul(out=pb[hb][0:rows, :], lhsT=cT3[:, k, 0:rows], rhs=w_out3[:, k, hb * 512:(hb + 1) * 512], start=(k == 0), stop=(k == 7)),
                                 r=[cn, "w_out_b"], w=[PB[hb]])
                        P.op("dve", lambda e, hb=hb, rows=rows, ga1_t=ga1_t: e.tensor_tensor(out=x1t[0:rows, hb * 512:(hb + 1) * 512], in0=pb[hb][0:rows, :], in1=ga1_t[0:rows, hb * 512:(hb + 1) * 512], op=ALU.mult),
                             r=[PB[hb], "ga1_bc", "mod_sb"], w=["x1t"])
                    P.op("pool", lambda e, rows=rows, xb=xb: e.tensor_tensor(out=x1t[0:rows, :], in0=x1t[0:rows, :], in1=xb[0:rows, :], op=ALU.add), r=["x1t", xn], w=["x1t"])
                    P.dma("sp", S["x1"][ti, 0:rows, :], x1t[0:rows, :], r=["x1t"], w=[f"x1_{ti}"])
                    if smp:
                        rms_mod(rows, x1t[0:rows, :], "x1t", a2_s[0:rows, :], "a2_s", mod_sb[0:rows, 3 * D:4 * D], "mod_sb", h2[0:rows, :], "h2")
                    else:
                        rms_mod(rows, x1t, "x1t", a2_bc, "a2_bc", sh2_bc, "sh2_bc", h2, "h2")
                    for k in range(8):
                        bank = 2 + k // 4
                        P.op("pe", lambda e, k=k, bank=bank, rows=rows: e.transpose(out=pb[bank][:, (k % 4) * 128:(k % 4) * 128 + rows], in_=h2[0:rows, k * 128:(k + 1) * 128], identity=ident[0:rows, 0:rows]),
                             r=["h2", "ident"], w=[PB[bank]])
                    for half_ in range(2):
                        evac(alt(), h2T3[:, half_ * 4:(half_ + 1) * 4, col:col + rows], pb[2 + half_][:, :].rearrange("p (k t) -> p k t", k=4)[:, :, 0:rows], [PB[2 + half_]], ["h2T"])
                    col += rows
                for fb in range(22):
                    for which in range(2):
                        bank = 4 + which * 2 + fb % 2
                        c0 = which * DFF + fb * 128
                        for k in range(8):
                            P.op("pe", lambda e, k=k, bank=bank, c0=c0, ntok=ntok: e.matmul(out=pb[bank][:, 0:ntok], lhsT=w_f3[:, k, c0:c0 + 128], rhs=h2T3[:, k, 0:ntok], start=(k == 0), stop=(k == 7)),
                                 r=["h2T", "w_f_b"], w=[PB[bank]])
                    gs_ = gsb[fb % 2]; gn = f"gsb{fb % 2}"
                    P.op("act", lambda e, fb=fb, gs_=gs_, ntok=ntok: e.activation(out=gs_[:, 0:ntok], in_=pb[4 + fb % 2][:, 0:ntok], func=AF.Silu), r=[PB[4 + fb % 2]], w=[gn])
                    P.op("dve", lambda e, fb=fb, gs_=gs_, ntok=ntok: e.tensor_tensor(out=uT3[:, fb, 0:ntok], in0=gs_[:, 0:ntok], in1=pb[6 + fb % 2][:, 0:ntok], op=ALU.mult),
                         r=[gn, PB[6 + fb % 2]], w=["uT"])
                P.dma("sp", S["uT"][g4], uT3, r=["uT"], w=[f"uT{g4}"])

            P.barrier()
            A.reset(persistD)
            ga2_bc = A.f32(D); gf_bc = A.f32(D)
            P.dma("sp", gf_bc, I["g_final"].to_broadcast([128, D]), w=["gf_bc"])
            P.dma("sp", ga2_bc, S["mod"][16:17, 5 * D:6 * D].to_broadcast([128, D]), r=["mod_scr"], w=["ga2_bc"])
            w_o_b = A.bf16(22 * D); w_o3 = w_o_b.rearrange("p (k n) -> p k n", k=22)
            persistD2 = A.off
            wst = [A.f32(22 * 256), A.f32(22 * 256)]
            for cb in range(4):
                ws3 = wst[cb % 2].rearrange("p (k n) -> p k n", k=22)
                P.dma("sp", ws3, I["w_ffn_out"][:, cb * 256:(cb + 1) * 256].rearrange("(k p) n -> p k n", p=128), w=[f"wst{cb % 2}"])
                P.op("pool" if cb % 2 else "dve", lambda e, ws3=ws3, cb=cb: e.tensor_copy(out=w_o3[:, :, cb * 256:(cb + 1) * 256], in_=ws3), r=[f"wst{cb % 2}"], w=["w_o_b"])
            P.seed_after_staging()
            A.reset(persistD2)
            uTg = [A.bf16(22 * 512), A.bf16(22 * 512)]
            x1b = [A.f32(D), A.f32(D)]
            x2 = A.f32(D); small = A.f32(16)
            yb = [A.f32(D), A.f32(D)]
            for g4, tiles in groups:
                ug = uTg[g4 % 2]; un = f"uTg{g4 % 2}"
                ug3 = ug.rearrange("p (f t) -> p f t", f=22)
                P.dma("sp", ug3, S["uT"][g4], r=[f"uT{g4}"], w=[un])
                col = 0
                for (ti, rows) in tiles:
                    smp = (rows == NS)
                    xb = x1b[ti % 2]; xn = f"x1b{ti % 2}"; yo = yb[ti % 2]; yn = f"yb{ti % 2}"
                    P.dma("sp", xb[0:rows, :], S["x1"][ti, 0:rows, :], r=[f"x1_{ti}"], w=[xn])
                    ga2_t = mod_sb[0:NS, 5 * D:6 * D] if smp else ga2_bc
                    for hb in range(2):
                        for kf in range(22):
                            P.op("pe", lambda e, kf=kf, hb=hb, rows=rows, col=col, ug3=ug3: e.matmul(out=pb[hb][0:rows, :], lhsT=ug3[:, kf, col:col + rows], rhs=w_o3[:, kf, hb * 512:(hb + 1) * 512], start=(kf == 0), stop=(kf == 21)),
                                 r=[un, "w_o_b"], w=[PB[hb]])
                        P.op("dve", lambda e, hb=hb, rows=rows, ga2_t=ga2_t: e.tensor_tensor(out=x2[0:rows, hb * 512:(hb + 1) * 512], in0=pb[hb][0:rows, :], in1=ga2_t[0:rows, hb * 512:(hb + 1) * 512], op=ALU.mult),
                             r=[PB[hb], "ga2_bc", "mod_sb"], w=["x2"])
                    P.op("pool", lambda e, rows=rows, xb=xb: e.tensor_tensor(out=x2[0:rows, :], in0=x2[0:rows, :], in1=xb[0:rows, :], op=ALU.add), r=["x2", xn], w=["x2"])
                    rms_mod(rows, x2[0:rows, :], "x2", gf_bc[0:rows, :], "gf_bc", None, None, yo[0:rows, :], yn)
                    P.dma("sp", O["y_s"] if smp else O["y_own"][ti * 128:(ti + 1) * 128, :], yo[0:rows, :], r=[yn])
                    col += rows

        P.build(ctx)
        global LAST_PROG
        LAST_PROG = P
    return nc


def rope_table(pos):
    pos = np.asarray(pos, np.float64)[:, None]
    invA = 500000.0 ** (-np.arange(16, dtype=np.float64) / 16)
    invI = 500000.0 ** (-np.arange(8, dtype=np.float64) / 8)
    angA = (pos.astype(np.float32) * invA.astype(np.float32)[None, :]).astype(np.float32)
    angI = (pos.astype(np.float32) * invI.astype(np.float32)[None, :]).astype(np.float32)
    t = np.concatenate([np.tile(np.cos(angA), (1, 4)), np.tile(np.sin(angA), (1, 4)),
                        np.tile(np.cos(angI), (1, 8)), np.tile(np.sin(angI), (1, 8))], axis=1)
    return np.ascontiguousarray(t.astype(np.float32))


_NC_CACHE = {}


def kernel(x_prompt, x_sample, c_prompt, c_sample, cache_k, cache_v, cache_idx_k, page_table, state_conv, state_ssm,
           w_ada, b_ada, g_norm1, w_in, w_conv, a_log, dt_bias, g_gdn_norm, w_out, g_norm2, w_ffn_in, w_ffn_out, g_final):
    f = lambda a: np.ascontiguousarray(np.asarray(a, dtype=np.float32))
    x_prompt = f(x_prompt); x_sample = f(x_sample)
    w_in_p = np.ascontiguousarray(f(w_in)[0][:, PERM])
    wc_p = np.ascontiguousarray(np.concatenate([f(w_conv)[0][:, 512:1536], f(w_conv)[0][:, 0:512]], axis=1))
    jj, cc = np.meshgrid(np.arange(128), np.arange(128), indexing="ij")
    GCONST = np.ascontiguousarray(np.concatenate([
        (jj <= cc).astype(np.float32),
        np.ones((128, 128), np.float32),
        np.where(cc >= jj, 1e4, 0.0).astype(np.float32),
        np.where(cc < jj, -1e4, 0.0).astype(np.float32),
        np.tile(np.eye(128, dtype=np.float32), (1, 4))], axis=1))
    CAUS = np.ascontiguousarray(np.where(np.arange(128)[None, :] > np.arange(128)[:, None], -30000.0, 0.0).astype(np.float32))
    pp = np.arange(128)
    TSEL = np.zeros((128, 257), np.float32)
    TSEL[:, :256] = np.tile((np.arange(8)[None, :] == (pp // 16)[:, None]).astype(np.float32), (1, 32))
    TSEL[:, 256] = pp % 16
    cik2 = f(cache_idx_k).reshape(-1, 64); ck2 = f(cache_k).reshape(-1, 256); cv2 = f(cache_v).reshape(-1, 256)
    EYE16 = np.ascontiguousarray(np.tile(np.eye(16, dtype=np.float32).reshape(1, 256), (128, 1)))
    in_maps = []
    for c in range(8):
        b, s = c // 2, c % 2
        own = slice(s * HALF, (s + 1) * HALF); oth = slice((1 - s) * HALF, (2 - s) * HALF)
        m = {
            "x_own": f(x_prompt[b, own]), "x_oth": f(x_prompt[b, oth]),
            "cin": f(np.concatenate([np.asarray(c_sample)[c * NS:(c + 1) * NS], np.asarray(c_prompt)[b:b + 1]], 0)),
            "xs": f(x_sample[c * NS:(c + 1) * NS, 0]),
            "w_ada": f(w_ada)[0], "b_ada": f(b_ada), "g1": f(g_norm1), "w_in": w_in_p, "w_conv": f(w_conv)[0],
            "a_log": f(a_log), "dt_bias": f(dt_bias), "g_gdn": f(g_gdn_norm), "w_out": f(w_out)[0], "g2": f(g_norm2),
            "w_ffn_in": f(w_ffn_in)[0], "w_ffn_out": f(w_ffn_out)[0], "g_final": f(g_final)[None, :],
            "tab_own": rope_table(np.arange(s * HALF, (s + 1) * HALF)), "tab_oth": rope_table(np.arange((1 - s) * HALF, (2 - s) * HALF)),
            "tab_s": rope_table(np.full(NS, 2048)),
            "flags": np.tile(np.array([[float(s), (s - 1) * 30000.0, 0, 0]], np.float32), (128, 1)),
            "ident": np.eye(128, dtype=np.float32),
            "state_conv": f(np.asarray(state_conv)[0, c * NS:(c + 1) * NS]),
            "gconst": GCONST, "wc_p": wc_p,
            "state_ssm": f(np.asarray(state_ssm)[0, c * NS:(c + 1) * NS]), "eye16": EYE16, "caus": CAUS,
            "pt": np.ascontiguousarray(np.asarray(page_table, np.int32)[c * NS:(c + 1) * NS].reshape(1, NS * 16)), "tsel": TSEL,
            "cache_ik": cik2, "cache_k": ck2, "cache_v": cv2,
        }
        if os.environ.get('MK_NOTS'):
            for k_ in ("cache_ik", "cache_k", "cache_v"):
                m.pop(k_)
        in_maps.append(m)
    if "nc" not in _NC_CACHE:
        _NC_CACHE["nc"] = build_program()
    res = run_bass_kernel_spmd(_NC_CACHE["nc"], in_maps, core_ids=list(range(8)))
    R = res.results
    B = 4
    y_prompt = np.zeros((B, T, D), np.float32); nk = np.zeros((1, B, T, 2, 128), np.float32); nv = np.zeros_like(nk)
    nik = np.zeros((1, B, T, 64), np.float32); nconv = np.zeros((1, B, 3, 1536), np.float32); nssm = np.zeros((1, B, 4, 128, 128), np.float32)
    y_s = np.zeros((128, 1, D), np.float32); ks = np.zeros((1, 128, 1, 2, 128), np.float32); vs = np.zeros_like(ks)
    iks = np.zeros((1, 128, 1, 64), np.float32); convs = np.zeros((1, 128, 3, 1536), np.float32); ssms = np.zeros((1, 128, 4, 128, 128), np.float32)
    for c in range(8):
        b, s = c // 2, c % 2
        own = slice(s * HALF, (s + 1) * HALF)
        r = R[c]
        y_prompt[b, own] = r["y_own"]; nk[0, b, own] = r["k_own"].reshape(HALF, 2, 128); nv[0, b, own] = r["v_own"].reshape(HALF, 2, 128)
        nik[0, b, own] = r["ik_own"]
        if s == 1:
            nconv[0, b] = r["conv_tail"]; nssm[0, b] = r["ssm_fin"]
        sl = slice(c * NS, (c + 1) * NS)
        y_s[sl, 0] = r["y_s"]; ks[0, sl, 0] = r["k_s"].reshape(NS, 2, 128); vs[0, sl, 0] = r["v_s"].reshape(NS, 2, 128)
        iks[0, sl, 0] = r["ik_s"]; convs[0, sl] = r["conv_s"]; ssms[0, sl] = r["ssm_s"]
    return (y_prompt, y_s, nk, nv, nik, nconv, nssm, ks, vs, iks, convs, ssms)
```

```python
import numpy as np
from contextlib import ExitStack
import concourse.bass as bass
import concourse.mybir as mybir
from concourse.bass_utils import run_bass_kernel_spmd

F32 = mybir.dt.float32
BF16 = mybir.dt.bfloat16
I32 = mybir.dt.int32
U32 = mybir.dt.uint32
AF = mybir.ActivationFunctionType
ALU = mybir.AluOpType
AX = mybir.AxisListType

import os
G_NT = int(os.environ.get("MK_GNT", "16"))
GSTOP = int(os.environ.get("MK_GSTOP", "99"))
NOS = int(os.environ.get("MK_NOS", "0"))
NOG = int(os.environ.get("MK_NOG", "0"))
ENGS = ["pe", "act", "dve", "pool", "sp"]
EPOCH = 12000
N_DMA_SEMS = 40
N_SW_SEMS = 12

D = 1024
T = 4096
HALF = 2048
NT = 16
NS = 16
INC = 3664
DFF = 2816
C_K, C_V, C_B, C_A, C_AK, C_AV, C_IK = 0, 512, 1024, 1028, 1032, 1288, 1544
N_OTH = 1608
C_Q, C_Z, C_AQ, C_IQ, C_IW = 1608, 2120, 2632, 3144, 3656
PERM = np.concatenate([np.arange(512, 1024), np.arange(1024, 1536), np.arange(2048, 2056),
                       np.arange(2568, 2824), np.arange(2824, 3080), np.arange(3592, 3656),
                       np.arange(0, 512), np.arange(1536, 2048), np.arange(2056, 2568),
                       np.arange(3080, 3592), np.arange(3656, 3664)])
GS_W = 2056
TABW = 256


class Prog:
    def __init__(self, nc):
        self.nc = nc
        self.ops = {e: [] for e in ENGS}
        self.res = {}
        self.seed = []
        self.dma_cum = [0] * (N_DMA_SEMS + N_SW_SEMS)
        self.dma_rr = 0
        self.sw_rr = 0

    def _deps(self, r, w):
        deps = []
        for name in r:
            st = self.res.get(name)
            if st and st[0] is not None:
                deps.append(st[0])
        for name in w:
            st = self.res.get(name)
            if st:
                if st[0] is not None:
                    deps.append(st[0])
                deps.extend(st[1])
            else:
                deps.extend(self.seed)
        return deps

    @staticmethod
    def _tkey(t):
        return (t[0], t[1])

    def _commit(self, tok, r, w):
        k = self._tkey(tok)
        for name in r:
            st = self.res.setdefault(name, [None, []])
            if not os.environ.get("MK_NOPRUNE"):
                st[1] = [t for t in st[1] if self._tkey(t) != k]
            st[1].append(tok)
        for name in w:
            self.res[name] = [tok, []]

    def op(self, eng, fn, r=(), w=()):
        deps = self._deps(r, w)
        tok = ("e", eng, len(self.ops[eng]))
        self.ops[eng].append({"deps": deps, "fn": fn, "dma": None, "sig": False})
        self._commit(tok, r, w)
        return tok

    def dma_raw(self, q, fn, r=(), w=(), sw=False):
        deps = self._deps(r, w)
        if sw:
            s = N_DMA_SEMS + self.sw_rr
            self.sw_rr = (self.sw_rr + 1) % N_SW_SEMS
        else:
            s = self.dma_rr
            self.dma_rr = (self.dma_rr + 1) % N_DMA_SEMS
        if self.dma_cum[s] > 0:
            deps.append(("d", s, self.dma_cum[s]))
        self.dma_cum[s] += 16
        tok = ("d", s, self.dma_cum[s])
        self.ops[q].append({"deps": deps, "fn": fn, "dma": (s, self.dma_cum[s]), "sig": False})
        self._commit(tok, r, w)
        return tok

    def dma(self, q, out, in_, r=(), w=(), **kw):
        return self.dma_raw(q, lambda e, out=out, in_=in_, kw=kw: e.dma_start(out=out, in_=in_, **kw), r, w)

    def seed_after_staging(self, names=("wst0", "wst1")):
        toks = list(self.seed)
        for n in names:
            st = self.res.get(n)
            if st:
                if st[0] is not None:
                    toks.append(st[0])
                toks.extend(st[1])
        self.seed = toks

    def barrier(self):
        deps_all = []
        for st in self.res.values():
            if st[0] is not None:
                deps_all.append(st[0])
            deps_all.extend(st[1])
        best = {}
        for t in ([] if os.environ.get("MK_NOPRUNE") else deps_all):
            k = self._tkey(t)
            if k not in best or best[k][2] < t[2]:
                best[k] = t
        for e in ENGS:
            for i in range(len(self.ops[e]) - 1, -1, -1):
                if self.ops[e][i]["dma"] is None:
                    best[("e", e)] = ("e", e, i)
                    break
        if not os.environ.get("MK_NOPRUNE"):
            deps_all = list(best.values())
        toks = []
        for e in ENGS:
            toks.append(("e", e, len(self.ops[e])))
            self.ops[e].append({"deps": list(deps_all), "fn": None, "dma": None, "sig": False})
        for e in ENGS:
            self.ops[e].append({"deps": list(toks), "fn": None, "dma": None, "sig": False})
        self.res = {}
        self.seed = []

    def build(self, ctx):
        nc = self.nc
        fin = [("d", s, c) for s, c in enumerate(self.dma_cum) if c > 0]
        self.ops["sp"].append({"deps": fin, "fn": None, "dma": None, "sig": False})
        for e in ENGS:
            for o in self.ops[e]:
                for d in o["deps"]:
                    if d[0] == "e":
                        if d[1] == "pe" and e == "pe":
                            continue
                        self.ops[d[1]][d[2]]["sig"] = True
        signo = {}
        nsig = {}
        for e in ENGS:
            c = 0
            for i, o in enumerate(self.ops[e]):
                if o["sig"]:
                    c += 1
                    signo[(e, i)] = c
            nsig[e] = c
        esem = {e: [ctx.enter_context(nc.semaphore(f"s_{e}_{k}")) for k in range(max(1, (nsig[e] + EPOCH - 1) // EPOCH))]
                for e in ENGS}
        dsem = [ctx.enter_context(nc.semaphore(f"s_dma_{k}")) for k in range(N_DMA_SEMS + N_SW_SEMS)]
        block = ctx.enter_context(nc.Block())
        eobj = {"pe": nc.tensor, "act": nc.scalar, "dve": nc.vector, "pool": nc.gpsimd, "sp": nc.sync}

        self.trace = {e: [] for e in ENGS}

        def body_for(e):
            def body(eng):
                known = {}
                tr = self.trace[e]
                for i, o in enumerate(self.ops[e]):
                    need = {}
                    for d in o["deps"]:
                        if d[0] == "e":
                            if d[1] == "pe" and e == "pe":
                                continue
                            key, val = ("e", d[1]), signo[(d[1], d[2])]
                        else:
                            key, val = ("d", d[1]), d[2]
                        if known.get(key, 0) >= val:
                            continue
                        if need.get(key, 0) < val:
                            need[key] = val
                    for key, val in need.items():
                        if key[0] == "e":
                            ep = (val - 1) // EPOCH
                            eng.wait_ge(esem[key[1]][ep], val - ep * EPOCH)
                            tr.append(("w", (key[1], ep), val - ep * EPOCH))
                        else:
                            eng.wait_ge(dsem[key[1]], val)
                            tr.append(("w", ("d", key[1]), val))
                        known[key] = val
                    if o["fn"] is None:
                        if o["sig"]:
                            sn = signo[(e, i)]
                            ep = (sn - 1) // EPOCH
                            eng.nop().then_inc(esem[e][ep], 1)
                            tr.append(("i", (e, ep), 1))
                        continue
                    ins = o["fn"](eng)
                    if o["dma"] is not None:
                        ins.then_inc(dsem[o["dma"][0]], 16)
                        tr.append(("i", ("d", o["dma"][0]), 16))
                    elif o["sig"]:
                        sn = signo[(e, i)]
                        ep = (sn - 1) // EPOCH
                        ins.then_inc(esem[e][ep], 1)
                        tr.append(("i", (e, ep), 1))
            return body

        block.tensor(body_for("pe"))
        block.scalar(body_for("act"))
        block.vector(body_for("dve"))
        block.gpsimd(body_for("pool"))
        block.sync(body_for("sp"))


class Arena:
    def __init__(self, t, n):
        self.t, self.n, self.off, self.uid = t, n, 0, 0

    def reset(self, to=0):
        self.off = to

    def f32(self, cols):
        a = self.t[:, self.off:self.off + cols]
        self.off += cols
        assert self.off <= self.n, ("arena overflow", self.off, self.n)
        return a

    def bf16(self, cols):
        c32 = (cols + 1) // 2
        a = self.t[:, self.off:self.off + c32].bitcast(BF16)
        self.off += c32
        assert self.off <= self.n, ("arena overflow", self.off, self.n)
        return a


def build_program(stage=99):
    nc = bass.Bass("TRN2", target_bir_lowering=False)
    dt_in = lambda name, shape, dt=F32: nc.dram_tensor(name, list(shape), dt, kind="ExternalInput").ap()
    dt_out = lambda name, shape, dt=F32: nc.dram_tensor(name, list(shape), dt, kind="ExternalOutput").ap()
    dt_scr = lambda name, shape, dt=F32: nc.dram_tensor(name, list(shape), dt, kind="Internal").ap()

    I = {}
    I["x_own"] = dt_in("x_own", [HALF, D]); I["x_oth"] = dt_in("x_oth", [HALF, D])
    I["cin"] = dt_in("cin", [17, D]); I["xs"] = dt_in("xs", [NS, D])
    I["w_ada"] = dt_in("w_ada", [D, 6 * D]); I["b_ada"] = dt_in("b_ada", [1, 6 * D])
    I["g1"] = dt_in("g1", [1, D]); I["w_in"] = dt_in("w_in", [D, INC])
    I["w_conv"] = dt_in("w_conv", [4, 1536]); I["a_log"] = dt_in("a_log", [1, 4]); I["dt_bias"] = dt_in("dt_bias", [1, 4])
    I["g_gdn"] = dt_in("g_gdn", [1, 128]); I["w_out"] = dt_in("w_out", [D, D]); I["g2"] = dt_in("g2", [1, D])
    I["w_ffn_in"] = dt_in("w_ffn_in", [D, 2 * DFF]); I["w_ffn_out"] = dt_in("w_ffn_out", [DFF, D]); I["g_final"] = dt_in("g_final", [1, D])
    I["tab_own"] = dt_in("tab_own", [HALF, TABW]); I["tab_oth"] = dt_in("tab_oth", [HALF, TABW]); I["tab_s"] = dt_in("tab_s", [NS, TABW])
    I["flags"] = dt_in("flags", [128, 4]); I["ident"] = dt_in("ident", [128, 128])
    I["state_conv"] = dt_in("state_conv", [NS, 3, 1536])
    I["gconst"] = dt_in("gconst", [128, 1024]); I["wc_p"] = dt_in("wc_p", [4, 1536])
    I["state_ssm"] = dt_in("state_ssm", [NS, 4, 128, 128]); I["eye16"] = dt_in("eye16", [128, 256])
    I["caus"] = dt_in("caus", [128, 128])
    I["pt"] = dt_in("pt", [1, NS * 16], I32); I["tsel"] = dt_in("tsel", [128, 257])
    NPHYS = 2560
    if not os.environ.get('MK_NOTS'):
        I["cache_ik"] = dt_in("cache_ik", [NPHYS * 128, 64]); I["cache_k"] = dt_in("cache_k", [NPHYS * 128, 256]); I["cache_v"] = dt_in("cache_v", [NPHYS * 128, 256])

    O = {}
    O["y_own"] = dt_out("y_own", [HALF, D]); O["k_own"] = dt_out("k_own", [HALF, 256]); O["v_own"] = dt_out("v_own", [HALF, 256])
    O["ik_own"] = dt_out("ik_own", [HALF, 64]); O["conv_tail"] = dt_out("conv_tail", [3, 1536]); O["ssm_fin"] = dt_out("ssm_fin", [4, 128, 128])
    O["y_s"] = dt_out("y_s", [NS, D]); O["k_s"] = dt_out("k_s", [NS, 256]); O["v_s"] = dt_out("v_s", [NS, 256]); O["ik_s"] = dt_out("ik_s", [NS, 64])
    O["conv_s"] = dt_out("conv_s", [NS, 3, 1536]); O["ssm_s"] = dt_out("ssm_s", [NS, 4, 128, 128])

    S = {}
    S["mod"] = dt_scr("mod_scr", [17, 6 * D])
    S["gs"] = dt_scr("gs_scr", [2, 3 + HALF, GS_W])
    S["qT"] = dt_scr("qT_scr", [NT, 128, 4, 128], BF16)
    S["iqT"] = dt_scr("iqT_scr", [NT, 128, 4, 128], BF16)
    S["cT"] = dt_scr("cT_scr", [NT + 1, 128, 8, 128], BF16)
    S["x1"] = dt_scr("x1_scr", [NT + 1, 128, D])
    S["uT"] = dt_scr("uT_scr", [5, 128, 22, 512], BF16)
    S["ps"] = dt_scr("ps_scr", [NS, INC])

    ctx = ExitStack()
    with ctx:
        ctx.enter_context(nc.allow_low_precision(reason="bf16 matmul operands, fp32 accumulate"))
        P = Prog(nc)
        ARN = 52800
        arena_t = ctx.enter_context(nc.sbuf_tensor("arena", [128, ARN], F32))
        A = Arena(arena_t, ARN)
        pb = [ctx.enter_context(nc.psum_tensor(f"pb{i}", [128, 512], F32))[:, :] for i in range(8)]
        PB = [f"pb{i}" for i in range(8)]
        cnt = [0]

        def alt():
            cnt[0] += 1
            return "act" if cnt[0] % 2 else "dve"

        def evac(eng, out, in_, r, w):
            if eng == "act":
                P.op("act", lambda e: e.activation(out=out, in_=in_, func=AF.Copy), r=r, w=w)
            else:
                P.op(eng, lambda e: e.tensor_copy(out=out, in_=in_), r=r, w=w)

        ident = A.f32(128)
        flags = A.f32(4)
        P.dma("sp", ident, I["ident"], w=["ident"])
        P.dma("sp", flags, I["flags"], w=["flags"])
        mod_sb = A.f32(6 * D)
        persist0 = A.off

        cs = A.f32(D)
        csT = A.f32(8 * 17)
        csT3 = csT.rearrange("p (k m) -> p k m", k=8)
        ones = A.f32(128)
        bstage = A.f32(512)
        P.op("dve", lambda e: e.memset(ones, 1.0), w=["ones"])
        P.dma("sp", cs[0:17, :], I["cin"], w=["cs"])
        P.op("act", lambda e: e.activation(out=cs[0:17, :], in_=cs[0:17, :], func=AF.Silu), r=["cs"], w=["cs"])
        for k in range(8):
            P.op("pe", lambda e, k=k: e.transpose(out=pb[0][:, k * 17:(k + 1) * 17], in_=cs[0:17, k * 128:(k + 1) * 128], identity=ident[0:17, 0:17]),
                 r=["cs", "ident"], w=[PB[0]])
        evac("dve", csT, pb[0][:, 0:8 * 17], [PB[0]], ["csT"])
        wst = [A.f32(8 * 512), A.f32(8 * 512)]
        for cb in range(12):
            ws = wst[cb % 2]
            ws3 = ws.rearrange("p (k n) -> p k n", k=8)
            P.dma("sp", ws3, I["w_ada"][:, cb * 512:(cb + 1) * 512].rearrange("(k p) n -> p k n", p=128), w=[f"wst{cb % 2}"])
            P.dma("sp", bstage[0:1, :], I["b_ada"][:, cb * 512:(cb + 1) * 512], w=["bstage"])
            bank = 1 + cb % 2
            for k in range(8):
                P.op("pe", lambda e, k=k, ws3=ws3, bank=bank: e.matmul(out=pb[bank][0:17, :], lhsT=csT3[:, k, :], rhs=ws3[:, k, :], start=(k == 0), stop=False),
                     r=["csT", f"wst{cb % 2}"], w=[PB[bank]])
            P.op("pe", lambda e, bank=bank: e.matmul(out=pb[bank][0:17, :], lhsT=ones[0:1, 0:17], rhs=bstage[0:1, :], start=False, stop=True),
                 r=["ones", "bstage"], w=[PB[bank]])
            evac("act", mod_sb[0:17, cb * 512:(cb + 1) * 512], pb[bank][0:17, :], [PB[bank]], ["mod_sb"])
        P.dma("sp", S["mod"], mod_sb[0:17, :], r=["mod_sb"], w=["mod_scr"])
        A.reset(persist0)
        def alloc_att():
            KT = A.bf16(2 * T); ikT = A.bf16(T); VA = A.bf16(32 * 2 * 130); iwabs = A.f32(NT * 8); iwsgn = A.f32(NT * 8)
            ksq_ = A.f32(64); qsq_ = A.f32(64)
            return KT, ikT, VA, iwabs, iwsgn, ksq_, qsq_
        OLDALLOC = bool(os.environ.get("MK_OLDALLOC"))
        if not OLDALLOC:
            KT, ikT, VA, iwabs, iwsgn, ksq, qsq = alloc_att()
        persist1 = A.off
        a1_bc = A.f32(D); sh1_bc = A.f32(D)
        g1_bc = A.f32(D); a1_s = A.f32(D)
        P.dma("sp", g1_bc, I["g1"].to_broadcast([128, D]), w=["g1_bc"])
        P.dma("sp", a1_bc, S["mod"][16:17, D:2 * D].to_broadcast([128, D]), r=["mod_scr"], w=["a1_bc"])
        P.dma("sp", sh1_bc, S["mod"][16:17, 0:D].to_broadcast([128, D]), r=["mod_scr"], w=["sh1_bc"])
        P.op("dve", lambda e: e.scalar_tensor_tensor(out=a1_bc, in0=a1_bc, scalar=1.0, in1=g1_bc, op0=ALU.add, op1=ALU.mult),
             r=["a1_bc", "g1_bc"], w=["a1_bc"])
        P.op("dve", lambda e: e.scalar_tensor_tensor(out=a1_s[0:NS, :], in0=mod_sb[0:NS, D:2 * D], scalar=1.0, in1=g1_bc[0:NS, :], op0=ALU.add, op1=ALU.mult),
             r=["mod_sb", "g1_bc"], w=["a1_s"])
        persistA = A.off

        w_in_b = A.bf16(8 * INC)
        w_in3 = w_in_b.rearrange("p (k n) -> p k n", k=8)
        if OLDALLOC:
            KT, ikT, VA, iwabs, iwsgn, ksq, qsq = alloc_att()
        KT3 = KT.rearrange("p (g s) -> p g s", g=2)
        VA4 = VA.rearrange("p (t g d) -> p t g d", t=32, g=2)
        persistB = A.off
        wst = [A.f32(8 * 512), A.f32(8 * 512)]
        nblk = (INC + 511) // 512
        for cb in range(nblk):
            c0, c1 = cb * 512, min(INC, cb * 512 + 512)
            ws3 = wst[cb % 2].rearrange("p (k n) -> p k n", k=8)
            P.dma("sp", ws3[:, :, 0:c1 - c0], I["w_in"][:, c0:c1].rearrange("(k p) n -> p k n", p=128), w=[f"wst{cb % 2}"])
            eng = "pool" if cb % 2 else "dve"
            P.op(eng, lambda e, ws3=ws3, c0=c0, c1=c1: e.tensor_copy(out=w_in3[:, :, c0:c1], in_=ws3[:, :, 0:c1 - c0]),
                 r=[f"wst{cb % 2}"], w=["w_in_b"])
        P.seed_after_staging()
        A.reset(persistB)
        xt = [A.f32(D), A.f32(D)]
        hh = A.f32(D)
        sq = A.f32(D)
        hT = [A.bf16(8 * 128), A.bf16(8 * 128)]
        Psb_l = [A.f32(INC), A.f32(INC)]
        tab = [A.f32(TABW), A.f32(TABW)]
        small = A.f32(16)
        rt = [A.f32(128) for _ in range(4)]
        ikd = A.f32(128)
        tstage = [A.bf16(4 * 128), A.bf16(4 * 128)]
        zrow = A.f32(GS_W)
        P.op("pool", lambda e: e.memset(zrow[0:3, :], 0.0), w=["zrow"])
        P.dma("sp", S["gs"][0, 0:3, :], zrow[0:3, :], r=["zrow"], w=["gs_pre0"])

        def rope(Psb, PN, rows, base, H, Dh, half, tb, coff, soff, tname):
            xv = Psb[0:rows, base:base + H * Dh].rearrange("p (h d) -> p h d", d=Dh)
            x1, x2 = xv[:, :, 0:half], xv[:, :, half:2 * half]
            cosv = tb[0:rows, coff:coff + H * half].rearrange("p (h i) -> p h i", i=half)
            sinv = tb[0:rows, soff:soff + H * half].rearrange("p (h i) -> p h i", i=half)
            t = [r_[0:rows, 0:H * half].rearrange("p (h i) -> p h i", i=half) for r_ in rt]
            P.op("dve", lambda e: e.tensor_tensor(out=t[0], in0=x1, in1=cosv, op=ALU.mult), r=[PN, tname], w=["rt0"])
            P.op("pool", lambda e: e.tensor_tensor(out=t[1], in0=x2, in1=sinv, op=ALU.mult), r=[PN, tname], w=["rt1"])
            P.op("dve", lambda e: e.tensor_tensor(out=t[2], in0=x2, in1=cosv, op=ALU.mult), r=[PN, tname], w=["rt2"])
            P.op("pool", lambda e: e.tensor_tensor(out=t[3], in0=x1, in1=sinv, op=ALU.mult), r=[PN, tname], w=["rt3"])
            P.op("dve", lambda e: e.tensor_tensor(out=x1, in0=t[0], in1=t[1], op=ALU.subtract), r=["rt0", "rt1", PN], w=[PN])
            P.op("dve", lambda e: e.tensor_tensor(out=x2, in0=t[2], in1=t[3], op=ALU.add), r=["rt2", "rt3", PN], w=[PN])

        itc = [0]

        def proj_tile(mode, ti):
            it = itc[0]; itc[0] += 1
            rows = NS if mode == 2 else 128
            Psb = Psb_l[it % 2]; PN = f"Psb{it % 2}"
            xb = xt[it % 2]; xn = f"xt{it % 2}"
            tb = tab[it % 2]; tn = f"tab{it % 2}"
            hTb = hT[it % 2]; hTn = f"hT{it % 2}"
            hT3 = hTb.rearrange("p (k t) -> p k t", k=8)
            if mode == 2:
                xsrc, tsrc = I["xs"], I["tab_s"]
                a_t, a_n, sh_t, sh_n = a1_s, "a1_s", mod_sb[:, 0:D], "mod_sb"
                ncols = INC
            else:
                xsrc = (I["x_oth"] if mode == 0 else I["x_own"])[ti * 128:(ti + 1) * 128, :]
                tsrc = (I["tab_oth"] if mode == 0 else I["tab_own"])[ti * 128:(ti + 1) * 128, :]
                a_t, a_n, sh_t, sh_n = a1_bc, "a1_bc", sh1_bc, "sh1_bc"
                ncols = INC if mode == 1 else (C_Q + 512 if ti == NT - 1 else N_OTH)
            P.dma("sp", xb[0:rows, :], xsrc, w=[xn])
            P.dma("sp", tb[0:rows, :], tsrc, w=[tn])
            ss, rs = small[0:rows, 0:1], small[0:rows, 1:2]
            P.op("act", lambda e: e.activation(out=sq[0:rows, :], in_=xb[0:rows, :], func=AF.Square, accum_out=ss), r=[xn], w=["sq", "ss"])
            P.op("dve", lambda e: e.tensor_scalar(out=rs, in0=ss, scalar1=1.0 / D, scalar2=1e-6, op0=ALU.mult, op1=ALU.add), r=["ss"], w=["rs"])
            P.op("act", lambda e: e.activation(out=rs, in_=rs, func=AF.Sqrt), r=["rs"], w=["rs"])
            P.op("dve", lambda e: e.reciprocal(out=rs, in_=rs), r=["rs"], w=["rs"])
            P.op("dve", lambda e: e.scalar_tensor_tensor(out=hh[0:rows, :], in0=xb[0:rows, :], scalar=rs, in1=a_t[0:rows, :], op0=ALU.mult, op1=ALU.mult),
                 r=[xn, "rs", a_n], w=["hh"])
            P.op("pool", lambda e: e.tensor_tensor(out=hh[0:rows, :], in0=hh[0:rows, :], in1=sh_t[0:rows, :], op=ALU.add), r=["hh", sh_n], w=["hh"])
            for k in range(8):
                bank = k // 4
                P.op("pe", lambda e, k=k, bank=bank: e.transpose(out=pb[bank][:, (k % 4) * 128:(k % 4) * 128 + rows], in_=hh[0:rows, k * 128:(k + 1) * 128], identity=ident[0:rows, 0:rows]),
                     r=["hh", "ident"], w=[PB[bank]])
            if rows == 128:
                evac("act", hTb[:, 0:512], pb[0], [PB[0]], [hTn])
                evac("dve", hTb[:, 512:1024], pb[1], [PB[1]], [hTn])
            else:
                for half_ in range(2):
                    evac("act" if half_ == 0 else "dve", hT3[:, half_ * 4:(half_ + 1) * 4, 0:rows], pb[half_].rearrange("p (k t) -> p k t", k=4)[:, :, 0:rows], [PB[half_]], [hTn])
            nb = (ncols + 511) // 512
            for cb in range(nb):
                c0, c1 = cb * 512, min(ncols, cb * 512 + 512)
                bank = 2 + cb % 4
                for k in range(8):
                    P.op("pe", lambda e, k=k, bank=bank, c0=c0, c1=c1: e.matmul(out=pb[bank][0:rows, 0:c1 - c0], lhsT=hT3[:, k, 0:rows], rhs=w_in3[:, k, c0:c1], start=(k == 0), stop=(k == 7)),
                         r=[hTn, "w_in_b"], w=[PB[bank]])
                evac(alt(), Psb[0:rows, c0:c1], pb[bank][0:rows, 0:c1 - c0], [PB[bank]], [PN])
            def stage2():
                if mode == 2:
                    cvs = sq[0:rows, :]
                    for j in range(2):
                        for hh_ in range(2):
                            P.dma("sp", sq[0:rows, 0:768], I["state_conv"][:, 1 + j, hh_ * 768:(hh_ + 1) * 768], w=["sq"])
                            P.dma("sp", O["conv_s"][:, j, hh_ * 768:(hh_ + 1) * 768], sq[0:rows, 0:768], r=["sq"])
                    P.dma("sp", O["conv_s"][:, 2, 0:512], Psb[0:rows, C_Q:C_Q + 512], r=[PN])
                    P.dma("sp", O["conv_s"][:, 2, 512:1536], Psb[0:rows, 0:1024], r=[PN])
                else:
                    row0 = 3 + ti * 128
                    P.dma("sp", S["gs"][mode, row0:row0 + 128, 0:1032], Psb[:, 0:1032], r=[PN], w=[f"gs{mode}_{ti}"])
                    if mode == 1:
                        P.dma("sp", S["gs"][mode, row0:row0 + 128, 1032:2056], Psb[:, C_Q:C_Q + 1024], r=[PN], w=[f"gs{mode}_{ti}"])
                    elif ti == NT - 1:
                        P.dma("sp", S["gs"][mode, row0:row0 + 128, 1032:1544], Psb[:, C_Q:C_Q + 512], r=[PN], w=[f"gs{mode}_{ti}"])
                rope(Psb, PN, rows, C_AK, 2, 128, 16, tb, 0, 64, tn)
                rope(Psb, PN, rows, C_IK, 1, 64, 8, tb, 128, 192, tn)
                if mode >= 1:
                    rope(Psb, PN, rows, C_AQ, 4, 128, 16, tb, 0, 64, tn)
                    rope(Psb, PN, rows, C_IQ, 8, 64, 8, tb, 128, 192, tn)
                if mode == 2:
                    P.dma("sp", O["k_s"], Psb[0:rows, C_AK:C_AK + 256], r=[PN])
                    P.dma("sp", O["v_s"], Psb[0:rows, C_AV:C_AV + 256], r=[PN])
                    P.dma("sp", O["ik_s"], Psb[0:rows, C_IK:C_IK + 64], r=[PN])
                    P.dma("sp", S["ps"], Psb[0:rows, :], r=[PN], w=["ps_scr"])
                    return
                if mode == 1:
                    P.dma("sp", O["k_own"][ti * 128:(ti + 1) * 128, :], Psb[:, C_AK:C_AK + 256], r=[PN])
                    P.dma("sp", O["v_own"][ti * 128:(ti + 1) * 128, :], Psb[:, C_AV:C_AV + 256], r=[PN])
                    P.dma("sp", O["ik_own"][ti * 128:(ti + 1) * 128, :], Psb[:, C_IK:C_IK + 64], r=[PN])
                slot = (0 if mode == 0 else 16) + ti
                for g in range(2):
                    P.op("act", lambda e, g=g: e.activation(out=sq[:, 0:128], in_=Psb[:, C_AK + g * 128:C_AK + (g + 1) * 128], func=AF.Square, accum_out=ksq[:, slot * 2 + g:slot * 2 + g + 1]),
                         r=[PN, "a1_bc"], w=["sq", "ksq"])
                if mode == 1:
                    for h in range(4):
                        P.op("act", lambda e, h=h: e.activation(out=sq[:, 0:128], in_=Psb[:, C_AQ + h * 128:C_AQ + (h + 1) * 128], func=AF.Square, accum_out=qsq[:, ti * 4 + h:ti * 4 + h + 1]),
                             r=[PN, "a1_bc"], w=["sq", "qsq"])
                P.op("dve", lambda e: e.tensor_copy(out=ikd[:, 0:64], in_=Psb[:, C_IK:C_IK + 64]), r=[PN], w=["ikd"])
                P.op("pool", lambda e: e.tensor_copy(out=ikd[:, 64:128], in_=Psb[:, C_IK:C_IK + 64]), r=[PN], w=["ikd"])
                for g in range(2):
                    P.op("pe", lambda e, g=g: e.transpose(out=pb[6][:, g * 128:(g + 1) * 128], in_=Psb[:, C_AK + g * 128:C_AK + (g + 1) * 128], identity=ident),
                         r=[PN, "ident"], w=[PB[6]])
                P.op("pe", lambda e: e.transpose(out=pb[6][:, 256:384], in_=ikd, identity=ident), r=["ikd", "ident"], w=[PB[6]])
                for g in range(2):
                    evac(alt(), KT3[:, g, slot * 128:(slot + 1) * 128], pb[6][:, g * 128:(g + 1) * 128], [PB[6]], ["KT"])
                evac(alt(), ikT[:, slot * 128:(slot + 1) * 128], pb[6][:, 256:384], [PB[6]], ["ikT"])
                P.op("pool", lambda e: e.memset(VA4[:, slot, :, 128:130], 1.0), r=["a1_bc"], w=["VA"])
                P.op("act", lambda e: e.activation(out=VA4[:, slot, :, 0:128], in_=Psb[:, C_AV:C_AV + 256].rearrange("p (g d) -> p g d", g=2), func=AF.Copy),
                     r=[PN], w=["VA"])
                if mode == 1:
                    ts_ = tstage[0]; tsn = "tstage0"
                    for h in range(4):
                        P.op("pe", lambda e, h=h: e.transpose(out=pb[7][:, h * 128:(h + 1) * 128], in_=Psb[:, C_AQ + h * 128:C_AQ + (h + 1) * 128], identity=ident),
                             r=[PN, "ident"], w=[PB[7]])
                    evac(alt(), ts_, pb[7], [PB[7]], [tsn])
                    P.dma("sp", S["qT"][ti], ts_.rearrange("p (h t) -> p h t", h=4), r=[tsn], w=[f"qT{ti}"])
                    ts2 = tstage[1]; tsn2 = "tstage1"
                    for h in range(4):
                        P.op("pe", lambda e, h=h: e.transpose(out=pb[7][:, h * 128:(h + 1) * 128], in_=Psb[:, C_IQ + h * 128:C_IQ + (h + 1) * 128], identity=ident),
                             r=[PN, "ident"], w=[PB[7]])
                    evac(alt(), ts2, pb[7], [PB[7]], [tsn2])
                    P.dma("sp", S["iqT"][ti], ts2.rearrange("p (h t) -> p h t", h=4), r=[tsn2], w=[f"iqT{ti}"])
                    P.op("act", lambda e: e.activation(out=iwabs[:, ti * 8:(ti + 1) * 8], in_=Psb[:, C_IW:C_IW + 8], func=AF.Abs, scale=8 ** -0.5),
                         r=[PN], w=["iwabs"])
                    P.op("act", lambda e: e.activation(out=iwsgn[:, ti * 8:(ti + 1) * 8], in_=Psb[:, C_IW:C_IW + 8], func=AF.Sign), r=[PN], w=["iwsgn"])
                    if ti == NT - 1:
                        P.dma("sp", O["conv_tail"][:, 0:512], S["gs"][1, 3 + HALF - 3:3 + HALF, 1032:1544], r=[f"gs1_{ti}"])
                        P.dma("sp", O["conv_tail"][:, 512:1536], S["gs"][1, 3 + HALF - 3:3 + HALF, 0:1024], r=[f"gs1_{ti}"])

            return stage2

        NA0 = int(os.environ.get("MK_NA0", NT)); NA1 = int(os.environ.get("MK_NA1", NT))
        tiles_a = [(0, ti) for ti in range(NT - NA0, NT)] + [(1, ti) for ti in range(NA1)] + ([] if NOS else [(2, 0)])
        pend = None
        for (m_, ti) in tiles_a:
            nxt = proj_tile(m_, ti)
            if pend is not None:
                pend()
            pend = nxt
        if pend is not None:
            pend()

        if not NOG:
            P.barrier()
            A.reset(persist1)
            cst = A.f32(1024)
            TRIU, ONESM, MASKL, MASKU = [cst[:, i * 128:(i + 1) * 128] for i in range(4)]
            ident4 = cst[:, 512:1024]
            P.dma("sp", cst, I["gconst"], w=["cst"])
            wc = [A.f32(1536) for _ in range(4)]
            for i in range(4):
                P.dma("sp", wc[i], I["wc_p"][i:i + 1, :].to_broadcast([128, 1536]), w=[f"wc{i}"])
            dtb = A.f32(4); negA = A.f32(4); ggd = A.f32(512)
            P.dma("sp", dtb, I["dt_bias"].to_broadcast([128, 4]), w=["dtb"])
            P.dma("sp", negA, I["a_log"].to_broadcast([128, 4]), w=["negA"])
            for h in range(4):
                P.dma("sp", ggd[:, h * 128:(h + 1) * 128], I["g_gdn"].to_broadcast([128, 128]), w=["ggd"])
            P.op("act", lambda e: e.activation(out=negA, in_=negA, func=AF.Exp), r=["negA"], w=["negA"])
            P.op("dve", lambda e: e.tensor_scalar(out=negA, in0=negA, scalar1=-1.0, scalar2=None, op0=ALU.mult), r=["negA"], w=["negA"])
            gs_base = A.off
            Sst = A.f32(512)
            P.op("dve", lambda e: e.memset(Sst, 0.0), w=["S"])
            X = [A.f32(GS_W) for _ in range(4)]
            cv = A.f32(1536); tmpa = A.f32(1536); tmpb = A.f32(1536)
            sm = A.f32(64)
            kn = A.f32(512); kt = A.f32(512); vb = A.f32(512); qn = A.f32(512); qt = A.f32(512)
            knT = A.f32(512); qnT = A.f32(512); qtT = A.f32(512)
            Dg = A.f32(512); dec = A.f32(512); decT = A.f32(512)
            Pbuf = [A.f32(512), A.f32(512)]; PTbuf = [A.f32(512), A.f32(512)]
            Wm = A.f32(512); ATm = A.f32(512); Rm = A.f32(512); vnew = A.f32(512); o_sb = A.f32(512); og = A.f32(512); szb = A.f32(512)
            cTst = A.bf16(512)
            pre = A.f32(GS_W)
            H4 = lambda ap, h: ap[:, h * 128:(h + 1) * 128]
            sc = lambda lo, h: sm[:, lo + h:lo + h + 1]

            def ts_mul(eng, out, in0, scal, r, w):
                if eng == "pool":
                    P.op("pool", lambda e: e.tensor_scalar(out=out, in0=in0, scalar1=scal, scalar2=1.0, op0=ALU.mult, op1=ALU.mult), r=r, w=w)
                else:
                    P.op("dve", lambda e: e.tensor_scalar(out=out, in0=in0, scalar1=scal, scalar2=None, op0=ALU.mult), r=r, w=w)

            for hf in range(2):
                own = (hf == 1)
                if own:
                    P.op("dve", lambda e: e.tensor_scalar(out=Sst, in0=Sst, scalar1=flags[:, 0:1], scalar2=None, op0=ALU.mult), r=["S", "flags"], w=["S"])
                    P.op("dve", lambda e: e.memset(pre[0:3, :], 0.0), w=["pre"])
                    P.dma("sp", pre[0:3, 0:1544], S["gs"][0, HALF:HALF + 3, 0:1544], w=["pre"])
                    P.op("dve", lambda e: e.tensor_scalar(out=pre[0:3, :], in0=pre[0:3, :], scalar1=flags[0:3, 0:1], scalar2=None, op0=ALU.mult), r=["pre", "flags"], w=["pre"])
                    P.dma("sp", S["gs"][1, 0:3, :], pre[0:3, :], r=["pre"], w=["gs_pre1"])
                segs = [(0, 1024, 0), (1024, 1536, 1032)] if own else [(0, 1024, 0)]
                nsc = 8 if own else 4
                for ti in range(G_NT):
                    W_ = GS_W if own else 1032
                    for i in range(4):
                        P.dma("sp", X[i][:, 0:W_], S["gs"][hf, ti * 128 + i:ti * 128 + i + 128, 0:W_], r=(["gs_pre1"] if (own and ti == 0) else []), w=[f"X{i}"])
                    for (d0, d1, s0) in segs:
                        n = d1 - d0
                        P.op("dve", lambda e, d0=d0, d1=d1, s0=s0, n=n: e.tensor_tensor(out=cv[:, d0:d1], in0=X[0][:, s0:s0 + n], in1=wc[0][:, d0:d1], op=ALU.mult), r=["X0", "wc0"], w=["cv"])
                        for i in range(1, 4):
                            tb_, tbn = (tmpa, "tmpa") if i % 2 else (tmpb, "tmpb")
                            P.op("pool", lambda e, i=i, d0=d0, d1=d1, s0=s0, n=n, tb_=tb_: e.tensor_tensor(out=tb_[:, d0:d1], in0=X[i][:, s0:s0 + n], in1=wc[i][:, d0:d1], op=ALU.mult), r=[f"X{i}", f"wc{i}"], w=[tbn])
                            P.op("dve", lambda e, d0=d0, d1=d1, tb_=tb_: e.tensor_tensor(out=cv[:, d0:d1], in0=cv[:, d0:d1], in1=tb_[:, d0:d1], op=ALU.add), r=["cv", tbn], w=["cv"])
                        P.op("act", lambda e, d0=d0, d1=d1: e.activation(out=cv[:, d0:d1], in_=cv[:, d0:d1], func=AF.Silu), r=["cv"], w=["cv"])
                    if GSTOP <= 1:
                        continue
                    P.op("pool", lambda e: e.tensor_tensor(out=tmpa[:, 0:512], in0=cv[:, 0:512], in1=cv[:, 0:512], op=ALU.mult), r=["cv"], w=["tmpa"])
                    P.op("dve", lambda e: e.tensor_reduce(out=sm[:, 0:4], in_=tmpa[:, 0:512].rearrange("p (h d) -> p h d", h=4), axis=AX.X, op=ALU.add), r=["tmpa"], w=["sm_ss"])
                    if own:
                        P.op("pool", lambda e: e.tensor_tensor(out=tmpb[:, 0:512], in0=cv[:, 1024:1536], in1=cv[:, 1024:1536], op=ALU.mult), r=["cv"], w=["tmpb"])
                        P.op("dve", lambda e: e.tensor_reduce(out=sm[:, 4:8], in_=tmpb[:, 0:512].rearrange("p (h d) -> p h d", h=4), axis=AX.X, op=ALU.add), r=["tmpb"], w=["sm_ss"])
                    P.op("dve", lambda e, nsc=nsc: e.tensor_scalar(out=sm[:, 0:nsc], in0=sm[:, 0:nsc], scalar1=1e-6, scalar2=None, op0=ALU.add), r=["sm_ss"], w=["sm_ss"])
                    P.op("act", lambda e, nsc=nsc: e.activation(out=sm[:, 0:nsc], in_=sm[:, 0:nsc], func=AF.Sqrt), r=["sm_ss"], w=["sm_ss"])
                    P.op("dve", lambda e, nsc=nsc: e.reciprocal(out=sm[:, 0:nsc], in_=sm[:, 0:nsc]), r=["sm_ss"], w=["sm_ss"])
                    if own:
                        P.op("dve", lambda e: e.tensor_scalar(out=sm[:, 4:8], in0=sm[:, 4:8], scalar1=128 ** -0.5, scalar2=None, op0=ALU.mult), r=["sm_ss"], w=["sm_ss"])
                    P.op("act", lambda e: e.activation(out=sm[:, 8:12], in_=X[3][:, 1024:1028], func=AF.Sigmoid), r=["X3"], w=["sm_b"])
                    P.op("dve", lambda e: e.tensor_tensor(out=sm[:, 12:16], in0=X[3][:, 1028:1032], in1=dtb, op=ALU.add), r=["X3", "dtb"], w=["sm_g"])
                    P.op("act", lambda e: e.activation(out=sm[:, 12:16], in_=sm[:, 12:16], func=AF.Exp), r=["sm_g"], w=["sm_g"])
                    P.op("act", lambda e: e.activation(out=sm[:, 12:16], in_=sm[:, 12:16], func=AF.Ln, bias=1.0), r=["sm_g"], w=["sm_g"])
                    P.op("dve", lambda e: e.tensor_tensor(out=sm[:, 12:16], in0=sm[:, 12:16], in1=negA, op=ALU.mult), r=["sm_g", "negA"], w=["sm_g"])
                    P.op("pe", lambda e: e.matmul(out=pb[3][:, 0:4], lhsT=TRIU, rhs=sm[:, 12:16], start=True, stop=True), r=["cst", "sm_g"], w=[PB[3]])
                    P.op("pe", lambda e: e.matmul(out=pb[3][:, 4:8], lhsT=ONESM, rhs=sm[:, 12:16], start=True, stop=True), r=["cst", "sm_g"], w=[PB[3]])
                    P.op("dve", lambda e: e.tensor_copy(out=sm[:, 16:24], in_=pb[3][:, 0:8]), r=[PB[3]], w=["sm_gc"])
                    P.op("dve", lambda e: e.tensor_copy(out=sm[:, 24:28], in_=sm[:, 16:20]), r=["sm_gc"], w=["sm_e"])
                    P.op("dve", lambda e: e.tensor_tensor(out=sm[:, 28:32], in0=sm[:, 20:24], in1=sm[:, 16:20], op=ALU.subtract), r=["sm_gc"], w=["sm_e"])
                    P.op("dve", lambda e: e.tensor_copy(out=sm[:, 32:36], in_=sm[:, 20:24]), r=["sm_gc"], w=["sm_e"])
                    P.op("act", lambda e: e.activation(out=sm[:, 24:36], in_=sm[:, 24:36], func=AF.Exp), r=["sm_e"], w=["sm_e"])
                    P.op("dve", lambda e: e.scalar_tensor_tensor(out=sm[:, 36:40], in0=sm[:, 8:12], scalar=-1.0, in1=sm[:, 24:28], op0=ALU.mult, op1=ALU.mult), r=["sm_b", "sm_e"], w=["sm_x"])
                    P.op("dve", lambda e: e.tensor_scalar(out=sm[:, 40:44], in0=sm[:, 16:20], scalar1=-1.0, scalar2=None, op0=ALU.mult), r=["sm_gc"], w=["sm_x"])
                    P.op("dve", lambda e: e.tensor_scalar(out=sm[:, 44:48], in0=sm[:, 8:12], scalar1=-1.0, scalar2=None, op0=ALU.mult), r=["sm_b"], w=["sm_x"])
                    if own:
                        P.op("dve", lambda e: e.tensor_tensor(out=sm[:, 48:52], in0=sm[:, 4:8], in1=sm[:, 24:28], op=ALU.mult), r=["sm_ss", "sm_e"], w=["sm_x"])
                    if GSTOP <= 2:
                        continue
                    for h in range(4):
                        ts_mul("dve", H4(kn, h), cv[:, h * 128:(h + 1) * 128], sc(0, h), ["cv", "sm_ss"], ["kn"])
                        ts_mul("pool", H4(kt, h), H4(kn, h), sc(28, h), ["kn", "sm_e"], ["kt"])
                        ts_mul("pool", H4(vb, h), cv[:, 512 + h * 128:512 + (h + 1) * 128], sc(8, h), ["cv", "sm_b"], ["vb"])
                        if own:
                            ts_mul("dve", H4(qn, h), cv[:, 1024 + h * 128:1024 + (h + 1) * 128], sc(4, h), ["cv", "sm_ss"], ["qn"])
                            ts_mul("pool", H4(qt, h), cv[:, 1024 + h * 128:1024 + (h + 1) * 128], sc(48, h), ["cv", "sm_x"], ["qt"])
                    for (src, sn, bank, dst, dn, eng) in ([(kn, "kn", 0, knT, "knT", "act")] + ([(qn, "qn", 1, qnT, "qnT", "dve"), (qt, "qt", 2, qtT, "qtT", "act")] if own else [])):
                        for h in range(4):
                            P.op("pe", lambda e, h=h, src=src, bank=bank: e.transpose(out=H4(pb[bank], h), in_=H4(src, h), identity=ident), r=[sn, "ident"], w=[PB[bank]])
                        evac(eng, dst, pb[bank], [PB[bank]], [dn])
                    if GSTOP <= 3:
                        continue
                    for h in range(4):
                        P.op("pe", lambda e, h=h: e.matmul(out=H4(pb[0], h), lhsT=H4(knT, h), rhs=H4(knT, h), start=True, stop=True), r=["knT"], w=[PB[0]])
                        P.op("pool", lambda e, h=h: e.tensor_scalar(out=H4(Dg, h), in0=ident, scalar1=sc(16, h), scalar2=1.0, op0=ALU.mult, op1=ALU.mult), r=["ident", "sm_gc"], w=["Dg"])
                    for h in range(4):
                        P.op("pe", lambda e, h=h: e.matmul(out=H4(pb[1], h), lhsT=ONESM, rhs=H4(Dg, h), start=True, stop=False), r=["cst", "Dg"], w=[PB[1]])
                        P.op("pe", lambda e, h=h: e.matmul(out=H4(pb[1], h), lhsT=ident, rhs=MASKL, start=False, stop=True), r=["cst", "ident"], w=[PB[1]])
                        P.op("act", lambda e, h=h: e.activation(out=H4(dec, h), in_=H4(pb[1], h), func=AF.Exp, scale=-1.0, bias=sc(16, h)), r=[PB[1], "sm_gc"], w=["dec"])
                        P.op("dve", lambda e, h=h: e.scalar_tensor_tensor(out=H4(Pbuf[0], h), in0=H4(pb[0], h), scalar=sc(44, h), in1=H4(dec, h), op0=ALU.mult, op1=ALU.mult),
                             r=[PB[0], "sm_x", "dec"], w=["P0"])
                    if own:
                        for h in range(4):
                            P.op("pe", lambda e, h=h: e.matmul(out=H4(pb[2], h), lhsT=ONESM, rhs=H4(Dg, h), start=True, stop=False), r=["cst", "Dg"], w=[PB[2]])
                            P.op("pe", lambda e, h=h: e.matmul(out=H4(pb[2], h), lhsT=ident, rhs=MASKU, start=False, stop=True), r=["cst", "ident"], w=[PB[2]])
                            P.op("act", lambda e, h=h: e.activation(out=H4(decT, h), in_=H4(pb[2], h), func=AF.Exp, scale=1.0, bias=sc(40, h)), r=[PB[2], "sm_x"], w=["decT"])
                            P.op("pe", lambda e, h=h: e.matmul(out=H4(pb[3], h), lhsT=H4(knT, h), rhs=H4(qnT, h), start=True, stop=True), r=["knT", "qnT"], w=[PB[3]])
                            P.op("dve", lambda e, h=h: e.tensor_tensor(out=H4(ATm, h), in0=H4(pb[3], h), in1=H4(decT, h), op=ALU.mult), r=[PB[3], "decT"], w=["ATm"])
                    if GSTOP <= 4:
                        continue
                    for h in range(4):
                        P.op("pe", lambda e, h=h: e.transpose(out=H4(pb[4], h), in_=H4(Pbuf[0], h), identity=ident), r=["P0", "ident"], w=[PB[4]])
                    evac("act", PTbuf[0], pb[4], [PB[4]], ["PT0"])
                    P.op("dve", lambda e: e.tensor_tensor(out=Wm, in0=PTbuf[0], in1=ident4, op=ALU.add), r=["PT0", "cst"], w=["Wm"])
                    for l in range(1, 7):
                        pc, ptc, pn_, ptn = Pbuf[(l - 1) % 2], PTbuf[(l - 1) % 2], Pbuf[l % 2], PTbuf[l % 2]
                        pcn, ptcn, pnn, ptnn = f"P{(l - 1) % 2}", f"PT{(l - 1) % 2}", f"P{l % 2}", f"PT{l % 2}"
                        for h in range(4):
                            P.op("pe", lambda e, h=h, pc=pc, ptc=ptc: e.matmul(out=H4(pb[4], h), lhsT=H4(ptc, h), rhs=H4(pc, h), start=True, stop=True), r=[pcn, ptcn], w=[PB[4]])
                        if l < 6:
                            for h in range(4):
                                P.op("pe", lambda e, h=h, pc=pc, ptc=ptc: e.matmul(out=H4(pb[5], h), lhsT=H4(pc, h), rhs=H4(ptc, h), start=True, stop=True), r=[pcn, ptcn], w=[PB[5]])
                        evac("act", pn_, pb[4], [PB[4]], [pnn])
                        if l < 6:
                            evac("dve", ptn, pb[5], [PB[5]], [ptnn])
                        for h in range(4):
                            P.op("pe", lambda e, h=h, pn_=pn_: e.matmul(out=H4(pb[6], h), lhsT=H4(pn_, h), rhs=H4(Wm, h), start=True, stop=True), r=[pnn, "Wm"], w=[PB[6]])
                        P.op("dve", lambda e: e.tensor_tensor(out=Wm, in0=Wm, in1=pb[6], op=ALU.add), r=["Wm", PB[6]], w=["Wm"])
                    if GSTOP <= 5:
                        continue
                    for h in range(4):
                        P.op("pe", lambda e, h=h: e.matmul(out=H4(pb[7], h), lhsT=H4(knT, h), rhs=H4(Sst, h), start=True, stop=True), r=["knT", "S"], w=[PB[7]])
                    for h in range(4):
                        P.op("dve", lambda e, h=h: e.scalar_tensor_tensor(out=H4(Rm, h), in0=H4(pb[7], h), scalar=sc(36, h), in1=H4(vb, h), op0=ALU.mult, op1=ALU.add),
                             r=[PB[7], "sm_x", "vb"], w=["Rm"])
                    for h in range(4):
                        P.op("pe", lambda e, h=h: e.matmul(out=H4(pb[0], h), lhsT=H4(Wm, h), rhs=H4(Rm, h), start=True, stop=True), r=["Wm", "Rm"], w=[PB[0]])
                    evac("act", vnew, pb[0], [PB[0]], ["vnew"])
                    if own:
                        for h in range(4):
                            P.op("pe", lambda e, h=h: e.matmul(out=H4(pb[1], h), lhsT=H4(qtT, h), rhs=H4(Sst, h), start=True, stop=False), r=["qtT", "S"], w=[PB[1]])
                            P.op("pe", lambda e, h=h: e.matmul(out=H4(pb[1], h), lhsT=H4(ATm, h), rhs=H4(vnew, h), start=False, stop=True), r=["ATm", "vnew"], w=[PB[1]])
                        evac("act", o_sb, pb[1], [PB[1]], ["o_sb"])
                    for h in range(4):
                        P.op("pe", lambda e, h=h: e.matmul(out=H4(pb[2], h), lhsT=H4(kt, h), rhs=H4(vnew, h), start=True, stop=True), r=["kt", "vnew"], w=[PB[2]])
                    for h in range(4):
                        P.op("dve", lambda e, h=h: e.scalar_tensor_tensor(out=H4(Sst, h), in0=H4(Sst, h), scalar=sc(32, h), in1=H4(pb[2], h), op0=ALU.mult, op1=ALU.add),
                             r=["S", "sm_e", PB[2]], w=["S"])
                    if own:
                        P.op("pool", lambda e: e.tensor_tensor(out=tmpa[:, 0:512], in0=o_sb, in1=o_sb, op=ALU.mult), r=["o_sb"], w=["tmpa"])
                        P.op("dve", lambda e: e.tensor_reduce(out=sm[:, 52:56], in_=tmpa[:, 0:512].rearrange("p (h d) -> p h d", h=4), axis=AX.X, op=ALU.add), r=["tmpa"], w=["sm_o"])
                        P.op("dve", lambda e: e.tensor_scalar(out=sm[:, 52:56], in0=sm[:, 52:56], scalar1=1.0 / 128, scalar2=1e-6, op0=ALU.mult, op1=ALU.add), r=["sm_o"], w=["sm_o"])
                        P.op("act", lambda e: e.activation(out=sm[:, 52:56], in_=sm[:, 52:56], func=AF.Sqrt), r=["sm_o"], w=["sm_o"])
                        P.op("dve", lambda e: e.reciprocal(out=sm[:, 52:56], in_=sm[:, 52:56]), r=["sm_o"], w=["sm_o"])
                        P.op("act", lambda e: e.activation(out=szb, in_=X[3][:, 1544:2056], func=AF.Silu), r=["X3"], w=["szb"])
                        for h in range(4):
                            P.op("dve", lambda e, h=h: e.scalar_tensor_tensor(out=H4(og, h), in0=H4(o_sb, h), scalar=sc(52, h), in1=H4(ggd, h), op0=ALU.mult, op1=ALU.mult),
                                 r=["o_sb", "sm_o", "ggd"], w=["og"])
                        P.op("pool", lambda e: e.tensor_tensor(out=og, in0=og, in1=szb, op=ALU.mult), r=["og", "szb"], w=["og"])
                        for h in range(4):
                            P.op("pe", lambda e, h=h: e.transpose(out=H4(pb[3], h), in_=H4(og, h), identity=ident), r=["og", "ident"], w=[PB[3]])
                        evac("act", cTst, pb[3], [PB[3]], ["cTst"])
                        P.dma("sp", S["cT"][ti][:, 0:4, :], cTst.rearrange("p (h t) -> p h t", h=4), r=["cTst"], w=[f"cTg{ti}"])
            P.dma("sp", O["ssm_fin"].rearrange("h d e -> d h e"), Sst.rearrange("p (h e) -> p h e", h=4), r=["S"])

            P.barrier()
            A.reset(gs_base)
            eye16 = A.f32(256)
            P.dma("sp", eye16, I["eye16"], w=["eye16"])
            Sall = A.f32(NS * 512); Sall4 = Sall.rearrange("p (s h e) -> p s h e", s=NS, h=4)
            for s_ in range(NS):
                P.dma("sp", Sall4[:, s_], I["state_ssm"][s_].rearrange("h d e -> d h e"), w=[f"S{s_}"])
            Ps2 = A.f32(INC); scv = A.f32(3 * 1536); scv3 = scv.rearrange("p (i c) -> p i c", i=3)
            P.dma("sp", Ps2[0:NS, :], S["ps"], w=["Ps2"])
            P.dma("sp", scv3[0:NS], I["state_conv"], w=["scv"])
            cvs = A.f32(1536); tms = A.f32(1536); sm2 = A.f32(64)
            kn2 = A.f32(512); qn2 = A.f32(512)
            knT2 = A.f32(64); qnT2 = A.f32(64)
            KTm = A.f32(1024); QTm = A.f32(1024)
            Km = [A.f32(512), A.f32(512)]
            EgD = A.f32(64); EGB = A.f32(64); Dl = A.f32(512); o_s = A.f32(512); og_s = A.f32(512); sz_s = A.f32(512)
            cTs = A.bf16(64)
            R = slice(0, NS)
            for (d0, d1, sc0, p0) in [(0, 1024, 512, 0), (1024, 1536, 0, C_Q)]:
                n = d1 - d0
                P.op("dve", lambda e, d0=d0, d1=d1, sc0=sc0, n=n: e.tensor_tensor(out=cvs[R, d0:d1], in0=scv3[R, 0, sc0:sc0 + n], in1=wc[0][R, d0:d1], op=ALU.mult), r=["scv", "wc0"], w=["cvs"])
                for i in range(1, 4):
                    src = (lambda i=i, sc0=sc0, n=n, p0=p0: scv3[R, i, sc0:sc0 + n] if i < 3 else Ps2[R, p0:p0 + n])()
                    P.op("pool", lambda e, i=i, d0=d0, d1=d1, src=src: e.tensor_tensor(out=tms[R, d0:d1], in0=src, in1=wc[i][R, d0:d1], op=ALU.mult), r=["scv", "Ps2", f"wc{i}"], w=["tms"])
                    P.op("dve", lambda e, d0=d0, d1=d1: e.tensor_tensor(out=cvs[R, d0:d1], in0=cvs[R, d0:d1], in1=tms[R, d0:d1], op=ALU.add), r=["cvs", "tms"], w=["cvs"])
                P.op("act", lambda e, d0=d0, d1=d1: e.activation(out=cvs[R, d0:d1], in_=cvs[R, d0:d1], func=AF.Silu), r=["cvs"], w=["cvs"])
            for (c0, o0) in [(0, 0), (1024, 4)]:
                P.op("pool", lambda e, c0=c0: e.tensor_tensor(out=tms[R, 0:512], in0=cvs[R, c0:c0 + 512], in1=cvs[R, c0:c0 + 512], op=ALU.mult), r=["cvs"], w=["tms"])
                P.op("dve", lambda e, o0=o0: e.tensor_reduce(out=sm2[R, o0:o0 + 4], in_=tms[R, 0:512].rearrange("p (h d) -> p h d", h=4), axis=AX.X, op=ALU.add), r=["tms"], w=["sm2"])
            P.op("dve", lambda e: e.tensor_scalar(out=sm2[R, 0:8], in0=sm2[R, 0:8], scalar1=1e-6, scalar2=None, op0=ALU.add), r=["sm2"], w=["sm2"])
            P.op("act", lambda e: e.activation(out=sm2[R, 0:8], in_=sm2[R, 0:8], func=AF.Sqrt), r=["sm2"], w=["sm2"])
            P.op("dve", lambda e: e.reciprocal(out=sm2[R, 0:8], in_=sm2[R, 0:8]), r=["sm2"], w=["sm2"])
            P.op("dve", lambda e: e.tensor_scalar(out=sm2[R, 4:8], in0=sm2[R, 4:8], scalar1=128 ** -0.5, scalar2=None, op0=ALU.mult), r=["sm2"], w=["sm2"])
            P.op("act", lambda e: e.activation(out=sm2[R, 8:12], in_=Ps2[R, C_B:C_B + 4], func=AF.Sigmoid), r=["Ps2"], w=["sm2b"])
            P.op("dve", lambda e: e.tensor_tensor(out=sm2[R, 12:16], in0=Ps2[R, C_A:C_A + 4], in1=dtb[R, :], op=ALU.add), r=["Ps2", "dtb"], w=["sm2g"])
            P.op("act", lambda e: e.activation(out=sm2[R, 12:16], in_=sm2[R, 12:16], func=AF.Exp), r=["sm2g"], w=["sm2g"])
            P.op("act", lambda e: e.activation(out=sm2[R, 12:16], in_=sm2[R, 12:16], func=AF.Ln, bias=1.0), r=["sm2g"], w=["sm2g"])
            P.op("dve", lambda e: e.tensor_tensor(out=sm2[R, 12:16], in0=sm2[R, 12:16], in1=negA[R, :], op=ALU.mult), r=["sm2g", "negA"], w=["sm2g"])
            P.op("act", lambda e: e.activation(out=sm2[R, 12:16], in_=sm2[R, 12:16], func=AF.Exp), r=["sm2g"], w=["sm2g"])
            P.op("dve", lambda e: e.tensor_scalar(out=sm2[R, 16:20], in0=sm2[R, 12:16], scalar1=-1.0, scalar2=None, op0=ALU.mult), r=["sm2g"], w=["sm2n"])
            for h in range(4):
                P.op("dve", lambda e, h=h: e.tensor_scalar(out=kn2[R, h * 128:(h + 1) * 128], in0=cvs[R, h * 128:(h + 1) * 128], scalar1=sm2[R, h:h + 1], scalar2=None, op0=ALU.mult), r=["cvs", "sm2"], w=["kn2"])
                P.op("dve", lambda e, h=h: e.tensor_scalar(out=qn2[R, h * 128:(h + 1) * 128], in0=cvs[R, 1024 + h * 128:1024 + (h + 1) * 128], scalar1=sm2[R, 4 + h:5 + h], scalar2=None, op0=ALU.mult), r=["cvs", "sm2"], w=["qn2"])
            for h in range(4):
                P.op("pe", lambda e, h=h: e.transpose(out=pb[6][:, h * 16:(h + 1) * 16], in_=kn2[R, h * 128:(h + 1) * 128], identity=ident[R, R]), r=["kn2", "ident"], w=[PB[6]])
                P.op("pe", lambda e, h=h: e.transpose(out=pb[6][:, 64 + h * 16:64 + (h + 1) * 16], in_=qn2[R, h * 128:(h + 1) * 128], identity=ident[R, R]), r=["qn2", "ident"], w=[PB[6]])
            evac("act", knT2, pb[6][:, 0:64], [PB[6]], ["knT2"])
            evac("dve", qnT2, pb[6][:, 64:128], [PB[6]], ["qnT2"])
            eye3 = eye16.rearrange("p (s m) -> p s m", s=NS)
            for h in range(4):
                for s_ in range(NS):
                    j = h * NS + s_
                    P.op("dve", lambda e, j=j, s_=s_: e.tensor_scalar(out=KTm[:, j * 16:(j + 1) * 16], in0=eye3[:, s_, :], scalar1=knT2[:, j:j + 1], scalar2=None, op0=ALU.mult), r=["eye16", "knT2"], w=["KTm"])
                    P.op("pool", lambda e, j=j, s_=s_: e.tensor_scalar(out=QTm[:, j * 16:(j + 1) * 16], in0=eye3[:, s_, :], scalar1=qnT2[:, j:j + 1], scalar2=1.0, op0=ALU.mult, op1=ALU.mult), r=["eye16", "qnT2"], w=["QTm"])
            for h in range(4):
                for s_ in range(NS):
                    j = h * NS + s_
                    P.op("pe", lambda e, h=h, s_=s_, j=j: e.matmul(out=pb[h][R, 0:128], lhsT=KTm[:, j * 16:(j + 1) * 16], rhs=Sall4[:, s_, h, :], start=(s_ == 0), stop=(s_ == NS - 1)),
                         r=["KTm", f"S{s_}"], w=[PB[h]])
            for h in range(4):
                P.op("dve", lambda e, h=h: e.scalar_tensor_tensor(out=Dl[R, h * 128:(h + 1) * 128], in0=pb[h][R, 0:128], scalar=sm2[R, 16 + h:17 + h], in1=cvs[R, 512 + h * 128:512 + (h + 1) * 128], op0=ALU.mult, op1=ALU.add),
                     r=[PB[h], "sm2n", "cvs"], w=["Dl"])
                P.op("dve", lambda e, h=h: e.tensor_scalar(out=Dl[R, h * 128:(h + 1) * 128], in0=Dl[R, h * 128:(h + 1) * 128], scalar1=sm2[R, 8 + h:9 + h], scalar2=None, op0=ALU.mult), r=["Dl", "sm2b"], w=["Dl"])
            for s_ in range(NS):
                P.op("dve", lambda e, s_=s_: e.tensor_scalar(out=EgD[R, s_ * 4:(s_ + 1) * 4], in0=sm2[R, 12:16], scalar1=ident[R, s_:s_ + 1], scalar2=None, op0=ALU.mult), r=["sm2g", "ident"], w=["EgD"])
            P.op("pe", lambda e: e.matmul(out=pb[6][:, 0:64], lhsT=ONESM[R, :], rhs=EgD[R, :], start=True, stop=True), r=["cst", "EgD"], w=[PB[6]])
            evac("act", EGB, pb[6][:, 0:64], [PB[6]], ["EGB"])
            for s_ in range(NS):
                km = Km[s_ % 2]; kmn = f"Km{s_ % 2}"
                bank = 4 + s_ % 2
                P.op("pool", lambda e, s_=s_, km=km: e.tensor_scalar(out=km[R, :], in0=kn2[R, :], scalar1=ident[R, s_:s_ + 1], scalar2=1.0, op0=ALU.mult, op1=ALU.mult), r=["kn2", "ident"], w=[kmn])
                for h in range(4):
                    P.op("pe", lambda e, h=h, km=km, bank=bank: e.matmul(out=pb[bank][:, h * 128:(h + 1) * 128], lhsT=km[R, h * 128:(h + 1) * 128], rhs=Dl[R, h * 128:(h + 1) * 128], start=True, stop=True),
                         r=[kmn, "Dl"], w=[PB[bank]])
                for h in range(4):
                    P.op("dve", lambda e, h=h, s_=s_, bank=bank: e.scalar_tensor_tensor(out=Sall4[:, s_, h, :], in0=Sall4[:, s_, h, :], scalar=EGB[:, s_ * 4 + h:s_ * 4 + h + 1], in1=pb[bank][:, h * 128:(h + 1) * 128], op0=ALU.mult, op1=ALU.add),
                         r=[f"S{s_}", "EGB", PB[bank]], w=[f"S{s_}"])
                P.dma("sp", O["ssm_s"][s_].rearrange("h d e -> d h e"), Sall4[:, s_], r=[f"S{s_}"])
            for h in range(4):
                for s_ in range(NS):
                    j = h * NS + s_
                    P.op("pe", lambda e, h=h, s_=s_, j=j: e.matmul(out=pb[h][R, 0:128], lhsT=QTm[:, j * 16:(j + 1) * 16], rhs=Sall4[:, s_, h, :], start=(s_ == 0), stop=(s_ == NS - 1)),
                         r=["QTm", f"S{s_}"], w=[PB[h]])
            for h in range(4):
                evac("act", o_s[R, h * 128:(h + 1) * 128], pb[h][R, 0:128], [PB[h]], ["o_s"])
            P.op("pool", lambda e: e.tensor_tensor(out=tms[R, 0:512], in0=o_s[R, :], in1=o_s[R, :], op=ALU.mult), r=["o_s"], w=["tms"])
            P.op("dve", lambda e: e.tensor_reduce(out=sm2[R, 20:24], in_=tms[R, 0:512].rearrange("p (h d) -> p h d", h=4), axis=AX.X, op=ALU.add), r=["tms"], w=["sm2o"])
            P.op("dve", lambda e: e.tensor_scalar(out=sm2[R, 20:24], in0=sm2[R, 20:24], scalar1=1.0 / 128, scalar2=1e-6, op0=ALU.mult, op1=ALU.add), r=["sm2o"], w=["sm2o"])
            P.op("act", lambda e: e.activation(out=sm2[R, 20:24], in_=sm2[R, 20:24], func=AF.Sqrt), r=["sm2o"], w=["sm2o"])
            P.op("dve", lambda e: e.reciprocal(out=sm2[R, 20:24], in_=sm2[R, 20:24]), r=["sm2o"], w=["sm2o"])
            P.op("act", lambda e: e.activation(out=sz_s[R, :], in_=Ps2[R, C_Z:C_Z + 512], func=AF.Silu), r=["Ps2"], w=["sz_s"])
            for h in range(4):
                P.op("dve", lambda e, h=h: e.scalar_tensor_tensor(out=og_s[R, h * 128:(h + 1) * 128], in0=o_s[R, h * 128:(h + 1) * 128], scalar=sm2[R, 20 + h:21 + h], in1=ggd[R, h * 128:(h + 1) * 128], op0=ALU.mult, op1=ALU.mult),
                     r=["o_s", "sm2o", "ggd"], w=["og_s"])
            P.op("pool", lambda e: e.tensor_tensor(out=og_s[R, :], in0=og_s[R, :], in1=sz_s[R, :], op=ALU.mult), r=["og_s", "sz_s"], w=["og_s"])
            for h in range(4):
                P.op("pe", lambda e, h=h: e.transpose(out=pb[7][:, h * 16:(h + 1) * 16], in_=og_s[R, h * 128:(h + 1) * 128], identity=ident[R, R]), r=["og_s", "ident"], w=[PB[7]])
            evac("act", cTs, pb[7][:, 0:64], [PB[7]], ["cTs"])
            P.dma("sp", S["cT"][NT][:, 0:4, 0:NS], cTs.rearrange("p (h t) -> p h t", h=4), r=["cTs"], w=["cTgs"])

        if not os.environ.get('MK_NOT'):
            P.barrier()
            A.reset(persist1)
            SCALE = 128 ** -0.5
            NEG = -30000.0
            NBIS = 14
            caus = A.f32(128); ii2 = A.bf16(256); onesr = A.f32(128)
            P.dma("sp", caus, I["caus"], w=["caus"])
            P.op("dve", lambda e: e.tensor_copy(out=ii2[:, 0:128], in_=ident), r=["ident"], w=["ii2"])
            P.op("dve", lambda e: e.tensor_copy(out=ii2[:, 128:256], in_=ident), r=["ident"], w=["ii2"])
            P.op("dve", lambda e: e.memset(onesr, 1.0), w=["onesr"])
            tsm = A.f32(256)
            krow = A.f32(128)
            P.op("dve", lambda e: e.tensor_reduce(out=tsm[:, 0:1], in_=ksq, axis=AX.X, op=ALU.max), r=["ksq"], w=["tsm0"])
            P.op("pe", lambda e: e.transpose(out=pb[0][0:1, 0:128], in_=tsm[:, 0:1], identity=ident), r=["tsm0", "ident"], w=[PB[0]])
            evac("dve", krow[0:1, :], pb[0][0:1, 0:128], [PB[0]], ["krow"])
            P.op("dve", lambda e: e.tensor_reduce(out=krow[0:1, 0:1], in_=krow[0:1, :], axis=AX.X, op=ALU.max), r=["krow"], w=["krow"])
            P.op("pe", lambda e: e.matmul(out=pb[0][:, 0:1], lhsT=onesr[0:1, :], rhs=krow[0:1, 0:1], start=True, stop=True), r=["onesr", "krow"], w=[PB[0]])
            evac("dve", tsm[:, 1:2], pb[0][:, 0:1], [PB[0]], ["tsm1"])
            P.op("dve", lambda e: e.tensor_reduce(out=tsm[:, 16:32], in_=qsq.rearrange("p (t h) -> p t h", h=4), axis=AX.X, op=ALU.max), r=["qsq"], w=["tsmq"])
            P.op("dve", lambda e: e.tensor_scalar(out=tsm[:, 32:48], in0=tsm[:, 16:32], scalar1=tsm[:, 1:2], scalar2=None, op0=ALU.mult), r=["tsmq", "tsm1"], w=["negm"])
            P.op("act", lambda e: e.activation(out=tsm[:, 32:48], in_=tsm[:, 32:48], func=AF.Sqrt), r=["negm"], w=["negm"])
            P.op("dve", lambda e: e.tensor_scalar(out=tsm[:, 32:48], in0=tsm[:, 32:48], scalar1=-1.0, scalar2=None, op0=ALU.mult), r=["negm"], w=["negm"])
            Iscs = [A.f32(T), A.f32(T)]
            junk = A.bf16(T); MBs_ = [A.bf16(T), A.bf16(T)]
            qTt = [A.bf16(512), A.bf16(512), A.bf16(512)]; iqTt = [A.bf16(512), A.bf16(512)]
            oaccs = [A.f32(4 * 130), A.f32(4 * 130)]
            rh = [A.bf16(512) for _ in range(4)]
            Dhs = [A.bf16(8 * 128), A.bf16(8 * 128)]
            PT = [A.bf16(256), A.bf16(256)]
            oatt = A.f32(512); cTa = A.bf16(512)
            bss = [A.f32(16), A.f32(16)]
            T_NT = int(os.environ.get("MK_TNT", NT))

            def t_index(j):
                p = j % 2
                Isc = Iscs[p]; In = f"Isc{p}"; Dh = Dhs[p]; Dn = f"Dh{p}"
                qb = qTt[j % 3]; qn_ = f"qTt{j % 3}"; iqb = iqTt[p]; iqn = f"iqTt{p}"
                qb3 = qb.rearrange("p (h t) -> p h t", h=4); iqb3 = iqb.rearrange("p (h t) -> p h t", h=4)
                P.dma("sp", iqb3, S["iqT"][j], w=[iqn])
                P.dma("sp", qb3, S["qT"][j], w=[qn_])
                ncol = HALF + 128 * (j + 1)
                for h in range(8):
                    P.op("dve", lambda e, h=h: e.tensor_scalar(out=Dh[:, h * 128:(h + 1) * 128], in0=ident, scalar1=iwsgn[:, j * 8 + h:j * 8 + h + 1], scalar2=None, op0=ALU.mult),
                         r=["ident", "iwsgn"], w=[Dn])
                nblk = (ncol + 511) // 512
                for kb in range(nblk):
                    c0, c1 = kb * 512, min(ncol, kb * 512 + 512)
                    w_ = c1 - c0
                    accb = 2 + kb % 2

                    def emit_S(h, c0=c0, c1=c1, w_=w_):
                        p_, hf_ = h // 2, h % 2
                        ba = h % 2
                        rb = rh[h % 4]; rbn = f"rh{h % 4}"
                        P.op("pe", lambda e: e.matmul(out=pb[ba][:, 0:w_], lhsT=iqb3[hf_ * 64:(hf_ + 1) * 64, p_, :], rhs=ikT[hf_ * 64:(hf_ + 1) * 64, c0:c1], start=True, stop=True),
                             r=[iqn, "ikT"], w=[PB[ba]])
                        P.op("act", lambda e: e.activation(out=rb[:, 0:w_], in_=pb[ba][:, 0:w_], func=AF.Relu, scale=iwabs[:, j * 8 + h:j * 8 + h + 1]),
                             r=[PB[ba], "iwabs"], w=[rbn])

                    def emit_D(h, w_=w_, accb=accb):
                        rb = rh[h % 4]; rbn = f"rh{h % 4}"
                        P.op("pe", lambda e: e.matmul(out=pb[accb][:, 0:w_], lhsT=Dh[:, h * 128:(h + 1) * 128], rhs=rb[:, 0:w_], start=(h == 0), stop=(h == 7)),
                             r=[Dn, rbn], w=[PB[accb]])

                    emit_S(0); emit_S(1)
                    for h in range(8):
                        emit_D(h)
                        if h + 2 < 8:
                            emit_S(h + 2)
                    if c0 < HALF:
                        P.op("act", lambda e, c0=c0, c1=c1, w_=w_, accb=accb: e.activation(out=Isc[:, c0:c1], in_=pb[accb][:, 0:w_], func=AF.Identity, bias=flags[:, 1:2]), r=[PB[accb], "flags"], w=[In])
                    else:
                        P.op("act", lambda e, c0=c0, c1=c1, w_=w_, accb=accb: e.activation(out=Isc[:, c0:c1], in_=pb[accb][:, 0:w_], func=AF.Copy), r=[PB[accb]], w=[In])
                P.op("pool", lambda e: e.tensor_tensor(out=Isc[:, ncol - 128:ncol], in0=Isc[:, ncol - 128:ncol], in1=caus, op=ALU.add), r=[In, "caus"], w=[In])

            def t_bisect(j):
                p = j % 2
                Isc = Iscs[p]; In = f"Isc{p}"; MB = MBs_[p]; Mn = f"MB{p}"; bs = bss[p]
                B = lambda n: f"bs{p}_{n}"
                ncol = HALF + 128 * (j + 1)
                lo, rng, thr, cntc, mm = bs[:, 0:1], bs[:, 1:2], bs[:, 2:3], bs[:, 3:4], bs[:, 4:5]
                P.op("dve", lambda e: e.tensor_reduce(out=lo, in_=Isc[:, 0:ncol], axis=AX.X, op=ALU.min), r=[In], w=[B("lo")])
                P.op("dve", lambda e: e.tensor_reduce(out=rng, in_=Isc[:, 0:ncol], axis=AX.X, op=ALU.max), r=[In], w=[B("rng")])
                P.op("dve", lambda e: e.tensor_scalar(out=thr, in0=rng, scalar1=-128.0, scalar2=None, op0=ALU.add), r=[B("rng")], w=[B("thr")])
                P.op("dve", lambda e: e.tensor_tensor(out=lo, in0=lo, in1=thr, op=ALU.max), r=[B("lo"), B("thr")], w=[B("lo")])
                P.op("dve", lambda e: e.tensor_tensor(out=rng, in0=rng, in1=lo, op=ALU.subtract), r=[B("rng"), B("lo")], w=[B("rng")])
                base = bs[:, 5:6]
                P.op("dve", lambda e: e.scalar_tensor_tensor(out=thr, in0=rng, scalar=0.5, in1=lo, op0=ALU.mult, op1=ALU.add), r=[B("rng"), B("lo")], w=[B("thr")])
                for it in range(NBIS):
                    st = 2.0 ** -(it + 1)
                    P.op("dve", lambda e: e.tensor_scalar(out=junk[:, 0:ncol], in0=Isc[:, 0:ncol], scalar1=thr, scalar2=None, op0=ALU.is_ge, op1=ALU.add, accum_out=cntc),
                         r=[In, B("thr")], w=["junk", B("cnt")])
                    P.op("dve", lambda e, st=st: e.scalar_tensor_tensor(out=base, in0=rng, scalar=-0.5 * st, in1=thr, op0=ALU.mult, op1=ALU.add), r=[B("rng"), B("thr")], w=[B("base")])
                    P.op("dve", lambda e: e.tensor_scalar(out=mm, in0=cntc, scalar1=255.5, scalar2=rng, op0=ALU.is_ge, op1=ALU.mult), r=[B("cnt"), B("rng")], w=[B("m")])
                    P.op("dve", lambda e, st=st: e.scalar_tensor_tensor(out=thr, in0=mm, scalar=st, in1=base, op0=ALU.mult, op1=ALU.add), r=[B("m"), B("base")], w=[B("thr")])
                P.op("dve", lambda e: e.scalar_tensor_tensor(out=lo, in0=rng, scalar=-(2.0 ** -(NBIS + 1)), in1=thr, op0=ALU.mult, op1=ALU.add), r=[B("rng"), B("thr")], w=[B("lo")])
                P.op("dve", lambda e: e.tensor_scalar(out=MB[:, 0:ncol], in0=Isc[:, 0:ncol], scalar1=lo, scalar2=NEG, op0=ALU.is_lt, op1=ALU.mult), r=[In, B("lo")], w=[Mn])
                P.op("dve", lambda e: e.tensor_scalar(out=MB[:, 0:ncol], in0=MB[:, 0:ncol], scalar1=tsm[:, 32 + j:33 + j], scalar2=None, op0=ALU.add), r=[Mn, "negm"], w=[Mn])

            def t_attend(j):
                p = j % 2
                MB = MBs_[p]; Mn = f"MB{p}"; bs = bss[p]
                qb = qTt[j % 3]; qn_ = f"qTt{j % 3}"
                qb3 = qb.rearrange("p (h t) -> p h t", h=4)
                oacc = oaccs[p]; oan = f"oacc{p}"
                ntile = 16 + j + 1
                seq = [(g, t) for g in range(2) for t in range(ntile)]

                def emit_ST(i):
                    g, t = seq[i]
                    sb_ = i % 2
                    ptb = PT[i % 2]; ptn = f"PT{i % 2}"
                    P.op("pe", lambda e: e.matmul(out=pb[sb_][:, 0:256], lhsT=KT3[:, g, t * 128:(t + 1) * 128], rhs=qb3[:, g * 2:(g + 1) * 2, :], start=True, stop=False),
                         r=["KT", qn_], w=[PB[sb_]])
                    P.op("pe", lambda e: e.matmul(out=pb[sb_][:, 0:256], lhsT=MB[:, t * 128:(t + 1) * 128], rhs=ii2, start=False, stop=True),
                         r=[Mn, "ii2"], w=[PB[sb_]])
                    P.op("act", lambda e: e.activation(out=ptb, in_=pb[sb_][:, 0:256], func=AF.Exp, scale=SCALE), r=[PB[sb_]], w=[ptn])

                def emit_PV(i):
                    g, t = seq[i]
                    ptb = PT[i % 2]; ptn = f"PT{i % 2}"
                    for h2i in range(2):
                        P.op("pe", lambda e, h2i=h2i: e.matmul(out=pb[4 + g * 2 + h2i][:, 0:130], lhsT=ptb[:, h2i * 128:(h2i + 1) * 128], rhs=VA4[:, t, g, :], start=(t == 0), stop=(t == ntile - 1)),
                             r=[ptn, "VA"], w=[PB[4 + g * 2 + h2i]])

                emit_ST(0)
                if len(seq) > 1:
                    emit_ST(1)
                for i in range(len(seq)):
                    emit_PV(i)
                    if i + 2 < len(seq):
                        emit_ST(i + 2)
                for h in range(4):
                    P.op("act", lambda e, h=h: e.activation(out=oacc[:, h * 130:(h + 1) * 130], in_=pb[4 + h][:, 0:130], func=AF.Copy), r=[PB[4 + h]], w=[oan])

            def t_final(j):
                p = j % 2
                bs = bss[p]; oacc = oaccs[p]; oan = f"oacc{p}"
                for h in range(4):
                    P.op("dve", lambda e, h=h: e.reciprocal(out=bs[:, 8 + h:9 + h], in_=oacc[:, h * 130 + 128:h * 130 + 129]), r=[oan], w=[f"bs{p}_r"])
                    P.op("dve", lambda e, h=h: e.tensor_scalar(out=oatt[:, h * 128:(h + 1) * 128], in0=oacc[:, h * 130:h * 130 + 128], scalar1=bs[:, 8 + h:9 + h], scalar2=None, op0=ALU.mult),
                         r=[oan, f"bs{p}_r"], w=["oatt"])
                for h in range(4):
                    P.op("pe", lambda e, h=h: e.transpose(out=pb[3][:, h * 128:(h + 1) * 128], in_=oatt[:, h * 128:(h + 1) * 128], identity=ident), r=["oatt", "ident"], w=[PB[3]])
                evac("act", cTa, pb[3], [PB[3]], ["cTa"])
                P.dma("sp", S["cT"][j][:, 4:8, :], cTa.rearrange("p (h t) -> p h t", h=4), r=["cTa"], w=[f"cTa{j}"])

            if T_NT > 0:
                t_index(0)
            if T_NT > 1:
                t_index(1)
            if T_NT > 0:
                t_bisect(0)
            for j in range(T_NT):
                t_attend(j)
                if j + 2 < T_NT:
                    t_index(j + 2)
                if j + 1 < T_NT:
                    t_bisect(j + 1)
                t_final(j)

        if not os.environ.get('MK_NOTS'):
            P.barrier()
            A.reset(persist0)
            zSCALE = 128 ** -0.5
            NPG = 16
            zidx_i = A.f32(32).bitcast(I32)
            zpt_i = A.f32(256).bitcast(I32)
            zidx_f = A.f32(256); zsel = A.f32(257); zix = A.f32(32)
            P.dma("sp", zpt_i[:, :], I["pt"].to_broadcast([128, 256]), w=["zpt_i"])
            P.dma("sp", zsel, I["tsel"], w=["zsel"])
            P.op("dve", lambda e: e.tensor_copy(out=zidx_f, in_=zpt_i[:, :]), r=["zpt_i"], w=["zidx_f"])
            P.op("dve", lambda e: e.tensor_tensor(out=zidx_f, in0=zidx_f, in1=zsel[:, 0:256], op=ALU.mult), r=["zidx_f", "zsel"], w=["zidx_f"])
            P.op("dve", lambda e: e.tensor_reduce(out=zix, in_=zidx_f.rearrange("p (q j) -> p q j", j=8), axis=AX.X, op=ALU.add), r=["zidx_f"], w=["zix"])
            P.op("dve", lambda e: e.tensor_scalar(out=zix, in0=zix, scalar1=16.0, scalar2=zsel[:, 256:257], op0=ALU.mult, op1=ALU.add), r=["zix", "zsel"], w=["zix"])
            P.op("dve", lambda e: e.tensor_copy(out=zidx_i[:, :], in_=zix), r=["zix"], w=["zidx_i"])
            zPs = A.f32(INC)
            P.dma("sp", zPs[0:NS, :], S["ps"], w=["zPs"])
            zones = A.f32(128)
            P.op("dve", lambda e: e.memset(zones, 1.0), w=["zones"])
            ziqT = A.f32(NS * 8)
            ziwT = A.f32(NS)
            zikn = A.f32(NS)
            zqT = A.bf16(4 * NS)
            zkTn = A.bf16(2 * NS)
            ziqT3 = ziqT.rearrange("p (s h) -> p s h", h=8)
            R = slice(0, NS)
            for h in range(8):
                P.op("pe", lambda e, h=h: e.transpose(out=pb[0][0:64, h * NS:(h + 1) * NS], in_=zPs[R, C_IQ + h * 64:C_IQ + (h + 1) * 64], identity=ident[R, R]), r=["zPs", "ident"], w=[PB[0]])
            P.op("dve", lambda e: e.tensor_copy(out=ziqT3[0:64], in_=pb[0][0:64, 0:8 * NS].rearrange("p (h s) -> p s h", h=8)), r=[PB[0]], w=["ziqT"])
            P.op("pe", lambda e: e.transpose(out=pb[1][0:8, 0:NS], in_=zPs[R, C_IW:C_IW + 8], identity=ident[R, R]), r=["zPs", "ident"], w=[PB[1]])
            P.op("dve", lambda e: e.tensor_scalar(out=ziwT[0:8, :], in0=pb[1][0:8, 0:NS], scalar1=8 ** -0.5, scalar2=None, op0=ALU.mult), r=[PB[1]], w=["ziwT"])
            P.op("pe", lambda e: e.transpose(out=pb[1][0:64, 64:64 + NS], in_=zPs[R, C_IK:C_IK + 64], identity=ident[R, R]), r=["zPs", "ident"], w=[PB[1]])
            P.op("dve", lambda e: e.tensor_copy(out=zikn[0:64, :], in_=pb[1][0:64, 64:64 + NS]), r=[PB[1]], w=["zikn"])
            for h in range(4):
                P.op("pe", lambda e, h=h: e.transpose(out=pb[2][:, h * NS:(h + 1) * NS], in_=zPs[R, C_AQ + h * 128:C_AQ + (h + 1) * 128], identity=ident[R, R]), r=["zPs", "ident"], w=[PB[2]])
            for g in range(2):
                P.op("pe", lambda e, g=g: e.transpose(out=pb[2][:, 64 + g * NS:64 + (g + 1) * NS], in_=zPs[R, C_AK + g * 128:C_AK + (g + 1) * 128], identity=ident[R, R]), r=["zPs", "ident"], w=[PB[2]])
            P.op("dve", lambda e: e.tensor_copy(out=zqT, in_=pb[2][:, 0:4 * NS]), r=[PB[2]], w=["zqT"])
            P.op("dve", lambda e: e.tensor_copy(out=zkTn, in_=pb[2][:, 64:64 + 2 * NS]), r=[PB[2]], w=["zkTn"])
            zvn_f = A.f32(NS * 256); zvn = A.bf16(NS * 2 * 130)
            zvn4 = zvn.rearrange("p (s g d) -> p s g d", s=NS, g=2)
            P.dma("sp", zvn_f[0:1, :].rearrange("p (s c) -> p s c", s=NS), S["ps"][:, C_AV:C_AV + 256].rearrange("(o s) c -> o s c", o=1), w=["zvn_f"])
            P.op("dve", lambda e: e.memset(zvn[0:1, :], 1.0), w=["zvn"])
            P.op("dve", lambda e: e.tensor_copy(out=zvn4[0:1, :, :, 0:128], in_=zvn_f[0:1, :].rearrange("p (s g d) -> p s g d", s=NS, g=2)), r=["zvn_f", "zvn"], w=["zvn"])
            NKS = 2049
            zIall = A.f32(NKS + 3); zikgs = [A.f32(NPG * 64), A.f32(NPG * 64)]; zikT = A.f32(NKS + 3)
            zik_rows = I["cache_ik"].rearrange("(r t) d -> r (t d)", t=8)
            zk_rows = I["cache_k"].rearrange("(r t) d -> r (t d)", t=8)
            zv_rows = I["cache_v"].rearrange("(r t) d -> r (t d)", t=8)
            zikg4s = [z_.rearrange("p (a t d) -> p a t d", a=2, t=8) for z_ in zikgs]

            def ts1_gather(s_):
                zg4 = zikg4s[s_ % 2]
                for a_ in range(2):
                    col = s_ * 2 + a_
                    P.dma_raw("pool", lambda e, a_=a_, col=col: e.indirect_dma_start(out=zg4[:, a_].rearrange("p t d -> p (t d)"), out_offset=None, in_=zik_rows, in_offset=bass.IndirectOffsetOnAxis(ap=zidx_i[:, col:col + 1], axis=0)),
                              r=["zidx_i"], w=[f"zikg{s_ % 2}"], sw=True)
            ts1_gather(0)
            zr8 = [A.f32(512), A.f32(512)]; zrw = A.f32(NKS + 3)
            for s_ in range(NS):
                if s_ + 1 < NS:
                    ts1_gather(s_ + 1)
                zikg4 = zikg4s[s_ % 2]; zikgn = f"zikg{s_ % 2}"
                for q4 in range(4):
                    for i4 in range(4):
                        bk = q4 * 4 + i4
                        t8, a_ = bk // 2, bk % 2
                        P.op("pe", lambda e, t8=t8, a_=a_, i4=i4, zikg4=zikg4: e.transpose(out=pb[3][0:64, i4 * 128:(i4 + 1) * 128], in_=zikg4[:, a_, t8, :], identity=ident), r=[zikgn, "ident"], w=[PB[3]])
                    evac("act" if q4 % 2 else "dve", zikT[0:64, q4 * 512:(q4 + 1) * 512], pb[3][0:64, :], [PB[3]], ["zikT"])
                P.op("dve", lambda e, s_=s_: e.tensor_copy(out=zikT[0:64, 2048:2049], in_=zikn[0:64, s_:s_ + 1]), r=["zikn"], w=["zikT"])
                for kb in range(5):
                    c0, c1 = kb * 512, min(NKS, kb * 512 + 512)
                    w_ = c1 - c0
                    rb = zr8[kb % 2]; rbn = f"zr8{kb % 2}"
                    P.op("pe", lambda e, s_=s_, c0=c0, c1=c1, w_=w_, kb=kb: e.matmul(out=pb[4 + kb % 2][0:8, 0:w_], lhsT=ziqT3[0:64, s_, :], rhs=zikT[0:64, c0:c1], start=True, stop=True), r=["ziqT", "zikT"], w=[PB[4 + kb % 2]])
                    P.op("dve", lambda e, s_=s_, w_=w_, kb=kb, rb=rb: e.tensor_scalar(out=rb[0:8, 0:w_], in0=pb[4 + kb % 2][0:8, 0:w_], scalar1=0.0, scalar2=ziwT[0:8, s_:s_ + 1], op0=ALU.max, op1=ALU.mult),
                         r=[PB[4 + kb % 2], "ziwT"], w=[rbn])
                    P.op("pe", lambda e, w_=w_, kb=kb, rb=rb: e.matmul(out=pb[6 + kb % 2][0:1, 0:w_], lhsT=zones[0:8, 0:1], rhs=rb[0:8, 0:w_], start=True, stop=True), r=["zones", rbn], w=[PB[6 + kb % 2]])
                    evac("act", zrw[0:1, c0:c1], pb[6 + kb % 2][0:1, 0:w_], [PB[6 + kb % 2]], ["zrw"])
                P.dma("sp", zIall[s_:s_ + 1, 0:NKS], zrw[0:1, 0:NKS], r=["zrw"], w=["zIall"])
            zbs = A.f32(16); zjunk = A.bf16(NKS + 3); zMB = A.f32(NKS + 3)
            zlo, zrng, zthr, zcnt, zmm = zbs[R, 0:1], zbs[R, 1:2], zbs[R, 2:3], zbs[R, 3:4], zbs[R, 4:5]
            P.op("dve", lambda e: e.tensor_reduce(out=zlo, in_=zIall[R, 0:NKS], axis=AX.X, op=ALU.min), r=["zIall"], w=["zlo"])
            P.op("dve", lambda e: e.tensor_reduce(out=zrng, in_=zIall[R, 0:NKS], axis=AX.X, op=ALU.max), r=["zIall"], w=["zrng"])
            P.op("dve", lambda e: e.tensor_scalar(out=zthr, in0=zrng, scalar1=-128.0, scalar2=None, op0=ALU.add), r=["zrng"], w=["zthr"])
            P.op("dve", lambda e: e.tensor_tensor(out=zlo, in0=zlo, in1=zthr, op=ALU.max), r=["zlo", "zthr"], w=["zlo"])
            P.op("dve", lambda e: e.tensor_tensor(out=zrng, in0=zrng, in1=zlo, op=ALU.subtract), r=["zrng", "zlo"], w=["zrng"])
            zbase = zbs[R, 5:6]
            ZNB = 14
            P.op("dve", lambda e: e.scalar_tensor_tensor(out=zthr, in0=zrng, scalar=0.5, in1=zlo, op0=ALU.mult, op1=ALU.add), r=["zrng", "zlo"], w=["zthr"])
            for it in range(ZNB):
                st = 2.0 ** -(it + 1)
                P.op("dve", lambda e: e.tensor_scalar(out=zjunk[R, 0:NKS], in0=zIall[R, 0:NKS], scalar1=zthr, scalar2=None, op0=ALU.is_ge, op1=ALU.add, accum_out=zcnt), r=["zIall", "zthr"], w=["zjunk", "zcnt"])
                P.op("dve", lambda e, st=st: e.scalar_tensor_tensor(out=zbase, in0=zrng, scalar=-0.5 * st, in1=zthr, op0=ALU.mult, op1=ALU.add), r=["zrng", "zthr"], w=["zbase"])
                P.op("dve", lambda e: e.tensor_scalar(out=zmm, in0=zcnt, scalar1=255.5, scalar2=zrng, op0=ALU.is_ge, op1=ALU.mult), r=["zcnt", "zrng"], w=["zmm"])
                P.op("dve", lambda e, st=st: e.scalar_tensor_tensor(out=zthr, in0=zmm, scalar=st, in1=zbase, op0=ALU.mult, op1=ALU.add), r=["zmm", "zbase"], w=["zthr"])
            P.op("dve", lambda e: e.scalar_tensor_tensor(out=zlo, in0=zrng, scalar=-(2.0 ** -(ZNB + 1)), in1=zthr, op0=ALU.mult, op1=ALU.add), r=["zrng", "zthr"], w=["zlo"])
            P.op("dve", lambda e: e.tensor_scalar(out=zMB[R, 0:NKS], in0=zIall[R, 0:NKS], scalar1=zlo, scalar2=None, op0=ALU.is_ge), r=["zIall", "zlo"], w=["zMB"])
            zMT = A.f32(NPG * NS); zMT3 = zMT.rearrange("p (g s) -> p g s", g=NPG); zMn = A.f32(NS)
            for pg in range(NPG):
                P.op("pe", lambda e, pg=pg: e.transpose(out=pb[0][:, pg * NS:(pg + 1) * NS], in_=zMB[R, pg * 128:(pg + 1) * 128], identity=ident[R, R]), r=["zMB", "ident"], w=[PB[0]])
            evac("dve", zMT, pb[0][:, 0:NPG * NS], [PB[0]], ["zMT"])
            P.op("pe", lambda e: e.transpose(out=pb[1][0:1, 0:NS], in_=zMB[R, 2048:2049], identity=ident[R, R]), r=["zMB", "ident"], w=[PB[1]])
            evac("dve", zMn[0:1, :], pb[1][0:1, 0:NS], [PB[1]], ["zMn"])
            zkgs = [A.f32(NPG * 256), A.f32(NPG * 256)]; zvgs = [A.f32(NPG * 256), A.f32(NPG * 256)]
            zkg4s = [z_.rearrange("p (a t c) -> p a t c", a=2, t=8) for z_ in zkgs]; zvg4s = [z_.rearrange("p (a t c) -> p a t c", a=2, t=8) for z_ in zvgs]

            def ts2_gather(s_):
                zk4 = zkg4s[s_ % 2]; zv4 = zvg4s[s_ % 2]
                for a_ in range(2):
                    col = s_ * 2 + a_
                    P.dma_raw("pool", lambda e, a_=a_, col=col: e.indirect_dma_start(out=zk4[:, a_].rearrange("p t c -> p (t c)"), out_offset=None, in_=zk_rows, in_offset=bass.IndirectOffsetOnAxis(ap=zidx_i[:, col:col + 1], axis=0)),
                              r=["zidx_i"], w=[f"zkg{s_ % 2}"], sw=True)
                    P.dma_raw("pool", lambda e, a_=a_, col=col: e.indirect_dma_start(out=zv4[:, a_].rearrange("p t c -> p (t c)"), out_offset=None, in_=zv_rows, in_offset=bass.IndirectOffsetOnAxis(ap=zidx_i[:, col:col + 1], axis=0)),
                              r=["zidx_i"], w=[f"zvg{s_ % 2}"], sw=True)
            ts2_gather(0)
            zKT = A.bf16(NPG * 256); zKT4 = zKT.rearrange("p (g k c) -> p g k c", g=NPG, k=2)
            zVb = A.bf16(NPG * 2 * 130); zVb4 = zVb.rearrange("p (g k d) -> p g k d", g=NPG, k=2)
            zP = A.bf16(NPG * 4); zP3 = zP.rearrange("p (g h) -> p g h", g=NPG); zPn = A.bf16(4)
            zsm = A.f32(16); zcr = A.f32(128)
            zo = A.f32(256); zoT = A.bf16(4 * NS); zoT3 = zoT.rearrange("p (h s) -> p h s", h=4)
            P.op("dve", lambda e: e.memset(zVb, 1.0), w=["zVb"])
            for s_ in range(NS):
                if s_ + 1 < NS:
                    ts2_gather(s_ + 1)
                zkg4 = zkg4s[s_ % 2]; zvg4 = zvg4s[s_ % 2]; zkgn = f"zkg{s_ % 2}"; zvgn = f"zvg{s_ % 2}"
                for a_ in range(2):
                    P.op("act", lambda e, a_=a_, zvg4=zvg4: e.activation(out=zVb.rearrange("p (t a k d) -> p a t k d", t=8, a=2, k=2)[:, a_, :, :, 0:128], in_=zvg4[:, a_].rearrange("p t (k d) -> p t k d", k=2), func=AF.Copy),
                         r=[zvgn, "zVb"], w=["zVb"])
                for pq in range(8):
                    for i2 in range(2):
                        bk = pq * 2 + i2
                        t8, a_ = bk // 2, bk % 2
                        for g in range(2):
                            P.op("pe", lambda e, t8=t8, a_=a_, g=g, i2=i2, pq=pq, zkg4=zkg4: e.transpose(out=pb[2 + pq % 2][:, (i2 * 2 + g) * 128:(i2 * 2 + g + 1) * 128], in_=zkg4[:, a_, t8, g * 128:(g + 1) * 128], identity=ident),
                                 r=[zkgn, "ident"], w=[PB[2 + pq % 2]])
                    evac("act" if pq % 2 else "dve", zKT[:, pq * 512:(pq + 1) * 512], pb[2 + pq % 2], [PB[2 + pq % 2]], ["zKT"])
                for pg in range(NPG):
                    for g in range(2):
                        P.op("pe", lambda e, pg=pg, g=g, s_=s_: e.matmul(out=pb[4][:, pg * 4 + g * 2:pg * 4 + g * 2 + 2], lhsT=zKT4[:, pg, g, :], rhs=zqT.rearrange("p (h s) -> p h s", h=4)[:, g * 2:(g + 1) * 2, s_], start=True, stop=True),
                             r=["zKT", "zqT"], w=[PB[4]])
                for g in range(2):
                    P.op("pe", lambda e, g=g, s_=s_: e.matmul(out=pb[5][0:1, g * 2:g * 2 + 2], lhsT=zkTn.rearrange("p (g s) -> p g s", g=2)[:, g, s_:s_ + 1], rhs=zqT.rearrange("p (h s) -> p h s", h=4)[:, g * 2:(g + 1) * 2, s_], start=True, stop=True),
                         r=["zkTn", "zqT"], w=[PB[5]])
                P.op("dve", lambda e: e.tensor_reduce(out=zsm[:, 0:1], in_=pb[4][:, 0:NPG * 4], axis=AX.X, op=ALU.max), r=[PB[4]], w=["zsm0"])
                P.op("pe", lambda e: e.transpose(out=pb[6][0:1, 0:128], in_=zsm[:, 0:1], identity=ident), r=["zsm0", "ident"], w=[PB[6]])
                evac("dve", zcr[0:1, :], pb[6][0:1, 0:128], [PB[6]], ["zcr"])
                P.op("dve", lambda e: e.tensor_reduce(out=zsm[0:1, 1:2], in_=zcr[0:1, :], axis=AX.X, op=ALU.max), r=["zcr"], w=["zsm1"])
                P.op("dve", lambda e: e.tensor_reduce(out=zsm[0:1, 2:3], in_=pb[5][0:1, 0:4], axis=AX.X, op=ALU.max), r=[PB[5]], w=["zsm2"])
                P.op("dve", lambda e: e.tensor_tensor(out=zsm[0:1, 1:2], in0=zsm[0:1, 1:2], in1=zsm[0:1, 2:3], op=ALU.max), r=["zsm1", "zsm2"], w=["zsm1"])
                P.op("dve", lambda e: e.tensor_scalar(out=zsm[0:1, 1:2], in0=zsm[0:1, 1:2], scalar1=-zSCALE, scalar2=None, op0=ALU.mult), r=["zsm1"], w=["zsm1"])
                P.op("pe", lambda e: e.matmul(out=pb[6][:, 128:129], lhsT=zones[0:1, :], rhs=zsm[0:1, 1:2], start=True, stop=True), r=["zones", "zsm1"], w=[PB[6]])
                evac("dve", zsm[:, 3:4], pb[6][:, 128:129], [PB[6]], ["zsm3"])
                P.op("act", lambda e: e.activation(out=zP, in_=pb[4][:, 0:NPG * 4], func=AF.Exp, scale=zSCALE, bias=zsm[:, 3:4]), r=[PB[4], "zsm3"], w=["zP"])
                P.op("act", lambda e: e.activation(out=zPn[0:1, :], in_=pb[5][0:1, 0:4], func=AF.Exp, scale=zSCALE, bias=zsm[0:1, 3:4]), r=[PB[5], "zsm3"], w=["zPn"])
                for h in range(4):
                    P.op("dve", lambda e, h=h, s_=s_: e.tensor_tensor(out=zP3[:, :, h], in0=zP3[:, :, h], in1=zMT3[:, :, s_], op=ALU.mult), r=["zP", "zMT"], w=["zP"])
                P.op("dve", lambda e, s_=s_: e.tensor_scalar(out=zPn[0:1, :], in0=zPn[0:1, :], scalar1=zMn[0:1, s_:s_ + 1], scalar2=None, op0=ALU.mult), r=["zPn", "zMn"], w=["zPn"])
                for g in range(2):
                    for pg in range(NPG):
                        P.op("pe", lambda e, pg=pg, g=g: e.matmul(out=pb[g][0:2, 0:130], lhsT=zP3[:, pg, g * 2:(g + 1) * 2], rhs=zVb4[:, pg, g, :], start=(pg == 0), stop=False), r=["zP", "zVb"], w=[PB[g]])
                    P.op("pe", lambda e, g=g, s_=s_: e.matmul(out=pb[g][0:2, 0:130], lhsT=zPn[0:1, g * 2:(g + 1) * 2], rhs=zvn4[0:1, s_, g, :], start=False, stop=True), r=["zPn", "zvn"], w=[PB[g]])
                for g in range(2):
                    P.op("dve", lambda e, g=g: e.reciprocal(out=zsm[0:2, 4 + g:5 + g], in_=pb[g][0:2, 128:129]), r=[PB[g]], w=["zsm4"])
                    P.op("dve", lambda e, g=g: e.tensor_scalar(out=zo[0:2, g * 128:(g + 1) * 128], in0=pb[g][0:2, 0:128], scalar1=zsm[0:2, 4 + g:5 + g], scalar2=None, op0=ALU.mult), r=[PB[g], "zsm4"], w=["zo"])
                for g in range(2):
                    P.op("pe", lambda e, g=g: e.transpose(out=pb[7][:, g * 2:(g + 1) * 2], in_=zo[0:2, g * 128:(g + 1) * 128], identity=ident[0:2, 0:2]), r=["zo", "ident"], w=[PB[7]])
                P.op("dve", lambda e, s_=s_: e.tensor_copy(out=zoT3[:, :, s_], in_=pb[7][:, 0:4]), r=[PB[7]], w=["zoT"])
            P.dma("sp", S["cT"][NT][:, 4:8, 0:NS], zoT3, r=["zoT"], w=["cTas"])

        if not os.environ.get('MK_NOD'):
            P.barrier()
            A.reset(persist0)
            zt = A.bf16(1024)
            P.op("pool", lambda e: e.memset(zt, 0.0), w=["zt"])
            if True:
                for ti in range(NT + 1):
                    if (ti == NT and os.environ.get('MK_NOTS')) or (ti < NT and (os.environ.get('MK_NOT') or ti >= int(os.environ.get("MK_TNT", NT)))):
                        P.dma("sp", S["cT"][ti][:, 4:8, :], zt[:, 0:512].rearrange("p (k t) -> p k t", k=4), r=["zt"], w=[f"cT{ti}"])
                    if ti == NT:
                        P.dma("sp", S["cT"][ti][:, 0:4, 16:128], zt[:, 0:448].rearrange("p (k t) -> p k t", k=4), r=["zt"], w=[f"cT{ti}"])
            g2_bc = A.f32(D); ga1_bc = A.f32(D); a2_bc = A.f32(D); sh2_bc = A.f32(D)
            a2_s = A.f32(D)
            P.dma("sp", g2_bc, I["g2"].to_broadcast([128, D]), w=["g2_bc"])
            P.dma("sp", ga1_bc, S["mod"][16:17, 2 * D:3 * D].to_broadcast([128, D]), r=["mod_scr"], w=["ga1_bc"])
            P.dma("sp", sh2_bc, S["mod"][16:17, 3 * D:4 * D].to_broadcast([128, D]), r=["mod_scr"], w=["sh2_bc"])
            P.dma("sp", a2_bc, S["mod"][16:17, 4 * D:5 * D].to_broadcast([128, D]), r=["mod_scr"], w=["a2_bc"])
            P.op("dve", lambda e: e.scalar_tensor_tensor(out=a2_bc, in0=a2_bc, scalar=1.0, in1=g2_bc, op0=ALU.add, op1=ALU.mult),
                 r=["a2_bc", "g2_bc"], w=["a2_bc"])
            P.op("dve", lambda e: e.scalar_tensor_tensor(out=a2_s[0:NS, :], in0=mod_sb[0:NS, 4 * D:5 * D], scalar=1.0, in1=g2_bc[0:NS, :], op0=ALU.add, op1=ALU.mult),
                 r=["mod_sb", "g2_bc"], w=["a2_s"])
            persistD = A.off
            w_out_b = A.bf16(8 * D); w_out3 = w_out_b.rearrange("p (k n) -> p k n", k=8)
            w_f_b = A.bf16(8 * 2 * DFF); w_f3 = w_f_b.rearrange("p (k n) -> p k n", k=8)
            persistD1 = A.off
            wst = [A.f32(8 * 512), A.f32(8 * 512)]
            for cb in range(2 + 11):
                ws3 = wst[cb % 2].rearrange("p (k n) -> p k n", k=8)
                if cb < 2:
                    src = I["w_out"][:, cb * 512:(cb + 1) * 512]; dst = w_out3[:, :, cb * 512:(cb + 1) * 512]; dn = "w_out_b"
                else:
                    src = I["w_ffn_in"][:, (cb - 2) * 512:(cb - 1) * 512]; dst = w_f3[:, :, (cb - 2) * 512:(cb - 1) * 512]; dn = "w_f_b"
                P.dma("sp", ws3, src.rearrange("(k p) n -> p k n", p=128), w=[f"wst{cb % 2}"])
                P.op("pool" if cb % 2 else "dve", lambda e, ws3=ws3, dst=dst: e.tensor_copy(out=dst, in_=ws3), r=[f"wst{cb % 2}"], w=[dn])
            P.seed_after_staging()
            A.reset(persistD1)
            xt = [A.f32(D), A.f32(D)]
            cTt = [A.bf16(8 * 128), A.bf16(8 * 128)]
            x1t = A.f32(D); h2 = A.f32(D); small = A.f32(16)
            h2T = A.bf16(8 * 512); h2T3 = h2T.rearrange("p (k t) -> p k t", k=8)
            uT = A.bf16(22 * 512); uT3 = uT.rearrange("p (f t) -> p f t", f=22)
            gsb = [A.f32(512), A.f32(512)]

            def rms_mod(rows, xin, xin_n, a_t, a_n, sh_t, sh_n, out, out_n):
                ss, rs = small[0:rows, 0:1], small[0:rows, 1:2]
                P.op("act", lambda e: e.activation(out=out, in_=xin, func=AF.Square, accum_out=ss), r=[xin_n], w=[out_n, "ss"])
                P.op("dve", lambda e: e.tensor_scalar(out=rs, in0=ss, scalar1=1.0 / D, scalar2=1e-6, op0=ALU.mult, op1=ALU.add), r=["ss"], w=["rs"])
                P.op("act", lambda e: e.activation(out=rs, in_=rs, func=AF.Sqrt), r=["rs"], w=["rs"])
                P.op("dve", lambda e: e.reciprocal(out=rs, in_=rs), r=["rs"], w=["rs"])
                P.op("dve", lambda e: e.scalar_tensor_tensor(out=out, in0=xin, scalar=rs, in1=a_t, op0=ALU.mult, op1=ALU.mult), r=[xin_n, "rs", a_n], w=[out_n])
                if sh_t is not None:
                    P.op("pool", lambda e: e.tensor_tensor(out=out, in0=out, in1=sh_t, op=ALU.add), r=[out_n, sh_n], w=[out_n])

            groups = [(g4, [(g4 * 4 + j, 128) for j in range(4)]) for g4 in range(4)] + [(4, [(NT, NS)])]
            for g4, tiles in groups:
                ntok = sum(r_ for _, r_ in tiles)
                col = 0
                for (ti, rows) in tiles:
                    it = ti
                    xb = xt[it % 2]; xn = f"xt{it % 2}"; cb_ = cTt[it % 2]; cn = f"cTt{it % 2}"
                    cT3 = cb_.rearrange("p (k t) -> p k t", k=8)
                    smp = (rows == NS)
                    P.dma("sp", xb[0:rows, :], I["xs"] if smp else I["x_own"][ti * 128:(ti + 1) * 128, :], w=[xn])
                    P.dma("sp", cT3, S["cT"][ti], r=[f"cT{ti}"], w=[cn])
                    ga1_t = mod_sb[0:NS, 2 * D:3 * D] if smp else ga1_bc
                    for hb in range(2):
                        for k in range(8):
                            P.op("pe", lambda e, k=k, hb=hb, cT3=cT3, rows=rows: e.matmul(out=pb[hb][0:rows, :], lhsT=cT3[:, k, 0:rows], rhs=w_out3[:, k, hb * 512:(hb + 1) * 512], start=(k == 0), stop=(k == 7)),
                                 r=[cn, "w_out_b"], w=[PB[hb]])
                        P.op("dve", lambda e, hb=hb, rows=rows, ga1_t=ga1_t: e.tensor_tensor(out=x1t[0:rows, hb * 512:(hb + 1) * 512], in0=pb[hb][0:rows, :], in1=ga1_t[0:rows, hb * 512:(hb + 1) * 512], op=ALU.mult),
                             r=[PB[hb], "ga1_bc", "mod_sb"], w=["x1t"])
                    P.op("pool", lambda e, rows=rows, xb=xb: e.tensor_tensor(out=x1t[0:rows, :], in0=x1t[0:rows, :], in1=xb[0:rows, :], op=ALU.add), r=["x1t", xn], w=["x1t"])
                    P.dma("sp", S["x1"][ti, 0:rows, :], x1t[0:rows, :], r=["x1t"], w=[f"x1_{ti}"])
                    if smp:
                        rms_mod(rows, x1t[0:rows, :], "x1t", a2_s[0:rows, :], "a2_s", mod_sb[0:rows, 3 * D:4 * D], "mod_sb", h2[0:rows, :], "h2")
                    else:
                        rms_mod(rows, x1t, "x1t", a2_bc, "a2_bc", sh2_bc, "sh2_bc", h2, "h2")
                    for k in range(8):
                        bank = 2 + k // 4
                        P.op("pe", lambda e, k=k, bank=bank, rows=rows: e.transpose(out=pb[bank][:, (k % 4) * 128:(k % 4) * 128 + rows], in_=h2[0:rows, k * 128:(k + 1) * 128], identity=ident[0:rows, 0:rows]),
                             r=["h2", "ident"], w=[PB[bank]])
                    for half_ in range(2):
                        evac(alt(), h2T3[:, half_ * 4:(half_ + 1) * 4, col:col + rows], pb[2 + half_][:, :].rearrange("p (k t) -> p k t", k=4)[:, :, 0:rows], [PB[2 + half_]], ["h2T"])
                    col += rows
                for fb in range(22):
                    for which in range(2):
                        bank = 4 + which * 2 + fb % 2
                        c0 = which * DFF + fb * 128
                        for k in range(8):
                            P.op("pe", lambda e, k=k, bank=bank, c0=c0, ntok=ntok: e.matmul(out=pb[bank][:, 0:ntok], lhsT=w_f3[:, k, c0:c0 + 128], rhs=h2T3[:, k, 0:ntok], start=(k == 0), stop=(k == 7)),
                                 r=["h2T", "w_f_b"], w=[PB[bank]])
                    gs_ = gsb[fb % 2]; gn = f"gsb{fb % 2}"
                    P.op("act", lambda e, fb=fb, gs_=gs_, ntok=ntok: e.activation(out=gs_[:, 0:ntok], in_=pb[4 + fb % 2][:, 0:ntok], func=AF.Silu), r=[PB[4 + fb % 2]], w=[gn])
                    P.op("dve", lambda e, fb=fb, gs_=gs_, ntok=ntok: e.tensor_tensor(out=uT3[:, fb, 0:ntok], in0=gs_[:, 0:ntok], in1=pb[6 + fb % 2][:, 0:ntok], op=ALU.mult),
                         r=[gn, PB[6 + fb % 2]], w=["uT"])
                P.dma("sp", S["uT"][g4], uT3, r=["uT"], w=[f"uT{g4}"])

            P.barrier()
            A.reset(persistD)
            ga2_bc = A.f32(D); gf_bc = A.f32(D)
            P.dma("sp", gf_bc, I["g_final"].to_broadcast([128, D]), w=["gf_bc"])
            P.dma("sp", ga2_bc, S["mod"][16:17, 5 * D:6 * D].to_broadcast([128, D]), r=["mod_scr"], w=["ga2_bc"])
            w_o_b = A.bf16(22 * D); w_o3 = w_o_b.rearrange("p (k n) -> p k n", k=22)
            persistD2 = A.off
            wst = [A.f32(22 * 256), A.f32(22 * 256)]
            for cb in range(4):
                ws3 = wst[cb % 2].rearrange("p (k n) -> p k n", k=22)
                P.dma("sp", ws3, I["w_ffn_out"][:, cb * 256:(cb + 1) * 256].rearrange("(k p) n -> p k n", p=128), w=[f"wst{cb % 2}"])
                P.op("pool" if cb % 2 else "dve", lambda e, ws3=ws3, cb=cb: e.tensor_copy(out=w_o3[:, :, cb * 256:(cb + 1) * 256], in_=ws3), r=[f"wst{cb % 2}"], w=["w_o_b"])
            P.seed_after_staging()
            A.reset(persistD2)
            uTg = [A.bf16(22 * 512), A.bf16(22 * 512)]
            x1b = [A.f32(D), A.f32(D)]
            x2 = A.f32(D); small = A.f32(16)
            yb = [A.f32(D), A.f32(D)]
            for g4, tiles in groups:
                ug = uTg[g4 % 2]; un = f"uTg{g4 % 2}"
                ug3 = ug.rearrange("p (f t) -> p f t", f=22)
                P.dma("sp", ug3, S["uT"][g4], r=[f"uT{g4}"], w=[un])
                col = 0
                for (ti, rows) in tiles:
                    smp = (rows == NS)
                    xb = x1b[ti % 2]; xn = f"x1b{ti % 2}"; yo = yb[ti % 2]; yn = f"yb{ti % 2}"
                    P.dma("sp", xb[0:rows, :], S["x1"][ti, 0:rows, :], r=[f"x1_{ti}"], w=[xn])
                    ga2_t = mod_sb[0:NS, 5 * D:6 * D] if smp else ga2_bc
                    for hb in range(2):
                        for kf in range(22):
                            P.op("pe", lambda e, kf=kf, hb=hb, rows=rows, col=col, ug3=ug3: e.matmul(out=pb[hb][0:rows, :], lhsT=ug3[:, kf, col:col + rows], rhs=w_o3[:, kf, hb * 512:(hb + 1) * 512], start=(kf == 0), stop=(kf == 21)),
                                 r=[un, "w_o_b"], w=[PB[hb]])
                        P.op("dve", lambda e, hb=hb, rows=rows, ga2_t=ga2_t: e.tensor_tensor(out=x2[0:rows, hb * 512:(hb + 1) * 512], in0=pb[hb][0:rows, :], in1=ga2_t[0:rows, hb * 512:(hb + 1) * 512], op=ALU.mult),
                             r=[PB[hb], "ga2_bc", "mod_sb"], w=["x2"])
                    P.op("pool", lambda e, rows=rows, xb=xb: e.tensor_tensor(out=x2[0:rows, :], in0=x2[0:rows, :], in1=xb[0:rows, :], op=ALU.add), r=["x2", xn], w=["x2"])
                    rms_mod(rows, x2[0:rows, :], "x2", gf_bc[0:rows, :], "gf_bc", None, None, yo[0:rows, :], yn)
                    P.dma("sp", O["y_s"] if smp else O["y_own"][ti * 128:(ti + 1) * 128, :], yo[0:rows, :], r=[yn])
                    col += rows

        P.build(ctx)
        global LAST_PROG
        LAST_PROG = P
    return nc


def rope_table(pos):
    pos = np.asarray(pos, np.float64)[:, None]
    invA = 500000.0 ** (-np.arange(16, dtype=np.float64) / 16)
    invI = 500000.0 ** (-np.arange(8, dtype=np.float64) / 8)
    angA = (pos.astype(np.float32) * invA.astype(np.float32)[None, :]).astype(np.float32)
    angI = (pos.astype(np.float32) * invI.astype(np.float32)[None, :]).astype(np.float32)
    t = np.concatenate([np.tile(np.cos(angA), (1, 4)), np.tile(np.sin(angA), (1, 4)),
                        np.tile(np.cos(angI), (1, 8)), np.tile(np.sin(angI), (1, 8))], axis=1)
    return np.ascontiguousarray(t.astype(np.float32))


_NC_CACHE = {}


def kernel(x_prompt, x_sample, c_prompt, c_sample, cache_k, cache_v, cache_idx_k, page_table, state_conv, state_ssm,
           w_ada, b_ada, g_norm1, w_in, w_conv, a_log, dt_bias, g_gdn_norm, w_out, g_norm2, w_ffn_in, w_ffn_out, g_final):
    f = lambda a: np.ascontiguousarray(np.asarray(a, dtype=np.float32))
    x_prompt = f(x_prompt); x_sample = f(x_sample)
    w_in_p = np.ascontiguousarray(f(w_in)[0][:, PERM])
    wc_p = np.ascontiguousarray(np.concatenate([f(w_conv)[0][:, 512:1536], f(w_conv)[0][:, 0:512]], axis=1))
    jj, cc = np.meshgrid(np.arange(128), np.arange(128), indexing="ij")
    GCONST = np.ascontiguousarray(np.concatenate([
        (jj <= cc).astype(np.float32),
        np.ones((128, 128), np.float32),
        np.where(cc >= jj, 1e4, 0.0).astype(np.float32),
        np.where(cc < jj, -1e4, 0.0).astype(np.float32),
        np.tile(np.eye(128, dtype=np.float32), (1, 4))], axis=1))
    CAUS = np.ascontiguousarray(np.where(np.arange(128)[None, :] > np.arange(128)[:, None], -30000.0, 0.0).astype(np.float32))
    pp = np.arange(128)
    TSEL = np.zeros((128, 257), np.float32)
    TSEL[:, :256] = np.tile((np.arange(8)[None, :] == (pp // 16)[:, None]).astype(np.float32), (1, 32))
    TSEL[:, 256] = pp % 16
    cik2 = f(cache_idx_k).reshape(-1, 64); ck2 = f(cache_k).reshape(-1, 256); cv2 = f(cache_v).reshape(-1, 256)
    EYE16 = np.ascontiguousarray(np.tile(np.eye(16, dtype=np.float32).reshape(1, 256), (128, 1)))
    in_maps = []
    for c in range(8):
        b, s = c // 2, c % 2
        own = slice(s * HALF, (s + 1) * HALF); oth = slice((1 - s) * HALF, (2 - s) * HALF)
        m = {
            "x_own": f(x_prompt[b, own]), "x_oth": f(x_prompt[b, oth]),
            "cin": f(np.concatenate([np.asarray(c_sample)[c * NS:(c + 1) * NS], np.asarray(c_prompt)[b:b + 1]], 0)),
            "xs": f(x_sample[c * NS:(c + 1) * NS, 0]),
            "w_ada": f(w_ada)[0], "b_ada": f(b_ada), "g1": f(g_norm1), "w_in": w_in_p, "w_conv": f(w_conv)[0],
            "a_log": f(a_log), "dt_bias": f(dt_bias), "g_gdn": f(g_gdn_norm), "w_out": f(w_out)[0], "g2": f(g_norm2),
            "w_ffn_in": f(w_ffn_in)[0], "w_ffn_out": f(w_ffn_out)[0], "g_final": f(g_final)[None, :],
            "tab_own": rope_table(np.arange(s * HALF, (s + 1) * HALF)), "tab_oth": rope_table(np.arange((1 - s) * HALF, (2 - s) * HALF)),
            "tab_s": rope_table(np.full(NS, 2048)),
            "flags": np.tile(np.array([[float(s), (s - 1) * 30000.0, 0, 0]], np.float32), (128, 1)),
            "ident": np.eye(128, dtype=np.float32),
            "state_conv": f(np.asarray(state_conv)[0, c * NS:(c + 1) * NS]),
            "gconst": GCONST, "wc_p": wc_p,
            "state_ssm": f(np.asarray(state_ssm)[0, c * NS:(c + 1) * NS]), "eye16": EYE16, "caus": CAUS,
            "pt": np.ascontiguousarray(np.asarray(page_table, np.int32)[c * NS:(c + 1) * NS].reshape(1, NS * 16)), "tsel": TSEL,
            "cache_ik": cik2, "cache_k": ck2, "cache_v": cv2,
        }
        if os.environ.get('MK_NOTS'):
            for k_ in ("cache_ik", "cache_k", "cache_v"):
                m.pop(k_)
        in_maps.append(m)
    if "nc" not in _NC_CACHE:
        _NC_CACHE["nc"] = build_program()
    res = run_bass_kernel_spmd(_NC_CACHE["nc"], in_maps, core_ids=list(range(8)))
    R = res.results
    B = 4
    y_prompt = np.zeros((B, T, D), np.float32); nk = np.zeros((1, B, T, 2, 128), np.float32); nv = np.zeros_like(nk)
    nik = np.zeros((1, B, T, 64), np.float32); nconv = np.zeros((1, B, 3, 1536), np.float32); nssm = np.zeros((1, B, 4, 128, 128), np.float32)
    y_s = np.zeros((128, 1, D), np.float32); ks = np.zeros((1, 128, 1, 2, 128), np.float32); vs = np.zeros_like(ks)
    iks = np.zeros((1, 128, 1, 64), np.float32); convs = np.zeros((1, 128, 3, 1536), np.float32); ssms = np.zeros((1, 128, 4, 128, 128), np.float32)
    for c in range(8):
        b, s = c // 2, c % 2
        own = slice(s * HALF, (s + 1) * HALF)
        r = R[c]
        y_prompt[b, own] = r["y_own"]; nk[0, b, own] = r["k_own"].reshape(HALF, 2, 128); nv[0, b, own] = r["v_own"].reshape(HALF, 2, 128)
        nik[0, b, own] = r["ik_own"]
        if s == 1:
            nconv[0, b] = r["conv_tail"]; nssm[0, b] = r["ssm_fin"]
        sl = slice(c * NS, (c + 1) * NS)
        y_s[sl, 0] = r["y_s"]; ks[0, sl, 0] = r["k_s"].reshape(NS, 2, 128); vs[0, sl, 0] = r["v_s"].reshape(NS, 2, 128)
        iks[0, sl, 0] = r["ik_s"]; convs[0, sl] = r["conv_s"]; ssms[0, sl] = r["ssm_s"]
    return (y_prompt, y_s, nk, nv, nik, nconv, nssm, ks, vs, iks, convs, ssms)
```

```python
import numpy as np
from contextlib import ExitStack
import concourse.bass as bass
import concourse.mybir as mybir
from concourse.bass_utils import run_bass_kernel_spmd

F32 = mybir.dt.float32
BF16 = mybir.dt.bfloat16
I32 = mybir.dt.int32
U32 = mybir.dt.uint32
AF = mybir.ActivationFunctionType
ALU = mybir.AluOpType
AX = mybir.AxisListType

import os
G_NT = int(os.environ.get("MK_GNT", "16"))
GSTOP = int(os.environ.get("MK_GSTOP", "99"))
NOS = int(os.environ.get("MK_NOS", "0"))
NOG = int(os.environ.get("MK_NOG", "0"))
ENGS = ["pe", "act", "dve", "pool", "sp"]
EPOCH = 12000
N_DMA_SEMS = 40
N_SW_SEMS = 12

D = 1024
T = 4096
HALF = 2048
NT = 16
NS = 16
INC = 3664
DFF = 2816
C_K, C_V, C_B, C_A, C_AK, C_AV, C_IK = 0, 512, 1024, 1028, 1032, 1288, 1544
N_OTH = 1608
C_Q, C_Z, C_AQ, C_IQ, C_IW = 1608, 2120, 2632, 3144, 3656
PERM = np.concatenate([np.arange(512, 1024), np.arange(1024, 1536), np.arange(2048, 2056),
                       np.arange(2568, 2824), np.arange(2824, 3080), np.arange(3592, 3656),
                       np.arange(0, 512), np.arange(1536, 2048), np.arange(2056, 2568),
                       np.arange(3080, 3592), np.arange(3656, 3664)])
GS_W = 2056
TABW = 256


class Prog:
    def __init__(self, nc):
        self.nc = nc
        self.ops = {e: [] for e in ENGS}
        self.res = {}
        self.seed = []
        self.dma_cum = [0] * (N_DMA_SEMS + N_SW_SEMS)
        self.dma_rr = 0
        self.sw_rr = 0

    def _deps(self, r, w):
        deps = []
        for name in r:
            st = self.res.get(name)
            if st and st[0] is not None:
                deps.append(st[0])
        for name in w:
            st = self.res.get(name)
            if st:
                if st[0] is not None:
                    deps.append(st[0])
                deps.extend(st[1])
            else:
                deps.extend(self.seed)
        return deps

    @staticmethod
    def _tkey(t):
        return (t[0], t[1])

    def _commit(self, tok, r, w):
        k = self._tkey(tok)
        for name in r:
            st = self.res.setdefault(name, [None, []])
            if not os.environ.get("MK_NOPRUNE"):
                st[1] = [t for t in st[1] if self._tkey(t) != k]
            st[1].append(tok)
        for name in w:
            self.res[name] = [tok, []]

    def op(self, eng, fn, r=(), w=()):
        deps = self._deps(r, w)
        tok = ("e", eng, len(self.ops[eng]))
        self.ops[eng].append({"deps": deps, "fn": fn, "dma": None, "sig": False})
        self._commit(tok, r, w)
        return tok

    def dma_raw(self, q, fn, r=(), w=(), sw=False):
        deps = self._deps(r, w)
        if sw:
            s = N_DMA_SEMS + self.sw_rr
            self.sw_rr = (self.sw_rr + 1) % N_SW_SEMS
        else:
            s = self.dma_rr
            self.dma_rr = (self.dma_rr + 1) % N_DMA_SEMS
        if self.dma_cum[s] > 0:
            deps.append(("d", s, self.dma_cum[s]))
        self.dma_cum[s] += 16
        tok = ("d", s, self.dma_cum[s])
        self.ops[q].append({"deps": deps, "fn": fn, "dma": (s, self.dma_cum[s]), "sig": False})
        self._commit(tok, r, w)
        return tok

    def dma(self, q, out, in_, r=(), w=(), **kw):
        return self.dma_raw(q, lambda e, out=out, in_=in_, kw=kw: e.dma_start(out=out, in_=in_, **kw), r, w)

    def seed_after_staging(self, names=("wst0", "wst1")):
        toks = list(self.seed)
        for n in names:
            st = self.res.get(n)
            if st:
                if st[0] is not None:
                    toks.append(st[0])
                toks.extend(st[1])
        self.seed = toks

    def barrier(self):
        deps_all = []
        for st in self.res.values():
            if st[0] is not None:
                deps_all.append(st[0])
            deps_all.extend(st[1])
        best = {}
        for t in ([] if os.environ.get("MK_NOPRUNE") else deps_all):
            k = self._tkey(t)
            if k not in best or best[k][2] < t[2]:
                best[k] = t
        for e in ENGS:
            for i in range(len(self.ops[e]) - 1, -1, -1):
                if self.ops[e][i]["dma"] is None:
                    best[("e", e)] = ("e", e, i)
                    break
        if not os.environ.get("MK_NOPRUNE"):
            deps_all = list(best.values())
        toks = []
        for e in ENGS:
            toks.append(("e", e, len(self.ops[e])))
            self.ops[e].append({"deps": list(deps_all), "fn": None, "dma": None, "sig": False})
        for e in ENGS:
            self.ops[e].append({"deps": list(toks), "fn": None, "dma": None, "sig": False})
        self.res = {}
        self.seed = []

    def build(self, ctx):
        nc = self.nc
        fin = [("d", s, c) for s, c in enumerate(self.dma_cum) if c > 0]
        self.ops["sp"].append({"deps": fin, "fn": None, "dma": None, "sig": False})
        for e in ENGS:
            for o in self.ops[e]:
                for d in o["deps"]:
                    if d[0] == "e":
                        if d[1] == "pe" and e == "pe":
                            continue
                        self.ops[d[1]][d[2]]["sig"] = True
        signo = {}
        nsig = {}
        for e in ENGS:
            c = 0
            for i, o in enumerate(self.ops[e]):
                if o["sig"]:
                    c += 1
                    signo[(e, i)] = c
            nsig[e] = c
        esem = {e: [ctx.enter_context(nc.semaphore(f"s_{e}_{k}")) for k in range(max(1, (nsig[e] + EPOCH - 1) // EPOCH))]
                for e in ENGS}
        dsem = [ctx.enter_context(nc.semaphore(f"s_dma_{k}")) for k in range(N_DMA_SEMS + N_SW_SEMS)]
        block = ctx.enter_context(nc.Block())
        eobj = {"pe": nc.tensor, "act": nc.scalar, "dve": nc.vector, "pool": nc.gpsimd, "sp": nc.sync}

        self.trace = {e: [] for e in ENGS}

        def body_for(e):
            def body(eng):
                known = {}
                tr = self.trace[e]
                for i, o in enumerate(self.ops[e]):
                    need = {}
                    for d in o["deps"]:
                        if d[0] == "e":
                            if d[1] == "pe" and e == "pe":
                                continue
                            key, val = ("e", d[1]), signo[(d[1], d[2])]
                        else:
                            key, val = ("d", d[1]), d[2]
                        if known.get(key, 0) >= val:
                            continue
                        if need.get(key, 0) < val:
                            need[key] = val
                    for key, val in need.items():
                        if key[0] == "e":
                            ep = (val - 1) // EPOCH
                            eng.wait_ge(esem[key[1]][ep], val - ep * EPOCH)
                            tr.append(("w", (key[1], ep), val - ep * EPOCH))
                        else:
                            eng.wait_ge(dsem[key[1]], val)
                            tr.append(("w", ("d", key[1]), val))
                        known[key] = val
                    if o["fn"] is None:
                        if o["sig"]:
                            sn = signo[(e, i)]
                            ep = (sn - 1) // EPOCH
                            eng.nop().then_inc(esem[e][ep], 1)
                            tr.append(("i", (e, ep), 1))
                        continue
                    ins = o["fn"](eng)
                    if o["dma"] is not None:
                        ins.then_inc(dsem[o["dma"][0]], 16)
                        tr.append(("i", ("d", o["dma"][0]), 16))
                    elif o["sig"]:
                        sn = signo[(e, i)]
                        ep = (sn - 1) // EPOCH
                        ins.then_inc(esem[e][ep], 1)
                        tr.append(("i", (e, ep), 1))
            return body

        block.tensor(body_for("pe"))
        block.scalar(body_for("act"))
        block.vector(body_for("dve"))
        block.gpsimd(body_for("pool"))
        block.sync(body_for("sp"))


class Arena:
    def __init__(self, t, n):
        self.t, self.n, self.off, self.uid = t, n, 0, 0

    def reset(self, to=0):
        self.off = to

    def f32(self, cols):
        a = self.t[:, self.off:self.off + cols]
        self.off += cols
        assert self.off <= self.n, ("arena overflow", self.off, self.n)
        return a

    def bf16(self, cols):
        c32 = (cols + 1) // 2
        a = self.t[:, self.off:self.off + c32].bitcast(BF16)
        self.off += c32
        assert self.off <= self.n, ("arena overflow", self.off, self.n)
        return a


def build_program(stage=99):
    nc = bass.Bass("TRN2", target_bir_lowering=False)
    dt_in = lambda name, shape, dt=F32: nc.dram_tensor(name, list(shape), dt, kind="ExternalInput").ap()
    dt_out = lambda name, shape, dt=F32: nc.dram_tensor(name, list(shape), dt, kind="ExternalOutput").ap()
    dt_scr = lambda name, shape, dt=F32: nc.dram_tensor(name, list(shape), dt, kind="Internal").ap()

    I = {}
    I["x_own"] = dt_in("x_own", [HALF, D]); I["x_oth"] = dt_in("x_oth", [HALF, D])
    I["cin"] = dt_in("cin", [17, D]); I["xs"] = dt_in("xs", [NS, D])
    I["w_ada"] = dt_in("w_ada", [D, 6 * D]); I["b_ada"] = dt_in("b_ada", [1, 6 * D])
    I["g1"] = dt_in("g1", [1, D]); I["w_in"] = dt_in("w_in", [D, INC])
    I["w_conv"] = dt_in("w_conv", [4, 1536]); I["a_log"] = dt_in("a_log", [1, 4]); I["dt_bias"] = dt_in("dt_bias", [1, 4])
    I["g_gdn"] = dt_in("g_gdn", [1, 128]); I["w_out"] = dt_in("w_out", [D, D]); I["g2"] = dt_in("g2", [1, D])
    I["w_ffn_in"] = dt_in("w_ffn_in", [D, 2 * DFF]); I["w_ffn_out"] = dt_in("w_ffn_out", [DFF, D]); I["g_final"] = dt_in("g_final", [1, D])
    I["tab_own"] = dt_in("tab_own", [HALF, TABW]); I["tab_oth"] = dt_in("tab_oth", [HALF, TABW]); I["tab_s"] = dt_in("tab_s", [NS, TABW])
    I["flags"] = dt_in("flags", [128, 4]); I["ident"] = dt_in("ident", [128, 128])
    I["state_conv"] = dt_in("state_conv", [NS, 3, 1536])
    I["gconst"] = dt_in("gconst", [128, 1024]); I["wc_p"] = dt_in("wc_p", [4, 1536])
    I["state_ssm"] = dt_in("state_ssm", [NS, 4, 128, 128]); I["eye16"] = dt_in("eye16", [128, 256])
    I["caus"] = dt_in("caus", [128, 128])
    I["pt"] = dt_in("pt", [1, NS * 16], I32); I["tsel"] = dt_in("tsel", [128, 257])
    NPHYS = 2560
    if not os.environ.get('MK_NOTS'):
        I["cache_ik"] = dt_in("cache_ik", [NPHYS * 128, 64]); I["cache_k"] = dt_in("cache_k", [NPHYS * 128, 256]); I["cache_v"] = dt_in("cache_v", [NPHYS * 128, 256])

    O = {}
    O["y_own"] = dt_out("y_own", [HALF, D]); O["k_own"] = dt_out("k_own", [HALF, 256]); O["v_own"] = dt_out("v_own", [HALF, 256])
    O["ik_own"] = dt_out("ik_own", [HALF, 64]); O["conv_tail"] = dt_out("conv_tail", [3, 1536]); O["ssm_fin"] = dt_out("ssm_fin", [4, 128, 128])
    O["y_s"] = dt_out("y_s", [NS, D]); O["k_s"] = dt_out("k_s", [NS, 256]); O["v_s"] = dt_out("v_s", [NS, 256]); O["ik_s"] = dt_out("ik_s", [NS, 64])
    O["conv_s"] = dt_out("conv_s", [NS, 3, 1536]); O["ssm_s"] = dt_out("ssm_s", [NS, 4, 128, 128])

    S = {}
    S["mod"] = dt_scr("mod_scr", [17, 6 * D])
    S["gs"] = dt_scr("gs_scr", [2, 3 + HALF, GS_W])
    S["qT"] = dt_scr("qT_scr", [NT, 128, 4, 128], BF16)
    S["iqT"] = dt_scr("iqT_scr", [NT, 128, 4, 128], BF16)
    S["cT"] = dt_scr("cT_scr", [NT + 1, 128, 8, 128], BF16)
    S["x1"] = dt_scr("x1_scr", [NT + 1, 128, D])
    S["uT"] = dt_scr("uT_scr", [5, 128, 22, 512], BF16)
    S["ps"] = dt_scr("ps_scr", [NS, INC])

    ctx = ExitStack()
    with ctx:
        ctx.enter_context(nc.allow_low_precision(reason="bf16 matmul operands, fp32 accumulate"))
        P = Prog(nc)
        ARN = 52800
        arena_t = ctx.enter_context(nc.sbuf_tensor("arena", [128, ARN], F32))
        A = Arena(arena_t, ARN)
        pb = [ctx.enter_context(nc.psum_tensor(f"pb{i}", [128, 512], F32))[:, :] for i in range(8)]
        PB = [f"pb{i}" for i in range(8)]
        cnt = [0]

        def alt():
            cnt[0] += 1
            return "act" if cnt[0] % 2 else "dve"

        def evac(eng, out, in_, r, w):
            if eng == "act":
                P.op("act", lambda e: e.activation(out=out, in_=in_, func=AF.Copy), r=r, w=w)
            else:
                P.op(eng, lambda e: e.tensor_copy(out=out, in_=in_), r=r, w=w)

        ident = A.f32(128)
        flags = A.f32(4)
        P.dma("sp", ident, I["ident"], w=["ident"])
        P.dma("sp", flags, I["flags"], w=["flags"])
        mod_sb = A.f32(6 * D)
        persist0 = A.off

        cs = A.f32(D)
        csT = A.f32(8 * 17)
        csT3 = csT.rearrange("p (k m) -> p k m", k=8)
        ones = A.f32(128)
        bstage = A.f32(512)
        P.op("dve", lambda e: e.memset(ones, 1.0), w=["ones"])
        P.dma("sp", cs[0:17, :], I["cin"], w=["cs"])
        P.op("act", lambda e: e.activation(out=cs[0:17, :], in_=cs[0:17, :], func=AF.Silu), r=["cs"], w=["cs"])
        for k in range(8):
            P.op("pe", lambda e, k=k: e.transpose(out=pb[0][:, k * 17:(k + 1) * 17], in_=cs[0:17, k * 128:(k + 1) * 128], identity=ident[0:17, 0:17]),
                 r=["cs", "ident"], w=[PB[0]])
        evac("dve", csT, pb[0][:, 0:8 * 17], [PB[0]], ["csT"])
        wst = [A.f32(8 * 512), A.f32(8 * 512)]
        for cb in range(12):
            ws = wst[cb % 2]
            ws3 = ws.rearrange("p (k n) -> p k n", k=8)
            P.dma("sp", ws3, I["w_ada"][:, cb * 512:(cb + 1) * 512].rearrange("(k p) n -> p k n", p=128), w=[f"wst{cb % 2}"])
            P.dma("sp", bstage[0:1, :], I["b_ada"][:, cb * 512:(cb + 1) * 512], w=["bstage"])
            bank = 1 + cb % 2
            for k in range(8):
                P.op("pe", lambda e, k=k, ws3=ws3, bank=bank: e.matmul(out=pb[bank][0:17, :], lhsT=csT3[:, k, :], rhs=ws3[:, k, :], start=(k == 0), stop=False),
                     r=["csT", f"wst{cb % 2}"], w=[PB[bank]])
            P.op("pe", lambda e, bank=bank: e.matmul(out=pb[bank][0:17, :], lhsT=ones[0:1, 0:17], rhs=bstage[0:1, :], start=False, stop=True),
                 r=["ones", "bstage"], w=[PB[bank]])
            evac("act", mod_sb[0:17, cb * 512:(cb + 1) * 512], pb[bank][0:17, :], [PB[bank]], ["mod_sb"])
        P.dma("sp", S["mod"], mod_sb[0:17, :], r=["mod_sb"], w=["mod_scr"])
        A.reset(persist0)
        def alloc_att():
            KT = A.bf16(2 * T); ikT = A.bf16(T); VA = A.bf16(32 * 2 * 130); iwabs = A.f32(NT * 8); iwsgn = A.f32(NT * 8)
            ksq_ = A.f32(64); qsq_ = A.f32(64)
            return KT, ikT, VA, iwabs, iwsgn, ksq_, qsq_
        OLDALLOC = bool(os.environ.get("MK_OLDALLOC"))
        if not OLDALLOC:
            KT, ikT, VA, iwabs, iwsgn, ksq, qsq = alloc_att()
        persist1 = A.off
        a1_bc = A.f32(D); sh1_bc = A.f32(D)
        g1_bc = A.f32(D); a1_s = A.f32(D)
        P.dma("sp", g1_bc, I["g1"].to_broadcast([128, D]), w=["g1_bc"])
        P.dma("sp", a1_bc, S["mod"][16:17, D:2 * D].to_broadcast([128, D]), r=["mod_scr"], w=["a1_bc"])
        P.dma("sp", sh1_bc, S["mod"][16:17, 0:D].to_broadcast([128, D]), r=["mod_scr"], w=["sh1_bc"])
        P.op("dve", lambda e: e.scalar_tensor_tensor(out=a1_bc, in0=a1_bc, scalar=1.0, in1=g1_bc, op0=ALU.add, op1=ALU.mult),
             r=["a1_bc", "g1_bc"], w=["a1_bc"])
        P.op("dve", lambda e: e.scalar_tensor_tensor(out=a1_s[0:NS, :], in0=mod_sb[0:NS, D:2 * D], scalar=1.0, in1=g1_bc[0:NS, :], op0=ALU.add, op1=ALU.mult),
             r=["mod_sb", "g1_bc"], w=["a1_s"])
        persistA = A.off

        w_in_b = A.bf16(8 * INC)
        w_in3 = w_in_b.rearrange("p (k n) -> p k n", k=8)
        if OLDALLOC:
            KT, ikT, VA, iwabs, iwsgn, ksq, qsq = alloc_att()
        KT3 = KT.rearrange("p (g s) -> p g s", g=2)
        VA4 = VA.rearrange("p (t g d) -> p t g d", t=32, g=2)
        persistB = A.off
        wst = [A.f32(8 * 512), A.f32(8 * 512)]
        nblk = (INC + 511) // 512
        for cb in range(nblk):
            c0, c1 = cb * 512, min(INC, cb * 512 + 512)
            ws3 = wst[cb % 2].rearrange("p (k n) -> p k n", k=8)
            P.dma("sp", ws3[:, :, 0:c1 - c0], I["w_in"][:, c0:c1].rearrange("(k p) n -> p k n", p=128), w=[f"wst{cb % 2}"])
            eng = "pool" if cb % 2 else "dve"
            P.op(eng, lambda e, ws3=ws3, c0=c0, c1=c1: e.tensor_copy(out=w_in3[:, :, c0:c1], in_=ws3[:, :, 0:c1 - c0]),
                 r=[f"wst{cb % 2}"], w=["w_in_b"])
        P.seed_after_staging()
        A.reset(persistB)
        xt = [A.f32(D), A.f32(D)]
        hh = A.f32(D)
        sq = A.f32(D)
        hT = [A.bf16(8 * 128), A.bf16(8 * 128)]
        Psb_l = [A.f32(INC), A.f32(INC)]
        tab = [A.f32(TABW), A.f32(TABW)]
        small = A.f32(16)
        rt = [A.f32(128) for _ in range(4)]
        ikd = A.f32(128)
        tstage = [A.bf16(4 * 128), A.bf16(4 * 128)]
        zrow = A.f32(GS_W)
        P.op("pool", lambda e: e.memset(zrow[0:3, :], 0.0), w=["zrow"])
        P.dma("sp", S["gs"][0, 0:3, :], zrow[0:3, :], r=["zrow"], w=["gs_pre0"])

        def rope(Psb, PN, rows, base, H, Dh, half, tb, coff, soff, tname):
            xv = Psb[0:rows, base:base + H * Dh].rearrange("p (h d) -> p h d", d=Dh)
            x1, x2 = xv[:, :, 0:half], xv[:, :, half:2 * half]
            cosv = tb[0:rows, coff:coff + H * half].rearrange("p (h i) -> p h i", i=half)
            sinv = tb[0:rows, soff:soff + H * half].rearrange("p (h i) -> p h i", i=half)
            t = [r_[0:rows, 0:H * half].rearrange("p (h i) -> p h i", i=half) for r_ in rt]
            P.op("dve", lambda e: e.tensor_tensor(out=t[0], in0=x1, in1=cosv, op=ALU.mult), r=[PN, tname], w=["rt0"])
            P.op("pool", lambda e: e.tensor_tensor(out=t[1], in0=x2, in1=sinv, op=ALU.mult), r=[PN, tname], w=["rt1"])
            P.op("dve", lambda e: e.tensor_tensor(out=t[2], in0=x2, in1=cosv, op=ALU.mult), r=[PN, tname], w=["rt2"])
            P.op("pool", lambda e: e.tensor_tensor(out=t[3], in0=x1, in1=sinv, op=ALU.mult), r=[PN, tname], w=["rt3"])
            P.op("dve", lambda e: e.tensor_tensor(out=x1, in0=t[0], in1=t[1], op=ALU.subtract), r=["rt0", "rt1", PN], w=[PN])
            P.op("dve", lambda e: e.tensor_tensor(out=x2, in0=t[2], in1=t[3], op=ALU.add), r=["rt2", "rt3", PN], w=[PN])

        itc = [0]

        def proj_tile(mode, ti):
            it = itc[0]; itc[0] += 1
            rows = NS if mode == 2 else 128
            Psb = Psb_l[it % 2]; PN = f"Psb{it % 2}"
            xb = xt[it % 2]; xn = f"xt{it % 2}"
            tb = tab[it % 2]; tn = f"tab{it % 2}"
            hTb = hT[it % 2]; hTn = f"hT{it % 2}"
            hT3 = hTb.rearrange("p (k t) -> p k t", k=8)
            if mode == 2:
                xsrc, tsrc = I["xs"], I["tab_s"]
                a_t, a_n, sh_t, sh_n = a1_s, "a1_s", mod_sb[:, 0:D], "mod_sb"
                ncols = INC
            else:
                xsrc = (I["x_oth"] if mode == 0 else I["x_own"])[ti * 128:(ti + 1) * 128, :]
                tsrc = (I["tab_oth"] if mode == 0 else I["tab_own"])[ti * 128:(ti + 1) * 128, :]
                a_t, a_n, sh_t, sh_n = a1_bc, "a1_bc", sh1_bc, "sh1_bc"
                ncols = INC if mode == 1 else (C_Q + 512 if ti == NT - 1 else N_OTH)
            P.dma("sp", xb[0:rows, :], xsrc, w=[xn])
            P.dma("sp", tb[0:rows, :], tsrc, w=[tn])
            ss, rs = small[0:rows, 0:1], small[0:rows, 1:2]
            P.op("act", lambda e: e.activation(out=sq[0:rows, :], in_=xb[0:rows, :], func=AF.Square, accum_out=ss), r=[xn], w=["sq", "ss"])
            P.op("dve", lambda e: e.tensor_scalar(out=rs, in0=ss, scalar1=1.0 / D, scalar2=1e-6, op0=ALU.mult, op1=ALU.add), r=["ss"], w=["rs"])
            P.op("act", lambda e: e.activation(out=rs, in_=rs, func=AF.Sqrt), r=["rs"], w=["rs"])
            P.op("dve", lambda e: e.reciprocal(out=rs, in_=rs), r=["rs"], w=["rs"])
            P.op("dve", lambda e: e.scalar_tensor_tensor(out=hh[0:rows, :], in0=xb[0:rows, :], scalar=rs, in1=a_t[0:rows, :], op0=ALU.mult, op1=ALU.mult),
                 r=[xn, "rs", a_n], w=["hh"])
            P.op("pool", lambda e: e.tensor_tensor(out=hh[0:rows, :], in0=hh[0:rows, :], in1=sh_t[0:rows, :], op=ALU.add), r=["hh", sh_n], w=["hh"])
            for k in range(8):
                bank = k // 4
                P.op("pe", lambda e, k=k, bank=bank: e.transpose(out=pb[bank][:, (k % 4) * 128:(k % 4) * 128 + rows], in_=hh[0:rows, k * 128:(k + 1) * 128], identity=ident[0:rows, 0:rows]),
                     r=["hh", "ident"], w=[PB[bank]])
            if rows == 128:
                evac("act", hTb[:, 0:512], pb[0], [PB[0]], [hTn])
                evac("dve", hTb[:, 512:1024], pb[1], [PB[1]], [hTn])
            else:
                for half_ in range(2):
                    evac("act" if half_ == 0 else "dve", hT3[:, half_ * 4:(half_ + 1) * 4, 0:rows], pb[half_].rearrange("p (k t) -> p k t", k=4)[:, :, 0:rows], [PB[half_]], [hTn])
            nb = (ncols + 511) // 512
            for cb in range(nb):
                c0, c1 = cb * 512, min(ncols, cb * 512 + 512)
                bank = 2 + cb % 4
                for k in range(8):
                    P.op("pe", lambda e, k=k, bank=bank, c0=c0, c1=c1: e.matmul(out=pb[bank][0:rows, 0:c1 - c0], lhsT=hT3[:, k, 0:rows], rhs=w_in3[:, k, c0:c1], start=(k == 0), stop=(k == 7)),
                         r=[hTn, "w_in_b"], w=[PB[bank]])
                evac(alt(), Psb[0:rows, c0:c1], pb[bank][0:rows, 0:c1 - c0], [PB[bank]], [PN])
            def stage2():
                if mode == 2:
                    cvs = sq[0:rows, :]
                    for j in range(2):
                        for hh_ in range(2):
                            P.dma("sp", sq[0:rows, 0:768], I["state_conv"][:, 1 + j, hh_ * 768:(hh_ + 1) * 768], w=["sq"])
                            P.dma("sp", O["conv_s"][:, j, hh_ * 768:(hh_ + 1) * 768], sq[0:rows, 0:768], r=["sq"])
                    P.dma("sp", O["conv_s"][:, 2, 0:512], Psb[0:rows, C_Q:C_Q + 512], r=[PN])
                    P.dma("sp", O["conv_s"][:, 2, 512:1536], Psb[0:rows, 0:1024], r=[PN])
                else:
                    row0 = 3 + ti * 128
                    P.dma("sp", S["gs"][mode, row0:row0 + 128, 0:1032], Psb[:, 0:1032], r=[PN], w=[f"gs{mode}_{ti}"])
                    if mode == 1:
                        P.dma("sp", S["gs"][mode, row0:row0 + 128, 1032:2056], Psb[:, C_Q:C_Q + 1024], r=[PN], w=[f"gs{mode}_{ti}"])
                    elif ti == NT - 1:
                        P.dma("sp", S["gs"][mode, row0:row0 + 128, 1032:1544], Psb[:, C_Q:C_Q + 512], r=[PN], w=[f"gs{mode}_{ti}"])
                rope(Psb, PN, rows, C_AK, 2, 128, 16, tb, 0, 64, tn)
                rope(Psb, PN, rows, C_IK, 1, 64, 8, tb, 128, 192, tn)
                if mode >= 1:
                    rope(Psb, PN, rows, C_AQ, 4, 128, 16, tb, 0, 64, tn)
                    rope(Psb, PN, rows, C_IQ, 8, 64, 8, tb, 128, 192, tn)
                if mode == 2:
                    P.dma("sp", O["k_s"], Psb[0:rows, C_AK:C_AK + 256], r=[PN])
                    P.dma("sp", O["v_s"], Psb[0:rows, C_AV:C_AV + 256], r=[PN])
                    P.dma("sp", O["ik_s"], Psb[0:rows, C_IK:C_IK + 64], r=[PN])
                    P.dma("sp", S["ps"], Psb[0:rows, :], r=[PN], w=["ps_scr"])
                    return
                if mode == 1:
                    P.dma("sp", O["k_own"][ti * 128:(ti + 1) * 128, :], Psb[:, C_AK:C_AK + 256], r=[PN])
                    P.dma("sp", O["v_own"][ti * 128:(ti + 1) * 128, :], Psb[:, C_AV:C_AV + 256], r=[PN])
                    P.dma("sp", O["ik_own"][ti * 128:(ti + 1) * 128, :], Psb[:, C_IK:C_IK + 64], r=[PN])
                slot = (0 if mode == 0 else 16) + ti
                for g in range(2):
                    P.op("act", lambda e, g=g: e.activation(out=sq[:, 0:128], in_=Psb[:, C_AK + g * 128:C_AK + (g + 1) * 128], func=AF.Square, accum_out=ksq[:, slot * 2 + g:slot * 2 + g + 1]),
                         r=[PN, "a1_bc"], w=["sq", "ksq"])
                if mode == 1:
                    for h in range(4):
                        P.op("act", lambda e, h=h: e.activation(out=sq[:, 0:128], in_=Psb[:, C_AQ + h * 128:C_AQ + (h + 1) * 128], func=AF.Square, accum_out=qsq[:, ti * 4 + h:ti * 4 + h + 1]),
                             r=[PN, "a1_bc"], w=["sq", "qsq"])
                P.op("dve", lambda e: e.tensor_copy(out=ikd[:, 0:64], in_=Psb[:, C_IK:C_IK + 64]), r=[PN], w=["ikd"])
                P.op("pool", lambda e: e.tensor_copy(out=ikd[:, 64:128], in_=Psb[:, C_IK:C_IK + 64]), r=[PN], w=["ikd"])
                for g in range(2):
                    P.op("pe", lambda e, g=g: e.transpose(out=pb[6][:, g * 128:(g + 1) * 128], in_=Psb[:, C_AK + g * 128:C_AK + (g + 1) * 128], identity=ident),
                         r=[PN, "ident"], w=[PB[6]])
                P.op("pe", lambda e: e.transpose(out=pb[6][:, 256:384], in_=ikd, identity=ident), r=["ikd", "ident"], w=[PB[6]])
                for g in range(2):
                    evac(alt(), KT3[:, g, slot * 128:(slot + 1) * 128], pb[6][:, g * 128:(g + 1) * 128], [PB[6]], ["KT"])
                evac(alt(), ikT[:, slot * 128:(slot + 1) * 128], pb[6][:, 256:384], [PB[6]], ["ikT"])
                P.op("pool", lambda e: e.memset(VA4[:, slot, :, 128:130], 1.0), r=["a1_bc"], w=["VA"])
                P.op("act", lambda e: e.activation(out=VA4[:, slot, :, 0:128], in_=Psb[:, C_AV:C_AV + 256].rearrange("p (g d) -> p g d", g=2), func=AF.Copy),
                     r=[PN], w=["VA"])
                if mode == 1:
                    ts_ = tstage[0]; tsn = "tstage0"
                    for h in range(4):
                        P.op("pe", lambda e, h=h: e.transpose(out=pb[7][:, h * 128:(h + 1) * 128], in_=Psb[:, C_AQ + h * 128:C_AQ + (h + 1) * 128], identity=ident),
                             r=[PN, "ident"], w=[PB[7]])
                    evac(alt(), ts_, pb[7], [PB[7]], [tsn])
                    P.dma("sp", S["qT"][ti], ts_.rearrange("p (h t) -> p h t", h=4), r=[tsn], w=[f"qT{ti}"])
                    ts2 = tstage[1]; tsn2 = "tstage1"
                    for h in range(4):
                        P.op("pe", lambda e, h=h: e.transpose(out=pb[7][:, h * 128:(h + 1) * 128], in_=Psb[:, C_IQ + h * 128:C_IQ + (h + 1) * 128], identity=ident),
                             r=[PN, "ident"], w=[PB[7]])
                    evac(alt(), ts2, pb[7], [PB[7]], [tsn2])
                    P.dma("sp", S["iqT"][ti], ts2.rearrange("p (h t) -> p h t", h=4), r=[tsn2], w=[f"iqT{ti}"])
                    P.op("act", lambda e: e.activation(out=iwabs[:, ti * 8:(ti + 1) * 8], in_=Psb[:, C_IW:C_IW + 8], func=AF.Abs, scale=8 ** -0.5),
                         r=[PN], w=["iwabs"])
                    P.op("act", lambda e: e.activation(out=iwsgn[:, ti * 8:(ti + 1) * 8], in_=Psb[:, C_IW:C_IW + 8], func=AF.Sign), r=[PN], w=["iwsgn"])
                    if ti == NT - 1:
                        P.dma("sp", O["conv_tail"][:, 0:512], S["gs"][1, 3 + HALF - 3:3 + HALF, 1032:1544], r=[f"gs1_{ti}"])
                        P.dma("sp", O["conv_tail"][:, 512:1536], S["gs"][1, 3 + HALF - 3:3 + HALF, 0:1024], r=[f"gs1_{ti}"])

            return stage2

        NA0 = int(os.environ.get("MK_NA0", NT)); NA1 = int(os.environ.get("MK_NA1", NT))
        tiles_a = [(0, ti) for ti in range(NT - NA0, NT)] + [(1, ti) for ti in range(NA1)] + ([] if NOS else [(2, 0)])
        pend = None
        for (m_, ti) in tiles_a:
            nxt = proj_tile(m_, ti)
            if pend is not None:
                pend()
            pend = nxt
        if pend is not None:
            pend()

        if not NOG:
            P.barrier()
            A.reset(persist1)
            cst = A.f32(1024)
            TRIU, ONESM, MASKL, MASKU = [cst[:, i * 128:(i + 1) * 128] for i in range(4)]
            ident4 = cst[:, 512:1024]
            P.dma("sp", cst, I["gconst"], w=["cst"])
            wc = [A.f32(1536) for _ in range(4)]
            for i in range(4):
                P.dma("sp", wc[i], I["wc_p"][i:i + 1, :].to_broadcast([128, 1536]), w=[f"wc{i}"])
            dtb = A.f32(4); negA = A.f32(4); ggd = A.f32(512)
            P.dma("sp", dtb, I["dt_bias"].to_broadcast([128, 4]), w=["dtb"])
            P.dma("sp", negA, I["a_log"].to_broadcast([128, 4]), w=["negA"])
            for h in range(4):
                P.dma("sp", ggd[:, h * 128:(h + 1) * 128], I["g_gdn"].to_broadcast([128, 128]), w=["ggd"])
            P.op("act", lambda e: e.activation(out=negA, in_=negA, func=AF.Exp), r=["negA"], w=["negA"])
            P.op("dve", lambda e: e.tensor_scalar(out=negA, in0=negA, scalar1=-1.0, scalar2=None, op0=ALU.mult), r=["negA"], w=["negA"])
            gs_base = A.off
            Sst = A.f32(512)
            P.op("dve", lambda e: e.memset(Sst, 0.0), w=["S"])
            X = [A.f32(GS_W) for _ in range(4)]
            cv = A.f32(1536); tmpa = A.f32(1536); tmpb = A.f32(1536)
            sm = A.f32(64)
            kn = A.f32(512); kt = A.f32(512); vb = A.f32(512); qn = A.f32(512); qt = A.f32(512)
            knT = A.f32(512); qnT = A.f32(512); qtT = A.f32(512)
            Dg = A.f32(512); dec = A.f32(512); decT = A.f32(512)
            Pbuf = [A.f32(512), A.f32(512)]; PTbuf = [A.f32(512), A.f32(512)]
            Wm = A.f32(512); ATm = A.f32(512); Rm = A.f32(512); vnew = A.f32(512); o_sb = A.f32(512); og = A.f32(512); szb = A.f32(512)
            cTst = A.bf16(512)
            pre = A.f32(GS_W)
            H4 = lambda ap, h: ap[:, h * 128:(h + 1) * 128]
            sc = lambda lo, h: sm[:, lo + h:lo + h + 1]

            def ts_mul(eng, out, in0, scal, r, w):
                if eng == "pool":
                    P.op("pool", lambda e: e.tensor_scalar(out=out, in0=in0, scalar1=scal, scalar2=1.0, op0=ALU.mult, op1=ALU.mult), r=r, w=w)
                else:
                    P.op("dve", lambda e: e.tensor_scalar(out=out, in0=in0, scalar1=scal, scalar2=None, op0=ALU.mult), r=r, w=w)

            for hf in range(2):
                own = (hf == 1)
                if own:
                    P.op("dve", lambda e: e.tensor_scalar(out=Sst, in0=Sst, scalar1=flags[:, 0:1], scalar2=None, op0=ALU.mult), r=["S", "flags"], w=["S"])
                    P.op("dve", lambda e: e.memset(pre[0:3, :], 0.0), w=["pre"])
                    P.dma("sp", pre[0:3, 0:1544], S["gs"][0, HALF:HALF + 3, 0:1544], w=["pre"])
                    P.op("dve", lambda e: e.tensor_scalar(out=pre[0:3, :], in0=pre[0:3, :], scalar1=flags[0:3, 0:1], scalar2=None, op0=ALU.mult), r=["pre", "flags"], w=["pre"])
                    P.dma("sp", S["gs"][1, 0:3, :], pre[0:3, :], r=["pre"], w=["gs_pre1"])
                segs = [(0, 1024, 0), (1024, 1536, 1032)] if own else [(0, 1024, 0)]
                nsc = 8 if own else 4
                for ti in range(G_NT):
                    W_ = GS_W if own else 1032
                    for i in range(4):
                        P.dma("sp", X[i][:, 0:W_], S["gs"][hf, ti * 128 + i:ti * 128 + i + 128, 0:W_], r=(["gs_pre1"] if (own and ti == 0) else []), w=[f"X{i}"])
                    for (d0, d1, s0) in segs:
                        n = d1 - d0
                        P.op("dve", lambda e, d0=d0, d1=d1, s0=s0, n=n: e.tensor_tensor(out=cv[:, d0:d1], in0=X[0][:, s0:s0 + n], in1=wc[0][:, d0:d1], op=ALU.mult), r=["X0", "wc0"], w=["cv"])
                        for i in range(1, 4):
                            tb_, tbn = (tmpa, "tmpa") if i % 2 else (tmpb, "tmpb")
                            P.op("pool", lambda e, i=i, d0=d0, d1=d1, s0=s0, n=n, tb_=tb_: e.tensor_tensor(out=tb_[:, d0:d1], in0=X[i][:, s0:s0 + n], in1=wc[i][:, d0:d1], op=ALU.mult), r=[f"X{i}", f"wc{i}"], w=[tbn])
                            P.op("dve", lambda e, d0=d0, d1=d1, tb_=tb_: e.tensor_tensor(out=cv[:, d0:d1], in0=cv[:, d0:d1], in1=tb_[:, d0:d1], op=ALU.add), r=["cv", tbn], w=["cv"])
                        P.op("act", lambda e, d0=d0, d1=d1: e.activation(out=cv[:, d0:d1], in_=cv[:, d0:d1], func=AF.Silu), r=["cv"], w=["cv"])
                    if GSTOP <= 1:
                        continue
                    P.op("pool", lambda e: e.tensor_tensor(out=tmpa[:, 0:512], in0=cv[:, 0:512], in1=cv[:, 0:512], op=ALU.mult), r=["cv"], w=["tmpa"])
                    P.op("dve", lambda e: e.tensor_reduce(out=sm[:, 0:4], in_=tmpa[:, 0:512].rearrange("p (h d) -> p h d", h=4), axis=AX.X, op=ALU.add), r=["tmpa"], w=["sm_ss"])
                    if own:
                        P.op("pool", lambda e: e.tensor_tensor(out=tmpb[:, 0:512], in0=cv[:, 1024:1536], in1=cv[:, 1024:1536], op=ALU.mult), r=["cv"], w=["tmpb"])
                        P.op("dve", lambda e: e.tensor_reduce(out=sm[:, 4:8], in_=tmpb[:, 0:512].rearrange("p (h d) -> p h d", h=4), axis=AX.X, op=ALU.add), r=["tmpb"], w=["sm_ss"])
                    P.op("dve", lambda e, nsc=nsc: e.tensor_scalar(out=sm[:, 0:nsc], in0=sm[:, 0:nsc], scalar1=1e-6, scalar2=None, op0=ALU.add), r=["sm_ss"], w=["sm_ss"])
                    P.op("act", lambda e, nsc=nsc: e.activation(out=sm[:, 0:nsc], in_=sm[:, 0:nsc], func=AF.Sqrt), r=["sm_ss"], w=["sm_ss"])
                    P.op("dve", lambda e, nsc=nsc: e.reciprocal(out=sm[:, 0:nsc], in_=sm[:, 0:nsc]), r=["sm_ss"], w=["sm_ss"])
                    if own:
                        P.op("dve", lambda e: e.tensor_scalar(out=sm[:, 4:8], in0=sm[:, 4:8], scalar1=128 ** -0.5, scalar2=None, op0=ALU.mult), r=["sm_ss"], w=["sm_ss"])
                    P.op("act", lambda e: e.activation(out=sm[:, 8:12], in_=X[3][:, 1024:1028], func=AF.Sigmoid), r=["X3"], w=["sm_b"])
                    P.op("dve", lambda e: e.tensor_tensor(out=sm[:, 12:16], in0=X[3][:, 1028:1032], in1=dtb, op=ALU.add), r=["X3", "dtb"], w=["sm_g"])
                    P.op("act", lambda e: e.activation(out=sm[:, 12:16], in_=sm[:, 12:16], func=AF.Exp), r=["sm_g"], w=["sm_g"])
                    P.op("act", lambda e: e.activation(out=sm[:, 12:16], in_=sm[:, 12:16], func=AF.Ln, bias=1.0), r=["sm_g"], w=["sm_g"])
                    P.op("dve", lambda e: e.tensor_tensor(out=sm[:, 12:16], in0=sm[:, 12:16], in1=negA, op=ALU.mult), r=["sm_g", "negA"], w=["sm_g"])
                    P.op("pe", lambda e: e.matmul(out=pb[3][:, 0:4], lhsT=TRIU, rhs=sm[:, 12:16], start=True, stop=True), r=["cst", "sm_g"], w=[PB[3]])
                    P.op("pe", lambda e: e.matmul(out=pb[3][:, 4:8], lhsT=ONESM, rhs=sm[:, 12:16], start=True, stop=True), r=["cst", "sm_g"], w=[PB[3]])
                    P.op("dve", lambda e: e.tensor_copy(out=sm[:, 16:24], in_=pb[3][:, 0:8]), r=[PB[3]], w=["sm_gc"])
                    P.op("dve", lambda e: e.tensor_copy(out=sm[:, 24:28], in_=sm[:, 16:20]), r=["sm_gc"], w=["sm_e"])
                    P.op("dve", lambda e: e.tensor_tensor(out=sm[:, 28:32], in0=sm[:, 20:24], in1=sm[:, 16:20], op=ALU.subtract), r=["sm_gc"], w=["sm_e"])
                    P.op("dve", lambda e: e.tensor_copy(out=sm[:, 32:36], in_=sm[:, 20:24]), r=["sm_gc"], w=["sm_e"])
                    P.op("act", lambda e: e.activation(out=sm[:, 24:36], in_=sm[:, 24:36], func=AF.Exp), r=["sm_e"], w=["sm_e"])
                    P.op("dve", lambda e: e.scalar_tensor_tensor(out=sm[:, 36:40], in0=sm[:, 8:12], scalar=-1.0, in1=sm[:, 24:28], op0=ALU.mult, op1=ALU.mult), r=["sm_b", "sm_e"], w=["sm_x"])
                    P.op("dve", lambda e: e.tensor_scalar(out=sm[:, 40:44], in0=sm[:, 16:20], scalar1=-1.0, scalar2=None, op0=ALU.mult), r=["sm_gc"], w=["sm_x"])
                    P.op("dve", lambda e: e.tensor_scalar(out=sm[:, 44:48], in0=sm[:, 8:12], scalar1=-1.0, scalar2=None, op0=ALU.mult), r=["sm_b"], w=["sm_x"])
                    if own:
                        P.op("dve", lambda e: e.tensor_tensor(out=sm[:, 48:52], in0=sm[:, 4:8], in1=sm[:, 24:28], op=ALU.mult), r=["sm_ss", "sm_e"], w=["sm_x"])
                    if GSTOP <= 2:
                        continue
                    for h in range(4):
                        ts_mul("dve", H4(kn, h), cv[:, h * 128:(h + 1) * 128], sc(0, h), ["cv", "sm_ss"], ["kn"])
                        ts_mul("pool", H4(kt, h), H4(kn, h), sc(28, h), ["kn", "sm_e"], ["kt"])
                        ts_mul("pool", H4(vb, h), cv[:, 512 + h * 128:512 + (h + 1) * 128], sc(8, h), ["cv", "sm_b"], ["vb"])
                        if own:
                            ts_mul("dve", H4(qn, h), cv[:, 1024 + h * 128:1024 + (h + 1) * 128], sc(4, h), ["cv", "sm_ss"], ["qn"])
                            ts_mul("pool", H4(qt, h), cv[:, 1024 + h * 128:1024 + (h + 1) * 128], sc(48, h), ["cv", "sm_x"], ["qt"])
                    for (src, sn, bank, dst, dn, eng) in ([(kn, "kn", 0, knT, "knT", "act")] + ([(qn, "qn", 1, qnT, "qnT", "dve"), (qt, "qt", 2, qtT, "qtT", "act")] if own else [])):
                        for h in range(4):
                            P.op("pe", lambda e, h=h, src=src, bank=bank: e.transpose(out=H4(pb[bank], h), in_=H4(src, h), identity=ident), r=[sn, "ident"], w=[PB[bank]])
                        evac(eng, dst, pb[bank], [PB[bank]], [dn])
                    if GSTOP <= 3:
                        continue
                    for h in range(4):
                        P.op("pe", lambda e, h=h: e.matmul(out=H4(pb[0], h), lhsT=H4(knT, h), rhs=H4(knT, h), start=True, stop=True), r=["knT"], w=[PB[0]])
                        P.op("pool", lambda e, h=h: e.tensor_scalar(out=H4(Dg, h), in0=ident, scalar1=sc(16, h), scalar2=1.0, op0=ALU.mult, op1=ALU.mult), r=["ident", "sm_gc"], w=["Dg"])
                    for h in range(4):
                        P.op("pe", lambda e, h=h: e.matmul(out=H4(pb[1], h), lhsT=ONESM, rhs=H4(Dg, h), start=True, stop=False), r=["cst", "Dg"], w=[PB[1]])
                        P.op("pe", lambda e, h=h: e.matmul(out=H4(pb[1], h), lhsT=ident, rhs=MASKL, start=False, stop=True), r=["cst", "ident"], w=[PB[1]])
                        P.op("act", lambda e, h=h: e.activation(out=H4(dec, h), in_=H4(pb[1], h), func=AF.Exp, scale=-1.0, bias=sc(16, h)), r=[PB[1], "sm_gc"], w=["dec"])
                        P.op("dve", lambda e, h=h: e.scalar_tensor_tensor(out=H4(Pbuf[0], h), in0=H4(pb[0], h), scalar=sc(44, h), in1=H4(dec, h), op0=ALU.mult, op1=ALU.mult),
                             r=[PB[0], "sm_x", "dec"], w=["P0"])
                    if own:
                        for h in range(4):
                            P.op("pe", lambda e, h=h: e.matmul(out=H4(pb[2], h), lhsT=ONESM, rhs=H4(Dg, h), start=True, stop=False), r=["cst", "Dg"], w=[PB[2]])
                            P.op("pe", lambda e, h=h: e.matmul(out=H4(pb[2], h), lhsT=ident, rhs=MASKU, start=False, stop=True), r=["cst", "ident"], w=[PB[2]])
                            P.op("act", lambda e, h=h: e.activation(out=H4(decT, h), in_=H4(pb[2], h), func=AF.Exp, scale=1.0, bias=sc(40, h)), r=[PB[2], "sm_x"], w=["decT"])
                            P.op("pe", lambda e, h=h: e.matmul(out=H4(pb[3], h), lhsT=H4(knT, h), rhs=H4(qnT, h), start=True, stop=True), r=["knT", "qnT"], w=[PB[3]])
                            P.op("dve", lambda e, h=h: e.tensor_tensor(out=H4(ATm, h), in0=H4(pb[3], h), in1=H4(decT, h), op=ALU.mult), r=[PB[3], "decT"], w=["ATm"])
                    if GSTOP <= 4:
                        continue
                    for h in range(4):
                        P.op("pe", lambda e, h=h: e.transpose(out=H4(pb[4], h), in_=H4(Pbuf[0], h), identity=ident), r=["P0", "ident"], w=[PB[4]])
                    evac("act", PTbuf[0], pb[4], [PB[4]], ["PT0"])
                    P.op("dve", lambda e: e.tensor_tensor(out=Wm, in0=PTbuf[0], in1=ident4, op=ALU.add), r=["PT0", "cst"], w=["Wm"])
                    for l in range(1, 7):
                        pc, ptc, pn_, ptn = Pbuf[(l - 1) % 2], PTbuf[(l - 1) % 2], Pbuf[l % 2], PTbuf[l % 2]
                        pcn, ptcn, pnn, ptnn = f"P{(l - 1) % 2}", f"PT{(l - 1) % 2}", f"P{l % 2}", f"PT{l % 2}"
                        for h in range(4):
                            P.op("pe", lambda e, h=h, pc=pc, ptc=ptc: e.matmul(out=H4(pb[4], h), lhsT=H4(ptc, h), rhs=H4(pc, h), start=True, stop=True), r=[pcn, ptcn], w=[PB[4]])
                        if l < 6:
                            for h in range(4):
                                P.op("pe", lambda e, h=h, pc=pc, ptc=ptc: e.matmul(out=H4(pb[5], h), lhsT=H4(pc, h), rhs=H4(ptc, h), start=True, stop=True), r=[pcn, ptcn], w=[PB[5]])
                        evac("act", pn_, pb[4], [PB[4]], [pnn])
                        if l < 6:
                            evac("dve", ptn, pb[5], [PB[5]], [ptnn])
                        for h in range(4):
                            P.op("pe", lambda e, h=h, pn_=pn_: e.matmul(out=H4(pb[6], h), lhsT=H4(pn_, h), rhs=H4(Wm, h), start=True, stop=True), r=[pnn, "Wm"], w=[PB[6]])
                        P.op("dve", lambda e: e.tensor_tensor(out=Wm, in0=Wm, in1=pb[6], op=ALU.add), r=["Wm", PB[6]], w=["Wm"])
                    if GSTOP <= 5:
                        continue
                    for h in range(4):
                        P.op("pe", lambda e, h=h: e.matmul(out=H4(pb[7], h), lhsT=H4(knT, h), rhs=H4(Sst, h), start=True, stop=True), r=["knT", "S"], w=[PB[7]])
                    for h in range(4):
                        P.op("dve", lambda e, h=h: e.scalar_tensor_tensor(out=H4(Rm, h), in0=H4(pb[7], h), scalar=sc(36, h), in1=H4(vb, h), op0=ALU.mult, op1=ALU.add),
                             r=[PB[7], "sm_x", "vb"], w=["Rm"])
                    for h in range(4):
                        P.op("pe", lambda e, h=h: e.matmul(out=H4(pb[0], h), lhsT=H4(Wm, h), rhs=H4(Rm, h), start=True, stop=True), r=["Wm", "Rm"], w=[PB[0]])
                    evac("act", vnew, pb[0], [PB[0]], ["vnew"])
                    if own:
                        for h in range(4):
                            P.op("pe", lambda e, h=h: e.matmul(out=H4(pb[1], h), lhsT=H4(qtT, h), rhs=H4(Sst, h), start=True, stop=False), r=["qtT", "S"], w=[PB[1]])
                            P.op("pe", lambda e, h=h: e.matmul(out=H4(pb[1], h), lhsT=H4(ATm, h), rhs=H4(vnew, h), start=False, stop=True), r=["ATm", "vnew"], w=[PB[1]])
                        evac("act", o_sb, pb[1], [PB[1]], ["o_sb"])
                    for h in range(4):
                        P.op("pe", lambda e, h=h: e.matmul(out=H4(pb[2], h), lhsT=H4(kt, h), rhs=H4(vnew, h), start=True, stop=True), r=["kt", "vnew"], w=[PB[2]])
                    for h in range(4):
                        P.op("dve", lambda e, h=h: e.scalar_tensor_tensor(out=H4(Sst, h), in0=H4(Sst, h), scalar=sc(32, h), in1=H4(pb[2], h), op0=ALU.mult, op1=ALU.add),
                             r=["S", "sm_e", PB[2]], w=["S"])
                    if own:
                        P.op("pool", lambda e: e.tensor_tensor(out=tmpa[:, 0:512], in0=o_sb, in1=o_sb, op=ALU.mult), r=["o_sb"], w=["tmpa"])
                        P.op("dve", lambda e: e.tensor_reduce(out=sm[:, 52:56], in_=tmpa[:, 0:512].rearrange("p (h d) -> p h d", h=4), axis=AX.X, op=ALU.add), r=["tmpa"], w=["sm_o"])
                        P.op("dve", lambda e: e.tensor_scalar(out=sm[:, 52:56], in0=sm[:, 52:56], scalar1=1.0 / 128, scalar2=1e-6, op0=ALU.mult, op1=ALU.add), r=["sm_o"], w=["sm_o"])
                        P.op("act", lambda e: e.activation(out=sm[:, 52:56], in_=sm[:, 52:56], func=AF.Sqrt), r=["sm_o"], w=["sm_o"])
                        P.op("dve", lambda e: e.reciprocal(out=sm[:, 52:56], in_=sm[:, 52:56]), r=["sm_o"], w=["sm_o"])
                        P.op("act", lambda e: e.activation(out=szb, in_=X[3][:, 1544:2056], func=AF.Silu), r=["X3"], w=["szb"])
                        for h in range(4):
                            P.op("dve", lambda e, h=h: e.scalar_tensor_tensor(out=H4(og, h), in0=H4(o_sb, h), scalar=sc(52, h), in1=H4(ggd, h), op0=ALU.mult, op1=ALU.mult),
                                 r=["o_sb", "sm_o", "ggd"], w=["og"])
                        P.op("pool", lambda e: e.tensor_tensor(out=og, in0=og, in1=szb, op=ALU.mult), r=["og", "szb"], w=["og"])
                        for h in range(4):
                            P.op("pe", lambda e, h=h: e.transpose(out=H4(pb[3], h), in_=H4(og, h), identity=ident), r=["og", "ident"], w=[PB[3]])
                        evac("act", cTst, pb[3], [PB[3]], ["cTst"])
                        P.dma("sp", S["cT"][ti][:, 0:4, :], cTst.rearrange("p (h t) -> p h t", h=4), r=["cTst"], w=[f"cTg{ti}"])
            P.dma("sp", O["ssm_fin"].rearrange("h d e -> d h e"), Sst.rearrange("p (h e) -> p h e", h=4), r=["S"])

            P.barrier()
            A.reset(gs_base)
            eye16 = A.f32(256)
            P.dma("sp", eye16, I["eye16"], w=["eye16"])
            Sall = A.f32(NS * 512); Sall4 = Sall.rearrange("p (s h e) -> p s h e", s=NS, h=4)
            for s_ in range(NS):
                P.dma("sp", Sall4[:, s_], I["state_ssm"][s_].rearrange("h d e -> d h e"), w=[f"S{s_}"])
            Ps2 = A.f32(INC); scv = A.f32(3 * 1536); scv3 = scv.rearrange("p (i c) -> p i c", i=3)
            P.dma("sp", Ps2[0:NS, :], S["ps"], w=["Ps2"])
            P.dma("sp", scv3[0:NS], I["state_conv"], w=["scv"])
            cvs = A.f32(1536); tms = A.f32(1536); sm2 = A.f32(64)
            kn2 = A.f32(512); qn2 = A.f32(512)
            knT2 = A.f32(64); qnT2 = A.f32(64)
            KTm = A.f32(1024); QTm = A.f32(1024)
            Km = [A.f32(512), A.f32(512)]
            EgD = A.f32(64); EGB = A.f32(64); Dl = A.f32(512); o_s = A.f32(512); og_s = A.f32(512); sz_s = A.f32(512)
            cTs = A.bf16(64)
            R = slice(0, NS)
            for (d0, d1, sc0, p0) in [(0, 1024, 512, 0), (1024, 1536, 0, C_Q)]:
                n = d1 - d0
                P.op("dve", lambda e, d0=d0, d1=d1, sc0=sc0, n=n: e.tensor_tensor(out=cvs[R, d0:d1], in0=scv3[R, 0, sc0:sc0 + n], in1=wc[0][R, d0:d1], op=ALU.mult), r=["scv", "wc0"], w=["cvs"])
                for i in range(1, 4):
                    src = (lambda i=i, sc0=sc0, n=n, p0=p0: scv3[R, i, sc0:sc0 + n] if i < 3 else Ps2[R, p0:p0 + n])()
                    P.op("pool", lambda e, i=i, d0=d0, d1=d1, src=src: e.tensor_tensor(out=tms[R, d0:d1], in0=src, in1=wc[i][R, d0:d1], op=ALU.mult), r=["scv", "Ps2", f"wc{i}"], w=["tms"])
                    P.op("dve", lambda e, d0=d0, d1=d1: e.tensor_tensor(out=cvs[R, d0:d1], in0=cvs[R, d0:d1], in1=tms[R, d0:d1], op=ALU.add), r=["cvs", "tms"], w=["cvs"])
                P.op("act", lambda e, d0=d0, d1=d1: e.activation(out=cvs[R, d0:d1], in_=cvs[R, d0:d1], func=AF.Silu), r=["cvs"], w=["cvs"])
            for (c0, o0) in [(0, 0), (1024, 4)]:
                P.op("pool", lambda e, c0=c0: e.tensor_tensor(out=tms[R, 0:512], in0=cvs[R, c0:c0 + 512], in1=cvs[R, c0:c0 + 512], op=ALU.mult), r=["cvs"], w=["tms"])
                P.op("dve", lambda e, o0=o0: e.tensor_reduce(out=sm2[R, o0:o0 + 4], in_=tms[R, 0:512].rearrange("p (h d) -> p h d", h=4), axis=AX.X, op=ALU.add), r=["tms"], w=["sm2"])
            P.op("dve", lambda e: e.tensor_scalar(out=sm2[R, 0:8], in0=sm2[R, 0:8], scalar1=1e-6, scalar2=None, op0=ALU.add), r=["sm2"], w=["sm2"])
            P.op("act", lambda e: e.activation(out=sm2[R, 0:8], in_=sm2[R, 0:8], func=AF.Sqrt), r=["sm2"], w=["sm2"])
            P.op("dve", lambda e: e.reciprocal(out=sm2[R, 0:8], in_=sm2[R, 0:8]), r=["sm2"], w=["sm2"])
            P.op("dve", lambda e: e.tensor_scalar(out=sm2[R, 4:8], in0=sm2[R, 4:8], scalar1=128 ** -0.5, scalar2=None, op0=ALU.mult), r=["sm2"], w=["sm2"])
            P.op("act", lambda e: e.activation(out=sm2[R, 8:12], in_=Ps2[R, C_B:C_B + 4], func=AF.Sigmoid), r=["Ps2"], w=["sm2b"])
            P.op("dve", lambda e: e.tensor_tensor(out=sm2[R, 12:16], in0=Ps2[R, C_A:C_A + 4], in1=dtb[R, :], op=ALU.add), r=["Ps2", "dtb"], w=["sm2g"])
            P.op("act", lambda e: e.activation(out=sm2[R, 12:16], in_=sm2[R, 12:16], func=AF.Exp), r=["sm2g"], w=["sm2g"])
            P.op("act", lambda e: e.activation(out=sm2[R, 12:16], in_=sm2[R, 12:16], func=AF.Ln, bias=1.0), r=["sm2g"], w=["sm2g"])
            P.op("dve", lambda e: e.tensor_tensor(out=sm2[R, 12:16], in0=sm2[R, 12:16], in1=negA[R, :], op=ALU.mult), r=["sm2g", "negA"], w=["sm2g"])
            P.op("act", lambda e: e.activation(out=sm2[R, 12:16], in_=sm2[R, 12:16], func=AF.Exp), r=["sm2g"], w=["sm2g"])
            P.op("dve", lambda e: e.tensor_scalar(out=sm2[R, 16:20], in0=sm2[R, 12:16], scalar1=-1.0, scalar2=None, op0=ALU.mult), r=["sm2g"], w=["sm2n"])
            for h in range(4):
                P.op("dve", lambda e, h=h: e.tensor_scalar(out=kn2[R, h * 128:(h + 1) * 128], in0=cvs[R, h * 128:(h + 1) * 128], scalar1=sm2[R, h:h + 1], scalar2=None, op0=ALU.mult), r=["cvs", "sm2"], w=["kn2"])
                P.op("dve", lambda e, h=h: e.tensor_scalar(out=qn2[R, h * 128:(h + 1) * 128], in0=cvs[R, 1024 + h * 128:1024 + (h + 1) * 128], scalar1=sm2[R, 4 + h:5 + h], scalar2=None, op0=ALU.mult), r=["cvs", "sm2"], w=["qn2"])
            for h in range(4):
                P.op("pe", lambda e, h=h: e.transpose(out=pb[6][:, h * 16:(h + 1) * 16], in_=kn2[R, h * 128:(h + 1) * 128], identity=ident[R, R]), r=["kn2", "ident"], w=[PB[6]])
                P.op("pe", lambda e, h=h: e.transpose(out=pb[6][:, 64 + h * 16:64 + (h + 1) * 16], in_=qn2[R, h * 128:(h + 1) * 128], identity=ident[R, R]), r=["qn2", "ident"], w=[PB[6]])
            evac("act", knT2, pb[6][:, 0:64], [PB[6]], ["knT2"])
            evac("dve", qnT2, pb[6][:, 64:128], [PB[6]], ["qnT2"])
            eye3 = eye16.rearrange("p (s m) -> p s m", s=NS)
            for h in range(4):
                for s_ in range(NS):
                    j = h * NS + s_
                    P.op("dve", lambda e, j=j, s_=s_: e.tensor_scalar(out=KTm[:, j * 16:(j + 1) * 16], in0=eye3[:, s_, :], scalar1=knT2[:, j:j + 1], scalar2=None, op0=ALU.mult), r=["eye16", "knT2"], w=["KTm"])
                    P.op("pool", lambda e, j=j, s_=s_: e.tensor_scalar(out=QTm[:, j * 16:(j + 1) * 16], in0=eye3[:, s_, :], scalar1=qnT2[:, j:j + 1], scalar2=1.0, op0=ALU.mult, op1=ALU.mult), r=["eye16", "qnT2"], w=["QTm"])
            for h in range(4):
                for s_ in range(NS):
                    j = h * NS + s_
                    P.op("pe", lambda e, h=h, s_=s_, j=j: e.matmul(out=pb[h][R, 0:128], lhsT=KTm[:, j * 16:(j + 1) * 16], rhs=Sall4[:, s_, h, :], start=(s_ == 0), stop=(s_ == NS - 1)),
                         r=["KTm", f"S{s_}"], w=[PB[h]])
            for h in range(4):
                P.op("dve", lambda e, h=h: e.scalar_tensor_tensor(out=Dl[R, h * 128:(h + 1) * 128], in0=pb[h][R, 0:128], scalar=sm2[R, 16 + h:17 + h], in1=cvs[R, 512 + h * 128:512 + (h + 1) * 128], op0=ALU.mult, op1=ALU.add),
                     r=[PB[h], "sm2n", "cvs"], w=["Dl"])
                P.op("dve", lambda e, h=h: e.tensor_scalar(out=Dl[R, h * 128:(h + 1) * 128], in0=Dl[R, h * 128:(h + 1) * 128], scalar1=sm2[R, 8 + h:9 + h], scalar2=None, op0=ALU.mult), r=["Dl", "sm2b"], w=["Dl"])
            for s_ in range(NS):
                P.op("dve", lambda e, s_=s_: e.tensor_scalar(out=EgD[R, s_ * 4:(s_ + 1) * 4], in0=sm2[R, 12:16], scalar1=ident[R, s_:s_ + 1], scalar2=None, op0=ALU.mult), r=["sm2g", "ident"], w=["EgD"])
            P.op("pe", lambda e: e.matmul(out=pb[6][:, 0:64], lhsT=ONESM[R, :], rhs=EgD[R, :], start=True, stop=True), r=["cst", "EgD"], w=[PB[6]])
            evac("act", EGB, pb[6][:, 0:64], [PB[6]], ["EGB"])
            for s_ in range(NS):
                km = Km[s_ % 2]; kmn = f"Km{s_ % 2}"
                bank = 4 + s_ % 2
                P.op("pool", lambda e, s_=s_, km=km: e.tensor_scalar(out=km[R, :], in0=kn2[R, :], scalar1=ident[R, s_:s_ + 1], scalar2=1.0, op0=ALU.mult, op1=ALU.mult), r=["kn2", "ident"], w=[kmn])
                for h in range(4):
                    P.op("pe", lambda e, h=h, km=km, bank=bank: e.matmul(out=pb[bank][:, h * 128:(h + 1) * 128], lhsT=km[R, h * 128:(h + 1) * 128], rhs=Dl[R, h * 128:(h + 1) * 128], start=True, stop=True),
                         r=[kmn, "Dl"], w=[PB[bank]])
                for h in range(4):
                    P.op("dve", lambda e, h=h, s_=s_, bank=bank: e.scalar_tensor_tensor(out=Sall4[:, s_, h, :], in0=Sall4[:, s_, h, :], scalar=EGB[:, s_ * 4 + h:s_ * 4 + h + 1], in1=pb[bank][:, h * 128:(h + 1) * 128], op0=ALU.mult, op1=ALU.add),
                         r=[f"S{s_}", "EGB", PB[bank]], w=[f"S{s_}"])
                P.dma("sp", O["ssm_s"][s_].rearrange("h d e -> d h e"), Sall4[:, s_], r=[f"S{s_}"])
            for h in range(4):
                for s_ in range(NS):
                    j = h * NS + s_
                    P.op("pe", lambda e, h=h, s_=s_, j=j: e.matmul(out=pb[h][R, 0:128], lhsT=QTm[:, j * 16:(j + 1) * 16], rhs=Sall4[:, s_, h, :], start=(s_ == 0), stop=(s_ == NS - 1)),
                         r=["QTm", f"S{s_}"], w=[PB[h]])
            for h in range(4):
                evac("act", o_s[R, h * 128:(h + 1) * 128], pb[h][R, 0:128], [PB[h]], ["o_s"])
            P.op("pool", lambda e: e.tensor_tensor(out=tms[R, 0:512], in0=o_s[R, :], in1=o_s[R, :], op=ALU.mult), r=["o_s"], w=["tms"])
            P.op("dve", lambda e: e.tensor_reduce(out=sm2[R, 20:24], in_=tms[R, 0:512].rearrange("p (h d) -> p h d", h=4), axis=AX.X, op=ALU.add), r=["tms"], w=["sm2o"])
            P.op("dve", lambda e: e.tensor_scalar(out=sm2[R, 20:24], in0=sm2[R, 20:24], scalar1=1.0 / 128, scalar2=1e-6, op0=ALU.mult, op1=ALU.add), r=["sm2o"], w=["sm2o"])
            P.op("act", lambda e: e.activation(out=sm2[R, 20:24], in_=sm2[R, 20:24], func=AF.Sqrt), r=["sm2o"], w=["sm2o"])
            P.op("dve", lambda e: e.reciprocal(out=sm2[R, 20:24], in_=sm2[R, 20:24]), r=["sm2o"], w=["sm2o"])
            P.op("act", lambda e: e.activation(out=sz_s[R, :], in_=Ps2[R, C_Z:C_Z + 512], func=AF.Silu), r=["Ps2"], w=["sz_s"])
            for h in range(4):
                P.op("dve", lambda e, h=h: e.scalar_tensor_tensor(out=og_s[R, h * 128:(h + 1) * 128], in0=o_s[R, h * 128:(h + 1) * 128], scalar=sm2[R, 20 + h:21 + h], in1=ggd[R, h * 128:(h + 1) * 128], op0=ALU.mult, op1=ALU.mult),
                     r=["o_s", "sm2o", "ggd"], w=["og_s"])
            P.op("pool", lambda e: e.tensor_tensor(out=og_s[R, :], in0=og_s[R, :], in1=sz_s[R, :], op=ALU.mult), r=["og_s", "sz_s"], w=["og_s"])
            for h in range(4):
                P.op("pe", lambda e, h=h: e.transpose(out=pb[7][:, h * 16:(h + 1) * 16], in_=og_s[R, h * 128:(h + 1) * 128], identity=ident[R, R]), r=["og_s", "ident"], w=[PB[7]])
            evac("act", cTs, pb[7][:, 0:64], [PB[7]], ["cTs"])
            P.dma("sp", S["cT"][NT][:, 0:4, 0:NS], cTs.rearrange("p (h t) -> p h t", h=4), r=["cTs"], w=["cTgs"])

        if not os.environ.get('MK_NOT'):
            P.barrier()
            A.reset(persist1)
            SCALE = 128 ** -0.5
            NEG = -30000.0
            NBIS = 14
            caus = A.f32(128); ii2 = A.bf16(256); onesr = A.f32(128)
            P.dma("sp", caus, I["caus"], w=["caus"])
            P.op("dve", lambda e: e.tensor_copy(out=ii2[:, 0:128], in_=ident), r=["ident"], w=["ii2"])
            P.op("dve", lambda e: e.tensor_copy(out=ii2[:, 128:256], in_=ident), r=["ident"], w=["ii2"])
            P.op("dve", lambda e: e.memset(onesr, 1.0), w=["onesr"])
            tsm = A.f32(256)
            krow = A.f32(128)
            P.op("dve", lambda e: e.tensor_reduce(out=tsm[:, 0:1], in_=ksq, axis=AX.X, op=ALU.max), r=["ksq"], w=["tsm0"])
            P.op("pe", lambda e: e.transpose(out=pb[0][0:1, 0:128], in_=tsm[:, 0:1], identity=ident), r=["tsm0", "ident"], w=[PB[0]])
            evac("dve", krow[0:1, :], pb[0][0:1, 0:128], [PB[0]], ["krow"])
            P.op("dve", lambda e: e.tensor_reduce(out=krow[0:1, 0:1], in_=krow[0:1, :], axis=AX.X, op=ALU.max), r=["krow"], w=["krow"])
            P.op("pe", lambda e: e.matmul(out=pb[0][:, 0:1], lhsT=onesr[0:1, :], rhs=krow[0:1, 0:1], start=True, stop=True), r=["onesr", "krow"], w=[PB[0]])
            evac("dve", tsm[:, 1:2], pb[0][:, 0:1], [PB[0]], ["tsm1"])
            P.op("dve", lambda e: e.tensor_reduce(out=tsm[:, 16:32], in_=qsq.rearrange("p (t h) -> p t h", h=4), axis=AX.X, op=ALU.max), r=["qsq"], w=["tsmq"])
            P.op("dve", lambda e: e.tensor_scalar(out=tsm[:, 32:48], in0=tsm[:, 16:32], scalar1=tsm[:, 1:2], scalar2=None, op0=ALU.mult), r=["tsmq", "tsm1"], w=["negm"])
            P.op("act", lambda e: e.activation(out=tsm[:, 32:48], in_=tsm[:, 32:48], func=AF.Sqrt), r=["negm"], w=["negm"])
            P.op("dve", lambda e: e.tensor_scalar(out=tsm[:, 32:48], in0=tsm[:, 32:48], scalar1=-1.0, scalar2=None, op0=ALU.mult), r=["negm"], w=["negm"])
            Iscs = [A.f32(T), A.f32(T)]
            junk = A.bf16(T); MBs_ = [A.bf16(T), A.bf16(T)]
            qTt = [A.bf16(512), A.bf16(512), A.bf16(512)]; iqTt = [A.bf16(512), A.bf16(512)]
            oaccs = [A.f32(4 * 130), A.f32(4 * 130)]
            rh = [A.bf16(512) for _ in range(4)]
            Dhs = [A.bf16(8 * 128), A.bf16(8 * 128)]
            PT = [A.bf16(256), A.bf16(256)]
            oatt = A.f32(512); cTa = A.bf16(512)
            bss = [A.f32(16), A.f32(16)]
            T_NT = int(os.environ.get("MK_TNT", NT))

            def t_index(j):
                p = j % 2
                Isc = Iscs[p]; In = f"Isc{p}"; Dh = Dhs[p]; Dn = f"Dh{p}"
                qb = qTt[j % 3]; qn_ = f"qTt{j % 3}"; iqb = iqTt[p]; iqn = f"iqTt{p}"
                qb3 = qb.rearrange("p (h t) -> p h t", h=4); iqb3 = iqb.rearrange("p (h t) -> p h t", h=4)
                P.dma("sp", iqb3, S["iqT"][j], w=[iqn])
                P.dma("sp", qb3, S["qT"][j], w=[qn_])
                ncol = HALF + 128 * (j + 1)
                for h in range(8):
                    P.op("dve", lambda e, h=h: e.tensor_scalar(out=Dh[:, h * 128:(h + 1) * 128], in0=ident, scalar1=iwsgn[:, j * 8 + h:j * 8 + h + 1], scalar2=None, op0=ALU.mult),
                         r=["ident", "iwsgn"], w=[Dn])
                nblk = (ncol + 511) // 512
                for kb in range(nblk):
                    c0, c1 = kb * 512, min(ncol, kb * 512 + 512)
                    w_ = c1 - c0
                    accb = 2 + kb % 2

                    def emit_S(h, c0=c0, c1=c1, w_=w_):
                        p_, hf_ = h // 2, h % 2
                        ba = h % 2
                        rb = rh[h % 4]; rbn = f"rh{h % 4}"
                        P.op("pe", lambda e: e.matmul(out=pb[ba][:, 0:w_], lhsT=iqb3[hf_ * 64:(hf_ + 1) * 64, p_, :], rhs=ikT[hf_ * 64:(hf_ + 1) * 64, c0:c1], start=True, stop=True),
                             r=[iqn, "ikT"], w=[PB[ba]])
                        P.op("act", lambda e: e.activation(out=rb[:, 0:w_], in_=pb[ba][:, 0:w_], func=AF.Relu, scale=iwabs[:, j * 8 + h:j * 8 + h + 1]),
                             r=[PB[ba], "iwabs"], w=[rbn])

                    def emit_D(h, w_=w_, accb=accb):
                        rb = rh[h % 4]; rbn = f"rh{h % 4}"
                        P.op("pe", lambda e: e.matmul(out=pb[accb][:, 0:w_], lhsT=Dh[:, h * 128:(h + 1) * 128], rhs=rb[:, 0:w_], start=(h == 0), stop=(h == 7)),
                             r=[Dn, rbn], w=[PB[accb]])

                    emit_S(0); emit_S(1)
                    for h in range(8):
                        emit_D(h)
                        if h + 2 < 8:
                            emit_S(h + 2)
                    if c0 < HALF:
                        P.op("act", lambda e, c0=c0, c1=c1, w_=w_, accb=accb: e.activation(out=Isc[:, c0:c1], in_=pb[accb][:, 0:w_], func=AF.Identity, bias=flags[:, 1:2]), r=[PB[accb], "flags"], w=[In])
                    else:
                        P.op("act", lambda e, c0=c0, c1=c1, w_=w_, accb=accb: e.activation(out=Isc[:, c0:c1], in_=pb[accb][:, 0:w_], func=AF.Copy), r=[PB[accb]], w=[In])
                P.op("pool", lambda e: e.tensor_tensor(out=Isc[:, ncol - 128:ncol], in0=Isc[:, ncol - 128:ncol], in1=caus, op=ALU.add), r=[In, "caus"], w=[In])

            def t_bisect(j):
                p = j % 2
                Isc = Iscs[p]; In = f"Isc{p}"; MB = MBs_[p]; Mn = f"MB{p}"; bs = bss[p]
                B = lambda n: f"bs{p}_{n}"
                ncol = HALF + 128 * (j + 1)
                lo, rng, thr, cntc, mm = bs[:, 0:1], bs[:, 1:2], bs[:, 2:3], bs[:, 3:4], bs[:, 4:5]
                P.op("dve", lambda e: e.tensor_reduce(out=lo, in_=Isc[:, 0:ncol], axis=AX.X, op=ALU.min), r=[In], w=[B("lo")])
                P.op("dve", lambda e: e.tensor_reduce(out=rng, in_=Isc[:, 0:ncol], axis=AX.X, op=ALU.max), r=[In], w=[B("rng")])
                P.op("dve", lambda e: e.tensor_scalar(out=thr, in0=rng, scalar1=-128.0, scalar2=None, op0=ALU.add), r=[B("rng")], w=[B("thr")])
                P.op("dve", lambda e: e.tensor_tensor(out=lo, in0=lo, in1=thr, op=ALU.max), r=[B("lo"), B("thr")], w=[B("lo")])
                P.op("dve", lambda e: e.tensor_tensor(out=rng, in0=rng, in1=lo, op=ALU.subtract), r=[B("rng"), B("lo")], w=[B("rng")])
                base = bs[:, 5:6]
                P.op("dve", lambda e: e.scalar_tensor_tensor(out=thr, in0=rng, scalar=0.5, in1=lo, op0=ALU.mult, op1=ALU.add), r=[B("rng"), B("lo")], w=[B("thr")])
                for it in range(NBIS):
                    st = 2.0 ** -(it + 1)
                    P.op("dve", lambda e: e.tensor_scalar(out=junk[:, 0:ncol], in0=Isc[:, 0:ncol], scalar1=thr, scalar2=None, op0=ALU.is_ge, op1=ALU.add, accum_out=cntc),
                         r=[In, B("thr")], w=["junk", B("cnt")])
                    P.op("dve", lambda e, st=st: e.scalar_tensor_tensor(out=base, in0=rng, scalar=-0.5 * st, in1=thr, op0=ALU.mult, op1=ALU.add), r=[B("rng"), B("thr")], w=[B("base")])
                    P.op("dve", lambda e: e.tensor_scalar(out=mm, in0=cntc, scalar1=255.5, scalar2=rng, op0=ALU.is_ge, op1=ALU.mult), r=[B("cnt"), B("rng")], w=[B("m")])
                    P.op("dve", lambda e, st=st: e.scalar_tensor_tensor(out=thr, in0=mm, scalar=st, in1=base, op0=ALU.mult, op1=ALU.add), r=[B("m"), B("base")], w=[B("thr")])
                P.op("dve", lambda e: e.scalar_tensor_tensor(out=lo, in0=rng, scalar=-(2.0 ** -(NBIS + 1)), in1=thr, op0=ALU.mult, op1=ALU.add), r=[B("rng"), B("thr")], w=[B("lo")])
                P.op("dve", lambda e: e.tensor_scalar(out=MB[:, 0:ncol], in0=Isc[:, 0:ncol], scalar1=lo, scalar2=NEG, op0=ALU.is_lt, op1=ALU.mult), r=[In, B("lo")], w=[Mn])
                P.op("dve", lambda e: e.tensor_scalar(out=MB[:, 0:ncol], in0=MB[:, 0:ncol], scalar1=tsm[:, 32 + j:33 + j], scalar2=None, op0=ALU.add), r=[Mn, "negm"], w=[Mn])

            def t_attend(j):
                p = j % 2
                MB = MBs_[p]; Mn = f"MB{p}"; bs = bss[p]
                qb = qTt[j % 3]; qn_ = f"qTt{j % 3}"
                qb3 = qb.rearrange("p (h t) -> p h t", h=4)
                oacc = oaccs[p]; oan = f"oacc{p}"
                ntile = 16 + j + 1
                seq = [(g, t) for g in range(2) for t in range(ntile)]

                def emit_ST(i):
                    g, t = seq[i]
                    sb_ = i % 2
                    ptb = PT[i % 2]; ptn = f"PT{i % 2}"
                    P.op("pe", lambda e: e.matmul(out=pb[sb_][:, 0:256], lhsT=KT3[:, g, t * 128:(t + 1) * 128], rhs=qb3[:, g * 2:(g + 1) * 2, :], start=True, stop=False),
                         r=["KT", qn_], w=[PB[sb_]])
                    P.op("pe", lambda e: e.matmul(out=pb[sb_][:, 0:256], lhsT=MB[:, t * 128:(t + 1) * 128], rhs=ii2, start=False, stop=True),
                         r=[Mn, "ii2"], w=[PB[sb_]])
                    P.op("act", lambda e: e.activation(out=ptb, in_=pb[sb_][:, 0:256], func=AF.Exp, scale=SCALE), r=[PB[sb_]], w=[ptn])

                def emit_PV(i):
                    g, t = seq[i]
                    ptb = PT[i % 2]; ptn = f"PT{i % 2}"
                    for h2i in range(2):
                        P.op("pe", lambda e, h2i=h2i: e.matmul(out=pb[4 + g * 2 + h2i][:, 0:130], lhsT=ptb[:, h2i * 128:(h2i + 1) * 128], rhs=VA4[:, t, g, :], start=(t == 0), stop=(t == ntile - 1)),
                             r=[ptn, "VA"], w=[PB[4 + g * 2 + h2i]])

                emit_ST(0)
                if len(seq) > 1:
                    emit_ST(1)
                for i in range(len(seq)):
                    emit_PV(i)
                    if i + 2 < len(seq):
                        emit_ST(i + 2)
                for h in range(4):
                    P.op("act", lambda e, h=h: e.activation(out=oacc[:, h * 130:(h + 1) * 130], in_=pb[4 + h][:, 0:130], func=AF.Copy), r=[PB[4 + h]], w=[oan])

            def t_final(j):
                p = j % 2
                bs = bss[p]; oacc = oaccs[p]; oan = f"oacc{p}"
                for h in range(4):
                    P.op("dve", lambda e, h=h: e.reciprocal(out=bs[:, 8 + h:9 + h], in_=oacc[:, h * 130 + 128:h * 130 + 129]), r=[oan], w=[f"bs{p}_r"])
                    P.op("dve", lambda e, h=h: e.tensor_scalar(out=oatt[:, h * 128:(h + 1) * 128], in0=oacc[:, h * 130:h * 130 + 128], scalar1=bs[:, 8 + h:9 + h], scalar2=None, op0=ALU.mult),
                         r=[oan, f"bs{p}_r"], w=["oatt"])
                for h in range(4):
                    P.op("pe", lambda e, h=h: e.transpose(out=pb[3][:, h * 128:(h + 1) * 128], in_=oatt[:, h * 128:(h + 1) * 128], identity=ident), r=["oatt", "ident"], w=[PB[3]])
                evac("act", cTa, pb[3], [PB[3]], ["cTa"])
                P.dma("sp", S["cT"][j][:, 4:8, :], cTa.rearrange("p (h t) -> p h t", h=4), r=["cTa"], w=[f"cTa{j}"])

            if T_NT > 0:
                t_index(0)
            if T_NT > 1:
                t_index(1)
            if T_NT > 0:
                t_bisect(0)
            for j in range(T_NT):
                t_attend(j)
                if j + 2 < T_NT:
                    t_index(j + 2)
                if j + 1 < T_NT:
                    t_bisect(j + 1)
                t_final(j)

        if not os.environ.get('MK_NOTS'):
            P.barrier()
            A.reset(persist0)
            zSCALE = 128 ** -0.5
            NPG = 16
            zidx_i = A.f32(32).bitcast(I32)
            zpt_i = A.f32(256).bitcast(I32)
            zidx_f = A.f32(256); zsel = A.f32(257); zix = A.f32(32)
            P.dma("sp", zpt_i[:, :], I["pt"].to_broadcast([128, 256]), w=["zpt_i"])
            P.dma("sp", zsel, I["tsel"], w=["zsel"])
            P.op("dve", lambda e: e.tensor_copy(out=zidx_f, in_=zpt_i[:, :]), r=["zpt_i"], w=["zidx_f"])
            P.op("dve", lambda e: e.tensor_tensor(out=zidx_f, in0=zidx_f, in1=zsel[:, 0:256], op=ALU.mult), r=["zidx_f", "zsel"], w=["zidx_f"])
            P.op("dve", lambda e: e.tensor_reduce(out=zix, in_=zidx_f.rearrange("p (q j) -> p q j", j=8), axis=AX.X, op=ALU.add), r=["zidx_f"], w=["zix"])
            P.op("dve", lambda e: e.tensor_scalar(out=zix, in0=zix, scalar1=16.0, scalar2=zsel[:, 256:257], op0=ALU.mult, op1=ALU.add), r=["zix", "zsel"], w=["zix"])
            P.op("dve", lambda e: e.tensor_copy(out=zidx_i[:, :], in_=zix), r=["zix"], w=["zidx_i"])
            zPs = A.f32(INC)
            P.dma("sp", zPs[0:NS, :], S["ps"], w=["zPs"])
            zones = A.f32(128)
            P.op("dve", lambda e: e.memset(zones, 1.0), w=["zones"])
            ziqT = A.f32(NS * 8)
            ziwT = A.f32(NS)
            zikn = A.f32(NS)
            zqT = A.bf16(4 * NS)
            zkTn = A.bf16(2 * NS)
            ziqT3 = ziqT.rearrange("p (s h) -> p s h", h=8)
            R = slice(0, NS)
            for h in range(8):
                P.op("pe", lambda e, h=h: e.transpose(out=pb[0][0:64, h * NS:(h + 1) * NS], in_=zPs[R, C_IQ + h * 64:C_IQ + (h + 1) * 64], identity=ident[R, R]), r=["zPs", "ident"], w=[PB[0]])
            P.op("dve", lambda e: e.tensor_copy(out=ziqT3[0:64], in_=pb[0][0:64, 0:8 * NS].rearrange("p (h s) -> p s h", h=8)), r=[PB[0]], w=["ziqT"])
            P.op("pe", lambda e: e.transpose(out=pb[1][0:8, 0:NS], in_=zPs[R, C_IW:C_IW + 8], identity=ident[R, R]), r=["zPs", "ident"], w=[PB[1]])
            P.op("dve", lambda e: e.tensor_scalar(out=ziwT[0:8, :], in0=pb[1][0:8, 0:NS], scalar1=8 ** -0.5, scalar2=None, op0=ALU.mult), r=[PB[1]], w=["ziwT"])
            P.op("pe", lambda e: e.transpose(out=pb[1][0:64, 64:64 + NS], in_=zPs[R, C_IK:C_IK + 64], identity=ident[R, R]), r=["zPs", "ident"], w=[PB[1]])
            P.op("dve", lambda e: e.tensor_copy(out=zikn[0:64, :], in_=pb[1][0:64, 64:64 + NS]), r=[PB[1]], w=["zikn"])
            for h in range(4):
                P.op("pe", lambda e, h=h: e.transpose(out=pb[2][:, h * NS:(h + 1) * NS], in_=zPs[R, C_AQ + h * 128:C_AQ + (h + 1) * 128], identity=ident[R, R]), r=["zPs", "ident"], w=[PB[2]])
            for g in range(2):
                P.op("pe", lambda e, g=g: e.transpose(out=pb[2][:, 64 + g * NS:64 + (g + 1) * NS], in_=zPs[R, C_AK + g * 128:C_AK + (g + 1) * 128], identity=ident[R, R]), r=["zPs", "ident"], w=[PB[2]])
            P.op("dve", lambda e: e.tensor_copy(out=zqT, in_=pb[2][:, 0:4 * NS]), r=[PB[2]], w=["zqT"])
            P.op("dve", lambda e: e.tensor_copy(out=zkTn, in_=pb[2][:, 64:64 + 2 * NS]), r=[PB[2]], w=["zkTn"])
            zvn_f = A.f32(NS * 256); zvn = A.bf16(NS * 2 * 130)
            zvn4 = zvn.rearrange("p (s g d) -> p s g d", s=NS, g=2)
            P.dma("sp", zvn_f[0:1, :].rearrange("p (s c) -> p s c", s=NS), S["ps"][:, C_AV:C_AV + 256].rearrange("(o s) c -> o s c", o=1), w=["zvn_f"])
            P.op("dve", lambda e: e.memset(zvn[0:1, :], 1.0), w=["zvn"])
            P.op("dve", lambda e: e.tensor_copy(out=zvn4[0:1, :, :, 0:128], in_=zvn_f[0:1, :].rearrange("p (s g d) -> p s g d", s=NS, g=2)), r=["zvn_f", "zvn"], w=["zvn"])
            NKS = 2049
            zIall = A.f32(NKS + 3); zikgs = [A.f32(NPG * 64), A.f32(NPG * 64)]; zikT = A.f32(NKS + 3)
            zik_rows = I["cache_ik"].rearrange("(r t) d -> r (t d)", t=8)
            zk_rows = I["cache_k"].rearrange("(r t) d -> r (t d)", t=8)
            zv_rows = I["cache_v"].rearrange("(r t) d -> r (t d)", t=8)
            zikg4s = [z_.rearrange("p (a t d) -> p a t d", a=2, t=8) for z_ in zikgs]

            def ts1_gather(s_):
                zg4 = zikg4s[s_ % 2]
                for a_ in range(2):
                    col = s_ * 2 + a_
                    P.dma_raw("pool", lambda e, a_=a_, col=col: e.indirect_dma_start(out=zg4[:, a_].rearrange("p t d -> p (t d)"), out_offset=None, in_=zik_rows, in_offset=bass.IndirectOffsetOnAxis(ap=zidx_i[:, col:col + 1], axis=0)),
                              r=["zidx_i"], w=[f"zikg{s_ % 2}"], sw=True)
            ts1_gather(0)
            zr8 = [A.f32(512), A.f32(512)]; zrw = A.f32(NKS + 3)
            for s_ in range(NS):
                if s_ + 1 < NS:
                    ts1_gather(s_ + 1)
                zikg4 = zikg4s[s_ % 2]; zikgn = f"zikg{s_ % 2}"
                for q4 in range(4):
                    for i4 in range(4):
                        bk = q4 * 4 + i4
                        t8, a_ = bk // 2, bk % 2
                        P.op("pe", lambda e, t8=t8, a_=a_, i4=i4, zikg4=zikg4: e.transpose(out=pb[3][0:64, i4 * 128:(i4 + 1) * 128], in_=zikg4[:, a_, t8, :], identity=ident), r=[zikgn, "ident"], w=[PB[3]])
                    evac("act" if q4 % 2 else "dve", zikT[0:64, q4 * 512:(q4 + 1) * 512], pb[3][0:64, :], [PB[3]], ["zikT"])
                P.op("dve", lambda e, s_=s_: e.tensor_copy(out=zikT[0:64, 2048:2049], in_=zikn[0:64, s_:s_ + 1]), r=["zikn"], w=["zikT"])
                for kb in range(5):
                    c0, c1 = kb * 512, min(NKS, kb * 512 + 512)
                    w_ = c1 - c0
                    rb = zr8[kb % 2]; rbn = f"zr8{kb % 2}"
                    P.op("pe", lambda e, s_=s_, c0=c0, c1=c1, w_=w_, kb=kb: e.matmul(out=pb[4 + kb % 2][0:8, 0:w_], lhsT=ziqT3[0:64, s_, :], rhs=zikT[0:64, c0:c1], start=True, stop=True), r=["ziqT", "zikT"], w=[PB[4 + kb % 2]])
                    P.op("dve", lambda e, s_=s_, w_=w_, kb=kb, rb=rb: e.tensor_scalar(out=rb[0:8, 0:w_], in0=pb[4 + kb % 2][0:8, 0:w_], scalar1=0.0, scalar2=ziwT[0:8, s_:s_ + 1], op0=ALU.max, op1=ALU.mult),
                         r=[PB[4 + kb % 2], "ziwT"], w=[rbn])
                    P.op("pe", lambda e, w_=w_, kb=kb, rb=rb: e.matmul(out=pb[6 + kb % 2][0:1, 0:w_], lhsT=zones[0:8, 0:1], rhs=rb[0:8, 0:w_], start=True, stop=True), r=["zones", rbn], w=[PB[6 + kb % 2]])
                    evac("act", zrw[0:1, c0:c1], pb[6 + kb % 2][0:1, 0:w_], [PB[6 + kb % 2]], ["zrw"])
                P.dma("sp", zIall[s_:s_ + 1, 0:NKS], zrw[0:1, 0:NKS], r=["zrw"], w=["zIall"])
            zbs = A.f32(16); zjunk = A.bf16(NKS + 3); zMB = A.f32(NKS + 3)
            zlo, zrng, zthr, zcnt, zmm = zbs[R, 0:1], zbs[R, 1:2], zbs[R, 2:3], zbs[R, 3:4], zbs[R, 4:5]
            P.op("dve", lambda e: e.tensor_reduce(out=zlo, in_=zIall[R, 0:NKS], axis=AX.X, op=ALU.min), r=["zIall"], w=["zlo"])
            P.op("dve", lambda e: e.tensor_reduce(out=zrng, in_=zIall[R, 0:NKS], axis=AX.X, op=ALU.max), r=["zIall"], w=["zrng"])
            P.op("dve", lambda e: e.tensor_scalar(out=zthr, in0=zrng, scalar1=-128.0, scalar2=None, op0=ALU.add), r=["zrng"], w=["zthr"])
            P.op("dve", lambda e: e.tensor_tensor(out=zlo, in0=zlo, in1=zthr, op=ALU.max), r=["zlo", "zthr"], w=["zlo"])
            P.op("dve", lambda e: e.tensor_tensor(out=zrng, in0=zrng, in1=zlo, op=ALU.subtract), r=["zrng", "zlo"], w=["zrng"])
            zbase = zbs[R, 5:6]
            ZNB = 14
            P.op("dve", lambda e: e.scalar_tensor_tensor(out=zthr, in0=zrng, scalar=0.5, in1=zlo, op0=ALU.mult, op1=ALU.add), r=["zrng", "zlo"], w=["zthr"])
            for it in range(ZNB):
                st = 2.0 ** -(it + 1)
                P.op("dve", lambda e: e.tensor_scalar(out=zjunk[R, 0:NKS], in0=zIall[R, 0:NKS], scalar1=zthr, scalar2=None, op0=ALU.is_ge, op1=ALU.add, accum_out=zcnt), r=["zIall", "zthr"], w=["zjunk", "zcnt"])
                P.op("dve", lambda e, st=st: e.scalar_tensor_tensor(out=zbase, in0=zrng, scalar=-0.5 * st, in1=zthr, op0=ALU.mult, op1=ALU.add), r=["zrng", "zthr"], w=["zbase"])
                P.op("dve", lambda e: e.tensor_scalar(out=zmm, in0=zcnt, scalar1=255.5, scalar2=zrng, op0=ALU.is_ge, op1=ALU.mult), r=["zcnt", "zrng"], w=["zmm"])
                P.op("dve", lambda e, st=st: e.scalar_tensor_tensor(out=zthr, in0=zmm, scalar=st, in1=zbase, op0=ALU.mult, op1=ALU.add), r=["zmm", "zbase"], w=["zthr"])
            P.op("dve", lambda e: e.scalar_tensor_tensor(out=zlo, in0=zrng, scalar=-(2.0 ** -(ZNB + 1)), in1=zthr, op0=ALU.mult, op1=ALU.add), r=["zrng", "zthr"], w=["zlo"])
            P.op("dve", lambda e: e.tensor_scalar(out=zMB[R, 0:NKS], in0=zIall[R, 0:NKS], scalar1=zlo, scalar2=None, op0=ALU.is_ge), r=["zIall", "zlo"], w=["zMB"])
            zMT = A.f32(NPG * NS); zMT3 = zMT.rearrange("p (g s) -> p g s", g=NPG); zMn = A.f32(NS)
            for pg in range(NPG):
                P.op("pe", lambda e, pg=pg: e.transpose(out=pb[0][:, pg * NS:(pg + 1) * NS], in_=zMB[R, pg * 128:(pg + 1) * 128], identity=ident[R, R]), r=["zMB", "ident"], w=[PB[0]])
            evac("dve", zMT, pb[0][:, 0:NPG * NS], [PB[0]], ["zMT"])
            P.op("pe", lambda e: e.transpose(out=pb[1][0:1, 0:NS], in_=zMB[R, 2048:2049], identity=ident[R, R]), r=["zMB", "ident"], w=[PB[1]])
            evac("dve", zMn[0:1, :], pb[1][0:1, 0:NS], [PB[1]], ["zMn"])
            zkgs = [A.f32(NPG * 256), A.f32(NPG * 256)]; zvgs = [A.f32(NPG * 256), A.f32(NPG * 256)]
            zkg4s = [z_.rearrange("p (a t c) -> p a t c", a=2, t=8) for z_ in zkgs]; zvg4s = [z_.rearrange("p (a t c) -> p a t c", a=2, t=8) for z_ in zvgs]

            def ts2_gather(s_):
                zk4 = zkg4s[s_ % 2]; zv4 = zvg4s[s_ % 2]
                for a_ in range(2):
                    col = s_ * 2 + a_
                    P.dma_raw("pool", lambda e, a_=a_, col=col: e.indirect_dma_start(out=zk4[:, a_].rearrange("p t c -> p (t c)"), out_offset=None, in_=zk_rows, in_offset=bass.IndirectOffsetOnAxis(ap=zidx_i[:, col:col + 1], axis=0)),
                              r=["zidx_i"], w=[f"zkg{s_ % 2}"], sw=True)
                    P.dma_raw("pool", lambda e, a_=a_, col=col: e.indirect_dma_start(out=zv4[:, a_].rearrange("p t c -> p (t c)"), out_offset=None, in_=zv_rows, in_offset=bass.IndirectOffsetOnAxis(ap=zidx_i[:, col:col + 1], axis=0)),
                              r=["zidx_i"], w=[f"zvg{s_ % 2}"], sw=True)
            ts2_gather(0)
            zKT = A.bf16(NPG * 256); zKT4 = zKT.rearrange("p (g k c) -> p g k c", g=NPG, k=2)
            zVb = A.bf16(NPG * 2 * 130); zVb4 = zVb.rearrange("p (g k d) -> p g k d", g=NPG, k=2)
            zP = A.bf16(NPG * 4); zP3 = zP.rearrange("p (g h) -> p g h", g=NPG); zPn = A.bf16(4)
            zsm = A.f32(16); zcr = A.f32(128)
            zo = A.f32(256); zoT = A.bf16(4 * NS); zoT3 = zoT.rearrange("p (h s) -> p h s", h=4)
            P.op("dve", lambda e: e.memset(zVb, 1.0), w=["zVb"])
            for s_ in range(NS):
                if s_ + 1 < NS:
                    ts2_gather(s_ + 1)
                zkg4 = zkg4s[s_ % 2]; zvg4 = zvg4s[s_ % 2]; zkgn = f"zkg{s_ % 2}"; zvgn = f"zvg{s_ % 2}"
                for a_ in range(2):
                    P.op("act", lambda e, a_=a_, zvg4=zvg4: e.activation(out=zVb.rearrange("p (t a k d) -> p a t k d", t=8, a=2, k=2)[:, a_, :, :, 0:128], in_=zvg4[:, a_].rearrange("p t (k d) -> p t k d", k=2), func=AF.Copy),
                         r=[zvgn, "zVb"], w=["zVb"])
                for pq in range(8):
                    for i2 in range(2):
                        bk = pq * 2 + i2
                        t8, a_ = bk // 2, bk % 2
                        for g in range(2):
                            P.op("pe", lambda e, t8=t8, a_=a_, g=g, i2=i2, pq=pq, zkg4=zkg4: e.transpose(out=pb[2 + pq % 2][:, (i2 * 2 + g) * 128:(i2 * 2 + g + 1) * 128], in_=zkg4[:, a_, t8, g * 128:(g + 1) * 128], identity=ident),
                                 r=[zkgn, "ident"], w=[PB[2 + pq % 2]])
                    evac("act" if pq % 2 else "dve", zKT[:, pq * 512:(pq + 1) * 512], pb[2 + pq % 2], [PB[2 + pq % 2]], ["zKT"])
                for pg in range(NPG):
                    for g in range(2):
                        P.op("pe", lambda e, pg=pg, g=g, s_=s_: e.matmul(out=pb[4][:, pg * 4 + g * 2:pg * 4 + g * 2 + 2], lhsT=zKT4[:, pg, g, :], rhs=zqT.rearrange("p (h s) -> p h s", h=4)[:, g * 2:(g + 1) * 2, s_], start=True, stop=True),
                             r=["zKT", "zqT"], w=[PB[4]])
                for g in range(2):
                    P.op("pe", lambda e, g=g, s_=s_: e.matmul(out=pb[5][0:1, g * 2:g * 2 + 2], lhsT=zkTn.rearrange("p (g s) -> p g s", g=2)[:, g, s_:s_ + 1], rhs=zqT.rearrange("p (h s) -> p h s", h=4)[:, g * 2:(g + 1) * 2, s_], start=True, stop=True),
                         r=["zkTn", "zqT"], w=[PB[5]])
                P.op("dve", lambda e: e.tensor_reduce(out=zsm[:, 0:1], in_=pb[4][:, 0:NPG * 4], axis=AX.X, op=ALU.max), r=[PB[4]], w=["zsm0"])
                P.op("pe", lambda e: e.transpose(out=pb[6][0:1, 0:128], in_=zsm[:, 0:1], identity=ident), r=["zsm0", "ident"], w=[PB[6]])
                evac("dve", zcr[0:1, :], pb[6][0:1, 0:128], [PB[6]], ["zcr"])
                P.op("dve", lambda e: e.tensor_reduce(out=zsm[0:1, 1:2], in_=zcr[0:1, :], axis=AX.X, op=ALU.max), r=["zcr"], w=["zsm1"])
                P.op("dve", lambda e: e.tensor_reduce(out=zsm[0:1, 2:3], in_=pb[5][0:1, 0:4], axis=AX.X, op=ALU.max), r=[PB[5]], w=["zsm2"])
                P.op("dve", lambda e: e.tensor_tensor(out=zsm[0:1, 1:2], in0=zsm[0:1, 1:2], in1=zsm[0:1, 2:3], op=ALU.max), r=["zsm1", "zsm2"], w=["zsm1"])
                P.op("dve", lambda e: e.tensor_scalar(out=zsm[0:1, 1:2], in0=zsm[0:1, 1:2], scalar1=-zSCALE, scalar2=None, op0=ALU.mult), r=["zsm1"], w=["zsm1"])
                P.op("pe", lambda e: e.matmul(out=pb[6][:, 128:129], lhsT=zones[0:1, :], rhs=zsm[0:1, 1:2], start=True, stop=True), r=["zones", "zsm1"], w=[PB[6]])
                evac("dve", zsm[:, 3:4], pb[6][:, 128:129], [PB[6]], ["zsm3"])
                P.op("act", lambda e: e.activation(out=zP, in_=pb[4][:, 0:NPG * 4], func=AF.Exp, scale=zSCALE, bias=zsm[:, 3:4]), r=[PB[4], "zsm3"], w=["zP"])
                P.op("act", lambda e: e.activation(out=zPn[0:1, :], in_=pb[5][0:1, 0:4], func=AF.Exp, scale=zSCALE, bias=zsm[0:1, 3:4]), r=[PB[5], "zsm3"], w=["zPn"])
                for h in range(4):
                    P.op("dve", lambda e, h=h, s_=s_: e.tensor_tensor(out=zP3[:, :, h], in0=zP3[:, :, h], in1=zMT3[:, :, s_], op=ALU.mult), r=["zP", "zMT"], w=["zP"])
                P.op("dve", lambda e, s_=s_: e.tensor_scalar(out=zPn[0:1, :], in0=zPn[0:1, :], scalar1=zMn[0:1, s_:s_ + 1], scalar2=None, op0=ALU.mult), r=["zPn", "zMn"], w=["zPn"])
                for g in range(2):
                    for pg in range(NPG):
                        P.op("pe", lambda e, pg=pg, g=g: e.matmul(out=pb[g][0:2, 0:130], lhsT=zP3[:, pg, g * 2:(g + 1) * 2], rhs=zVb4[:, pg, g, :], start=(pg == 0), stop=False), r=["zP", "zVb"], w=[PB[g]])
                    P.op("pe", lambda e, g=g, s_=s_: e.matmul(out=pb[g][0:2, 0:130], lhsT=zPn[0:1, g * 2:(g + 1) * 2], rhs=zvn4[0:1, s_, g, :], start=False, stop=True), r=["zPn", "zvn"], w=[PB[g]])
                for g in range(2):
                    P.op("dve", lambda e, g=g: e.reciprocal(out=zsm[0:2, 4 + g:5 + g], in_=pb[g][0:2, 128:129]), r=[PB[g]], w=["zsm4"])
                    P.op("dve", lambda e, g=g: e.tensor_scalar(out=zo[0:2, g * 128:(g + 1) * 128], in0=pb[g][0:2, 0:128], scalar1=zsm[0:2, 4 + g:5 + g], scalar2=None, op0=ALU.mult), r=[PB[g], "zsm4"], w=["zo"])
                for g in range(2):
                    P.op("pe", lambda e, g=g: e.transpose(out=pb[7][:, g * 2:(g + 1) * 2], in_=zo[0:2, g * 128:(g + 1) * 128], identity=ident[0:2, 0:2]), r=["zo", "ident"], w=[PB[7]])
                P.op("dve", lambda e, s_=s_: e.tensor_copy(out=zoT3[:, :, s_], in_=pb[7][:, 0:4]), r=[PB[7]], w=["zoT"])
            P.dma("sp", S["cT"][NT][:, 4:8, 0:NS], zoT3, r=["zoT"], w=["cTas"])

        if not os.environ.get('MK_NOD'):
            P.barrier()
            A.reset(persist0)
            zt = A.bf16(1024)
            P.op("pool", lambda e: e.memset(zt, 0.0), w=["zt"])
            if True:
                for ti in range(NT + 1):
                    if (ti == NT and os.environ.get('MK_NOTS')) or (ti < NT and (os.environ.get('MK_NOT') or ti >= int(os.environ.get("MK_TNT", NT)))):
                        P.dma("sp", S["cT"][ti][:, 4:8, :], zt[:, 0:512].rearrange("p (k t) -> p k t", k=4), r=["zt"], w=[f"cT{ti}"])
                    if ti == NT:
                        P.dma("sp", S["cT"][ti][:, 0:4, 16:128], zt[:, 0:448].rearrange("p (k t) -> p k t", k=4), r=["zt"], w=[f"cT{ti}"])
            g2_bc = A.f32(D); ga1_bc = A.f32(D); a2_bc = A.f32(D); sh2_bc = A.f32(D)
            a2_s = A.f32(D)
            P.dma("sp", g2_bc, I["g2"].to_broadcast([128, D]), w=["g2_bc"])
            P.dma("sp", ga1_bc, S["mod"][16:17, 2 * D:3 * D].to_broadcast([128, D]), r=["mod_scr"], w=["ga1_bc"])
            P.dma("sp", sh2_bc, S["mod"][16:17, 3 * D:4 * D].to_broadcast([128, D]), r=["mod_scr"], w=["sh2_bc"])
            P.dma("sp", a2_bc, S["mod"][16:17, 4 * D:5 * D].to_broadcast([128, D]), r=["mod_scr"], w=["a2_bc"])
            P.op("dve", lambda e: e.scalar_tensor_tensor(out=a2_bc, in0=a2_bc, scalar=1.0, in1=g2_bc, op0=ALU.add, op1=ALU.mult),
                 r=["a2_bc", "g2_bc"], w=["a2_bc"])
            P.op("dve", lambda e: e.scalar_tensor_tensor(out=a2_s[0:NS, :], in0=mod_sb[0:NS, 4 * D:5 * D], scalar=1.0, in1=g2_bc[0:NS, :], op0=ALU.add, op1=ALU.mult),
                 r=["mod_sb", "g2_bc"], w=["a2_s"])
            persistD = A.off
            w_out_b = A.bf16(8 * D); w_out3 = w_out_b.rearrange("p (k n) -> p k n", k=8)
            w_f_b = A.bf16(8 * 2 * DFF); w_f3 = w_f_b.rearrange("p (k n) -> p k n", k=8)
            persistD1 = A.off
            wst = [A.f32(8 * 512), A.f32(8 * 512)]
            for cb in range(2 + 11):
                ws3 = wst[cb % 2].rearrange("p (k n) -> p k n", k=8)
                if cb < 2:
                    src = I["w_out"][:, cb * 512:(cb + 1) * 512]; dst = w_out3[:, :, cb * 512:(cb + 1) * 512]; dn = "w_out_b"
                else:
                    src = I["w_ffn_in"][:, (cb - 2) * 512:(cb - 1) * 512]; dst = w_f3[:, :, (cb - 2) * 512:(cb - 1) * 512]; dn = "w_f_b"
                P.dma("sp", ws3, src.rearrange("(k p) n -> p k n", p=128), w=[f"wst{cb % 2}"])
                P.op("pool" if cb % 2 else "dve", lambda e, ws3=ws3, dst=dst: e.tensor_copy(out=dst, in_=ws3), r=[f"wst{cb % 2}"], w=[dn])
            P.seed_after_staging()
            A.reset(persistD1)
            xt = [A.f32(D), A.f32(D)]
            cTt = [A.bf16(8 * 128), A.bf16(8 * 128)]
            x1t = A.f32(D); h2 = A.f32(D); small = A.f32(16)
            h2T = A.bf16(8 * 512); h2T3 = h2T.rearrange("p (k t) -> p k t", k=8)
            uT = A.bf16(22 * 512); uT3 = uT.rearrange("p (f t) -> p f t", f=22)
            gsb = [A.f32(512), A.f32(512)]

            def rms_mod(rows, xin, xin_n, a_t, a_n, sh_t, sh_n, out, out_n):
                ss, rs = small[0:rows, 0:1], small[0:rows, 1:2]
                P.op("act", lambda e: e.activation(out=out, in_=xin, func=AF.Square, accum_out=ss), r=[xin_n], w=[out_n, "ss"])
                P.op("dve", lambda e: e.tensor_scalar(out=rs, in0=ss, scalar1=1.0 / D, scalar2=1e-6, op0=ALU.mult, op1=ALU.add), r=["ss"], w=["rs"])
                P.op("act", lambda e: e.activation(out=rs, in_=rs, func=AF.Sqrt), r=["rs"], w=["rs"])
                P.op("dve", lambda e: e.reciprocal(out=rs, in_=rs), r=["rs"], w=["rs"])
                P.op("dve", lambda e: e.scalar_tensor_tensor(out=out, in0=xin, scalar=rs, in1=a_t, op0=ALU.mult, op1=ALU.mult), r=[xin_n, "rs", a_n], w=[out_n])
                if sh_t is not None:
                    P.op("pool", lambda e: e.tensor_tensor(out=out, in0=out, in1=sh_t, op=ALU.add), r=[out_n, sh_n], w=[out_n])

            groups = [(g4, [(g4 * 4 + j, 128) for j in range(4)]) for g4 in range(4)] + [(4, [(NT, NS)])]
            for g4, tiles in groups:
                ntok = sum(r_ for _, r_ in tiles)
                col = 0
                for (ti, rows) in tiles:
                    it = ti
                    xb = xt[it % 2]; xn = f"xt{it % 2}"; cb_ = cTt[it % 2]; cn = f"cTt{it % 2}"
                    cT3 = cb_.rearrange("p (k t) -> p k t", k=8)
                    smp = (rows == NS)
                    P.dma("sp", xb[0:rows, :], I["xs"] if smp else I["x_own"][ti * 128:(ti + 1) * 128, :], w=[xn])
                    P.dma("sp", cT3, S["cT"][ti], r=[f"cT{ti}"], w=[cn])
                    ga1_t = mod_sb[0:NS, 2 * D:3 * D] if smp else ga1_bc
                    for hb in range(2):
                        for k in range(8):
                            P.op("pe", lambda e, k=k, hb=hb, cT3=cT3, rows=rows: e.matmul(out=pb[hb][0:rows, :], lhsT=cT3[:, k, 0:rows], rhs=w_out3[:, k, hb * 512:(hb + 1) * 512], start=(k == 0), stop=(k == 7)),
                                 r=[cn, "w_out_b"], w=[PB[hb]])
                        P.op("dve", lambda e, hb=hb, rows=rows, ga1_t=ga1_t: e.tensor_tensor(out=x1t[0:rows, hb * 512:(hb + 1) * 512], in0=pb[hb][0:rows, :], in1=ga1_t[0:rows, hb * 512:(hb + 1) * 512], op=ALU.mult),
                             r=[PB[hb], "ga1_bc", "mod_sb"], w=["x1t"])
                    P.op("pool", lambda e, rows=rows, xb=xb: e.tensor_tensor(out=x1t[0:rows, :], in0=x1t[0:rows, :], in1=xb[0:rows, :], op=ALU.add), r=["x1t", xn], w=["x1t"])
                    P.dma("sp", S["x1"][ti, 0:rows, :], x1t[0:rows, :], r=["x1t"], w=[f"x1_{ti}"])
                    if smp:
                        rms_mod(rows, x1t[0:rows, :], "x1t", a2_s[0:rows, :], "a2_s", mod_sb[0:rows, 3 * D:4 * D], "mod_sb", h2[0:rows, :], "h2")
                    else:
                        rms_mod(rows, x1t, "x1t", a2_bc, "a2_bc", sh2_bc, "sh2_bc", h2, "h2")
                    for k in range(8):
                        bank = 2 + k // 4
                        P.op("pe", lambda e, k=k, bank=bank, rows=rows: e.transpose(out=pb[bank][:, (k % 4) * 128:(k % 4) * 128 + rows], in_=h2[0:rows, k * 128:(k + 1) * 128], identity=ident[0:rows, 0:rows]),
                             r=["h2", "ident"], w=[PB[bank]])
                    for half_ in range(2):
                        evac(alt(), h2T3[:, half_ * 4:(half_ + 1) * 4, col:col + rows], pb[2 + half_][:, :].rearrange("p (k t) -> p k t", k=4)[:, :, 0:rows], [PB[2 + half_]], ["h2T"])
                    col += rows
                for fb in range(22):
                    for which in range(2):
                        bank = 4 + which * 2 + fb % 2
                        c0 = which * DFF + fb * 128
                        for k in range(8):
                            P.op("pe", lambda e, k=k, bank=bank, c0=c0, ntok=ntok: e.matmul(out=pb[bank][:, 0:ntok], lhsT=w_f3[:, k, c0:c0 + 128], rhs=h2T3[:, k, 0:ntok], start=(k == 0), stop=(k == 7)),
                                 r=["h2T", "w_f_b"], w=[PB[bank]])
                    gs_ = gsb[fb % 2]; gn = f"gsb{fb % 2}"
                    P.op("act", lambda e, fb=fb, gs_=gs_, ntok=ntok: e.activation(out=gs_[:, 0:ntok], in_=pb[4 + fb % 2][:, 0:ntok], func=AF.Silu), r=[PB[4 + fb % 2]], w=[gn])
                    P.op("dve", lambda e, fb=fb, gs_=gs_, ntok=ntok: e.tensor_tensor(out=uT3[:, fb, 0:ntok], in0=gs_[:, 0:ntok], in1=pb[6 + fb % 2][:, 0:ntok], op=ALU.mult),
                         r=[gn, PB[6 + fb % 2]], w=["uT"])
                P.dma("sp", S["uT"][g4], uT3, r=["uT"], w=[f"uT{g4}"])

            P.barrier()
            A.reset(persistD)
            ga2_bc = A.f32(D); gf_bc = A.f32(D)
            P.dma("sp", gf_bc, I["g_final"].to_broadcast([128, D]), w=["gf_bc"])
            P.dma("sp", ga2_bc, S["mod"][16:17, 5 * D:6 * D].to_broadcast([128, D]), r=["mod_scr"], w=["ga2_bc"])
            w_o_b = A.bf16(22 * D); w_o3 = w_o_b.rearrange("p (k n) -> p k n", k=22)
            persistD2 = A.off
            wst = [A.f32(22 * 256), A.f32(22 * 256)]
            for cb in range(4):
                ws3 = wst[cb % 2].rearrange("p (k n) -> p k n", k=22)
                P.dma("sp", ws3, I["w_ffn_out"][:, cb * 256:(cb + 1) * 256].rearrange("(k p) n -> p k n", p=128), w=[f"wst{cb % 2}"])
                P.op("pool" if cb % 2 else "dve", lambda e, ws3=ws3, cb=cb: e.tensor_copy(out=w_o3[:, :, cb * 256:(cb + 1) * 256], in_=ws3), r=[f"wst{cb % 2}"], w=["w_o_b"])
            P.seed_after_staging()
            A.reset(persistD2)
            uTg = [A.bf16(22 * 512), A.bf16(22 * 512)]
            x1b = [A.f32(D), A.f32(D)]
            x2 = A.f32(D); small = A.f32(16)
            yb = [A.f32(D), A.f32(D)]
            for g4, tiles in groups:
                ug = uTg[g4 % 2]; un = f"uTg{g4 % 2}"
                ug3 = ug.rearrange("p (f t) -> p f t", f=22)
                P.dma("sp", ug3, S["uT"][g4], r=[f"uT{g4}"], w=[un])
                col = 0
                for (ti, rows) in tiles:
                    smp = (rows == NS)
                    xb = x1b[ti % 2]; xn = f"x1b{ti % 2}"; yo = yb[ti % 2]; yn = f"yb{ti % 2}"
                    P.dma("sp", xb[0:rows, :], S["x1"][ti, 0:rows, :], r=[f"x1_{ti}"], w=[xn])
                    ga2_t = mod_sb[0:NS, 5 * D:6 * D] if smp else ga2_bc
                    for hb in range(2):
                        for kf in range(22):
                            P.op("pe", lambda e, kf=kf, hb=hb, rows=rows, col=col, ug3=ug3, bk_=hb + 2 * (ti % 2): e.matmul(out=pb[bk_][0:rows, :], lhsT=ug3[:, kf, col:col + rows], rhs=w_o3[:, kf, hb * 512:(hb + 1) * 512], start=(kf == 0), stop=(kf == 21)),
                                 r=[un, "w_o_b"], w=[PB[hb + 2 * (ti % 2)]])
                        P.op("dve", lambda e, hb=hb, rows=rows, ga2_t=ga2_t, bk_=hb + 2 * (ti % 2): e.tensor_tensor(out=x2[0:rows, hb * 512:(hb + 1) * 512], in0=pb[bk_][0:rows, :], in1=ga2_t[0:rows, hb * 512:(hb + 1) * 512], op=ALU.mult),
                             r=[PB[hb + 2 * (ti % 2)], "ga2_bc", "mod_sb"], w=["x2"])
                    P.op("pool", lambda e, rows=rows, xb=xb: e.tensor_tensor(out=x2[0:rows, :], in0=x2[0:rows, :], in1=xb[0:rows, :], op=ALU.add), r=["x2", xn], w=["x2"])
                    rms_mod(rows, x2[0:rows, :], "x2", gf_bc[0:rows, :], "gf_bc", None, None, yo[0:rows, :], yn)
                    P.dma("sp", O["y_s"] if smp else O["y_own"][ti * 128:(ti + 1) * 128, :], yo[0:rows, :], r=[yn])
                    col += rows

        P.build(ctx)
        global LAST_PROG
        LAST_PROG = P
    return nc


def rope_table(pos):
    pos = np.asarray(pos, np.float64)[:, None]
    invA = 500000.0 ** (-np.arange(16, dtype=np.float64) / 16)
    invI = 500000.0 ** (-np.arange(8, dtype=np.float64) / 8)
    angA = (pos.astype(np.float32) * invA.astype(np.float32)[None, :]).astype(np.float32)
    angI = (pos.astype(np.float32) * invI.astype(np.float32)[None, :]).astype(np.float32)
    t = np.concatenate([np.tile(np.cos(angA), (1, 4)), np.tile(np.sin(angA), (1, 4)),
                        np.tile(np.cos(angI), (1, 8)), np.tile(np.sin(angI), (1, 8))], axis=1)
    return np.ascontiguousarray(t.astype(np.float32))


_NC_CACHE = {}


def kernel(x_prompt, x_sample, c_prompt, c_sample, cache_k, cache_v, cache_idx_k, page_table, state_conv, state_ssm,
           w_ada, b_ada, g_norm1, w_in, w_conv, a_log, dt_bias, g_gdn_norm, w_out, g_norm2, w_ffn_in, w_ffn_out, g_final):
    f = lambda a: np.ascontiguousarray(np.asarray(a, dtype=np.float32))
    x_prompt = f(x_prompt); x_sample = f(x_sample)
    w_in_p = np.ascontiguousarray(f(w_in)[0][:, PERM])
    wc_p = np.ascontiguousarray(np.concatenate([f(w_conv)[0][:, 512:1536], f(w_conv)[0][:, 0:512]], axis=1))
    jj, cc = np.meshgrid(np.arange(128), np.arange(128), indexing="ij")
    GCONST = np.ascontiguousarray(np.concatenate([
        (jj <= cc).astype(np.float32),
        np.ones((128, 128), np.float32),
        np.where(cc >= jj, 1e4, 0.0).astype(np.float32),
        np.where(cc < jj, -1e4, 0.0).astype(np.float32),
        np.tile(np.eye(128, dtype=np.float32), (1, 4))], axis=1))
    CAUS = np.ascontiguousarray(np.where(np.arange(128)[None, :] > np.arange(128)[:, None], -30000.0, 0.0).astype(np.float32))
    pp = np.arange(128)
    TSEL = np.zeros((128, 257), np.float32)
    TSEL[:, :256] = np.tile((np.arange(8)[None, :] == (pp // 16)[:, None]).astype(np.float32), (1, 32))
    TSEL[:, 256] = pp % 16
    cik2 = f(cache_idx_k).reshape(-1, 64); ck2 = f(cache_k).reshape(-1, 256); cv2 = f(cache_v).reshape(-1, 256)
    EYE16 = np.ascontiguousarray(np.tile(np.eye(16, dtype=np.float32).reshape(1, 256), (128, 1)))
    in_maps = []
    for c in range(8):
        b, s = c // 2, c % 2
        own = slice(s * HALF, (s + 1) * HALF); oth = slice((1 - s) * HALF, (2 - s) * HALF)
        m = {
            "x_own": f(x_prompt[b, own]), "x_oth": f(x_prompt[b, oth]),
            "cin": f(np.concatenate([np.asarray(c_sample)[c * NS:(c + 1) * NS], np.asarray(c_prompt)[b:b + 1]], 0)),
            "xs": f(x_sample[c * NS:(c + 1) * NS, 0]),
            "w_ada": f(w_ada)[0], "b_ada": f(b_ada), "g1": f(g_norm1), "w_in": w_in_p, "w_conv": f(w_conv)[0],
            "a_log": f(a_log), "dt_bias": f(dt_bias), "g_gdn": f(g_gdn_norm), "w_out": f(w_out)[0], "g2": f(g_norm2),
            "w_ffn_in": f(w_ffn_in)[0], "w_ffn_out": f(w_ffn_out)[0], "g_final": f(g_final)[None, :],
            "tab_own": rope_table(np.arange(s * HALF, (s + 1) * HALF)), "tab_oth": rope_table(np.arange((1 - s) * HALF, (2 - s) * HALF)),
            "tab_s": rope_table(np.full(NS, 2048)),
            "flags": np.tile(np.array([[float(s), (s - 1) * 30000.0, 0, 0]], np.float32), (128, 1)),
            "ident": np.eye(128, dtype=np.float32),
            "state_conv": f(np.asarray(state_conv)[0, c * NS:(c + 1) * NS]),
            "gconst": GCONST, "wc_p": wc_p,
            "state_ssm": f(np.asarray(state_ssm)[0, c * NS:(c + 1) * NS]), "eye16": EYE16, "caus": CAUS,
            "pt": np.ascontiguousarray(np.asarray(page_table, np.int32)[c * NS:(c + 1) * NS].reshape(1, NS * 16)), "tsel": TSEL,
            "cache_ik": cik2, "cache_k": ck2, "cache_v": cv2,
        }
        if os.environ.get('MK_NOTS'):
            for k_ in ("cache_ik", "cache_k", "cache_v"):
                m.pop(k_)
        in_maps.append(m)
    if "nc" not in _NC_CACHE:
        _NC_CACHE["nc"] = build_program()
    res = run_bass_kernel_spmd(_NC_CACHE["nc"], in_maps, core_ids=list(range(8)))
    R = res.results
    B = 4
    y_prompt = np.zeros((B, T, D), np.float32); nk = np.zeros((1, B, T, 2, 128), np.float32); nv = np.zeros_like(nk)
    nik = np.zeros((1, B, T, 64), np.float32); nconv = np.zeros((1, B, 3, 1536), np.float32); nssm = np.zeros((1, B, 4, 128, 128), np.float32)
    y_s = np.zeros((128, 1, D), np.float32); ks = np.zeros((1, 128, 1, 2, 128), np.float32); vs = np.zeros_like(ks)
    iks = np.zeros((1, 128, 1, 64), np.float32); convs = np.zeros((1, 128, 3, 1536), np.float32); ssms = np.zeros((1, 128, 4, 128, 128), np.float32)
    for c in range(8):
        b, s = c // 2, c % 2
        own = slice(s * HALF, (s + 1) * HALF)
        r = R[c]
        y_prompt[b, own] = r["y_own"]; nk[0, b, own] = r["k_own"].reshape(HALF, 2, 128); nv[0, b, own] = r["v_own"].reshape(HALF, 2, 128)
        nik[0, b, own] = r["ik_own"]
        if s == 1:
            nconv[0, b] = r["conv_tail"]; nssm[0, b] = r["ssm_fin"]
        sl = slice(c * NS, (c + 1) * NS)
        y_s[sl, 0] = r["y_s"]; ks[0, sl, 0] = r["k_s"].reshape(NS, 2, 128); vs[0, sl, 0] = r["v_s"].reshape(NS, 2, 128)
        iks[0, sl, 0] = r["ik_s"]; convs[0, sl] = r["conv_s"]; ssms[0, sl] = r["ssm_s"]
    return (y_prompt, y_s, nk, nv, nik, nconv, nssm, ks, vs, iks, convs, ssms)
```
